# Optimizing a Trainium2 kernel written in Bass

```python
import math
import jax, jax.numpy as jnp
from jax import lax
import numpy as np

D_MODEL = 1024
BATCH = 32
SEQ = 2048
DEPTH = 4

N_MIXERS = 3
N_SSD_LAYERS = (DEPTH + 2) // N_MIXERS
N_S5_LAYERS = (DEPTH + 1) // N_MIXERS
N_DSA_LAYERS = DEPTH // N_MIXERS

NORM_EPS = 1e-6
FFN_DIM = 2816

SSD_INNER = 2 * D_MODEL
SSD_HEAD_DIM = 64
SSD_HEADS = SSD_INNER // SSD_HEAD_DIM
SSD_GROUPS = 8
SSD_HEADS_PER_GROUP = SSD_HEADS // SSD_GROUPS
SSD_STATE = 128
SSD_CONV = 4
SSD_CONV_DIM = SSD_INNER + 2 * SSD_GROUPS * SSD_STATE
SSD_PROJ = SSD_INNER + SSD_CONV_DIM + SSD_HEADS
SCAN_CHUNK_MAX = 256

S5_GROUP_WIDTH = 16
S5_GROUPS = D_MODEL // S5_GROUP_WIDTH
S5_STATE = 64

ATTN_HEADS = 16
ATTN_HEAD_DIM = D_MODEL // ATTN_HEADS
IDX_HEADS = 8
IDX_DIM = 64
TOPK_MAX = 256
QUERY_BLOCK = 128
ROPE_THETA = 10000.0
DSA_Q = ATTN_HEADS * ATTN_HEAD_DIM
DSA_SPLITS = (DSA_Q,
              DSA_Q + ATTN_HEAD_DIM,
              DSA_Q + 2 * ATTN_HEAD_DIM,
              DSA_Q + 2 * ATTN_HEAD_DIM + IDX_HEADS * IDX_DIM,
              DSA_Q + 2 * ATTN_HEAD_DIM + IDX_HEADS * IDX_DIM + IDX_DIM)
DSA_PROJ = DSA_SPLITS[-1] + IDX_HEADS

kernel_name = "hybrid_ssd_s5_dsa_macaron"


def rmsnorm(x, g):
    xf = x.astype(jnp.float32)
    xf = xf * lax.rsqrt(jnp.mean(xf * xf, axis=-1, keepdims=True) + NORM_EPS)
    return xf.astype(x.dtype) * g


def swiglu(h, w_in, w_out):
    gate, up = jnp.split(h @ w_in, 2, axis=-1)
    return (jax.nn.silu(gate) * up) @ w_out


def rope(t, pos):
    d = t.shape[-1]
    half = d // 2
    inv = ROPE_THETA ** (-jnp.arange(half, dtype=jnp.float32) / half)
    ang = pos.astype(jnp.float32)[:, None] * inv
    ang = ang.reshape((1, pos.shape[0]) + (1,) * (t.ndim - 3) + (half,))
    cos, sin = jnp.cos(ang), jnp.sin(ang)
    tf = t.astype(jnp.float32)
    t1, t2 = tf[..., :half], tf[..., half:]
    return jnp.concatenate([t1 * cos - t2 * sin, t2 * cos + t1 * sin], axis=-1).astype(t.dtype)


def chunk_len(L):
    return math.gcd(L, SCAN_CHUNK_MAX)


def causal_depthwise_conv(u, w, b):
    K = w.shape[0]
    L = u.shape[1]
    up = jnp.pad(u, ((0, 0), (K - 1, 0), (0, 0)))
    out = up[:, 0:L] * w[0]
    for k in range(1, K):
        out = out + up[:, k:k + L] * w[k]
    return out + b


def ssd_chunked_scan(xs, dt, A, bm, cm):
    bsz, L = xs.shape[:2]
    Q = chunk_len(L)
    n_chunks = L // Q

    def to_chunks(t):
        return jnp.moveaxis(t.reshape((bsz, n_chunks, Q) + t.shape[2:]), 1, 0)

    causal = jnp.tril(jnp.ones((Q, Q), dtype=bool))

    def step(state, inp):
        xc, dtc, bc, cc = inp
        cs = jnp.cumsum(dtc * A, axis=1)
        seg = cs[:, :, None] - cs[:, None]
        decay = jnp.exp(jnp.where(causal[None, :, :, None, None], seg, -jnp.inf))
        cb = jnp.einsum('btgn,bsgn->btsg', cc, bc)
        dtx = dtc[..., None] * xc
        y_diag = jnp.einsum('btsgk,bsgkp->btgkp', cb[..., None] * decay, dtx)
        y_off = jnp.einsum('btgn,bgkpn->btgkp', cc, state) * jnp.exp(cs)[..., None]
        to_end = jnp.exp(cs[:, -1:] - cs)
        new_state = (state * jnp.exp(cs[:, -1])[..., None, None]
                     + jnp.einsum('bsgn,bsgkp->bgkpn', bc, to_end[..., None] * dtx))
        return new_state, y_diag + y_off

    state0 = jnp.zeros((bsz, SSD_GROUPS, SSD_HEADS_PER_GROUP, SSD_HEAD_DIM, SSD_STATE), jnp.float32)
    _, ys = lax.scan(step, state0, (to_chunks(xs), to_chunks(dt), to_chunks(bm), to_chunks(cm)))
    return jnp.moveaxis(ys, 0, 1).reshape(xs.shape)


def ssd_mixer(h, in_proj, conv_w, conv_b, dt_bias, a_log, d, gate_norm, out_proj):
    bsz, L, _ = h.shape
    f32 = jnp.float32
    z, xbc, dt = jnp.split(h @ in_proj, [SSD_INNER, SSD_INNER + SSD_CONV_DIM], axis=-1)
    xbc = jax.nn.silu(causal_depthwise_conv(xbc, conv_w, conv_b))
    xs, bm, cm = jnp.split(xbc, [SSD_INNER, SSD_INNER + SSD_GROUPS * SSD_STATE], axis=-1)
    xs = xs.astype(f32).reshape(bsz, L, SSD_GROUPS, SSD_HEADS_PER_GROUP, SSD_HEAD_DIM)
    bm = bm.astype(f32).reshape(bsz, L, SSD_GROUPS, SSD_STATE)
    cm = cm.astype(f32).reshape(bsz, L, SSD_GROUPS, SSD_STATE)
    dt = jax.nn.softplus(dt.astype(f32) + dt_bias.astype(f32)).reshape(bsz, L, SSD_GROUPS, SSD_HEADS_PER_GROUP)
    A = -jnp.exp(a_log.astype(f32)).reshape(SSD_GROUPS, SSD_HEADS_PER_GROUP)
    y = ssd_chunked_scan(xs, dt, A, bm, cm)
    y = y + d.astype(f32).reshape(SSD_GROUPS, SSD_HEADS_PER_GROUP, 1) * xs
    y = y.reshape(bsz, L, SSD_GROUPS, -1) * jax.nn.silu(z.astype(f32)).reshape(bsz, L, SSD_GROUPS, -1)
    y = y * lax.rsqrt(jnp.mean(y * y, axis=-1, keepdims=True) + NORM_EPS)
    y = y.reshape(bsz, L, SSD_INNER).astype(h.dtype) * gate_norm
    return y @ out_proj


def s5_binop(e1, e2):
    a1, b1 = e1
    a2, b2 = e2
    return a1 * a2, a2 * b1 + b2


def s5_mixer(h, b_re, b_im, c_re, c_im, lam_re, lam_im, log_step, d, glu_w, glu_b):
    bsz, L, dm = h.shape
    f32 = jnp.float32
    lam = lax.complex(lam_re.astype(f32), lam_im.astype(f32))
    step = jnp.exp(log_step.astype(f32))[:, None]
    lam_bar = jnp.exp(lam * step)
    b_bar = ((lam_bar - 1.0) / lam)[..., None] * lax.complex(b_re.astype(f32), b_im.astype(f32))
    c = lax.complex(c_re.astype(f32), c_im.astype(f32))
    u = h.astype(f32).reshape(bsz, L, S5_GROUPS, S5_GROUP_WIDTH)
    Q = chunk_len(L)
    n_chunks = L // Q
    u_chunks = jnp.moveaxis(u.reshape(bsz, n_chunks, Q, S5_GROUPS, S5_GROUP_WIDTH), 1, 0)
    a = jnp.broadcast_to(lam_bar, (1, Q, S5_GROUPS, S5_STATE))

    def step_fn(h0, uc):
        bu = jnp.einsum('bqgc,gpc->bqgp', uc.astype(jnp.complex64), b_bar)
        bu = bu.at[:, 0].add(lam_bar * h0)
        _, states = lax.associative_scan(s5_binop, (a, bu), axis=1)
        yc = jnp.einsum('gcp,bqgp->bqgc', c, states).real
        return states[:, -1], yc

    h0 = jnp.zeros((bsz, S5_GROUPS, S5_STATE), jnp.complex64)
    _, ys = lax.scan(step_fn, h0, u_chunks)
    y = jnp.moveaxis(ys, 0, 1).reshape(bsz, L, dm) + d.astype(f32) * u.reshape(bsz, L, dm)
    g = jax.nn.gelu(y).astype(h.dtype)
    val, gate = jnp.split(g @ glu_w + glu_b, 2, axis=-1)
    return val * jax.nn.sigmoid(gate)


def dsa_mixer(h, w_in, w_out):
    bsz, L, _ = h.shape
    f32 = jnp.float32
    q, k, v, qi, ki, wi = jnp.split(h @ w_in, list(DSA_SPLITS), axis=-1)
    pos = jnp.arange(L)
    q = rope(q.reshape(bsz, L, ATTN_HEADS, ATTN_HEAD_DIM), pos)
    k = rope(k, pos)
    qi = rope(qi.reshape(bsz, L, IDX_HEADS, IDX_DIM), pos)
    ki = rope(ki, pos)
    wi = wi * (IDX_HEADS ** -0.5 * IDX_DIM ** -0.5)
    top_k = min(TOPK_MAX, L // 4)
    n_blocks = L // QUERY_BLOCK

    def blocks(t):
        return jnp.moveaxis(t.reshape((bsz, n_blocks, QUERY_BLOCK) + t.shape[2:]), 1, 0)

    def attend_block(inp):
        qb, qib, wb, tb = inp
        idx = jnp.einsum('bqhs,bqh->bqs', jax.nn.relu(jnp.einsum('bqhd,bsd->bqhs', qib, ki)), wb).astype(f32)
        idx = jnp.where(pos[None, None, :] <= tb[None, :, None], idx, -jnp.inf)
        _, sel = lax.top_k(idx, top_k)
        k_sel = jax.vmap(lambda kk, ii: kk[ii])(k, sel)
        v_sel = jax.vmap(lambda vv, ii: vv[ii])(v, sel)
        logits = jnp.einsum('bqhd,bqkd->bqhk', qb, k_sel).astype(f32) * (ATTN_HEAD_DIM ** -0.5)
        logits = jnp.where((sel <= tb[None, :, None])[:, :, None, :], logits, -jnp.inf)
        p = jax.nn.softmax(logits, axis=-1).astype(v.dtype)
        return jnp.einsum('bqhk,bqkd->bqhd', p, v_sel)

    o = lax.map(attend_block, (blocks(q), blocks(qi), blocks(wi), pos.reshape(n_blocks, QUERY_BLOCK)))
    o = jnp.moveaxis(o, 0, 1).reshape(bsz, L, ATTN_HEADS * ATTN_HEAD_DIM)
    return o @ w_out


def setup_inputs(seed: int = 0) -> dict:
    key = jax.random.key(seed)
    ks = iter(jax.random.split(key, 48))
    f32 = jnp.float32

    def nrm(shape, scale):
        return jax.random.normal(next(ks), shape, f32) * scale

    def gain(shape):
        return 1.0 + 0.02 * jax.random.normal(next(ks), shape, f32)

    x = nrm((BATCH, SEQ, D_MODEL), 1.0)
    ffn1_norm = gain((DEPTH, D_MODEL))
    ffn1_w_in = nrm((DEPTH, D_MODEL, 2 * FFN_DIM), D_MODEL ** -0.5)
    ffn1_w_out = nrm((DEPTH, FFN_DIM, D_MODEL), FFN_DIM ** -0.5)
    mix_norm = gain((DEPTH, D_MODEL))
    ffn2_norm = gain((DEPTH, D_MODEL))
    ffn2_w_in = nrm((DEPTH, D_MODEL, 2 * FFN_DIM), D_MODEL ** -0.5)
    ffn2_w_out = nrm((DEPTH, FFN_DIM, D_MODEL), FFN_DIM ** -0.5)

    nA = N_SSD_LAYERS
    ssd_in_proj = nrm((nA, D_MODEL, SSD_PROJ), D_MODEL ** -0.5)
    ssd_conv_w = nrm((nA, SSD_CONV, SSD_CONV_DIM), SSD_CONV ** -0.5)
    ssd_conv_b = nrm((nA, SSD_CONV_DIM), 0.01)
    dt0 = jnp.exp(jax.random.uniform(next(ks), (nA, SSD_HEADS), f32, math.log(1e-3), math.log(1e-1)))
    ssd_dt_bias = dt0 + jnp.log(-jnp.expm1(-dt0))
    ssd_a_log = jnp.log(jax.random.uniform(next(ks), (nA, SSD_HEADS), f32, 1.0, 16.0))
    ssd_d = gain((nA, SSD_HEADS))
    ssd_gate_norm = gain((nA, SSD_INNER))
    ssd_out_proj = nrm((nA, SSD_INNER, D_MODEL), SSD_INNER ** -0.5)

    nB = N_S5_LAYERS
    s5_b_re = nrm((nB, S5_GROUPS, S5_STATE, S5_GROUP_WIDTH), (2 * S5_GROUP_WIDTH) ** -0.5)
    s5_b_im = nrm((nB, S5_GROUPS, S5_STATE, S5_GROUP_WIDTH), (2 * S5_GROUP_WIDTH) ** -0.5)
    s5_c_re = nrm((nB, S5_GROUPS, S5_GROUP_WIDTH, S5_STATE), (2 * S5_STATE) ** -0.5)
    s5_c_im = nrm((nB, S5_GROUPS, S5_GROUP_WIDTH, S5_STATE), (2 * S5_STATE) ** -0.5)
    s5_lam_re = -0.5 + nrm((nB, S5_GROUPS, S5_STATE), 0.01)
    s5_lam_im = math.pi * jnp.arange(S5_STATE, dtype=f32) + nrm((nB, S5_GROUPS, S5_STATE), 0.01)
    s5_log_step = jax.random.uniform(next(ks), (nB, S5_GROUPS), f32, math.log(1e-3), math.log(1e-1))
    s5_d = nrm((nB, D_MODEL), 1.0)
    s5_glu_w = nrm((nB, D_MODEL, 2 * D_MODEL), D_MODEL ** -0.5)
    s5_glu_b = nrm((nB, 2 * D_MODEL), 0.01)

    nC = N_DSA_LAYERS
    dsa_in_proj = nrm((nC, D_MODEL, DSA_PROJ), D_MODEL ** -0.5)
    dsa_out_proj = nrm((nC, DSA_Q, D_MODEL), DSA_Q ** -0.5)

    final_norm = gain((D_MODEL,))
    return {"x": x, "ffn1_norm": ffn1_norm, "ffn1_w_in": ffn1_w_in, "ffn1_w_out": ffn1_w_out,
            "mix_norm": mix_norm, "ffn2_norm": ffn2_norm, "ffn2_w_in": ffn2_w_in, "ffn2_w_out": ffn2_w_out,
            "ssd_in_proj": ssd_in_proj, "ssd_conv_w": ssd_conv_w, "ssd_conv_b": ssd_conv_b,
            "ssd_dt_bias": ssd_dt_bias, "ssd_a_log": ssd_a_log, "ssd_d": ssd_d,
            "ssd_gate_norm": ssd_gate_norm, "ssd_out_proj": ssd_out_proj,
            "s5_b_re": s5_b_re, "s5_b_im": s5_b_im, "s5_c_re": s5_c_re, "s5_c_im": s5_c_im,
            "s5_lam_re": s5_lam_re, "s5_lam_im": s5_lam_im, "s5_log_step": s5_log_step,
            "s5_d": s5_d, "s5_glu_w": s5_glu_w, "s5_glu_b": s5_glu_b,
            "dsa_in_proj": dsa_in_proj, "dsa_out_proj": dsa_out_proj, "final_norm": final_norm}


def reference(x, ffn1_norm, ffn1_w_in, ffn1_w_out, mix_norm, ffn2_norm, ffn2_w_in, ffn2_w_out,
              ssd_in_proj, ssd_conv_w, ssd_conv_b, ssd_dt_bias, ssd_a_log, ssd_d, ssd_gate_norm, ssd_out_proj,
              s5_b_re, s5_b_im, s5_c_re, s5_c_im, s5_lam_re, s5_lam_im, s5_log_step, s5_d, s5_glu_w, s5_glu_b,
              dsa_in_proj, dsa_out_proj, final_norm):
    for i in range(DEPTH):
        j = i // N_MIXERS
        kind = i % N_MIXERS
        x = x + 0.5 * swiglu(rmsnorm(x, ffn1_norm[i]), ffn1_w_in[i], ffn1_w_out[i])
        h = rmsnorm(x, mix_norm[i])
        if kind == 0:
            m = ssd_mixer(h, ssd_in_proj[j], ssd_conv_w[j], ssd_conv_b[j], ssd_dt_bias[j],
                          ssd_a_log[j], ssd_d[j], ssd_gate_norm[j], ssd_out_proj[j])
        elif kind == 1:
            m = s5_mixer(h, s5_b_re[j], s5_b_im[j], s5_c_re[j], s5_c_im[j], s5_lam_re[j],
                         s5_lam_im[j], s5_log_step[j], s5_d[j], s5_glu_w[j], s5_glu_b[j])
        else:
            m = dsa_mixer(h, dsa_in_proj[j], dsa_out_proj[j])
        x = x + m
        x = x + 0.5 * swiglu(rmsnorm(x, ffn2_norm[i]), ffn2_w_in[i], ffn2_w_out[i])
    return rmsnorm(x, final_norm)
```

```python
import math
import os
from contextlib import ExitStack

import numpy as np
import concourse.bass as bass
import concourse.mybir as mybir
from concourse.bass_utils import run_bass_kernel_spmd

F32 = mybir.dt.float32
BF16 = mybir.dt.bfloat16
AF = mybir.ActivationFunctionType
ALU = mybir.AluOpType
AX = mybir.AxisListType

D = 1024
DC = 8
FFN = 2816
FC = 22
EPS = 1e-6
N_CORES = 8

ENGS = ("pe", "act", "dve", "pool", "sp")
EPOCH = 12000


class Prog:
    def __init__(self, nc, es):
        self.nc = nc
        self.es = es
        self.eng = {"pe": nc.tensor, "act": nc.scalar, "dve": nc.vector,
                    "pool": nc.gpsimd, "sp": nc.sync}
        self.nsem = 0
        self.esem = {}
        self.ecnt = {}
        self.eepoch = {}
        for e in ENGS:
            self.eepoch[e] = 0
            self.ecnt[e] = 0
            self.esem[e] = self._newsem()
        self.seen = {e: {} for e in ENGS}
        self.res = {}
        self.dsem = {}
        self.free_ev = {}
        self.ninst = 0

    def _newsem(self):
        self.nsem += 1
        return self.es.enter_context(self.nc.semaphore("s%d" % self.nsem))

    def _need(self, eng, deps):
        out = {}
        for ev in deps:
            if ev is None:
                continue
            sem, val, kind = ev
            if kind == "pe" and eng == "pe":
                continue
            nm = sem.name
            if self.seen[eng].get(nm, 0) >= val:
                continue
            if nm not in out or out[nm][1] < val:
                out[nm] = ev
        return list(out.values())

    def I(self, eng, fn, R=(), W=(), dma=None):
        pr = [r for r in R if isinstance(r, str) and r.startswith("pb")]
        if pr:
            R = [r for r in R if r not in pr]
            W = list(W) + [r for r in pr if r not in W]
        deps = []
        for r in R:
            ent = self.res.get(r)
            if ent is not None:
                deps.append(ent[0])
        for w in W:
            ent = self.res.get(w)
            if ent is not None:
                deps.append(ent[0])
                deps.extend(ent[1].values())
            else:
                deps.extend(self.free_ev.values())
        need = self._need(eng, deps)
        e = self.eng[eng]
        for sem, val, kind in need:
            e.wait_ge(sem, val)
            self.seen[eng][sem.name] = val
        ins = fn(e)
        if dma is not None:
            ent = self.dsem.get(dma)
            if ent is None:
                ent = [self._newsem(), 0]
                self.dsem[dma] = ent
            ent[1] += 16
            ins.then_inc(ent[0], 16)
            ev = (ent[0], ent[1], "dma")
        else:
            if self.ecnt[eng] >= EPOCH:
                self.esem[eng] = self._newsem()
                self.ecnt[eng] = 0
                self.eepoch[eng] += 1
            self.ecnt[eng] += 1
            ins.then_inc(self.esem[eng], 1)
            ev = (self.esem[eng], self.ecnt[eng], eng)
        for r in R:
            ent = self.res.get(r)
            if ent is None:
                ent = [None, {}]
                self.res[r] = ent
            ent[1][ev[0].name] = ev
        for w in W:
            self.res[w] = [ev, {}]
        self.ninst += 1
        return ev

    def release(self, keys):
        for k in keys:
            ent = self.res.pop(k, None)
            if ent is None:
                continue
            for ev in [ent[0]] + list(ent[1].values()):
                if ev is None:
                    continue
                nm = ev[0].name
                if nm not in self.free_ev or self.free_ev[nm][1] < ev[1]:
                    self.free_ev[nm] = ev

    def release_prefix(self, prefixes):
        ks = [k for k in self.res if (k[0] if isinstance(k, tuple) else k) in prefixes]
        self.release(ks)

    def finish(self):
        allev = list(self.free_ev.values())
        for ent in self.res.values():
            allev.append(ent[0])
            allev.extend(ent[1].values())
        for eng in ENGS:
            for sem, val, kind in self._need(eng, allev):
                if kind == eng and eng != "sp":
                    pass
                self.eng[eng].wait_ge(sem, val)
                self.seen[eng][sem.name] = val


_UID = [0]


def sb(nc, es, name, shape, dt):
    _UID[0] += 1
    return es.enter_context(nc.sbuf_tensor("%s_u%d" % (name, _UID[0]), shape, dt))


def ps(nc, es, name, shape, dt):
    return es.enter_context(nc.psum_tensor(name, shape, dt))


class Cfg:
    def __init__(self, nseq=4, L=2048, layers=(0, 1, 2, 0), do_ffn=True):
        self.nseq = nseq
        self.L = L
        self.layers = tuple(layers)
        self.do_ffn = do_ffn
        self.depth = len(self.layers)
        self.n_ssd = sum(1 for k in self.layers if k == 0)
        self.n_s5 = sum(1 for k in self.layers if k == 1)
        self.n_dsa = sum(1 for k in self.layers if k == 2)


TWO_PI = 2.0 * math.pi
GELU_C0 = math.sqrt(2.0 / math.pi)
GELU_C1 = 0.044715
MAGIC = 12582912.0


def build(cfg):
    nc = bass.Bass("TRN2", target_bir_lowering=False)
    L = cfg.L
    NS = cfg.nseq
    NT = L // 512
    NB = L // 128
    depth = cfg.depth

    def din(name, shape, dt=F32):
        return nc.dram_tensor(name, list(shape), dt, kind="ExternalInput").ap()

    x_d = din("x", [NS, L, D])
    y_d = nc.dram_tensor("y", [NS, L, D], F32, kind="ExternalOutput").ap()
    gains_d = din("gains", [128, 3 * depth + 1, DC])
    ident_d = din("ident", [128, 128])
    wi_d = din("ffn_wi", [2 * depth, FC, 128, DC * 256])
    wo_d = din("ffn_wo", [2 * depth, 2, DC, 128, 11 * 128])
    if cfg.n_s5:
        n5 = cfg.n_s5
        s5p_d = din("s5_par", [n5, 3, 128, 64])
        s5b_d = din("s5_b", [n5, 2, 64, 1024])
        s5c_d = din("s5_c", [n5, 2, 128, 1024])
        s5k_d = din("s5_k", [128, 16])
        s5d_d = din("s5_dv", [n5, 128, 24])
        s5w_d = din("s5_glu", [n5, DC, 128, DC * 256])
        iota_d = din("iota16", [128, L], mybir.dt.int16)

    if cfg.n_ssd:
        n0 = cfg.n_ssd
        ssd_win_d = din("ssd_win", [n0, 24, 128, DC * 256])
        ssd_wdt_d = din("ssd_wdt", [n0, 128, DC * 32])
        ssd_wout_d = din("ssd_wout", [n0, DC, 128, 16 * 128])
        ssd_cw_d = din("ssd_cw", [n0, 128, 32 * 5])
        ssd_hp_d = din("ssd_hp", [n0, 128, 3 * 32 + 16])
        ssd_k_d = din("ssd_k", [128, 128 + 512 + 128])

    if cfg.n_dsa:
        n2 = cfg.n_dsa
        dsa_win_d = din("dsa_win", [n2, 14, 128, DC * 256])
        dsa_wvw_d = din("dsa_wvw", [n2, 128, DC * 72])
        dsa_wo_d = din("dsa_wo", [n2, 4, 128, DC * 256])
        dsa_rope_d = din("dsa_rope", [2, 128, L])
        dsa_k_d = din("dsa_k", [128, 640])

    with ExitStack() as es:
        P = Prog(nc, es)
        I = P.I
        xT = sb(nc, es, "xT", [128, DC, L], F32)
        hT = sb(nc, es, "hT", [128, DC, L], BF16)
        gains = sb(nc, es, "gains_sb", [128, 3 * depth + 1, DC], F32)
        ident = sb(nc, es, "ident_sb", [128, 128], F32)
        ones_bf = sb(nc, es, "ones_bf", [128, 128], BF16)
        epsb = sb(nc, es, "epsb", [128, 1], F32)
        negpi = sb(nc, es, "negpi", [128, 1], F32)
        sq = [sb(nc, es, "sq%d" % i, [128, 512], BF16) for i in range(2)]
        tA = [sb(nc, es, "tA%d" % i, [128, 512], F32) for i in range(2)]
        tB = [sb(nc, es, "tB%d" % i, [128, 512], F32) for i in range(2)]
        NWI = 2
        wi = [sb(nc, es, "wi%d" % i, [128, DC, 256], BF16) for i in range(NWI)]
        pb = [ps(nc, es, "pb%d" % i, [128, 512], F32) for i in range(7)]
        pbh = ps(nc, es, "pbh", [128, 1024], BF16)
        ident_bf = sb(nc, es, "ident_bf", [128, 128], BF16)

        cnt = {"wi": 0, "wo": 0, "sq": 0, "tA": 0, "tB": 0, "xin": 0, "pg": 0, "po": 0}

        I("sp", lambda e: e.dma_start(out=gains[:], in_=gains_d), W=["gains"], dma="gains")
        I("sp", lambda e: e.dma_start(out=ident[:], in_=ident_d), W=["ident"], dma="ident")
        I("dve", lambda e: e.memset(ones_bf[:], 1.0), W=["ones"])
        I("dve", lambda e: e.tensor_copy(out=ident_bf[:], in_=ident[:]), R=["ident"], W=["ident_bf"])
        oneb = sb(nc, es, "oneb", [128, 1], F32)
        I("dve", lambda e: e.memset(oneb[:], 1.0), W=["oneb"])
        I("dve", lambda e: e.memset(epsb[:], EPS), W=["epsb"])
        I("dve", lambda e: e.memset(negpi[:], -math.pi), W=["negpi"])
        halfpi = sb(nc, es, "halfpi", [128, 1], F32)
        I("dve", lambda e: e.memset(halfpi[:], math.pi / 2), W=["negpi"])

        def nxt(name, n=2):
            k = cnt[name] % n
            cnt[name] += 1
            return k

        def rmsnorm_to(dst_fn, gidx, dst_key_fn, tt):
            ts = slice(tt * 512, (tt + 1) * 512)
            pn = pb[6]
            for c in range(DC):
                k = nxt("sq")
                I("act", lambda e, c=c, k=k: e.activation(out=sq[k][:], in_=xT[:, c, ts], func=AF.Square),
                  R=[("xT", c, tt)], W=[("sq", k)])
                I("pe", lambda e, c=c, k=k: e.matmul(pn[:], lhsT=ones_bf[:], rhs=sq[k][:],
                                                      start=(c == 0), stop=(c == DC - 1)),
                  R=[("sq", k), "ones"], W=["pb6"])
            k = nxt("tA")
            I("act", lambda e, k=k: e.activation(out=tA[k][:], in_=pn[:], func=AF.Sqrt,
                                                 scale=1.0 / D, bias=epsb[:]),
              R=["pb6", "epsb"], W=[("tA", k)])
            I("dve", lambda e, k=k: e.reciprocal(out=tA[k][:], in_=tA[k][:]),
              R=[("tA", k)], W=[("tA", k)])
            for c in range(DC):
                I("dve", lambda e, c=c, k=k: e.scalar_tensor_tensor(
                    out=dst_fn(c), in0=xT[:, c, ts], scalar=gains[:, gidx, c:c + 1], in1=tA[k][:],
                    op0=ALU.mult, op1=ALU.mult),
                  R=[("xT", c, tt), ("tA", k), "gains"], W=[dst_key_fn(c)])

        def norm_to_hT(gidx):
            for tt in range(NT):
                ts = slice(tt * 512, (tt + 1) * 512)
                rmsnorm_to(lambda c, ts=ts: hT[:, c, ts], gidx, lambda c, tt=tt: ("hT", c, tt), tt)

        def load_w256(src_ap, key="wi"):
            s = nxt("wi", NWI)
            I("pool", lambda e, s=s: e.dma_start(out=wi[s][:], in_=src_ap.rearrange("p (k f) -> p k f", k=DC)),
              W=[("wi", s)], dma=("wi", s))
            return s

        def mm_pair(s, src, src_key_fn, tt):
            ts = slice(tt * 512, (tt + 1) * 512)
            b = nxt("pg")
            pg, pu = pb[2 * b], pb[2 * b + 1]
            for k in range(DC):
                I("pe", lambda e, k=k: e.matmul(pg[:], lhsT=wi[s][:, k, 0:128], rhs=src[:, k, ts],
                                                start=(k == 0), stop=(k == DC - 1)),
                  R=[("wi", s), src_key_fn(k, tt)], W=["pb%d" % (2 * b)])
            for k in range(DC):
                I("pe", lambda e, k=k: e.matmul(pu[:], lhsT=wi[s][:, k, 128:256], rhs=src[:, k, ts],
                                                start=(k == 0), stop=(k == DC - 1)),
                  R=[("wi", s), src_key_fn(k, tt)], W=["pb%d" % (2 * b + 1)])
            return pg, pu, "pb%d" % (2 * b), "pb%d" % (2 * b + 1)

        def ffn(fidx, gidx):
            norm_to_hT(gidx)
            with ExitStack() as sc:
                aT = sb(nc, sc, "aT", [128, 11, L], BF16)
                NWO = 3
                wo = [sb(nc, sc, "wo%d" % i, [128, 11, 128], BF16) for i in range(NWO)]
                for hf in range(2):
                    for j in range(11):
                        fc = hf * 11 + j
                        s = load_w256(wi_d[fidx, fc])
                        for tt in range(NT):
                            ts = slice(tt * 512, (tt + 1) * 512)
                            pg, pu, kg, ku = mm_pair(s, hT, lambda k, tt: ("hT", k, tt), tt)
                            g = nxt("tB")
                            I("act", lambda e, g=g, pg=pg: e.activation(out=tB[g][:], in_=pg[:], func=AF.Silu),
                              R=[kg], W=[("tB", g)])
                            I("dve", lambda e, g=g, pu=pu, j=j, ts=ts: e.tensor_tensor(
                                out=aT[:, j, ts], in0=tB[g][:], in1=pu[:], op=ALU.mult),
                              R=[("tB", g), ku], W=[("aT", j, tt)])
                    for c in range(DC):
                        s = nxt("wo", NWO)
                        I("pool", lambda e, s=s, c=c, hf=hf: e.dma_start(
                            out=wo[s][:], in_=wo_d[fidx, hf, c].rearrange("p (j d) -> p j d", j=11)),
                          W=[("wo", s)], dma=("wo", s))
                        for tt in range(NT):
                            ts = slice(tt * 512, (tt + 1) * 512)
                            b = nxt("po")
                            po = pb[4 + b]
                            for j in range(11):
                                I("pe", lambda e, s=s, j=j, ts=ts, po=po: e.matmul(
                                    po[:], lhsT=wo[s][:, j, :], rhs=aT[:, j, ts],
                                    start=(j == 0), stop=(j == 10)),
                                  R=[("wo", s), ("aT", j, tt)], W=["pb%d" % (4 + b)])
                            I("dve", lambda e, po=po, c=c, ts=ts: e.scalar_tensor_tensor(
                                out=xT[:, c, ts], in0=po[:], scalar=0.5, in1=xT[:, c, ts],
                                op0=ALU.mult, op1=ALU.add),
                              R=["pb%d" % (4 + b), ("xT", c, tt)], W=[("xT", c, tt)])
                P.release_prefix({"aT", "wo"})


        def ssd_mixer(j0, gidx):
            norm_to_hT(gidx)
            TT2 = 256
            with ExitStack() as sc:
                def T(name, shape, dt=F32):
                    return sb(nc, sc, "ssd_" + name, shape, dt)
                xc = T("xc", [128, 32, TT2], BF16)
                zT = T("zT", [128, 16, TT2], BF16)
                ynT = T("ynT", [128, 16, TT2], BF16)
                halo = T("halo", [128, 32, 3])
                xp = [T("xp%d" % i, [128, 3 + TT2]) for i in range(1)]
                acc = [T("acc%d" % i, [128, TT2]) for i in range(1)]
                wdt = T("wdt", [128, DC, 32], BF16)
                cw = T("cw", [128, 32, 5])
                hp = T("hp", [128, 3 * 32 + 16])
                kc = T("kc", [128, 128 + 512 + 128])
                Aneg = T("Aneg", [128, 32])
                dtr = T("dtr", [128, 32])
                dtv = T("dtv", [128, 32])
                av = T("av", [128, 32])
                ncs = T("ncs", [128, 32])
                ecs = T("ecs", [128, 32])
                cl = T("cl", [128, 32])
                wl = T("wl", [128, 32])
                dec = T("dec", [128, 32])
                xs_tok = T("xs_tok", [128, 32, 64], BF16)
                dtx = T("dtx", [128, 32, 64], BF16)
                Btok = T("Btok", [128, 8, 128], BF16)
                aU = [[T("aU%d_%d" % (i, j), [128, 4, 128], BF16) for j in range(3)] for i in range(1)]
                asp = [T("asp%d" % j, [128, 32], BF16) for j in range(3)]
                ares = [T("ares%d" % j, [128, 32]) for j in range(2)]
                kcb = T("kcb", [128, 640], BF16)
                Lt = [T("Lt%d" % i, [128, 4, 128], BF16) for i in range(1)]
                Mt = [T("Mt%d" % i, [128, 4, 128], BF16) for i in range(1)]
                CBs = [T("CBs%d" % i, [128, 128], BF16) for i in range(1)]
                tm1 = [tA[i][:, 0:256] for i in range(2)]
                tm2 = [tB[i][:, 0:256] for i in range(2)]
                Y = T("Y", [128, 2048])
                sz = T("sz", [128, 2048])
                Ynb = T("Ynb", [128, 2048], BF16)
                ss = T("ss", [128, 8])
                S = T("S", [128, 8, 256])
                Sbf = T("Sbf", [128, 8, 256], BF16)
                U = kc[:, 0:128]
                causal4 = kc[:, 128:640]
                ones_f = kc[:, 640:768]
                dtb, alog, Dbc, gn = hp[:, 0:32], hp[:, 32:64], hp[:, 64:96], hp[:, 96:112]

                def K_(n):
                    return "ssd_" + n

                def ld(dst, src, key, eng="sp"):
                    I(eng, lambda e: e.dma_start(out=dst, in_=src), W=[key], dma=key)
                ld(wdt[:], ssd_wdt_d[j0].rearrange("p (k f) -> p k f", k=DC), K_("wdt"), "pool")
                ld(cw[:], ssd_cw_d[j0].rearrange("p (c f) -> p c f", c=32), K_("cw"))
                ld(hp[:], ssd_hp_d[j0], K_("hp"))
                ld(kc[:], ssd_k_d, K_("kc"))
                I("dve", lambda e: e.tensor_copy(out=kcb[:], in_=kc[:, 0:640]), R=[K_("kc")], W=[K_("kcb")])
                Ub = kcb[:, 0:128]
                c4b = kcb[:, 128:640]
                I("act", lambda e: e.activation(out=Aneg[:], in_=alog, func=AF.Exp), R=[K_("hp")], W=[K_("Aneg")])
                I("dve", lambda e: e.tensor_scalar(out=Aneg[:], in0=Aneg[:], scalar1=-1.0, scalar2=None, op0=ALU.mult),
                  R=[K_("Aneg")], W=[K_("Aneg")])
                I("dve", lambda e: e.memset(S[:], 0.0), W=[K_("S")])
                I("dve", lambda e: e.memset(Sbf[:], 0.0), W=[K_("Sbf")])
                I("dve", lambda e: e.memset(halo[:], 0.0), W=[K_("halo")])
                cn = {"xp": 0, "g": 0}

                for t2 in range(L // TT2):
                    tsl = slice(t2 * TT2, (t2 + 1) * TT2)
                    tt = (t2 * TT2) // 512
                    for blk in range(24):
                        s = load_w256(ssd_win_d[j0, blk])
                        b = nxt("pg")
                        pg, pu = pb[2 * b], pb[2 * b + 1]
                        for hh, (pp, kp) in enumerate(((pg, "pb%d" % (2 * b)), (pu, "pb%d" % (2 * b + 1)))):
                            for k in range(DC):
                                I("pe", lambda e, k=k, pp=pp, hh=hh: e.matmul(pp[:, 0:TT2], lhsT=wi[s][:, k, hh * 128:(hh + 1) * 128],
                                                                             rhs=hT[:, k, tsl], start=(k == 0), stop=(k == DC - 1)),
                                  R=[("wi", s), ("hT", k, tt)], W=[kp])
                            ch = 2 * blk + hh
                            if ch < 16:
                                I("act", lambda e, pp=pp, ch=ch: e.activation(out=zT[:, ch, :], in_=pp[:, 0:TT2], func=AF.Copy),
                                  R=[kp], W=[K_("zT")])
                            else:
                                xch = ch - 16
                                r = 0
                                cn["xp"] += 1
                                I("act", lambda e, pp=pp, r=r: e.activation(out=xp[r][:, 3:3 + TT2], in_=pp[:, 0:TT2], func=AF.Copy),
                                  R=[kp], W=[(K_("xp"), r)])
                                I("pool", lambda e, r=r, xch=xch: e.tensor_copy(out=xp[r][:, 0:3], in_=halo[:, xch, :]),
                                  R=[(K_("halo"), xch), K_("halo")], W=[(K_("xp"), r)])
                                I("pool", lambda e, r=r, xch=xch: e.tensor_copy(out=halo[:, xch, :], in_=xp[r][:, TT2:TT2 + 3]),
                                  R=[(K_("xp"), r)], W=[(K_("halo"), xch)])
                                I("dve", lambda e, r=r, xch=xch: e.tensor_scalar(out=acc[r][:], in0=xp[r][:, 0:TT2], scalar1=cw[:, xch, 0:1],
                                                                                scalar2=None, op0=ALU.mult),
                                  R=[(K_("xp"), r), K_("cw")], W=[(K_("acc"), r)])
                                for tap in range(1, 4):
                                    I("dve", lambda e, r=r, xch=xch, tap=tap: e.scalar_tensor_tensor(
                                        out=acc[r][:], in0=xp[r][:, tap:tap + TT2], scalar=cw[:, xch, tap:tap + 1], in1=acc[r][:],
                                        op0=ALU.mult, op1=ALU.add),
                                      R=[(K_("xp"), r), K_("cw"), (K_("acc"), r)], W=[(K_("acc"), r)])
                                I("act", lambda e, r=r, xch=xch: e.activation(out=xc[:, xch, :], in_=acc[r][:], func=AF.Silu,
                                                                             bias=cw[:, xch, 4:5]),
                                  R=[(K_("acc"), r), K_("cw")], W=[(K_("xc"), xch)])
                    STG = int(os.environ.get('DBG_SSD', 99))
                    for ck in range(TT2 // 128 if STG >= 2 else 0):
                        lsl = slice(ck * 128, (ck + 1) * 128)
                        asl = slice(t2 * TT2 + ck * 128, t2 * TT2 + (ck + 1) * 128)
                        pdt = pb[3][:, 0:32]
                        pcs = pb[3][:, 64:96]
                        for k in range(DC):
                            I("pe", lambda e, k=k: e.matmul(pdt, lhsT=hT[:, k, asl], rhs=wdt[:, k, :], start=(k == 0), stop=(k == DC - 1)),
                              R=[("hT", k, tt), K_("wdt")], W=["pb3"])
                        SUB = int(os.environ.get('DBG_SUB', 99))
                        if SUB < 2:
                            continue
                        I("dve", lambda e: e.tensor_tensor(out=dtr[:], in0=pdt, in1=dtb, op=ALU.add), R=["pb3", K_("hp")], W=[K_("dtr")])
                        if SUB < 3:
                            continue
                        I("act", lambda e: e.activation(out=dtr[:], in_=dtr[:], func=AF.Exp), R=[K_("dtr")], W=[K_("dtr")])
                        I("act", lambda e: e.activation(out=dtv[:], in_=dtr[:], func=AF.Ln, bias=oneb[:]), R=[K_("dtr"), "oneb"], W=[K_("dtv")])
                        if SUB < 4:
                            continue
                        I("dve", lambda e: e.tensor_tensor(out=av[:], in0=dtv[:], in1=Aneg[:], op=ALU.mult), R=[K_("dtv"), K_("Aneg")], W=[K_("av")])
                        if SUB < 5:
                            continue
                        I("dve", lambda e: e.tensor_copy(out=asp[0][:], in_=av[:]), R=[K_("av")], W=[(K_("asp"), 0)])
                        I("dve", lambda e: e.tensor_tensor(out=ares[0][:], in0=av[:], in1=asp[0][:], op=ALU.subtract),
                          R=[K_("av"), (K_("asp"), 0)], W=[(K_("ares"), 0)])
                        I("dve", lambda e: e.tensor_copy(out=asp[1][:], in_=ares[0][:]), R=[(K_("ares"), 0)], W=[(K_("asp"), 1)])
                        I("dve", lambda e: e.tensor_tensor(out=ares[1][:], in0=ares[0][:], in1=asp[1][:], op=ALU.subtract),
                          R=[(K_("ares"), 0), (K_("asp"), 1)], W=[(K_("ares"), 1)])
                        I("dve", lambda e: e.tensor_copy(out=asp[2][:], in_=ares[1][:]), R=[(K_("ares"), 1)], W=[(K_("asp"), 2)])
                        for j3 in range(3):
                            I("pe", lambda e, j3=j3: e.matmul(pcs, lhsT=Ub, rhs=asp[j3][:], start=(j3 == 0), stop=(j3 == 2)),
                              R=[K_("kcb"), (K_("asp"), j3)], W=["pb3"])
                        I("dve", lambda e: e.tensor_scalar(out=ncs[:], in0=pcs, scalar1=-1.0, scalar2=None, op0=ALU.mult),
                          R=["pb3"], W=[K_("ncs")])
                        I("act", lambda e: e.activation(out=ecs[:], in_=pcs, func=AF.Exp), R=["pb3"], W=[K_("ecs")])
                        if STG < 3:
                            continue
                        for r in range(2):
                            for q in range(8):
                                I("pe", lambda e, r=r, q=q: e.transpose(out=pbh[:, q * 128:(q + 1) * 128], in_=xc[:, 8 * r + q, lsl],
                                                                       identity=ident_bf[:]),
                                  R=[(K_("xc"), 8 * r + q), "ident_bf"], W=["pbh"])
                            I("act", lambda e, r=r: e.activation(out=xs_tok[:, 16 * r:16 * r + 16, :],
                                                                 in_=pbh[:].rearrange("p (h d) -> p h d", h=16), func=AF.Copy),
                              R=["pbh"], W=[K_("xs_tok")])
                            I("dve", lambda e, r=r: e.tensor_tensor(out=dtx[:, 16 * r:16 * r + 16, :],
                                                                    in0=pbh[:].rearrange("p (h d) -> p h d", h=16),
                                                                    in1=dtv[:, 16 * r:16 * r + 16].unsqueeze(2).to_broadcast([128, 16, 64]),
                                                                    op=ALU.mult),
                              R=["pbh", K_("dtv")], W=[K_("dtx")])
                        for q in range(8):
                            I("pe", lambda e, q=q: e.transpose(out=pbh[:, q * 128:(q + 1) * 128], in_=xc[:, 16 + q, lsl], identity=ident_bf[:]),
                              R=[(K_("xc"), 16 + q), "ident_bf"], W=["pbh"])
                        I("act", lambda e: e.activation(out=Btok[:], in_=pbh[:].rearrange("p (g n) -> p g n", g=8), func=AF.Copy),
                          R=["pbh"], W=[K_("Btok")])
                        if STG < 4:
                            continue
                        for g in range(8):
                            i2 = cn["g"] % 2
                            cn["g"] += 1
                            X = pb[i2]
                            kX = "pb%d" % i2
                            for j3 in range(3):
                                I("dve", lambda e, g=g, i2=i2, j3=j3: e.tensor_tensor(
                                    out=aU[0][j3][:], in0=Ub.unsqueeze(1).to_broadcast([128, 4, 128]),
                                    in1=asp[j3][:, 4 * g:4 * g + 4].unsqueeze(2).to_broadcast([128, 4, 128]), op=ALU.mult),
                                  R=[K_("kcb"), (K_("asp"), j3)], W=[(K_("aU"), 0, j3)])
                                I("pe", lambda e, i2=i2, X=X, j3=j3: e.matmul(X[:], lhsT=ones_bf[:], rhs=aU[0][j3][:].rearrange("p h t -> p (h t)"),
                                                                          start=(j3 == 0), stop=False),
                                  R=["ones", (K_("aU"), 0, j3)], W=[kX])
                            I("pe", lambda e, X=X: e.matmul(X[:], lhsT=ident_bf[:], rhs=c4b, start=False, stop=True),
                              R=[K_("kcb"), "ident_bf"], W=[kX])
                            for h in range(4):
                                I("act", lambda e, h=h, g=g, i2=i2, X=X: e.activation(
                                    out=Lt[0][:, h, :], in_=X[:, h * 128:(h + 1) * 128], func=AF.Exp,
                                    bias=ncs[:, 4 * g + h:4 * g + h + 1]),
                                  R=[kX, K_("ncs")], W=[(K_("Lt"), 0)])
                            I("dve", lambda e, g=g, X=X: e.tensor_copy(
                                out=cl[:, 4 * g:4 * g + 4].unsqueeze(2),
                                in_=X[:].rearrange("p (h t) -> p h t", h=4)[:, :, 127:128]),
                              R=[kX], W=[(K_("cl"), g)])
                            CBp = pb[2][:, 0:128]
                            I("pe", lambda e, g=g, CBp=CBp: e.matmul(CBp, lhsT=xc[:, 16 + g, lsl], rhs=xc[:, 24 + g, lsl], start=True, stop=True),
                              R=[(K_("xc"), 16 + g), (K_("xc"), 24 + g)], W=["pb2"])
                            I("act", lambda e, i2=i2, CBp=CBp: e.activation(out=CBs[0][:], in_=CBp, func=AF.Copy),
                              R=["pb2"], W=[(K_("CBs"), 0)])
                            I("dve", lambda e, i2=i2: e.tensor_tensor(out=Mt[0][:], in0=Lt[0][:],
                                                                      in1=CBs[0][:].unsqueeze(1).to_broadcast([128, 4, 128]), op=ALU.mult),
                              R=[(K_("Lt"), 0), (K_("CBs"), 0)], W=[(K_("Mt"), 0)])
                            pY = pb[5 + i2]
                            kY = "pb%d" % (5 + i2)
                            for h in range(4):
                                I("pe", lambda e, h=h, g=g, i2=i2: e.matmul(pY[:, 256 + h * 64:256 + (h + 1) * 64], lhsT=Mt[0][:, h, :],
                                                                           rhs=dtx[:, 4 * g + h, :], start=True, stop=True),
                                  R=[(K_("Mt"), 0), K_("dtx")], W=[kY])
                            I("pe", lambda e, g=g: e.matmul(pY[:, 0:256], lhsT=xc[:, 24 + g, lsl], rhs=Sbf[:, g, :], start=True, stop=True),
                              R=[(K_("xc"), 24 + g), (K_("Sbf"), g)], W=[kY])
                            I("dve", lambda e, g=g, i2=i2: e.tensor_tensor(
                                out=tm1[i2].rearrange("p (h d) -> p h d", h=4), in0=pY[:, 0:256].rearrange("p (h d) -> p h d", h=4),
                                in1=ecs[:, 4 * g:4 * g + 4].unsqueeze(2).to_broadcast([128, 4, 64]), op=ALU.mult),
                              R=[kY, K_("ecs")], W=[("tA", i2)])
                            I("dve", lambda e, i2=i2: e.tensor_tensor(out=tm1[i2], in0=tm1[i2], in1=pY[:, 256:512], op=ALU.add),
                              R=[kY, ("tA", i2)], W=[("tA", i2)])
                            I("pool", lambda e, g=g, i2=i2: e.tensor_tensor(
                                out=tm2[i2].rearrange("p (h d) -> p h d", h=4), in0=xs_tok[:, 4 * g:4 * g + 4, :],
                                in1=Dbc[:, 4 * g:4 * g + 4].unsqueeze(2).to_broadcast([128, 4, 64]), op=ALU.mult),
                              R=[K_("xs_tok"), K_("hp")], W=[("tB", i2)])
                            I("pool", lambda e, g=g, i2=i2: e.tensor_tensor(out=Y[:, g * 256:(g + 1) * 256], in0=tm1[i2], in1=tm2[i2], op=ALU.add),
                              R=[("tA", i2), ("tB", i2)], W=[(K_("Y"), g)])
                        if STG < 5:
                            continue
                        I("dve", lambda e: e.tensor_tensor(out=wl[:], in0=cl[:], in1=ncs[:], op=ALU.add),
                          R=[(K_("cl"), g) for g in range(8)] + [K_("ncs")], W=[K_("wl")])
                        I("act", lambda e: e.activation(out=wl[:], in_=wl[:], func=AF.Exp), R=[K_("wl")], W=[K_("wl")])
                        I("act", lambda e: e.activation(out=dec[:], in_=cl[:], func=AF.Exp), R=[(K_("cl"), g) for g in range(8)], W=[K_("dec")])
                        I("dve", lambda e: e.tensor_tensor(out=dtx[:], in0=dtx[:], in1=wl[:].unsqueeze(2).to_broadcast([128, 32, 64]), op=ALU.mult),
                          R=[K_("dtx"), K_("wl")], W=[K_("dtx")])
                        for g in range(8):
                            i2 = g % 2
                            SU = pb[4][:, 0:256]
                            I("pe", lambda e, g=g, SU=SU: e.matmul(SU, lhsT=Btok[:, g, :], rhs=dtx[:, 4 * g:4 * g + 4, :].rearrange("p h d -> p (h d)"),
                                                                  start=True, stop=True),
                              R=[K_("Btok"), K_("dtx")], W=["pb4"])
                            I("pool", lambda e, g=g: e.tensor_tensor(
                                out=S[:, g, :].rearrange("p (h d) -> p h d", h=4), in0=S[:, g, :].rearrange("p (h d) -> p h d", h=4),
                                in1=dec[:, 4 * g:4 * g + 4].unsqueeze(2).to_broadcast([128, 4, 64]), op=ALU.mult),
                              R=[(K_("S"), g), K_("S"), K_("dec")], W=[(K_("S"), g)])
                            I("dve", lambda e, g=g, SU=SU: e.tensor_tensor(out=S[:, g, :], in0=S[:, g, :], in1=SU, op=ALU.add),
                              R=[(K_("S"), g), "pb4"], W=[(K_("S"), g)])
                            I("act", lambda e, g=g: e.activation(out=Sbf[:, g, :], in_=S[:, g, :], func=AF.Copy),
                              R=[(K_("S"), g), K_("Sbf")], W=[(K_("Sbf"), g)])
                        if STG < 6:
                            continue
                        for r in range(2):
                            for q in range(8):
                                I("pe", lambda e, r=r, q=q: e.transpose(out=pbh[:, q * 128:(q + 1) * 128], in_=zT[:, 8 * r + q, lsl],
                                                                       identity=ident_bf[:]),
                                  R=[K_("zT"), "ident_bf"], W=["pbh"])
                            I("act", lambda e, r=r: e.activation(out=sz[:, r * 1024:(r + 1) * 1024], in_=pbh[:], func=AF.Silu),
                              R=["pbh"], W=[K_("sz")])
                        I("dve", lambda e: e.tensor_tensor(out=Y[:], in0=Y[:], in1=sz[:], op=ALU.mult),
                          R=[(K_("Y"), g) for g in range(8)] + [K_("sz")], W=[K_("Y2")])
                        I("act", lambda e: e.activation(out=sz[:], in_=Y[:], func=AF.Square), R=[K_("Y2")], W=[K_("sz")])
                        I("dve", lambda e: e.tensor_reduce(out=ss[:], in_=sz[:].rearrange("p (g c) -> p g c", g=8), axis=AX.X, op=ALU.add),
                          R=[K_("sz")], W=[K_("ss")])
                        I("act", lambda e: e.activation(out=ss[:], in_=ss[:], func=AF.Sqrt, scale=1.0 / 256.0, bias=epsb[:]),
                          R=[K_("ss"), "epsb"], W=[K_("ss")])
                        I("dve", lambda e: e.reciprocal(out=ss[:], in_=ss[:]), R=[K_("ss")], W=[K_("ss")])
                        I("dve", lambda e: e.tensor_tensor(out=Ynb[:].rearrange("p (g c) -> p g c", g=8), in0=Y[:].rearrange("p (g c) -> p g c", g=8),
                                                           in1=ss[:].unsqueeze(2).to_broadcast([128, 8, 256]), op=ALU.mult),
                          R=[K_("Y2"), K_("ss")], W=[K_("Ynb")] + [(K_("Y"), g) for g in range(8)])
                        for r in range(2):
                            for q in range(8):
                                I("pe", lambda e, r=r, q=q: e.transpose(out=pbh[:, q * 128:(q + 1) * 128],
                                                                       in_=Ynb[:, (8 * r + q) * 128:(8 * r + q + 1) * 128], identity=ident_bf[:]),
                                  R=[K_("Ynb"), "ident_bf"], W=["pbh"])
                            I("dve", lambda e, r=r: e.tensor_tensor(out=ynT[:, 8 * r:8 * r + 8, lsl], in0=pbh[:].rearrange("p (c t) -> p c t", c=8),
                                                                    in1=gn[:, 8 * r:8 * r + 8].unsqueeze(2).to_broadcast([128, 8, 128]), op=ALU.mult),
                              R=["pbh", K_("hp")], W=[K_("ynT")])
                    for dc in range(DC if STG >= 7 else 0):
                        s = load_w256(ssd_wout_d[j0, dc])
                        wv = wi[s][:].rearrange("p k (a f) -> p (k a) f", a=2)
                        b = nxt("po")
                        po = pb[4 + b] if False else pb[6]
                        for c in range(16):
                            I("pe", lambda e, c=c, wv=wv: e.matmul(pb[6][:, 0:TT2], lhsT=wv[:, c, :], rhs=ynT[:, c, :], start=(c == 0), stop=(c == 15)),
                              R=[("wi", s), K_("ynT")], W=["pb6"])
                        I("dve", lambda e, dc=dc: e.tensor_tensor(out=xT[:, dc, tsl], in0=xT[:, dc, tsl], in1=pb[6][:, 0:TT2], op=ALU.add),
                          R=["pb6", ("xT", dc, tt)], W=[("xT", dc, tt)])
                P.release([k for k in list(P.res) if (k[0] if isinstance(k, tuple) else k).startswith("ssd_")])


        def dsa_mixer(j2, gidx):
            norm_to_hT(gidx)
            NEG = -1.0e30
            TOPK = min(256, L // 4)
            NR = TOPK // 8
            with ExitStack() as sc:
                def T(name, shape, dt=F32):
                    return sb(nc, sc, "dsa_" + name, shape, dt)

                def K_(n):
                    return "dsa_" + n
                qT = T("qT", [128, 8, L], BF16)
                K2T = T("K2T", [128, L], BF16)
                qiT = T("qiT", [128, 4, L], BF16)
                ki2T = T("ki2T", [128, L], BF16)
                Vaug = T("Vaug", [128, NB, 65], BF16)
                witok = T("witok", [128, NB, 8])
                wvw = T("wvw", [128, DC, 72], BF16)
                I("pool", lambda e: e.dma_start(out=wvw[:], in_=dsa_wvw_d[j2].rearrange("p (k f) -> p k f", k=DC)),
                  W=[K_("wvw")], dma=K_("wvw"))
                I("dve", lambda e: e.memset(Vaug[:], 1.0), W=[K_("Vaug")])
                with ExitStack() as sc2:
                    cosT = sb(nc, sc2, "dsa_cos", [128, L], F32)
                    sinS = sb(nc, sc2, "dsa_sin", [128, L], F32)
                    I("sp", lambda e: e.dma_start(out=cosT[:], in_=dsa_rope_d[0]), W=[K_("cos")], dma=K_("cos"))
                    I("sp", lambda e: e.dma_start(out=sinS[:], in_=dsa_rope_d[1]), W=[K_("sin")], dma=K_("sin"))
                    for blk in range(14):
                        s = load_w256(dsa_win_d[j2, blk])
                        if blk < 8:
                            dst, dkey, scl = (lambda ts, blk=blk: qT[:, blk, ts]), K_("qT"), 0.125
                        elif blk == 8:
                            dst, dkey, scl = (lambda ts: K2T[:, ts]), K_("K2T"), 1.0
                        elif blk < 13:
                            dst, dkey, scl = (lambda ts, blk=blk: qiT[:, blk - 9, ts]), K_("qiT"), 1.0
                        else:
                            dst, dkey, scl = (lambda ts: ki2T[:, ts]), K_("ki2T"), 1.0
                        for tt in range(NT):
                            ts = slice(tt * 512, (tt + 1) * 512)
                            pg, pu, kg, ku = mm_pair(s, hT, lambda k, tt: ("hT", k, tt), tt)
                            k1 = nxt("tA")
                            k2 = nxt("tB")
                            I("dve", lambda e, k1=k1, pg=pg, ts=ts, scl=scl: e.scalar_tensor_tensor(
                                out=tA[k1][:], in0=pg[:], scalar=scl, in1=cosT[:, ts], op0=ALU.mult, op1=ALU.mult),
                              R=[kg, K_("cos")], W=[("tA", k1)])
                            I("dve", lambda e, k2=k2, pu=pu, ts=ts, scl=scl: e.scalar_tensor_tensor(
                                out=tB[k2][:], in0=pu[:], scalar=scl, in1=sinS[:, ts], op0=ALU.mult, op1=ALU.mult),
                              R=[ku, K_("sin")], W=[("tB", k2)])
                            I("pool", lambda e, k1=k1, k2=k2, ts=ts, dst=dst: e.tensor_tensor(out=dst(ts), in0=tA[k1][:], in1=tB[k2][:], op=ALU.add),
                              R=[("tA", k1), ("tB", k2)], W=[(dkey, tt)])
                    P.release([K_("cos"), K_("sin")])
                for tb in range(NB):
                    bsl = slice(tb * 128, (tb + 1) * 128)
                    pv = pb[4][:, 0:72]
                    for k in range(DC):
                        I("pe", lambda e, k=k: e.matmul(pv, lhsT=hT[:, k, bsl], rhs=wvw[:, k, :], start=(k == 0), stop=(k == DC - 1)),
                          R=[("hT", k, tb // 4), K_("wvw")], W=["pb4"])
                    I("act", lambda e, tb=tb: e.activation(out=Vaug[:, tb, 0:64], in_=pb[4][:, 0:64], func=AF.Copy),
                      R=["pb4", K_("Vaug")], W=[(K_("Vaug"), tb)])
                    I("dve", lambda e, tb=tb: e.tensor_scalar(out=witok[:, tb, :], in0=pb[4][:, 64:72], scalar1=8.0 ** -0.5 * 64.0 ** -0.5,
                                                              scalar2=None, op0=ALU.mult),
                      R=["pb4"], W=[(K_("witok"), tb)])
                with ExitStack() as sc3:
                    def T3(name, shape, dt=F32):
                        return sb(nc, sc3, "dsa_" + name, shape, dt)
                    idx = T3("idx", [128, L])
                    work = T3("work", [128, L])
                    mb = T3("mb", [128, L], BF16)
                    Pt = [T3("Pt%d" % i, [128, 512], BF16) for i in range(2)]
                    Otok = T3("Otok", [128, 16, 64], BF16)
                    OT = T3("OT", [128, DC, 256], BF16)
                    kd = T3("kd", [128, 640])
                    irep = T3("irep", [128, 512], BF16)
                    m8 = T3("m8", [128, 8])
                    rcp = T3("rcp", [128, 16, 1])
                    zer = T3("zer", [128, 512], BF16)
                    I("dve", lambda e: e.memset(zer[:], 0.0), W=[K_("zer")])
                    I("sp", lambda e: e.dma_start(out=kd[:], in_=dsa_k_d), W=[K_("kd")], dma=K_("kd"))
                    I("dve", lambda e: e.tensor_copy(out=irep[:], in_=kd[:, 128:640]), R=[K_("kd")], W=[K_("irep")])
                    negtri = kd[:, 0:128]
                    cn = {"pi": 0, "ps": 0}
                    for qb in range(NB):
                        qsl = slice(qb * 128, (qb + 1) * 128)
                        S = 128 * (qb + 1)
                        npc = (S + 511) // 512
                        for hi in range(8):
                            c, half = hi // 2, hi % 2
                            hs = slice(half * 64, (half + 1) * 64)
                            for pc in range(npc):
                                cols = min(512, S - 512 * pc)
                                csl = slice(512 * pc, 512 * pc + cols)
                                ib = cn["pi"] % 2
                                cn["pi"] += 1
                                pI = pb[5 + ib]
                                I("pe", lambda e, c=c, hs=hs, csl=csl, cols=cols, pI=pI: e.matmul(
                                    pI[:, 0:cols], lhsT=qiT[hs, c, qsl], rhs=ki2T[hs, csl], start=True, stop=True),
                                  R=[(K_("qiT"), qb // 4), (K_("ki2T"), pc)], W=["pb%d" % (5 + ib)])
                                k1 = nxt("tA")
                                I("act", lambda e, k1=k1, cols=cols, pI=pI: e.activation(out=tA[k1][:, 0:cols], in_=pI[:, 0:cols], func=AF.Relu),
                                  R=["pb%d" % (5 + ib)], W=[("tA", k1)])
                                wcol = witok[:, qb, hi:hi + 1]
                                if hi == 0:
                                    I("dve", lambda e, k1=k1, cols=cols, csl=csl, wcol=wcol: e.tensor_scalar(
                                        out=idx[:, csl], in0=tA[k1][:, 0:cols], scalar1=wcol, scalar2=None, op0=ALU.mult),
                                      R=[("tA", k1), (K_("witok"), qb)], W=[(K_("idx"), pc)])
                                else:
                                    I("dve", lambda e, k1=k1, cols=cols, csl=csl, wcol=wcol: e.scalar_tensor_tensor(
                                        out=idx[:, csl], in0=tA[k1][:, 0:cols], scalar=wcol, in1=idx[:, csl], op0=ALU.mult, op1=ALU.add),
                                      R=[("tA", k1), (K_("witok"), qb), (K_("idx"), pc)], W=[(K_("idx"), pc)])
                        allidx = [(K_("idx"), pc) for pc in range(npc)]
                        I("dve", lambda e, S=S: e.tensor_tensor(out=idx[:, S - 128:S], in0=idx[:, S - 128:S], in1=negtri, op=ALU.add),
                          R=allidx + [K_("kd")], W=allidx)
                        if qb >= TOPK // 128:
                            for rd in range(NR):
                                src = idx if rd == 0 else work
                                I("dve", lambda e, src=src, S=S: e.max(out=m8[:], in_=src[:, 0:S]),
                                  R=allidx + [K_("work")], W=[K_("m8")])
                                if rd < NR - 1:
                                    I("dve", lambda e, src=src, S=S: e.match_replace(out=work[:, 0:S], in_to_replace=m8[:], in_values=src[:, 0:S],
                                                                                   imm_value=NEG),
                                      R=allidx + [K_("m8")], W=[K_("work")])
                            I("dve", lambda e, S=S: e.tensor_scalar(out=mb[:, 0:S], in0=idx[:, 0:S], scalar1=m8[:, 7:8], scalar2=-30000.0,
                                                                    op0=ALU.is_lt, op1=ALU.mult),
                              R=allidx + [K_("m8")], W=[K_("mb")])
                        else:
                            I("dve", lambda e, S=S: e.tensor_scalar(out=mb[:, 0:S], in0=idx[:, 0:S], scalar1=-1.0e29, scalar2=-30000.0,
                                                                    op0=ALU.is_lt, op1=ALU.mult),
                              R=allidx, W=[K_("mb")])
                        for bk, nh in ((2, 7), (3, 7), (4, 2)):
                            I("pe", lambda e, bk=bk, nh=nh: e.matmul(pb[bk][:, 0:nh * 65], lhsT=zer[:, 0:128], rhs=zer[:, 0:nh * 65],
                                                                    start=True, stop=False),
                              R=[K_("zer")], W=["pb%d" % bk])
                        for sbk in range(qb + 1):
                            ssl = slice(sbk * 128, (sbk + 1) * 128)
                            for half in range(2):
                                hs = slice(half * 64, (half + 1) * 64)
                                for cg in range(2):
                                    ib = cn["ps"] % 2
                                    cn["ps"] += 1
                                    pS = pb[ib]
                                    I("pe", lambda e, hs=hs, cg=cg, ssl=ssl, pS=pS: e.matmul(
                                        pS[:], lhsT=K2T[hs, ssl], rhs=qT[hs, 4 * cg:4 * cg + 4, qsl], start=True, stop=False),
                                      R=[(K_("K2T"), sbk // 4), (K_("qT"), qb // 4)], W=["pb%d" % ib])
                                    I("pe", lambda e, ssl=ssl, pS=pS: e.matmul(pS[:], lhsT=mb[:, ssl], rhs=irep[:], start=False, stop=True),
                                      R=[K_("mb"), K_("irep")], W=["pb%d" % ib])
                                    I("act", lambda e, ib=ib, pS=pS: e.activation(out=Pt[ib][:], in_=pS[:], func=AF.Exp),
                                      R=["pb%d" % ib], W=[(K_("Pt"), ib)])
                                    for j in range(4):
                                        head = 2 * (4 * cg + j) + half
                                        bk, off = 2 + head // 7, (head % 7) * 65
                                        I("pe", lambda e, ib=ib, j=j, bk=bk, off=off, sbk=sbk: e.matmul(
                                            pb[bk][:, off:off + 65], lhsT=Pt[ib][:, j * 128:(j + 1) * 128], rhs=Vaug[:, sbk, :],
                                            start=False, stop=False),
                                          R=[(K_("Pt"), ib), (K_("Vaug"), sbk), K_("Vaug")], W=["pb%d" % bk])
                        for bk, nh in ((2, 7), (3, 7), (4, 2)):
                            I("pe", lambda e, bk=bk, nh=nh: e.matmul(pb[bk][:, 0:nh * 65], lhsT=zer[:, 0:128], rhs=zer[:, 0:nh * 65],
                                                                    start=False, stop=True),
                              R=[K_("zer")], W=["pb%d" % bk])
                        for bk, h0, nh in ((2, 0, 7), (3, 7, 7), (4, 14, 2)):
                            pv3 = pb[bk][:, 0:nh * 65].rearrange("p (h e) -> p h e", e=65)
                            I("dve", lambda e, pv3=pv3, h0=h0, nh=nh: e.reciprocal(out=rcp[:, h0:h0 + nh, :], in_=pv3[:, :, 64:65]),
                              R=["pb%d" % bk], W=[(K_("rcp"), bk)])
                            I("dve", lambda e, pv3=pv3, h0=h0, nh=nh: e.tensor_tensor(
                                out=Otok[:, h0:h0 + nh, :], in0=pv3[:, :, 0:64], in1=rcp[:, h0:h0 + nh, :].to_broadcast([128, nh, 64]), op=ALU.mult),
                              R=["pb%d" % bk, (K_("rcp"), bk)], W=[(K_("Otok"), bk)])
                        lq = qb % 2
                        for c in range(8):
                            I("pe", lambda e, c=c: e.transpose(out=pbh[:, c * 128:(c + 1) * 128],
                                                               in_=Otok[:, 2 * c:2 * c + 2, :].rearrange("p h d -> p (h d)"), identity=ident_bf[:]),
                              R=[(K_("Otok"), 2), (K_("Otok"), 3), (K_("Otok"), 4), "ident_bf"], W=["pbh"])
                        I("act", lambda e, lq=lq: e.activation(out=OT[:, :, lq * 128:(lq + 1) * 128],
                                                               in_=pbh[:].rearrange("p (c t) -> p c t", c=8), func=AF.Copy),
                          R=["pbh"], W=[K_("OT")])
                        if lq == 1:
                            osl = slice((qb - 1) * 128, (qb + 1) * 128)
                            tt = qb // 4
                            for b4 in range(4):
                                s = load_w256(dsa_wo_d[j2, b4])
                                for hh in range(2):
                                    dc = 2 * b4 + hh
                                    po = pb[5 + hh]
                                    for k in range(DC):
                                        I("pe", lambda e, k=k, hh=hh, po=po: e.matmul(po[:, 0:256], lhsT=wi[s][:, k, hh * 128:(hh + 1) * 128],
                                                                                     rhs=OT[:, k, :], start=(k == 0), stop=(k == DC - 1)),
                                          R=[("wi", s), K_("OT")], W=["pb%d" % (5 + hh)])
                                    I("dve", lambda e, dc=dc, po=po: e.tensor_tensor(out=xT[:, dc, osl], in0=xT[:, dc, osl], in1=po[:, 0:256], op=ALU.add),
                                      R=["pb%d" % (5 + hh), ("xT", dc, tt)], W=[("xT", dc, tt)])
                P.release([k for k in list(P.res) if (k[0] if isinstance(k, tuple) else k).startswith("dsa_")])

        def s5_mixer(j5, gidx):
            norm_to_hT(gidx)
            with ExitStack() as sc:
                def T(name, shape, dt=F32):
                    return sb(nc, sc, "s5_" + name, shape, dt)
                gT = T("gT", [128, DC, L], BF16)
                iota = T("iota", [128, L], mybir.dt.int16)
                par = T("par", [128, 3, 64])
                bre = T("bre", [64, 1024])
                bim = T("bim", [64, 1024])
                cst1 = T("cst1", [128, 1024])
                cst2 = T("cst2", [128, 1024])
                kk = T("kk", [128, 16])
                dv = T("dv", [128, 24])
                names = ["step", "lr", "th", "r", "f", "raw", "fs", "fc", "ms0", "mc0", "nr", "ni",
                         "den", "inv", "cr", "ci", "u1", "u2"]
                pt_ = {n: T(n, [128, 64]) for n in names}
                state = T("state", [128, 64])
                z = {n: T(n, [64, 128]) for n in ["t1", "t2", "zre", "zim", "nzre"]}
                wA = T("wA", [128, 8, 128], BF16)
                wA2 = T("wA2", [128, 8, 128], BF16)
                W1 = T("W1", [128, 8, 128], BF16)
                W2 = T("W2", [128, 8, 128], BF16)
                raw = T("rawt", [128, 512])
                msin = T("msin", [128, 512])
                mcos = T("mcos", [128, 512])
                bt = T("bt", [128, 512])
                st = T("st", [128, 512])
                rfull = T("rfull", [128, 512])
                P1 = T("P1", [128, 512], BF16)
                P2 = T("P2", [128, 512], BF16)
                yv = T("yv", [128, 512])
                zz = T("zz", [128, 512])

                def ld(dst, src, key):
                    I("sp", lambda e: e.dma_start(out=dst, in_=src), W=[key], dma=key)
                ld(iota[:], iota_d, "s5_iota")
                ld(par[:], s5p_d[j5].rearrange("a p g -> p a g"), "s5_par")
                ld(bre[:], s5b_d[j5, 0], "s5_bre")
                ld(bim[:], s5b_d[j5, 1], "s5_bim")
                ld(cst1[:], s5c_d[j5, 0], "s5_cst1")
                ld(cst2[:], s5c_d[j5, 1], "s5_cst2")
                ld(kk[:], s5k_d, "s5_kk")
                ld(dv[:], s5d_d[j5], "s5_dv")

                def dv_(fn, R, W):
                    I("dve", fn, R=["s5_" + r for r in R], W=["s5_" + w for w in W])

                def po_(fn, R, W):
                    I("pool", fn, R=["s5_" + r for r in R], W=["s5_" + w for w in W])

                def ac_(fn, R, W):
                    I("act", fn, R=["s5_" + r for r in R] + ["negpi"], W=["s5_" + w for w in W])
                p = pt_
                lre, lim, lst = par[:, 0, :], par[:, 1, :], par[:, 2, :]
                ac_(lambda e: e.activation(out=p["step"][:], in_=lst, func=AF.Exp), ["par"], ["step"])
                dv_(lambda e: e.tensor_tensor(out=p["lr"][:], in0=lre, in1=p["step"][:], op=ALU.mult), ["par", "step"], ["lr"])
                dv_(lambda e: e.tensor_tensor(out=p["th"][:], in0=lim, in1=p["step"][:], op=ALU.mult), ["par", "step"], ["th"])
                ac_(lambda e: e.activation(out=p["r"][:], in_=p["lr"][:], func=AF.Exp), ["lr"], ["r"])
                dv_(lambda e: e.tensor_scalar(out=p["f"][:], in0=p["th"][:], scalar1=1.0 / TWO_PI, scalar2=None, op0=ALU.mult), ["th"], ["f"])
                dv_(lambda e: e.tensor_scalar(out=p["raw"][:], in0=p["f"][:], scalar1=MAGIC, scalar2=None, op0=ALU.add), ["f"], ["raw"])
                dv_(lambda e: e.scalar_tensor_tensor(out=p["fs"][:], in0=p["raw"][:], scalar=MAGIC, in1=p["f"][:], op0=ALU.subtract, op1=ALU.subtract), ["raw", "f"], ["fs"])
                dv_(lambda e: e.scalar_tensor_tensor(out=p["fc"][:], in0=p["fs"][:], scalar=-1.0, in1=p["fs"][:], op0=ALU.mult, op1=ALU.max), ["fs"], ["fc"])
                ac_(lambda e: e.activation(out=p["ms0"][:], in_=p["fs"][:], func=AF.Sin, scale=-TWO_PI), ["fs"], ["ms0"])
                ac_(lambda e: e.activation(out=p["mc0"][:], in_=p["fc"][:], func=AF.Sin, scale=-TWO_PI, bias=halfpi[:]), ["fc"], ["mc0"])
                dv_(lambda e: e.tensor_tensor(out=p["u1"][:], in0=p["mc0"][:], in1=p["r"][:], op=ALU.mult), ["mc0", "r"], ["u1"])
                dv_(lambda e: e.tensor_scalar(out=p["nr"][:], in0=p["u1"][:], scalar1=-1.0, scalar2=None, op0=ALU.add), ["u1"], ["nr"])
                dv_(lambda e: e.tensor_tensor(out=p["u2"][:], in0=p["ms0"][:], in1=p["r"][:], op=ALU.mult), ["ms0", "r"], ["u2"])
                dv_(lambda e: e.tensor_copy(out=p["ni"][:], in_=p["u2"][:]), ["u2"], ["ni"])
                dv_(lambda e: e.tensor_tensor(out=p["u1"][:], in0=lre, in1=lre, op=ALU.mult), ["par", "nr"], ["u1"])
                dv_(lambda e: e.tensor_tensor(out=p["u2"][:], in0=lim, in1=lim, op=ALU.mult), ["par", "ni"], ["u2"])
                dv_(lambda e: e.tensor_tensor(out=p["den"][:], in0=p["u1"][:], in1=p["u2"][:], op=ALU.add), ["u1", "u2"], ["den"])
                dv_(lambda e: e.reciprocal(out=p["inv"][:], in_=p["den"][:]), ["den"], ["inv"])
                dv_(lambda e: e.tensor_tensor(out=p["u1"][:], in0=p["nr"][:], in1=lre, op=ALU.mult), ["nr", "par", "den"], ["u1"])
                dv_(lambda e: e.tensor_tensor(out=p["u2"][:], in0=p["ni"][:], in1=lim, op=ALU.mult), ["ni", "par", "den"], ["u2"])
                dv_(lambda e: e.tensor_tensor(out=p["cr"][:], in0=p["u1"][:], in1=p["u2"][:], op=ALU.add), ["u1", "u2"], ["cr"])
                dv_(lambda e: e.tensor_tensor(out=p["cr"][:], in0=p["cr"][:], in1=p["inv"][:], op=ALU.mult), ["cr", "inv"], ["cr"])
                dv_(lambda e: e.tensor_tensor(out=p["u1"][:], in0=p["ni"][:], in1=lre, op=ALU.mult), ["ni", "par", "cr"], ["u1"])
                dv_(lambda e: e.tensor_tensor(out=p["u2"][:], in0=p["nr"][:], in1=lim, op=ALU.mult), ["nr", "par", "cr"], ["u2"])
                dv_(lambda e: e.tensor_tensor(out=p["ci"][:], in0=p["u1"][:], in1=p["u2"][:], op=ALU.subtract), ["u1", "u2"], ["ci"])
                dv_(lambda e: e.tensor_tensor(out=p["ci"][:], in0=p["ci"][:], in1=p["inv"][:], op=ALU.mult), ["ci", "inv"], ["ci"])

                for ct in range(DC):
                    gs = slice(8 * ct, 8 * ct + 8)
                    cs_ = slice(128 * ct, 128 * ct + 128)

                    def bc(t):
                        return t[0:64, gs].unsqueeze(2).to_broadcast([64, 8, 16])

                    def v3(t):
                        return t[:].rearrange("p (g c) -> p g c", g=8)
                    b_re = bre[:, cs_].rearrange("p (g c) -> p g c", g=8)
                    b_im = bim[:, cs_].rearrange("p (g c) -> p g c", g=8)
                    dv_(lambda e: e.tensor_tensor(out=v3(z["t1"]), in0=b_re, in1=bc(p["cr"]), op=ALU.mult), ["bre", "cr"], ["t1"])
                    dv_(lambda e: e.tensor_tensor(out=v3(z["t2"]), in0=b_im, in1=bc(p["ci"]), op=ALU.mult), ["bim", "ci"], ["t2"])
                    dv_(lambda e: e.tensor_tensor(out=z["zre"][:], in0=z["t1"][:], in1=z["t2"][:], op=ALU.subtract), ["t1", "t2", "wA", "wA2"], ["zre"])
                    dv_(lambda e: e.tensor_scalar(out=z["nzre"][:], in0=z["zre"][:], scalar1=-1.0, scalar2=None, op0=ALU.mult), ["zre", "wA2"], ["nzre"])
                    dv_(lambda e: e.tensor_tensor(out=v3(z["t1"]), in0=b_im, in1=bc(p["cr"]), op=ALU.mult), ["bim", "cr", "zre"], ["t1"])
                    dv_(lambda e: e.tensor_tensor(out=v3(z["t2"]), in0=b_re, in1=bc(p["ci"]), op=ALU.mult), ["bre", "ci", "zre"], ["t2"])
                    dv_(lambda e: e.tensor_tensor(out=z["zim"][:], in0=z["t1"][:], in1=z["t2"][:], op=ALU.add), ["t1", "t2", "wA", "wA2"], ["zim"])
                    pT = pb[6]
                    id64 = ident[0:64, 0:64]
                    for q, src in enumerate(["zre", "zim", "zim", "nzre"]):
                        I("pe", lambda e, q=q, src=src: e.transpose(out=pT[:, q * 64:(q + 1) * 64], in_=z[src][:], identity=id64),
                          R=["s5_" + src, "ident"], W=["pb6"])
                    for g_ in range(8):
                        I("dve", lambda e, g_=g_: e.tensor_scalar(out=wA[:, g_, :], in0=pT[:, 0:128], scalar1=kk[:, 2 + g_:3 + g_],
                                                                 scalar2=None, op0=ALU.mult),
                          R=["pb6", "s5_kk"], W=["s5_wA"])
                        I("dve", lambda e, g_=g_: e.tensor_scalar(out=wA2[:, g_, :], in0=pT[:, 128:256], scalar1=kk[:, 2 + g_:3 + g_],
                                                                 scalar2=None, op0=ALU.mult),
                          R=["pb6", "s5_kk"], W=["s5_wA2"])
                    I("pool", lambda e: e.memset(W1[:], 0.0), W=["s5_W1"])
                    I("pool", lambda e: e.memset(W2[:], 0.0), W=["s5_W2"])
                    for g_ in range(8):
                        gcs = slice(128 * ct + 16 * g_, 128 * ct + 16 * g_ + 16)
                        I("dve", lambda e, g_=g_, gcs=gcs: e.tensor_scalar(out=W1[:, g_, 16 * g_:16 * g_ + 16], in0=cst1[:, gcs],
                                                                          scalar1=kk[:, 0:1], scalar2=None, op0=ALU.mult),
                          R=["s5_cst1", "s5_kk"], W=["s5_W1"])
                        I("dve", lambda e, g_=g_, gcs=gcs: e.tensor_scalar(out=W2[:, g_, 16 * g_:16 * g_ + 16], in0=cst2[:, gcs],
                                                                          scalar1=kk[:, 1:2], scalar2=None, op0=ALU.mult),
                          R=["s5_cst2", "s5_kk"], W=["s5_W2"])
                    for q in range(NT):
                        ts = slice(q * 512, (q + 1) * 512)
                        py = pb[5]
                        for g_ in range(8):
                            g = 8 * ct + g_
                            fcol = p["f"][:, g:g + 1]
                            I("dve", lambda e, fcol=fcol: e.tensor_scalar(out=raw[:], in0=iota[:, ts], scalar1=fcol, scalar2=None,
                                                                         op0=ALU.mult),
                              R=["s5_iota", "s5_f"], W=["s5_rawt"])
                            I("pool", lambda e: e.tensor_scalar(out=msin[:], in0=raw[:], scalar1=MAGIC, scalar2=None, op0=ALU.add),
                              R=["s5_rawt"], W=["s5_msin"])
                            I("dve", lambda e: e.scalar_tensor_tensor(out=msin[:], in0=msin[:], scalar=MAGIC, in1=raw[:],
                                                                     op0=ALU.subtract, op1=ALU.subtract),
                              R=["s5_rawt", "s5_msin"], W=["s5_msin"])
                            I("dve", lambda e: e.scalar_tensor_tensor(out=mcos[:], in0=msin[:], scalar=-1.0, in1=msin[:], op0=ALU.mult, op1=ALU.max),
                              R=["s5_msin"], W=["s5_mcos"])
                            I("act", lambda e: e.activation(out=msin[:], in_=msin[:], func=AF.Sin, scale=-TWO_PI),
                              R=["s5_msin"], W=["s5_msin"])
                            I("act", lambda e: e.activation(out=mcos[:], in_=mcos[:], func=AF.Sin, scale=-TWO_PI, bias=halfpi[:]),
                              R=["s5_mcos", "negpi"], W=["s5_mcos"])
                            pA, pA2 = pb[0], pb[1]
                            I("pe", lambda e, g_=g_: e.matmul(pA[:], lhsT=wA[:, g_, :], rhs=hT[:, ct, ts], start=True, stop=True),
                              R=["s5_wA", ("hT", ct, q)], W=["pb0"])
                            I("pe", lambda e, g_=g_: e.matmul(pA2[:], lhsT=wA2[:, g_, :], rhs=hT[:, ct, ts], start=True, stop=True),
                              R=["s5_wA2", ("hT", ct, q)], W=["pb1"])
                            k1 = nxt("tA")
                            k2 = nxt("tB")
                            I("dve", lambda e, k1=k1: e.tensor_tensor(out=tA[k1][:], in0=mcos[:], in1=pA[:], op=ALU.mult),
                              R=["s5_mcos", "pb0"], W=[("tA", k1)])
                            I("dve", lambda e, k2=k2: e.tensor_tensor(out=tB[k2][:], in0=msin[:], in1=pA2[:], op=ALU.mult),
                              R=["s5_msin", "pb1"], W=[("tB", k2)])
                            I("pool", lambda e, k1=k1, k2=k2: e.tensor_tensor(out=bt[:], in0=tA[k1][:], in1=tB[k2][:], op=ALU.add),
                              R=[("tA", k1), ("tB", k2)], W=["s5_bt"])
                            if q == 0:
                                I("dve", lambda e, g=g: e.tensor_tensor_scan(out=st[:], data0=p["r"][:, g:g + 1].to_broadcast([128, 512]), data1=bt[:], initial=0.0,
                                                                       op0=ALU.mult, op1=ALU.add),
                                  R=["s5_r", "s5_bt"], W=["s5_st"])
                            else:
                                I("dve", lambda e, g=g: e.tensor_tensor_scan(out=st[:], data0=p["r"][:, g:g + 1].to_broadcast([128, 512]), data1=bt[:],
                                                                            initial=state[:, g:g + 1], op0=ALU.mult, op1=ALU.add),
                                  R=["s5_r", "s5_bt", ("s5_state", g)], W=["s5_st"])
                            I("dve", lambda e, g=g: e.tensor_copy(out=state[:, g:g + 1], in_=st[:, 511:512]),
                              R=["s5_st"], W=[("s5_state", g)])
                            I("pool", lambda e: e.tensor_tensor(out=P1[:], in0=mcos[:], in1=st[:], op=ALU.mult),
                              R=["s5_mcos", "s5_st"], W=["s5_P1"])
                            I("pool", lambda e: e.tensor_tensor(out=P2[:], in0=msin[:], in1=st[:], op=ALU.mult),
                              R=["s5_msin", "s5_st"], W=["s5_P2"])
                            I("pe", lambda e, g_=g_: e.matmul(py[:], lhsT=W1[:, g_, :], rhs=P1[:], start=(g_ == 0), stop=False),
                              R=["s5_W1", "s5_P1"], W=["pb5"])
                            I("pe", lambda e, g_=g_: e.matmul(py[:], lhsT=W2[:, g_, :], rhs=P2[:], start=False, stop=(g_ == 7)),
                              R=["s5_W2", "s5_P2"], W=["pb5"])
                        I("dve", lambda e: e.scalar_tensor_tensor(out=yv[:], in0=hT[:, ct, ts], scalar=dv[:, ct:ct + 1], in1=py[:],
                                                                 op0=ALU.mult, op1=ALU.add),
                          R=[("hT", ct, q), "s5_dv", "pb5"], W=["s5_yv"])
                        I("act", lambda e: e.activation(out=zz[:], in_=yv[:], func=AF.Square, scale=math.sqrt(GELU_C1)),
                          R=["s5_yv"], W=["s5_zz"])
                        I("dve", lambda e: e.scalar_tensor_tensor(out=zz[:], in0=zz[:], scalar=1.0, in1=yv[:], op0=ALU.add, op1=ALU.mult),
                          R=["s5_zz", "s5_yv"], W=["s5_zz"])
                        I("act", lambda e: e.activation(out=zz[:], in_=zz[:], func=AF.Sigmoid, scale=2.0 * GELU_C0),
                          R=["s5_zz"], W=["s5_zz"])
                        I("dve", lambda e: e.tensor_tensor(out=gT[:, ct, ts], in0=yv[:], in1=zz[:], op=ALU.mult),
                          R=["s5_zz", "s5_yv"], W=[("s5_gT", ct, q)])
                for oc in range(DC):
                    s = load_w256(s5w_d[j5, oc])
                    for tt in range(NT):
                        ts = slice(tt * 512, (tt + 1) * 512)
                        pv, pgt, kv, kg = mm_pair(s, gT, lambda k, tt: ("s5_gT", k, tt), tt)
                        k2 = nxt("tB")
                        I("act", lambda e, k2=k2: e.activation(out=tB[k2][:], in_=pgt[:], func=AF.Sigmoid, bias=dv[:, 16 + oc:17 + oc]),
                          R=[kg, "s5_dv"], W=[("tB", k2)])
                        k1 = nxt("tA")
                        I("dve", lambda e, k1=k1, k2=k2: e.scalar_tensor_tensor(out=tA[k1][:], in0=pv[:], scalar=dv[:, 8 + oc:9 + oc],
                                                                               in1=tB[k2][:], op0=ALU.add, op1=ALU.mult),
                          R=[kv, "s5_dv", ("tB", k2)], W=[("tA", k1)])
                        I("pool", lambda e, k1=k1: e.tensor_tensor(out=xT[:, oc, ts], in0=xT[:, oc, ts], in1=tA[k1][:], op=ALU.add),
                          R=[("tA", k1), ("xT", oc, tt)], W=[("xT", oc, tt)])
                P.release([k for k in list(P.res) if (k[0] if isinstance(k, tuple) else k).startswith("s5_")])

        for sq_i in range(NS):
            with ExitStack() as sc:
                xin = [sb(nc, sc, "xin%d" % i, [128, D], F32) for i in range(2)]
                for tb in range(NB):
                    s = nxt("xin")
                    I("sp", lambda e, s=s, tb=tb: e.dma_start(out=xin[s][:], in_=x_d[sq_i, tb * 128:(tb + 1) * 128, :]),
                      W=[("xin", s)], dma=("xin", s))
                    for half in range(2):
                        pt = pb[6]
                        for cc in range(4):
                            c = half * 4 + cc
                            I("pe", lambda e, s=s, c=c, cc=cc: e.transpose(
                                out=pt[:, cc * 128:(cc + 1) * 128], in_=xin[s][:, c * 128:(c + 1) * 128],
                                identity=ident[:]),
                              R=[("xin", s), "ident"], W=["pb6"])
                        I("act", lambda e, half=half, tb=tb: e.activation(
                            out=xT[:, half * 4:half * 4 + 4, tb * 128:(tb + 1) * 128],
                            in_=pt[:].rearrange("p (c t) -> p c t", c=4), func=AF.Copy),
                          R=["pb6"], W=[("xT", half * 4 + cc, tb // 4) for cc in range(4)])
                P.release_prefix({"xin"})
            i5 = 0
            i0 = 0
            i2 = 0
            for li, kind in enumerate(cfg.layers):
                if cfg.do_ffn:
                    ffn(2 * li, 3 * li)
                if kind == 1:
                    s5_mixer(i5, 3 * li + 1)
                    i5 += 1
                if kind == 0:
                    ssd_mixer(i0, 3 * li + 1)
                    i0 += 1
                if kind == 2:
                    dsa_mixer(i2, 3 * li + 1)
                    i2 += 1
                if cfg.do_ffn:
                    ffn(2 * li + 1, 3 * li + 2)
            with ExitStack() as sc:
                xin = [sb(nc, sc, "xin%d" % i, [128, D], F32) for i in range(2)]
                for tt in range(NT):
                    ts = slice(tt * 512, (tt + 1) * 512)
                    rmsnorm_to(lambda c, ts=ts: xT[:, c, ts], 3 * depth, lambda c, tt=tt: ("xT", c, tt), tt)
                    for q in range(4):
                        tb = tt * 4 + q
                        s = nxt("xin")
                        for half in range(2):
                            pt = pb[6]
                            for cc in range(4):
                                c = half * 4 + cc
                                I("pe", lambda e, c=c, cc=cc, tb=tb: e.transpose(
                                    out=pt[:, cc * 128:(cc + 1) * 128], in_=xT[:, c, tb * 128:(tb + 1) * 128],
                                    identity=ident[:]),
                                  R=[("xT", c, tt), "ident"], W=["pb6"])
                            I("act", lambda e, half=half, s=s: e.activation(
                                out=xin[s][:, half * 512:(half + 1) * 512], in_=pt[:], func=AF.Copy),
                              R=["pb6"], W=[("xin", s)])
                        I("sp", lambda e, s=s, tb=tb: e.dma_start(out=y_d[sq_i, tb * 128:(tb + 1) * 128, :], in_=xin[s][:]),
                          R=[("xin", s)], dma=("yout", s))
                P.release_prefix({"xin"})
        P.finish()
        print("instructions:", P.ninst, "sems:", P.nsem)
    return nc


def w256_layout(w, ncol_blocks, col_a, col_b):
    out = np.empty((ncol_blocks, 128, DC, 256), np.float32)
    wk = w.reshape(DC, 128, -1)
    for b in range(ncol_blocks):
        out[b, :, :, 0:128] = wk[:, :, col_a + b * 128: col_a + (b + 1) * 128].transpose(1, 0, 2)
        out[b, :, :, 128:256] = wk[:, :, col_b + b * 128: col_b + (b + 1) * 128].transpose(1, 0, 2)
    return out.reshape(ncol_blocks, 128, DC * 256)


def prep_ssd(inp, cfg, m):
    f32 = np.float32
    n0 = cfg.n_ssd
    win = np.empty((n0, 24, 128, DC, 256), f32)
    wdt = np.empty((n0, 128, DC * 32), f32)
    wout = np.empty((n0, DC, 128, 16 * 128), f32)
    cwb = np.empty((n0, 128, 32, 5), f32)
    hp = np.empty((n0, 128, 112), f32)
    for j in range(n0):
        W = inp["ssd_in_proj"][j]
        wk = W.reshape(DC, 128, -1)
        for b in range(24):
            win[j, b] = wk[:, :, 256 * b:256 * (b + 1)].transpose(1, 0, 2)
        wdt[j] = wk[:, :, 6144:6176].transpose(1, 0, 2).reshape(128, DC * 32)
        Wo = inp["ssd_out_proj"][j].reshape(16, 128, DC, 128)
        wout[j] = Wo.transpose(2, 1, 0, 3).reshape(DC, 128, 16 * 128)
        cw = inp["ssd_conv_w"][j].reshape(4, 32, 128)
        cwb[j, :, :, 0:4] = cw.transpose(2, 1, 0)
        cwb[j, :, :, 4] = inp["ssd_conv_b"][j].reshape(32, 128).T
        hp[j, :, 0:32] = np.broadcast_to(inp["ssd_dt_bias"][j][None, :], (128, 32))
        hp[j, :, 32:64] = np.broadcast_to(inp["ssd_a_log"][j][None, :], (128, 32))
        hp[j, :, 64:96] = np.broadcast_to(inp["ssd_d"][j][None, :], (128, 32))
        hp[j, :, 96:112] = inp["ssd_gate_norm"][j].reshape(16, 128).T
    kc = np.zeros((128, 768), f32)
    tri = (np.arange(128)[:, None] <= np.arange(128)[None, :])
    kc[:, 0:128] = tri.astype(f32)
    cb = np.where(np.arange(128)[:, None] > np.arange(128)[None, :], -30000.0, 0.0).astype(f32)
    kc[:, 128:640] = np.tile(cb, (1, 4))
    kc[:, 640:768] = 1.0
    m.update({"ssd_win": win.reshape(n0, 24, 128, DC * 256), "ssd_wdt": wdt, "ssd_wout": wout,
              "ssd_cw": cwb.reshape(n0, 128, 160), "ssd_hp": hp, "ssd_k": kc})


def w256_pairs(wa, wb):
    out = np.empty((128, DC, 256), np.float32)
    out[:, :, 0:128] = wa.reshape(DC, 128, 128).transpose(1, 0, 2)
    out[:, :, 128:256] = wb.reshape(DC, 128, 128).transpose(1, 0, 2)
    return out.reshape(128, DC * 256)


def prep_dsa(inp, cfg, m):
    f32 = np.float32
    n2 = cfg.n_dsa
    L = cfg.L
    win = np.empty((n2, 14, 128, DC * 256), f32)
    wvw = np.empty((n2, 128, DC * 72), f32)
    wo = np.empty((n2, 4, 128, DC * 256), f32)
    perm = np.concatenate([np.arange(32, 64), np.arange(0, 32)])
    perm128 = np.concatenate([perm, 64 + perm])
    for j in range(n2):
        W = inp["dsa_in_proj"][j]
        Wq, Wk, Wv = W[:, 0:1024], W[:, 1024:1088], W[:, 1088:1152]
        Wqi, Wki, Wwi = W[:, 1152:1664], W[:, 1664:1728], W[:, 1728:1736]
        blocks = [Wq[:, c * 128:(c + 1) * 128] for c in range(8)]
        blocks.append(np.concatenate([Wk, Wk], 1))
        blocks += [Wqi[:, c * 128:(c + 1) * 128] for c in range(4)]
        blocks.append(np.concatenate([Wki, Wki], 1))
        for b, A in enumerate(blocks):
            win[j, b] = w256_pairs(A, A[:, perm128])
        vw = np.concatenate([Wv, Wwi], 1)
        wvw[j] = vw.reshape(DC, 128, 72).transpose(1, 0, 2).reshape(128, DC * 72)
        Wo = inp["dsa_out_proj"][j]
        for b in range(4):
            wo[j, b] = w256_pairs(Wo[:, 256 * b:256 * b + 128], Wo[:, 256 * b + 128:256 * b + 256])
    inv = (10000.0 ** (-np.arange(32, dtype=np.float64) / 32.0))
    ang = np.arange(L, dtype=np.float64)[None, :] * inv[np.arange(128) % 32][:, None]
    sgn = np.where((np.arange(128) % 64) < 32, -1.0, 1.0)[:, None]
    rope = np.stack([np.cos(ang), np.sin(ang) * sgn], 0).astype(f32)
    kd = np.zeros((128, 640), f32)
    kd[:, 0:128] = np.where(np.arange(128)[None, :] > np.arange(128)[:, None], -1.0e30, 0.0)
    kd[:, 128:640] = np.tile(np.eye(128, dtype=f32), (1, 4))
    m.update({"dsa_win": win, "dsa_wvw": wvw, "dsa_wo": wo, "dsa_rope": rope, "dsa_k": kd})


def prep_common(inp, cfg):
    depth = cfg.depth
    f32 = np.float32
    g = []
    for i in range(depth):
        g += [inp["ffn1_norm"][i], inp["mix_norm"][i], inp["ffn2_norm"][i]]
    g.append(inp["final_norm"])
    g = np.stack(g, 0).astype(f32)
    gains = np.ascontiguousarray(g.reshape(-1, DC, 128).transpose(2, 0, 1))
    wi = np.empty((2 * depth, FC, 128, DC * 256), f32)
    wo = np.empty((2 * depth, 2, DC, 128, 11 * 128), f32)
    for i in range(depth):
        for which, (kin, kout) in enumerate((("ffn1_w_in", "ffn1_w_out"), ("ffn2_w_in", "ffn2_w_out"))):
            w_in = inp[kin][i]
            w_out = inp[kout][i]
            wi[2 * i + which] = w256_layout(w_in, FC, 0, FFN)
            b = w_out.reshape(2, 11, 128, DC, 128)
            b = b.transpose(0, 3, 2, 1, 4)
            wo[2 * i + which] = b.reshape(2, DC, 128, 11 * 128)
    m = {"gains": gains, "ident": np.eye(128, dtype=f32), "ffn_wi": wi, "ffn_wo": wo}
    if cfg.n_s5:
        n5 = cfg.n_s5
        par = np.empty((n5, 3, 128, 64), f32)
        sb_ = np.empty((n5, 2, 64, 1024), f32)
        sc_ = np.empty((n5, 2, 128, 1024), f32)
        sd_ = np.empty((n5, 128, 24), f32)
        sw_ = np.empty((n5, DC, 128, DC * 256), f32)
        for j in range(n5):
            lre = inp["s5_lam_re"][j].T
            lim = inp["s5_lam_im"][j].T
            lst = np.broadcast_to(inp["s5_log_step"][j][None, :], (64, 64))
            for a, t in enumerate((lre, lim, lst)):
                par[j, a, 0:64] = t
                par[j, a, 64:128] = t
            sb_[j, 0] = inp["s5_b_re"][j].transpose(1, 0, 2).reshape(64, 1024)
            sb_[j, 1] = inp["s5_b_im"][j].transpose(1, 0, 2).reshape(64, 1024)
            cre = inp["s5_c_re"][j].transpose(2, 0, 1).reshape(64, 1024)
            cim = inp["s5_c_im"][j].transpose(2, 0, 1).reshape(64, 1024)
            sc_[j, 0, 0:64] = cre
            sc_[j, 0, 64:128] = cim
            sc_[j, 1, 0:64] = cim
            sc_[j, 1, 64:128] = cre
            sd_[j, :, 0:8] = inp["s5_d"][j].reshape(DC, 128).T
            sd_[j, :, 8:24] = inp["s5_glu_b"][j].reshape(16, 128).T
            sw_[j] = w256_layout(inp["s5_glu_w"][j], DC, 0, D)
        kk = np.zeros((128, 16), f32)
        kk[0:64, 0] = 1.0
        kk[64:128, 0] = -1.0
        kk[:, 1] = -1.0
        for g_ in range(8):
            kk[16 * g_:16 * g_ + 16, 2 + g_] = 1.0
        m.update({"s5_par": par, "s5_b": sb_, "s5_c": sc_, "s5_k": kk, "s5_dv": sd_, "s5_glu": sw_,
                  "iota16": np.broadcast_to(np.arange(cfg.L, dtype=np.int16)[None, :], (128, cfg.L)).copy()})
    if cfg.n_ssd:
        prep_ssd(inp, cfg, m)
    if cfg.n_dsa:
        prep_dsa(inp, cfg, m)
    return m


def kernel(**inputs):
    cfg = Cfg()
    inp = {k: np.asarray(v) for k, v in inputs.items()}
    common = prep_common(inp, cfg)
    x = inp["x"].astype(np.float32)
    B = x.shape[0]
    per = B // N_CORES
    nc = build(cfg)
    in_maps = []
    for c in range(N_CORES):
        m = dict(common)
        m["x"] = np.ascontiguousarray(x[c * per:(c + 1) * per])
        in_maps.append(m)
    res = run_bass_kernel_spmd(nc, in_maps, core_ids=list(range(N_CORES)))
    out = np.concatenate([r["y"] for r in res.results], axis=0)
    return out.astype(np.float32)
```

```python
import math
import os
from contextlib import ExitStack

import numpy as np
import concourse.bass as bass
import concourse.mybir as mybir
from concourse.bass_utils import run_bass_kernel_spmd

F32 = mybir.dt.float32
BF16 = mybir.dt.bfloat16
AF = mybir.ActivationFunctionType
ALU = mybir.AluOpType
AX = mybir.AxisListType

D = 1024
DC = 8
FFN = 2816
FC = 22
EPS = 1e-6
N_CORES = 8

ENGS = ("pe", "act", "dve", "pool", "sp")
EPOCH = 12000


class Prog:
    def __init__(self, nc, es):
        self.nc = nc
        self.es = es
        self.eng = {"pe": nc.tensor, "act": nc.scalar, "dve": nc.vector,
                    "pool": nc.gpsimd, "sp": nc.sync}
        self.nsem = 0
        self.esem = {}
        self.ecnt = {}
        self.eepoch = {}
        for e in ENGS:
            self.eepoch[e] = 0
            self.ecnt[e] = 0
            self.esem[e] = self._newsem()
        self.seen = {e: {} for e in ENGS}
        self.res = {}
        self.dsem = {}
        self.free_ev = {}
        self.ninst = 0

    def _newsem(self):
        self.nsem += 1
        return self.es.enter_context(self.nc.semaphore("s%d" % self.nsem))

    def _need(self, eng, deps):
        out = {}
        for ev in deps:
            if ev is None:
                continue
            sem, val, kind = ev
            if kind == "pe" and eng == "pe":
                continue
            nm = sem.name
            if self.seen[eng].get(nm, 0) >= val:
                continue
            if nm not in out or out[nm][1] < val:
                out[nm] = ev
        return list(out.values())

    def I(self, eng, fn, R=(), W=(), dma=None):
        pr = [r for r in R if isinstance(r, str) and r.startswith("pb")]
        if pr:
            R = [r for r in R if r not in pr]
            W = list(W) + [r for r in pr if r not in W]
        deps = []
        for r in R:
            ent = self.res.get(r)
            if ent is not None:
                deps.append(ent[0])
        for w in W:
            ent = self.res.get(w)
            if ent is not None:
                deps.append(ent[0])
                deps.extend(ent[1].values())
            else:
                deps.extend(self.free_ev.values())
        need = self._need(eng, deps)
        e = self.eng[eng]
        for sem, val, kind in need:
            e.wait_ge(sem, val)
            self.seen[eng][sem.name] = val
        ins = fn(e)
        if dma is not None:
            ent = self.dsem.get(dma)
            if ent is None:
                ent = [self._newsem(), 0]
                self.dsem[dma] = ent
            ent[1] += 16
            ins.then_inc(ent[0], 16)
            ev = (ent[0], ent[1], "dma")
        else:
            if self.ecnt[eng] >= EPOCH:
                self.esem[eng] = self._newsem()
                self.ecnt[eng] = 0
                self.eepoch[eng] += 1
            self.ecnt[eng] += 1
            ins.then_inc(self.esem[eng], 1)
            ev = (self.esem[eng], self.ecnt[eng], eng)
        for r in R:
            ent = self.res.get(r)
            if ent is None:
                ent = [None, {}]
                self.res[r] = ent
            ent[1][ev[0].name] = ev
        for w in W:
            self.res[w] = [ev, {}]
        self.ninst += 1
        return ev

    def release(self, keys):
        for k in keys:
            ent = self.res.pop(k, None)
            if ent is None:
                continue
            for ev in [ent[0]] + list(ent[1].values()):
                if ev is None:
                    continue
                nm = ev[0].name
                if nm not in self.free_ev or self.free_ev[nm][1] < ev[1]:
                    self.free_ev[nm] = ev

    def release_prefix(self, prefixes):
        ks = [k for k in self.res if (k[0] if isinstance(k, tuple) else k) in prefixes]
        self.release(ks)

    def finish(self):
        allev = list(self.free_ev.values())
        for ent in self.res.values():
            allev.append(ent[0])
            allev.extend(ent[1].values())
        for eng in ENGS:
            for sem, val, kind in self._need(eng, allev):
                if kind == eng and eng != "sp":
                    pass
                self.eng[eng].wait_ge(sem, val)
                self.seen[eng][sem.name] = val


_UID = [0]


def sb(nc, es, name, shape, dt):
    _UID[0] += 1
    return es.enter_context(nc.sbuf_tensor("%s_u%d" % (name, _UID[0]), shape, dt))


def ps(nc, es, name, shape, dt):
    return es.enter_context(nc.psum_tensor(name, shape, dt))


class Cfg:
    def __init__(self, nseq=4, L=2048, layers=(0, 1, 2, 0), do_ffn=True):
        self.nseq = nseq
        self.L = L
        self.layers = tuple(layers)
        self.do_ffn = do_ffn
        self.depth = len(self.layers)
        self.n_ssd = sum(1 for k in self.layers if k == 0)
        self.n_s5 = sum(1 for k in self.layers if k == 1)
        self.n_dsa = sum(1 for k in self.layers if k == 2)


TWO_PI = 2.0 * math.pi
GELU_C0 = math.sqrt(2.0 / math.pi)
GELU_C1 = 0.044715
MAGIC = 12582912.0


def build(cfg):
    nc = bass.Bass("TRN2", target_bir_lowering=False)
    L = cfg.L
    NS = cfg.nseq
    NT = L // 512
    NB = L // 128
    depth = cfg.depth

    def din(name, shape, dt=F32):
        return nc.dram_tensor(name, list(shape), dt, kind="ExternalInput").ap()

    x_d = din("x", [NS, L, D])
    y_d = nc.dram_tensor("y", [NS, L, D], F32, kind="ExternalOutput").ap()
    gains_d = din("gains", [128, 3 * depth + 1, DC])
    ident_d = din("ident", [128, 128])
    wi_d = din("ffn_wi", [2 * depth, FC, 128, DC * 256])
    wo_d = din("ffn_wo", [2 * depth, 2, DC, 128, 11 * 128])
    if cfg.n_s5:
        n5 = cfg.n_s5
        s5p_d = din("s5_par", [n5, 3, 128, 64])
        s5b_d = din("s5_b", [n5, 2, 64, 1024])
        s5c_d = din("s5_c", [n5, 2, 128, 1024])
        s5k_d = din("s5_k", [128, 16])
        s5d_d = din("s5_dv", [n5, 128, 24])
        s5w_d = din("s5_glu", [n5, DC, 128, DC * 256])
        iota_d = din("iota16", [128, L], mybir.dt.int16)

    if cfg.n_ssd:
        n0 = cfg.n_ssd
        ssd_win_d = din("ssd_win", [n0, 24, 128, DC * 256])
        ssd_wdt_d = din("ssd_wdt", [n0, 128, DC * 32])
        ssd_wout_d = din("ssd_wout", [n0, DC, 128, 16 * 128])
        ssd_cw_d = din("ssd_cw", [n0, 128, 32 * 5])
        ssd_hp_d = din("ssd_hp", [n0, 128, 3 * 32 + 16])
        ssd_k_d = din("ssd_k", [128, 128 + 512 + 128])

    if cfg.n_dsa:
        n2 = cfg.n_dsa
        dsa_win_d = din("dsa_win", [n2, 14, 128, DC * 256])
        dsa_wvw_d = din("dsa_wvw", [n2, 128, DC * 72])
        dsa_wo_d = din("dsa_wo", [n2, 4, 128, DC * 256])
        dsa_rope_d = din("dsa_rope", [2, 128, L])
        dsa_k_d = din("dsa_k", [128, 640])

    with ExitStack() as es:
        P = Prog(nc, es)
        I = P.I
        xT = sb(nc, es, "xT", [128, DC, L], F32)
        hT = sb(nc, es, "hT", [128, DC, L], BF16)
        gains = sb(nc, es, "gains_sb", [128, 3 * depth + 1, DC], F32)
        ident = sb(nc, es, "ident_sb", [128, 128], F32)
        ones_bf = sb(nc, es, "ones_bf", [128, 128], BF16)
        epsb = sb(nc, es, "epsb", [128, 1], F32)
        negpi = sb(nc, es, "negpi", [128, 1], F32)
        sq = [sb(nc, es, "sq%d" % i, [128, 512], BF16) for i in range(2)]
        tA = [sb(nc, es, "tA%d" % i, [128, 512], F32) for i in range(2)]
        tB = [sb(nc, es, "tB%d" % i, [128, 512], F32) for i in range(2)]
        NWI = 2
        wi = [sb(nc, es, "wi%d" % i, [128, DC, 256], BF16) for i in range(NWI)]
        pb = [ps(nc, es, "pb%d" % i, [128, 512], F32) for i in range(7)]
        pbh = ps(nc, es, "pbh", [128, 1024], BF16)
        ident_bf = sb(nc, es, "ident_bf", [128, 128], BF16)

        cnt = {"wi": 0, "wo": 0, "sq": 0, "tA": 0, "tB": 0, "xin": 0, "pg": 0, "po": 0}

        I("sp", lambda e: e.dma_start(out=gains[:], in_=gains_d), W=["gains"], dma="gains")
        I("sp", lambda e: e.dma_start(out=ident[:], in_=ident_d), W=["ident"], dma="ident")
        I("dve", lambda e: e.memset(ones_bf[:], 1.0), W=["ones"])
        I("dve", lambda e: e.tensor_copy(out=ident_bf[:], in_=ident[:]), R=["ident"], W=["ident_bf"])
        oneb = sb(nc, es, "oneb", [128, 1], F32)
        I("dve", lambda e: e.memset(oneb[:], 1.0), W=["oneb"])
        I("dve", lambda e: e.memset(epsb[:], EPS), W=["epsb"])
        I("dve", lambda e: e.memset(negpi[:], -math.pi), W=["negpi"])
        halfpi = sb(nc, es, "halfpi", [128, 1], F32)
        I("dve", lambda e: e.memset(halfpi[:], math.pi / 2), W=["negpi"])

        def nxt(name, n=2):
            k = cnt[name] % n
            cnt[name] += 1
            return k

        def rmsnorm_to(dst_fn, gidx, dst_key_fn, tt):
            ts = slice(tt * 512, (tt + 1) * 512)
            pn = pb[6]
            for c in range(DC):
                k = nxt("sq")
                I("act", lambda e, c=c, k=k: e.activation(out=sq[k][:], in_=xT[:, c, ts], func=AF.Square),
                  R=[("xT", c, tt)], W=[("sq", k)])
                I("pe", lambda e, c=c, k=k: e.matmul(pn[:], lhsT=ones_bf[:], rhs=sq[k][:],
                                                      start=(c == 0), stop=(c == DC - 1)),
                  R=[("sq", k), "ones"], W=["pb6"])
            k = nxt("tA")
            I("act", lambda e, k=k: e.activation(out=tA[k][:], in_=pn[:], func=AF.Sqrt,
                                                 scale=1.0 / D, bias=epsb[:]),
              R=["pb6", "epsb"], W=[("tA", k)])
            I("dve", lambda e, k=k: e.reciprocal(out=tA[k][:], in_=tA[k][:]),
              R=[("tA", k)], W=[("tA", k)])
            for c in range(DC):
                I("dve", lambda e, c=c, k=k: e.scalar_tensor_tensor(
                    out=dst_fn(c), in0=xT[:, c, ts], scalar=gains[:, gidx, c:c + 1], in1=tA[k][:],
                    op0=ALU.mult, op1=ALU.mult),
                  R=[("xT", c, tt), ("tA", k), "gains"], W=[dst_key_fn(c)])

        def norm_to_hT(gidx):
            for tt in range(NT):
                ts = slice(tt * 512, (tt + 1) * 512)
                rmsnorm_to(lambda c, ts=ts: hT[:, c, ts], gidx, lambda c, tt=tt: ("hT", c, tt), tt)

        def load_w256(src_ap, key="wi"):
            s = nxt("wi", NWI)
            I("pool", lambda e, s=s: e.dma_start(out=wi[s][:], in_=src_ap.rearrange("p (k f) -> p k f", k=DC)),
              W=[("wi", s)], dma=("wi", s))
            return s

        def mm_pair(s, src, src_key_fn, tt):
            ts = slice(tt * 512, (tt + 1) * 512)
            b = nxt("pg")
            pg, pu = pb[2 * b], pb[2 * b + 1]
            for k in range(DC):
                I("pe", lambda e, k=k: e.matmul(pg[:], lhsT=wi[s][:, k, 0:128], rhs=src[:, k, ts],
                                                start=(k == 0), stop=(k == DC - 1)),
                  R=[("wi", s), src_key_fn(k, tt)], W=["pb%d" % (2 * b)])
            for k in range(DC):
                I("pe", lambda e, k=k: e.matmul(pu[:], lhsT=wi[s][:, k, 128:256], rhs=src[:, k, ts],
                                                start=(k == 0), stop=(k == DC - 1)),
                  R=[("wi", s), src_key_fn(k, tt)], W=["pb%d" % (2 * b + 1)])
            return pg, pu, "pb%d" % (2 * b), "pb%d" % (2 * b + 1)

        def ffn(fidx, gidx):
            norm_to_hT(gidx)
            with ExitStack() as sc:
                aT = sb(nc, sc, "aT", [128, 11, L], BF16)
                NWO = 3
                wo = [sb(nc, sc, "wo%d" % i, [128, 11, 128], BF16) for i in range(NWO)]
                for hf in range(2):
                    for j in range(11):
                        fc = hf * 11 + j
                        s = load_w256(wi_d[fidx, fc])
                        for tt in range(NT):
                            ts = slice(tt * 512, (tt + 1) * 512)
                            pg, pu, kg, ku = mm_pair(s, hT, lambda k, tt: ("hT", k, tt), tt)
                            g = nxt("tB")
                            I("act", lambda e, g=g, pg=pg: e.activation(out=tB[g][:], in_=pg[:], func=AF.Silu),
                              R=[kg], W=[("tB", g)])
                            I("dve", lambda e, g=g, pu=pu, j=j, ts=ts: e.tensor_tensor(
                                out=aT[:, j, ts], in0=tB[g][:], in1=pu[:], op=ALU.mult),
                              R=[("tB", g), ku], W=[("aT", j, tt)])
                    for c in range(DC):
                        s = nxt("wo", NWO)
                        I("pool", lambda e, s=s, c=c, hf=hf: e.dma_start(
                            out=wo[s][:], in_=wo_d[fidx, hf, c].rearrange("p (j d) -> p j d", j=11)),
                          W=[("wo", s)], dma=("wo", s))
                        for tt in range(NT):
                            ts = slice(tt * 512, (tt + 1) * 512)
                            b = nxt("po")
                            po = pb[4 + b]
                            for j in range(11):
                                I("pe", lambda e, s=s, j=j, ts=ts, po=po: e.matmul(
                                    po[:], lhsT=wo[s][:, j, :], rhs=aT[:, j, ts],
                                    start=(j == 0), stop=(j == 10)),
                                  R=[("wo", s), ("aT", j, tt)], W=["pb%d" % (4 + b)])
                            I("dve", lambda e, po=po, c=c, ts=ts: e.scalar_tensor_tensor(
                                out=xT[:, c, ts], in0=po[:], scalar=0.5, in1=xT[:, c, ts],
                                op0=ALU.mult, op1=ALU.add),
                              R=["pb%d" % (4 + b), ("xT", c, tt)], W=[("xT", c, tt)])
                P.release_prefix({"aT", "wo"})


        def ssd_mixer(j0, gidx):
            norm_to_hT(gidx)
            TT2 = 256
            with ExitStack() as sc:
                def T(name, shape, dt=F32):
                    return sb(nc, sc, "ssd_" + name, shape, dt)
                xc = T("xc", [128, 32, TT2], BF16)
                zT = T("zT", [128, 16, TT2], BF16)
                ynT = T("ynT", [128, 16, TT2], BF16)
                halo = T("halo", [128, 32, 3])
                xp = [T("xp%d" % i, [128, 3 + TT2]) for i in range(2)]
                acc = [T("acc%d" % i, [128, TT2]) for i in range(2)]
                wdt = T("wdt", [128, DC, 32], BF16)
                cw = T("cw", [128, 32, 5])
                hp = T("hp", [128, 3 * 32 + 16])
                kc = T("kc", [128, 128 + 512 + 128])
                Aneg = T("Aneg", [128, 32])
                dtr = T("dtr", [128, 32])
                dtv = T("dtv", [128, 32])
                av = T("av", [128, 32])
                ncs = T("ncs", [128, 32])
                ecs = T("ecs", [128, 32])
                cl = T("cl", [128, 32])
                wl = T("wl", [128, 32])
                dec = T("dec", [128, 32])
                xs_tok = T("xs_tok", [128, 32, 64], BF16)
                dtx = T("dtx", [128, 32, 64], BF16)
                Btok = T("Btok", [128, 8, 128], BF16)
                aU = [[T("aU%d_%d" % (i, j), [128, 4, 128], BF16) for j in range(3)] for i in range(1)]
                asp = [T("asp%d" % j, [128, 32], BF16) for j in range(3)]
                ares = [T("ares%d" % j, [128, 32]) for j in range(2)]
                kcb = T("kcb", [128, 640], BF16)
                Lt = [T("Lt%d" % i, [128, 4, 128], BF16) for i in range(1)]
                Mt = [T("Mt%d" % i, [128, 4, 128], BF16) for i in range(2)]
                CBs = [T("CBs%d" % i, [128, 128], BF16) for i in range(1)]
                tm1 = [tA[i][:, 0:256] for i in range(2)]
                tm2 = [tB[i][:, 0:256] for i in range(2)]
                Y = T("Y", [128, 2048])
                sz = T("sz", [128, 2048])
                Ynb = T("Ynb", [128, 2048], BF16)
                ss = T("ss", [128, 8])
                S = T("S", [128, 8, 256])
                Sbf = T("Sbf", [128, 8, 256], BF16)
                U = kc[:, 0:128]
                causal4 = kc[:, 128:640]
                ones_f = kc[:, 640:768]
                dtb, alog, Dbc, gn = hp[:, 0:32], hp[:, 32:64], hp[:, 64:96], hp[:, 96:112]

                def K_(n):
                    return "ssd_" + n

                def ld(dst, src, key, eng="sp"):
                    I(eng, lambda e: e.dma_start(out=dst, in_=src), W=[key], dma=key)
                ld(wdt[:], ssd_wdt_d[j0].rearrange("p (k f) -> p k f", k=DC), K_("wdt"), "pool")
                ld(cw[:], ssd_cw_d[j0].rearrange("p (c f) -> p c f", c=32), K_("cw"))
                ld(hp[:], ssd_hp_d[j0], K_("hp"))
                ld(kc[:], ssd_k_d, K_("kc"))
                I("dve", lambda e: e.tensor_copy(out=kcb[:], in_=kc[:, 0:640]), R=[K_("kc")], W=[K_("kcb")])
                Ub = kcb[:, 0:128]
                c4b = kcb[:, 128:640]
                I("act", lambda e: e.activation(out=Aneg[:], in_=alog, func=AF.Exp), R=[K_("hp")], W=[K_("Aneg")])
                I("dve", lambda e: e.tensor_scalar(out=Aneg[:], in0=Aneg[:], scalar1=-1.0, scalar2=None, op0=ALU.mult),
                  R=[K_("Aneg")], W=[K_("Aneg")])
                I("dve", lambda e: e.memset(S[:], 0.0), W=[K_("S")])
                I("dve", lambda e: e.memset(Sbf[:], 0.0), W=[K_("Sbf")])
                I("dve", lambda e: e.memset(halo[:], 0.0), W=[K_("halo")])
                cn = {"xp": 0, "g": 0}
                print("ssd sbuf remaining", nc.sbuf_bytes_remaining)

                for t2 in range(L // TT2):
                    tsl = slice(t2 * TT2, (t2 + 1) * TT2)
                    tt = (t2 * TT2) // 512
                    for blk in range(24):
                        s = load_w256(ssd_win_d[j0, blk])
                        b = nxt("pg")
                        pg, pu = pb[2 * b], pb[2 * b + 1]
                        for hh, (pp, kp) in enumerate(((pg, "pb%d" % (2 * b)), (pu, "pb%d" % (2 * b + 1)))):
                            for k in range(DC):
                                I("pe", lambda e, k=k, pp=pp, hh=hh: e.matmul(pp[:, 0:TT2], lhsT=wi[s][:, k, hh * 128:(hh + 1) * 128],
                                                                             rhs=hT[:, k, tsl], start=(k == 0), stop=(k == DC - 1)),
                                  R=[("wi", s), ("hT", k, tt)], W=[kp])
                            ch = 2 * blk + hh
                            if ch < 16:
                                I("act", lambda e, pp=pp, ch=ch: e.activation(out=zT[:, ch, :], in_=pp[:, 0:TT2], func=AF.Copy),
                                  R=[kp], W=[K_("zT")])
                            else:
                                xch = ch - 16
                                r = cn["xp"] % 2
                                cn["xp"] += 1
                                I("act", lambda e, pp=pp, r=r: e.activation(out=xp[r][:, 3:3 + TT2], in_=pp[:, 0:TT2], func=AF.Copy),
                                  R=[kp], W=[(K_("xp"), r)])
                                I("act", lambda e, r=r, xch=xch: e.activation(out=xp[r][:, 0:3], in_=halo[:, xch, :], func=AF.Copy),
                                  R=[(K_("halo"), xch), K_("halo")], W=[(K_("xp"), r)])
                                I("act", lambda e, r=r, xch=xch: e.activation(out=halo[:, xch, :], in_=xp[r][:, TT2:TT2 + 3], func=AF.Copy),
                                  R=[(K_("xp"), r)], W=[(K_("halo"), xch)])
                                I("dve", lambda e, r=r, xch=xch: e.tensor_scalar(out=acc[r][:], in0=xp[r][:, 0:TT2], scalar1=cw[:, xch, 0:1],
                                                                                scalar2=None, op0=ALU.mult),
                                  R=[(K_("xp"), r), K_("cw")], W=[(K_("acc"), r)])
                                for tap in range(1, 4):
                                    I("dve", lambda e, r=r, xch=xch, tap=tap: e.scalar_tensor_tensor(
                                        out=acc[r][:], in0=xp[r][:, tap:tap + TT2], scalar=cw[:, xch, tap:tap + 1], in1=acc[r][:],
                                        op0=ALU.mult, op1=ALU.add),
                                      R=[(K_("xp"), r), K_("cw"), (K_("acc"), r)], W=[(K_("acc"), r)])
                                I("act", lambda e, r=r, xch=xch: e.activation(out=xc[:, xch, :], in_=acc[r][:], func=AF.Silu,
                                                                             bias=cw[:, xch, 4:5]),
                                  R=[(K_("acc"), r), K_("cw")], W=[(K_("xc"), xch)])
                    STG = int(os.environ.get('DBG_SSD', 99))
                    for ck in range(TT2 // 128 if STG >= 2 else 0):
                        lsl = slice(ck * 128, (ck + 1) * 128)
                        asl = slice(t2 * TT2 + ck * 128, t2 * TT2 + (ck + 1) * 128)
                        pdt = pb[3][:, 0:32]
                        pcs = pb[3][:, 64:96]
                        for k in range(DC):
                            I("pe", lambda e, k=k: e.matmul(pdt, lhsT=hT[:, k, asl], rhs=wdt[:, k, :], start=(k == 0), stop=(k == DC - 1)),
                              R=[("hT", k, tt), K_("wdt")], W=["pb3"])
                        SUB = int(os.environ.get('DBG_SUB', 99))
                        if SUB < 2:
                            continue
                        I("dve", lambda e: e.tensor_tensor(out=dtr[:], in0=pdt, in1=dtb, op=ALU.add), R=["pb3", K_("hp")], W=[K_("dtr")])
                        if SUB < 3:
                            continue
                        I("act", lambda e: e.activation(out=dtr[:], in_=dtr[:], func=AF.Exp), R=[K_("dtr")], W=[K_("dtr")])
                        I("act", lambda e: e.activation(out=dtv[:], in_=dtr[:], func=AF.Ln, bias=oneb[:]), R=[K_("dtr"), "oneb"], W=[K_("dtv")])
                        if SUB < 4:
                            continue
                        I("dve", lambda e: e.tensor_tensor(out=av[:], in0=dtv[:], in1=Aneg[:], op=ALU.mult), R=[K_("dtv"), K_("Aneg")], W=[K_("av")])
                        if SUB < 5:
                            continue
                        I("dve", lambda e: e.tensor_copy(out=asp[0][:], in_=av[:]), R=[K_("av")], W=[(K_("asp"), 0)])
                        I("dve", lambda e: e.tensor_tensor(out=ares[0][:], in0=av[:], in1=asp[0][:], op=ALU.subtract),
                          R=[K_("av"), (K_("asp"), 0)], W=[(K_("ares"), 0)])
                        I("dve", lambda e: e.tensor_copy(out=asp[1][:], in_=ares[0][:]), R=[(K_("ares"), 0)], W=[(K_("asp"), 1)])
                        I("dve", lambda e: e.tensor_tensor(out=ares[1][:], in0=ares[0][:], in1=asp[1][:], op=ALU.subtract),
                          R=[(K_("ares"), 0), (K_("asp"), 1)], W=[(K_("ares"), 1)])
                        I("dve", lambda e: e.tensor_copy(out=asp[2][:], in_=ares[1][:]), R=[(K_("ares"), 1)], W=[(K_("asp"), 2)])
                        for j3 in range(3):
                            I("pe", lambda e, j3=j3: e.matmul(pcs, lhsT=Ub, rhs=asp[j3][:], start=(j3 == 0), stop=(j3 == 2)),
                              R=[K_("kcb"), (K_("asp"), j3)], W=["pb3"])
                        I("dve", lambda e: e.tensor_scalar(out=ncs[:], in0=pcs, scalar1=-1.0, scalar2=None, op0=ALU.mult),
                          R=["pb3"], W=[K_("ncs")])
                        I("act", lambda e: e.activation(out=ecs[:], in_=pcs, func=AF.Exp), R=["pb3"], W=[K_("ecs")])
                        if STG < 3:
                            continue
                        for r in range(2):
                            for q in range(8):
                                I("pe", lambda e, r=r, q=q: e.transpose(out=pbh[:, q * 128:(q + 1) * 128], in_=xc[:, 8 * r + q, lsl],
                                                                       identity=ident_bf[:]),
                                  R=[(K_("xc"), 8 * r + q), "ident_bf"], W=["pbh"])
                            I("dve", lambda e, r=r: e.tensor_tensor(out=xs_tok[:, 16 * r:16 * r + 16, :],
                                                                    in0=pbh[:].rearrange("p (h d) -> p h d", h=16),
                                                                    in1=Dbc[:, 16 * r:16 * r + 16].unsqueeze(2).to_broadcast([128, 16, 64]),
                                                                    op=ALU.mult),
                              R=["pbh", K_("hp")], W=[K_("xs_tok")])
                            I("dve", lambda e, r=r: e.tensor_tensor(out=dtx[:, 16 * r:16 * r + 16, :],
                                                                    in0=pbh[:].rearrange("p (h d) -> p h d", h=16),
                                                                    in1=dtv[:, 16 * r:16 * r + 16].unsqueeze(2).to_broadcast([128, 16, 64]),
                                                                    op=ALU.mult),
                              R=["pbh", K_("dtv")], W=[K_("dtx")])
                        for q in range(8):
                            I("pe", lambda e, q=q: e.transpose(out=pbh[:, q * 128:(q + 1) * 128], in_=xc[:, 16 + q, lsl], identity=ident_bf[:]),
                              R=[(K_("xc"), 16 + q), "ident_bf"], W=["pbh"])
                        I("act", lambda e: e.activation(out=Btok[:], in_=pbh[:].rearrange("p (g n) -> p g n", g=8), func=AF.Copy),
                          R=["pbh"], W=[K_("Btok")])
                        if STG < 4:
                            continue
                        def GA(g):
                            i2 = g % 2
                            X = pb[i2]
                            kX = "pb%d" % i2
                            for j3 in range(3):
                                I("dve", lambda e, g=g, i2=i2, j3=j3: e.tensor_tensor(
                                    out=aU[0][j3][:], in0=Ub.unsqueeze(1).to_broadcast([128, 4, 128]),
                                    in1=asp[j3][:, 4 * g:4 * g + 4].unsqueeze(2).to_broadcast([128, 4, 128]), op=ALU.mult),
                                  R=[K_("kcb"), (K_("asp"), j3)], W=[(K_("aU"), 0, j3)])
                                I("pe", lambda e, i2=i2, X=X, j3=j3: e.matmul(X[:], lhsT=ones_bf[:], rhs=aU[0][j3][:].rearrange("p h t -> p (h t)"),
                                                                          start=(j3 == 0), stop=False),
                                  R=["ones", (K_("aU"), 0, j3)], W=[kX])
                            I("pe", lambda e, X=X: e.matmul(X[:], lhsT=ident_bf[:], rhs=c4b, start=False, stop=True),
                              R=[K_("kcb"), "ident_bf"], W=[kX])
                            for h in range(4):
                                I("act", lambda e, h=h, g=g, i2=i2, X=X: e.activation(
                                    out=Lt[0][:, h, :], in_=X[:, h * 128:(h + 1) * 128], func=AF.Exp,
                                    bias=ncs[:, 4 * g + h:4 * g + h + 1]),
                                  R=[kX, K_("ncs")], W=[(K_("Lt"), 0)])
                            I("dve", lambda e, g=g, X=X: e.tensor_copy(
                                out=cl[:, 4 * g:4 * g + 4].unsqueeze(2),
                                in_=X[:].rearrange("p (h t) -> p h t", h=4)[:, :, 127:128]),
                              R=[kX], W=[(K_("cl"), g)])
                            CBp = pb[2][:, 0:128]
                            I("pe", lambda e, g=g, CBp=CBp: e.matmul(CBp, lhsT=xc[:, 16 + g, lsl], rhs=xc[:, 24 + g, lsl], start=True, stop=True),
                              R=[(K_("xc"), 16 + g), (K_("xc"), 24 + g)], W=["pb2"])
                            I("act", lambda e, i2=i2, CBp=CBp: e.activation(out=CBs[0][:], in_=CBp, func=AF.Copy),
                              R=["pb2"], W=[(K_("CBs"), 0)])
                            I("dve", lambda e, i2=i2: e.tensor_tensor(out=Mt[i2][:], in0=Lt[0][:],
                                                                      in1=CBs[0][:].unsqueeze(1).to_broadcast([128, 4, 128]), op=ALU.mult),
                              R=[(K_("Lt"), 0), (K_("CBs"), 0)], W=[(K_("Mt"), i2)])
                        def GB(g):
                            i2 = g % 2
                            pY = pb[5 + i2]
                            kY = "pb%d" % (5 + i2)
                            for h in range(4):
                                I("pe", lambda e, h=h, g=g, i2=i2: e.matmul(pY[:, 256 + h * 64:256 + (h + 1) * 64], lhsT=Mt[i2][:, h, :],
                                                                           rhs=dtx[:, 4 * g + h, :], start=True, stop=True),
                                  R=[(K_("Mt"), i2), K_("dtx")], W=[kY])
                            I("pe", lambda e, g=g: e.matmul(pY[:, 0:256], lhsT=xc[:, 24 + g, lsl], rhs=Sbf[:, g, :], start=True, stop=True),
                              R=[(K_("xc"), 24 + g), (K_("Sbf"), g)], W=[kY])
                            I("dve", lambda e, g=g, i2=i2: e.tensor_tensor(
                                out=tm1[i2].rearrange("p (h d) -> p h d", h=4), in0=pY[:, 0:256].rearrange("p (h d) -> p h d", h=4),
                                in1=ecs[:, 4 * g:4 * g + 4].unsqueeze(2).to_broadcast([128, 4, 64]), op=ALU.mult),
                              R=[kY, K_("ecs")], W=[("tA", i2)])
                            I("dve", lambda e, i2=i2: e.tensor_tensor(out=tm1[i2], in0=tm1[i2], in1=pY[:, 256:512], op=ALU.add),
                              R=[kY, ("tA", i2)], W=[("tA", i2)])
                            I("dve", lambda e, g=g, i2=i2: e.tensor_tensor(out=Y[:, g * 256:(g + 1) * 256], in0=tm1[i2],
                                                                         in1=xs_tok[:, 4 * g:4 * g + 4, :].rearrange("p h d -> p (h d)"), op=ALU.add),
                              R=[("tA", i2), K_("xs_tok")], W=[(K_("Y"), g)])
                        GA(0)
                        for g in range(8):
                            if g + 1 < 8:
                                GA(g + 1)
                            GB(g)
                        I("dve", lambda e: e.tensor_tensor(out=wl[:], in0=cl[:], in1=ncs[:], op=ALU.add),
                          R=[(K_("cl"), g) for g in range(8)] + [K_("ncs")], W=[K_("wl")])
                        I("act", lambda e: e.activation(out=wl[:], in_=wl[:], func=AF.Exp), R=[K_("wl")], W=[K_("wl")])
                        I("act", lambda e: e.activation(out=dec[:], in_=cl[:], func=AF.Exp), R=[(K_("cl"), g) for g in range(8)], W=[K_("dec")])
                        I("dve", lambda e: e.tensor_tensor(out=dtx[:], in0=dtx[:], in1=wl[:].unsqueeze(2).to_broadcast([128, 32, 64]), op=ALU.mult),
                          R=[K_("dtx"), K_("wl")], W=[K_("dtx")])
                        for g in range(8):
                            i2 = g % 2
                            SU = pb[4][:, 0:256]
                            I("pe", lambda e, g=g, SU=SU: e.matmul(SU, lhsT=Btok[:, g, :], rhs=dtx[:, 4 * g:4 * g + 4, :].rearrange("p h d -> p (h d)"),
                                                                  start=True, stop=True),
                              R=[K_("Btok"), K_("dtx")], W=["pb4"])
                            I("dve", lambda e, g=g: e.tensor_tensor(
                                out=S[:, g, :].rearrange("p (h d) -> p h d", h=4), in0=S[:, g, :].rearrange("p (h d) -> p h d", h=4),
                                in1=dec[:, 4 * g:4 * g + 4].unsqueeze(2).to_broadcast([128, 4, 64]), op=ALU.mult),
                              R=[(K_("S"), g), K_("S"), K_("dec")], W=[(K_("S"), g)])
                            I("dve", lambda e, g=g, SU=SU: e.tensor_tensor(out=S[:, g, :], in0=S[:, g, :], in1=SU, op=ALU.add),
                              R=[(K_("S"), g), "pb4"], W=[(K_("S"), g)])
                            I("act", lambda e, g=g: e.activation(out=Sbf[:, g, :], in_=S[:, g, :], func=AF.Copy),
                              R=[(K_("S"), g), K_("Sbf")], W=[(K_("Sbf"), g)])
                        if STG < 6:
                            continue
                        for r in range(2):
                            for q in range(8):
                                I("pe", lambda e, r=r, q=q: e.transpose(out=pbh[:, q * 128:(q + 1) * 128], in_=zT[:, 8 * r + q, lsl],
                                                                       identity=ident_bf[:]),
                                  R=[K_("zT"), "ident_bf"], W=["pbh"])
                            I("act", lambda e, r=r: e.activation(out=sz[:, r * 1024:(r + 1) * 1024], in_=pbh[:], func=AF.Silu),
                              R=["pbh"], W=[K_("sz")])
                        I("dve", lambda e: e.tensor_tensor(out=Y[:], in0=Y[:], in1=sz[:], op=ALU.mult),
                          R=[(K_("Y"), g) for g in range(8)] + [K_("sz")], W=[K_("Y2")])
                        I("act", lambda e: e.activation(out=sz[:], in_=Y[:], func=AF.Square), R=[K_("Y2")], W=[K_("sz")])
                        I("dve", lambda e: e.tensor_reduce(out=ss[:], in_=sz[:].rearrange("p (g c) -> p g c", g=8), axis=AX.X, op=ALU.add),
                          R=[K_("sz")], W=[K_("ss")])
                        I("act", lambda e: e.activation(out=ss[:], in_=ss[:], func=AF.Sqrt, scale=1.0 / 256.0, bias=epsb[:]),
                          R=[K_("ss"), "epsb"], W=[K_("ss")])
                        I("dve", lambda e: e.reciprocal(out=ss[:], in_=ss[:]), R=[K_("ss")], W=[K_("ss")])
                        I("dve", lambda e: e.tensor_tensor(out=Ynb[:].rearrange("p (g c) -> p g c", g=8), in0=Y[:].rearrange("p (g c) -> p g c", g=8),
                                                           in1=ss[:].unsqueeze(2).to_broadcast([128, 8, 256]), op=ALU.mult),
                          R=[K_("Y2"), K_("ss")], W=[K_("Ynb")] + [(K_("Y"), g) for g in range(8)])
                        for r in range(2):
                            for q in range(8):
                                I("pe", lambda e, r=r, q=q: e.transpose(out=pbh[:, q * 128:(q + 1) * 128],
                                                                       in_=Ynb[:, (8 * r + q) * 128:(8 * r + q + 1) * 128], identity=ident_bf[:]),
                                  R=[K_("Ynb"), "ident_bf"], W=["pbh"])
                            I("dve", lambda e, r=r: e.tensor_tensor(out=ynT[:, 8 * r:8 * r + 8, lsl], in0=pbh[:].rearrange("p (c t) -> p c t", c=8),
                                                                    in1=gn[:, 8 * r:8 * r + 8].unsqueeze(2).to_broadcast([128, 8, 128]), op=ALU.mult),
                              R=["pbh", K_("hp")], W=[K_("ynT")])
                    for dc in range(DC if STG >= 7 else 0):
                        s = load_w256(ssd_wout_d[j0, dc])
                        wv = wi[s][:].rearrange("p k (a f) -> p (k a) f", a=2)
                        b = nxt("po")
                        po = pb[4 + b] if False else pb[6]
                        for c in range(16):
                            I("pe", lambda e, c=c, wv=wv: e.matmul(pb[6][:, 0:TT2], lhsT=wv[:, c, :], rhs=ynT[:, c, :], start=(c == 0), stop=(c == 15)),
                              R=[("wi", s), K_("ynT")], W=["pb6"])
                        I("dve", lambda e, dc=dc: e.tensor_tensor(out=xT[:, dc, tsl], in0=xT[:, dc, tsl], in1=pb[6][:, 0:TT2], op=ALU.add),
                          R=["pb6", ("xT", dc, tt)], W=[("xT", dc, tt)])
                P.release([k for k in list(P.res) if (k[0] if isinstance(k, tuple) else k).startswith("ssd_")])


        def dsa_mixer(j2, gidx):
            norm_to_hT(gidx)
            NEG = -1.0e30
            TOPK = min(256, L // 4)
            NR = TOPK // 8
            with ExitStack() as sc:
                def T(name, shape, dt=F32):
                    return sb(nc, sc, "dsa_" + name, shape, dt)

                def K_(n):
                    return "dsa_" + n
                qT = T("qT", [128, 8, L], BF16)
                K2T = T("K2T", [128, L], BF16)
                qiT = T("qiT", [128, 4, L], BF16)
                ki2T = T("ki2T", [128, L], BF16)
                Vaug = T("Vaug", [128, NB, 65], BF16)
                witok = T("witok", [128, NB, 8])
                I("dve", lambda e: e.memset(Vaug[:], 1.0), W=[K_("Vaug")])
                with ExitStack() as sc2:
                    wvw = sb(nc, sc2, "dsa_wvw", [128, DC, 72], BF16)
                    I("pool", lambda e: e.dma_start(out=wvw[:], in_=dsa_wvw_d[j2].rearrange("p (k f) -> p k f", k=DC)),
                      W=[K_("wvw")], dma=K_("wvw"))
                    cosT = sb(nc, sc2, "dsa_cos", [128, L], F32)
                    sinS = sb(nc, sc2, "dsa_sin", [128, L], F32)
                    I("sp", lambda e: e.dma_start(out=cosT[:], in_=dsa_rope_d[0]), W=[K_("cos")], dma=K_("cos"))
                    I("sp", lambda e: e.dma_start(out=sinS[:], in_=dsa_rope_d[1]), W=[K_("sin")], dma=K_("sin"))
                    for blk in range(14):
                        s = load_w256(dsa_win_d[j2, blk])
                        if blk < 8:
                            dst, dkey, scl = (lambda ts, blk=blk: qT[:, blk, ts]), K_("qT"), 0.125
                        elif blk == 8:
                            dst, dkey, scl = (lambda ts: K2T[:, ts]), K_("K2T"), 1.0
                        elif blk < 13:
                            dst, dkey, scl = (lambda ts, blk=blk: qiT[:, blk - 9, ts]), K_("qiT"), 1.0
                        else:
                            dst, dkey, scl = (lambda ts: ki2T[:, ts]), K_("ki2T"), 1.0
                        for tt in range(NT):
                            ts = slice(tt * 512, (tt + 1) * 512)
                            pg, pu, kg, ku = mm_pair(s, hT, lambda k, tt: ("hT", k, tt), tt)
                            k1 = nxt("tA")
                            k2 = nxt("tB")
                            I("dve", lambda e, k1=k1, pg=pg, ts=ts, scl=scl: e.scalar_tensor_tensor(
                                out=tA[k1][:], in0=pg[:], scalar=scl, in1=cosT[:, ts], op0=ALU.mult, op1=ALU.mult),
                              R=[kg, K_("cos")], W=[("tA", k1)])
                            I("dve", lambda e, k2=k2, pu=pu, ts=ts, scl=scl: e.scalar_tensor_tensor(
                                out=tB[k2][:], in0=pu[:], scalar=scl, in1=sinS[:, ts], op0=ALU.mult, op1=ALU.mult),
                              R=[ku, K_("sin")], W=[("tB", k2)])
                            I("pool", lambda e, k1=k1, k2=k2, ts=ts, dst=dst: e.tensor_tensor(out=dst(ts), in0=tA[k1][:], in1=tB[k2][:], op=ALU.add),
                              R=[("tA", k1), ("tB", k2)], W=[(dkey, tt)])
                    for tb in range(NB):
                        bsl = slice(tb * 128, (tb + 1) * 128)
                        pv = pb[4][:, 0:72]
                        for k in range(DC):
                            I("pe", lambda e, k=k: e.matmul(pv, lhsT=hT[:, k, bsl], rhs=wvw[:, k, :], start=(k == 0), stop=(k == DC - 1)),
                              R=[("hT", k, tb // 4), K_("wvw")], W=["pb4"])
                        I("act", lambda e, tb=tb: e.activation(out=Vaug[:, tb, 0:64], in_=pb[4][:, 0:64], func=AF.Copy),
                          R=["pb4", K_("Vaug")], W=[(K_("Vaug"), tb)])
                        I("dve", lambda e, tb=tb: e.tensor_scalar(out=witok[:, tb, :], in0=pb[4][:, 64:72], scalar1=8.0 ** -0.5 * 64.0 ** -0.5,
                                                                  scalar2=None, op0=ALU.mult),
                          R=["pb4"], W=[(K_("witok"), tb)])
                    P.release([K_("cos"), K_("sin"), K_("wvw")])
                with ExitStack() as sc3:
                    def T3(name, shape, dt=F32):
                        return sb(nc, sc3, "dsa_" + name, shape, dt)
                    idx_ = [T3("idx%d" % i, [128, L]) for i in range(2)]
                    mb_ = [T3("mb%d" % i, [128, L], BF16) for i in range(2)]
                    Pt = [T3("Pt%d" % i, [128, 512], BF16) for i in range(2)]
                    Otok = T3("Otok", [128, 16, 64], BF16)
                    OT = T3("OT", [128, DC, 128], BF16)
                    kd = T3("kd", [128, 128])
                    irep = T3("irep", [128, 512], BF16)
                    m8 = T3("m8", [128, 8])
                    rcp = T3("rcp", [128, 16, 1])
                    zer = T3("zer", [128, 512], BF16)
                    I("dve", lambda e: e.memset(zer[:], 0.0), W=[K_("zer")])
                    I("sp", lambda e: e.dma_start(out=kd[:], in_=dsa_k_d[:, 0:128]), W=[K_("kd")], dma=K_("kd"))
                    I("pool", lambda e: e.dma_start(out=irep[:], in_=dsa_k_d[:, 128:640]), W=[K_("irep")], dma=K_("irep"))
                    negtri = kd[:, 0:128]
                    cn = {"pi": 0, "ps": 0}
                    def stageA(qb):
                        qsl = slice(qb * 128, (qb + 1) * 128)
                        S = 128 * (qb + 1)
                        bi = qb % 2
                        idx, mb = idx_[bi], mb_[bi]
                        npc = (S + 511) // 512
                        for hi in range(8):
                            c, half = hi // 2, hi % 2
                            hs = slice(half * 64, (half + 1) * 64)
                            for pc in range(npc):
                                cols = min(512, S - 512 * pc)
                                csl = slice(512 * pc, 512 * pc + cols)
                                ib = cn["pi"] % 2
                                cn["pi"] += 1
                                pI = pb[5 + ib]
                                I("pe", lambda e, c=c, hs=hs, csl=csl, cols=cols, pI=pI: e.matmul(
                                    pI[:, 0:cols], lhsT=qiT[hs, c, qsl], rhs=ki2T[hs, csl], start=True, stop=True),
                                  R=[(K_("qiT"), qb // 4), (K_("ki2T"), pc)], W=["pb%d" % (5 + ib)])
                                k1 = nxt("tA")
                                I("act", lambda e, k1=k1, cols=cols, pI=pI: e.activation(out=tA[k1][:, 0:cols], in_=pI[:, 0:cols], func=AF.Relu),
                                  R=["pb%d" % (5 + ib)], W=[("tA", k1)])
                                wcol = witok[:, qb, hi:hi + 1]
                                if hi == 0:
                                    I("dve", lambda e, k1=k1, cols=cols, csl=csl, wcol=wcol, idx=idx: e.tensor_scalar(
                                        out=idx[:, csl], in0=tA[k1][:, 0:cols], scalar1=wcol, scalar2=None, op0=ALU.mult),
                                      R=[("tA", k1), (K_("witok"), qb)], W=[(K_("idx"), bi, pc)])
                                else:
                                    I("dve", lambda e, k1=k1, cols=cols, csl=csl, wcol=wcol, idx=idx: e.scalar_tensor_tensor(
                                        out=idx[:, csl], in0=tA[k1][:, 0:cols], scalar=wcol, in1=idx[:, csl], op0=ALU.mult, op1=ALU.add),
                                      R=[("tA", k1), (K_("witok"), qb), (K_("idx"), bi, pc)], W=[(K_("idx"), bi, pc)])
                        allidx = [(K_("idx"), bi, pc) for pc in range(npc)]
                        I("dve", lambda e, S=S, idx=idx: e.tensor_tensor(out=idx[:, S - 128:S], in0=idx[:, S - 128:S], in1=negtri, op=ALU.add),
                          R=allidx + [K_("kd")], W=allidx)
                        if qb >= TOPK // 128:
                            for rd in range(NR):
                                I("dve", lambda e, S=S, idx=idx: e.max(out=m8[:], in_=idx[:, 0:S]), R=allidx, W=[K_("m8")])
                                I("dve", lambda e, S=S, idx=idx: e.match_replace(out=idx[:, 0:S], in_to_replace=m8[:], in_values=idx[:, 0:S],
                                                                                imm_value=NEG),
                                  R=[K_("m8")], W=allidx)
                            I("dve", lambda e, S=S, idx=idx, mb=mb: e.tensor_scalar(out=mb[:, 0:S], in0=idx[:, 0:S], scalar1=NEG, scalar2=-30000.0,
                                                                                  op0=ALU.not_equal, op1=ALU.mult),
                              R=allidx, W=[(K_("mb"), bi)])
                        else:
                            I("dve", lambda e, S=S, idx=idx, mb=mb: e.tensor_scalar(out=mb[:, 0:S], in0=idx[:, 0:S], scalar1=-1.0e29, scalar2=-30000.0,
                                                                                  op0=ALU.is_lt, op1=ALU.mult),
                              R=allidx, W=[(K_("mb"), bi)])
                    def stageB(qb):
                        qsl = slice(qb * 128, (qb + 1) * 128)
                        bi = qb % 2
                        mb = mb_[bi]
                        for bk, nh in ((2, 7), (3, 7), (4, 2)):
                            I("pe", lambda e, bk=bk, nh=nh: e.matmul(pb[bk][:, 0:nh * 65], lhsT=zer[:, 0:128], rhs=zer[:, 0:nh * 65],
                                                                    start=True, stop=False),
                              R=[K_("zer")], W=["pb%d" % bk])
                        for sbk in range(qb + 1):
                            ssl = slice(sbk * 128, (sbk + 1) * 128)
                            for half in range(2):
                                hs = slice(half * 64, (half + 1) * 64)
                                for cg in range(2):
                                    ib = cn["ps"] % 2
                                    cn["ps"] += 1
                                    pS = pb[ib]
                                    I("pe", lambda e, hs=hs, cg=cg, ssl=ssl, pS=pS: e.matmul(
                                        pS[:], lhsT=K2T[hs, ssl], rhs=qT[hs, 4 * cg:4 * cg + 4, qsl], start=True, stop=False),
                                      R=[(K_("K2T"), sbk // 4), (K_("qT"), qb // 4)], W=["pb%d" % ib])
                                    I("pe", lambda e, ssl=ssl, pS=pS, mb=mb: e.matmul(pS[:], lhsT=mb[:, ssl], rhs=irep[:], start=False, stop=True),
                                      R=[(K_("mb"), bi), K_("irep")], W=["pb%d" % ib])
                                    I("act", lambda e, ib=ib, pS=pS: e.activation(out=Pt[ib][:], in_=pS[:], func=AF.Exp),
                                      R=["pb%d" % ib], W=[(K_("Pt"), ib)])
                                    for j in range(4):
                                        head = 2 * (4 * cg + j) + half
                                        bk, off = 2 + head // 7, (head % 7) * 65
                                        I("pe", lambda e, ib=ib, j=j, bk=bk, off=off, sbk=sbk: e.matmul(
                                            pb[bk][:, off:off + 65], lhsT=Pt[ib][:, j * 128:(j + 1) * 128], rhs=Vaug[:, sbk, :],
                                            start=False, stop=False),
                                          R=[(K_("Pt"), ib), (K_("Vaug"), sbk), K_("Vaug")], W=["pb%d" % bk])
                        for bk, nh in ((2, 7), (3, 7), (4, 2)):
                            I("pe", lambda e, bk=bk, nh=nh: e.matmul(pb[bk][:, 0:nh * 65], lhsT=zer[:, 0:128], rhs=zer[:, 0:nh * 65],
                                                                    start=False, stop=True),
                              R=[K_("zer")], W=["pb%d" % bk])
                        for bk, h0, nh in ((2, 0, 7), (3, 7, 7), (4, 14, 2)):
                            pv3 = pb[bk][:, 0:nh * 65].rearrange("p (h e) -> p h e", e=65)
                            I("dve", lambda e, pv3=pv3, h0=h0, nh=nh: e.reciprocal(out=rcp[:, h0:h0 + nh, :], in_=pv3[:, :, 64:65]),
                              R=["pb%d" % bk], W=[(K_("rcp"), bk)])
                            I("dve", lambda e, pv3=pv3, h0=h0, nh=nh: e.tensor_tensor(
                                out=Otok[:, h0:h0 + nh, :], in0=pv3[:, :, 0:64], in1=rcp[:, h0:h0 + nh, :].to_broadcast([128, nh, 64]), op=ALU.mult),
                              R=["pb%d" % bk, (K_("rcp"), bk)], W=[(K_("Otok"), bk)])
                        lq = 0
                        for c in range(8):
                            I("pe", lambda e, c=c: e.transpose(out=pbh[:, c * 128:(c + 1) * 128],
                                                               in_=Otok[:, 2 * c:2 * c + 2, :].rearrange("p h d -> p (h d)"), identity=ident_bf[:]),
                              R=[(K_("Otok"), 2), (K_("Otok"), 3), (K_("Otok"), 4), "ident_bf"], W=["pbh"])
                        I("act", lambda e, lq=lq: e.activation(out=OT[:, :, lq * 128:(lq + 1) * 128],
                                                               in_=pbh[:].rearrange("p (c t) -> p c t", c=8), func=AF.Copy),
                          R=["pbh"], W=[K_("OT")])
                        if True:
                            osl = slice(qb * 128, (qb + 1) * 128)
                            tt = qb // 4
                            for b4 in range(4):
                                s = load_w256(dsa_wo_d[j2, b4])
                                for hh in range(2):
                                    dc = 2 * b4 + hh
                                    po = pb[5 + hh]
                                    for k in range(DC):
                                        I("pe", lambda e, k=k, hh=hh, po=po: e.matmul(po[:, 0:128], lhsT=wi[s][:, k, hh * 128:(hh + 1) * 128],
                                                                                     rhs=OT[:, k, :], start=(k == 0), stop=(k == DC - 1)),
                                          R=[("wi", s), K_("OT")], W=["pb%d" % (5 + hh)])
                                    I("dve", lambda e, dc=dc, po=po: e.tensor_tensor(out=xT[:, dc, osl], in0=xT[:, dc, osl], in1=po[:, 0:128], op=ALU.add),
                                      R=["pb%d" % (5 + hh), ("xT", dc, tt)], W=[("xT", dc, tt)])
                    stageA(0)
                    for qb in range(NB):
                        if qb + 1 < NB:
                            stageA(qb + 1)
                        stageB(qb)
                P.release([k for k in list(P.res) if (k[0] if isinstance(k, tuple) else k).startswith("dsa_")])

        def s5_mixer(j5, gidx):
            norm_to_hT(gidx)
            with ExitStack() as sc:
                def T(name, shape, dt=F32):
                    return sb(nc, sc, "s5_" + name, shape, dt)
                gT = T("gT", [128, DC, L], BF16)
                iota = T("iota", [128, L], mybir.dt.int16)
                par = T("par", [128, 3, 64])
                bre2 = [T("bre%d" % i, [64, 128]) for i in range(2)]
                bim2 = [T("bim%d" % i, [64, 128]) for i in range(2)]
                cst12 = [T("cst1%d" % i, [128, 128]) for i in range(2)]
                cst22 = [T("cst2%d" % i, [128, 128]) for i in range(2)]
                magic = T("magic", [128, 1])
                kk = T("kk", [128, 16])
                dv = T("dv", [128, 24])
                names = ["step", "lr", "th", "r", "f", "raw", "fs", "fc", "ms0", "mc0", "nr", "ni",
                         "den", "inv", "cr", "ci", "u1", "u2"]
                pt_ = {n: T(n, [128, 64]) for n in names}
                state = T("state", [128, 64])
                z = {n: T(n, [64, 128]) for n in ["t1", "t2", "zre", "zim", "nzre"]}
                wA = T("wA", [128, 8, 128], BF16)
                wA2 = T("wA2", [128, 8, 128], BF16)
                W1 = T("W1", [128, 8, 128], BF16)
                W2 = T("W2", [128, 8, 128], BF16)
                raw_ = [T("rawt%d" % i, [128, 512]) for i in range(2)]
                msin_ = [T("msin%d" % i, [128, 512]) for i in range(2)]
                mcos_ = [T("mcos%d" % i, [128, 512]) for i in range(2)]
                bt_ = [T("bt%d" % i, [128, 512]) for i in range(2)]
                st_ = [T("st%d" % i, [128, 512]) for i in range(2)]
                P1_ = [T("P1%d" % i, [128, 512], BF16) for i in range(2)]
                P2_ = [T("P2%d" % i, [128, 512], BF16) for i in range(2)]
                I("dve", lambda e: e.memset(magic[:], MAGIC), W=["s5_magic"])
                itc = [0]
                yv = T("yv", [128, 512])
                zz = T("zz", [128, 512])

                def ld(dst, src, key):
                    I("sp", lambda e: e.dma_start(out=dst, in_=src), W=[key], dma=key)
                ld(iota[:], iota_d, "s5_iota")
                ld(par[:], s5p_d[j5].rearrange("a p g -> p a g"), "s5_par")
                ld(kk[:], s5k_d, "s5_kk")
                ld(dv[:], s5d_d[j5], "s5_dv")

                def dv_(fn, R, W):
                    I("dve", fn, R=["s5_" + r for r in R], W=["s5_" + w for w in W])

                def po_(fn, R, W):
                    I("pool", fn, R=["s5_" + r for r in R], W=["s5_" + w for w in W])

                def ac_(fn, R, W):
                    I("act", fn, R=["s5_" + r for r in R] + ["negpi"], W=["s5_" + w for w in W])
                p = pt_
                lre, lim, lst = par[:, 0, :], par[:, 1, :], par[:, 2, :]
                ac_(lambda e: e.activation(out=p["step"][:], in_=lst, func=AF.Exp), ["par"], ["step"])
                dv_(lambda e: e.tensor_tensor(out=p["lr"][:], in0=lre, in1=p["step"][:], op=ALU.mult), ["par", "step"], ["lr"])
                dv_(lambda e: e.tensor_tensor(out=p["th"][:], in0=lim, in1=p["step"][:], op=ALU.mult), ["par", "step"], ["th"])
                ac_(lambda e: e.activation(out=p["r"][:], in_=p["lr"][:], func=AF.Exp), ["lr"], ["r"])
                dv_(lambda e: e.tensor_scalar(out=p["f"][:], in0=p["th"][:], scalar1=1.0 / TWO_PI, scalar2=None, op0=ALU.mult), ["th"], ["f"])
                dv_(lambda e: e.tensor_scalar(out=p["raw"][:], in0=p["f"][:], scalar1=MAGIC, scalar2=None, op0=ALU.add), ["f"], ["raw"])
                dv_(lambda e: e.scalar_tensor_tensor(out=p["fs"][:], in0=p["raw"][:], scalar=MAGIC, in1=p["f"][:], op0=ALU.subtract, op1=ALU.subtract), ["raw", "f"], ["fs"])
                dv_(lambda e: e.scalar_tensor_tensor(out=p["fc"][:], in0=p["fs"][:], scalar=-1.0, in1=p["fs"][:], op0=ALU.mult, op1=ALU.max), ["fs"], ["fc"])
                ac_(lambda e: e.activation(out=p["ms0"][:], in_=p["fs"][:], func=AF.Sin, scale=-TWO_PI), ["fs"], ["ms0"])
                ac_(lambda e: e.activation(out=p["mc0"][:], in_=p["fc"][:], func=AF.Sin, scale=-TWO_PI, bias=halfpi[:]), ["fc"], ["mc0"])
                dv_(lambda e: e.tensor_tensor(out=p["u1"][:], in0=p["mc0"][:], in1=p["r"][:], op=ALU.mult), ["mc0", "r"], ["u1"])
                dv_(lambda e: e.tensor_scalar(out=p["nr"][:], in0=p["u1"][:], scalar1=-1.0, scalar2=None, op0=ALU.add), ["u1"], ["nr"])
                dv_(lambda e: e.tensor_tensor(out=p["u2"][:], in0=p["ms0"][:], in1=p["r"][:], op=ALU.mult), ["ms0", "r"], ["u2"])
                dv_(lambda e: e.tensor_copy(out=p["ni"][:], in_=p["u2"][:]), ["u2"], ["ni"])
                dv_(lambda e: e.tensor_tensor(out=p["u1"][:], in0=lre, in1=lre, op=ALU.mult), ["par", "nr"], ["u1"])
                dv_(lambda e: e.tensor_tensor(out=p["u2"][:], in0=lim, in1=lim, op=ALU.mult), ["par", "ni"], ["u2"])
                dv_(lambda e: e.tensor_tensor(out=p["den"][:], in0=p["u1"][:], in1=p["u2"][:], op=ALU.add), ["u1", "u2"], ["den"])
                dv_(lambda e: e.reciprocal(out=p["inv"][:], in_=p["den"][:]), ["den"], ["inv"])
                dv_(lambda e: e.tensor_tensor(out=p["u1"][:], in0=p["nr"][:], in1=lre, op=ALU.mult), ["nr", "par", "den"], ["u1"])
                dv_(lambda e: e.tensor_tensor(out=p["u2"][:], in0=p["ni"][:], in1=lim, op=ALU.mult), ["ni", "par", "den"], ["u2"])
                dv_(lambda e: e.tensor_tensor(out=p["cr"][:], in0=p["u1"][:], in1=p["u2"][:], op=ALU.add), ["u1", "u2"], ["cr"])
                dv_(lambda e: e.tensor_tensor(out=p["cr"][:], in0=p["cr"][:], in1=p["inv"][:], op=ALU.mult), ["cr", "inv"], ["cr"])
                dv_(lambda e: e.tensor_tensor(out=p["u1"][:], in0=p["ni"][:], in1=lre, op=ALU.mult), ["ni", "par", "cr"], ["u1"])
                dv_(lambda e: e.tensor_tensor(out=p["u2"][:], in0=p["nr"][:], in1=lim, op=ALU.mult), ["nr", "par", "cr"], ["u2"])
                dv_(lambda e: e.tensor_tensor(out=p["ci"][:], in0=p["u1"][:], in1=p["u2"][:], op=ALU.subtract), ["u1", "u2"], ["ci"])
                dv_(lambda e: e.tensor_tensor(out=p["ci"][:], in0=p["ci"][:], in1=p["inv"][:], op=ALU.mult), ["ci", "inv"], ["ci"])

                for ct in range(DC):
                    gs = slice(8 * ct, 8 * ct + 8)
                    cs_ = slice(128 * ct, 128 * ct + 128)
                    cb_ = ct % 2
                    bre, bim, cst1, cst2 = bre2[cb_], bim2[cb_], cst12[cb_], cst22[cb_]
                    I("sp", lambda e: e.dma_start(out=bre[:], in_=s5b_d[j5, 0][:, cs_]), W=["s5_bre"], dma=("s5_bre", cb_))
                    I("sp", lambda e: e.dma_start(out=bim[:], in_=s5b_d[j5, 1][:, cs_]), W=["s5_bim"], dma=("s5_bim", cb_))
                    I("sp", lambda e: e.dma_start(out=cst1[:], in_=s5c_d[j5, 0][:, cs_]), W=["s5_cst1"], dma=("s5_cst1", cb_))
                    I("sp", lambda e: e.dma_start(out=cst2[:], in_=s5c_d[j5, 1][:, cs_]), W=["s5_cst2"], dma=("s5_cst2", cb_))

                    def bc(t):
                        return t[0:64, gs].unsqueeze(2).to_broadcast([64, 8, 16])

                    def v3(t):
                        return t[:].rearrange("p (g c) -> p g c", g=8)
                    b_re = bre[:].rearrange("p (g c) -> p g c", g=8)
                    b_im = bim[:].rearrange("p (g c) -> p g c", g=8)
                    dv_(lambda e: e.tensor_tensor(out=v3(z["t1"]), in0=b_re, in1=bc(p["cr"]), op=ALU.mult), ["bre", "cr"], ["t1"])
                    dv_(lambda e: e.tensor_tensor(out=v3(z["t2"]), in0=b_im, in1=bc(p["ci"]), op=ALU.mult), ["bim", "ci"], ["t2"])
                    dv_(lambda e: e.tensor_tensor(out=z["zre"][:], in0=z["t1"][:], in1=z["t2"][:], op=ALU.subtract), ["t1", "t2", "wA", "wA2"], ["zre"])
                    dv_(lambda e: e.tensor_scalar(out=z["nzre"][:], in0=z["zre"][:], scalar1=-1.0, scalar2=None, op0=ALU.mult), ["zre", "wA2"], ["nzre"])
                    dv_(lambda e: e.tensor_tensor(out=v3(z["t1"]), in0=b_im, in1=bc(p["cr"]), op=ALU.mult), ["bim", "cr", "zre"], ["t1"])
                    dv_(lambda e: e.tensor_tensor(out=v3(z["t2"]), in0=b_re, in1=bc(p["ci"]), op=ALU.mult), ["bre", "ci", "zre"], ["t2"])
                    dv_(lambda e: e.tensor_tensor(out=z["zim"][:], in0=z["t1"][:], in1=z["t2"][:], op=ALU.add), ["t1", "t2", "wA", "wA2"], ["zim"])
                    pT = pb[6]
                    id64 = ident[0:64, 0:64]
                    for q, src in enumerate(["zre", "zim", "zim", "nzre"]):
                        I("pe", lambda e, q=q, src=src: e.transpose(out=pT[:, q * 64:(q + 1) * 64], in_=z[src][:], identity=id64),
                          R=["s5_" + src, "ident"], W=["pb6"])
                    for g_ in range(8):
                        I("dve", lambda e, g_=g_: e.tensor_scalar(out=wA[:, g_, :], in0=pT[:, 0:128], scalar1=kk[:, 2 + g_:3 + g_],
                                                                 scalar2=None, op0=ALU.mult),
                          R=["pb6", "s5_kk"], W=["s5_wA"])
                        I("dve", lambda e, g_=g_: e.tensor_scalar(out=wA2[:, g_, :], in0=pT[:, 128:256], scalar1=kk[:, 2 + g_:3 + g_],
                                                                 scalar2=None, op0=ALU.mult),
                          R=["pb6", "s5_kk"], W=["s5_wA2"])
                    I("dve", lambda e: e.memset(W1[:], 0.0), W=["s5_W1"])
                    I("dve", lambda e: e.memset(W2[:], 0.0), W=["s5_W2"])
                    for g_ in range(8):
                        gcs = slice(16 * g_, 16 * g_ + 16)
                        I("dve", lambda e, g_=g_, gcs=gcs: e.tensor_scalar(out=W1[:, g_, 16 * g_:16 * g_ + 16], in0=cst1[:, gcs],
                                                                          scalar1=kk[:, 0:1], scalar2=None, op0=ALU.mult),
                          R=["s5_cst1", "s5_kk"], W=["s5_W1"])
                        I("dve", lambda e, g_=g_, gcs=gcs: e.tensor_scalar(out=W2[:, g_, 16 * g_:16 * g_ + 16], in0=cst2[:, gcs],
                                                                          scalar1=kk[:, 1:2], scalar2=None, op0=ALU.mult),
                          R=["s5_cst2", "s5_kk"], W=["s5_W2"])
                    for q in range(NT):
                        ts = slice(q * 512, (q + 1) * 512)
                        py = pb[5]
                        for g_ in range(8):
                            g = 8 * ct + g_
                            fcol = p["f"][:, g:g + 1]
                            si = itc[0] % 2
                            itc[0] += 1
                            raw, msin, mcos, bt, st, P1, P2 = raw_[si], msin_[si], mcos_[si], bt_[si], st_[si], P1_[si], P2_[si]
                            kr, ks, kc_, kb, kst, k1_, k2_ = [("s5_" + n, si) for n in ("rawt", "msin", "mcos", "bt", "st", "P1", "P2")]
                            I("act", lambda e, fcol=fcol, raw=raw: e.activation(out=raw[:], in_=iota[:, ts], func=AF.Identity, scale=fcol),
                              R=["s5_iota", "s5_f"], W=[kr])
                            I("act", lambda e, fcol=fcol, msin=msin: e.activation(out=msin[:], in_=iota[:, ts], func=AF.Identity, scale=fcol,
                                                                               bias=magic[:]),
                              R=["s5_iota", "s5_f", "s5_magic"], W=[ks])
                            I("dve", lambda e, msin=msin, raw=raw: e.scalar_tensor_tensor(out=msin[:], in0=msin[:], scalar=MAGIC, in1=raw[:],
                                                                                     op0=ALU.subtract, op1=ALU.subtract),
                              R=[kr, ks], W=[ks])
                            I("dve", lambda e, msin=msin, mcos=mcos: e.scalar_tensor_tensor(out=mcos[:], in0=msin[:], scalar=-1.0, in1=msin[:],
                                                                                       op0=ALU.mult, op1=ALU.max),
                              R=[ks], W=[kc_])
                            I("act", lambda e, msin=msin: e.activation(out=msin[:], in_=msin[:], func=AF.Sin, scale=-TWO_PI),
                              R=[ks], W=[ks])
                            I("act", lambda e, mcos=mcos: e.activation(out=mcos[:], in_=mcos[:], func=AF.Sin, scale=-TWO_PI, bias=halfpi[:]),
                              R=[kc_, "negpi"], W=[kc_])
                            pA, pA2 = pb[2 * si], pb[2 * si + 1]
                            kpA, kpA2 = "pb%d" % (2 * si), "pb%d" % (2 * si + 1)
                            I("pe", lambda e, g_=g_, pA=pA: e.matmul(pA[:], lhsT=wA[:, g_, :], rhs=hT[:, ct, ts], start=True, stop=True),
                              R=["s5_wA", ("hT", ct, q)], W=[kpA])
                            I("pe", lambda e, g_=g_, pA2=pA2: e.matmul(pA2[:], lhsT=wA2[:, g_, :], rhs=hT[:, ct, ts], start=True, stop=True),
                              R=["s5_wA2", ("hT", ct, q)], W=[kpA2])
                            k1 = nxt("tA")
                            k2 = nxt("tB")
                            I("dve", lambda e, k1=k1, mcos=mcos, pA=pA: e.tensor_tensor(out=tA[k1][:], in0=mcos[:], in1=pA[:], op=ALU.mult),
                              R=[kc_, kpA], W=[("tA", k1)])
                            I("dve", lambda e, k2=k2, msin=msin, pA2=pA2: e.tensor_tensor(out=tB[k2][:], in0=msin[:], in1=pA2[:], op=ALU.mult),
                              R=[ks, kpA2], W=[("tB", k2)])
                            I("pool", lambda e, k1=k1, k2=k2, bt=bt: e.tensor_tensor(out=bt[:], in0=tA[k1][:], in1=tB[k2][:], op=ALU.add),
                              R=[("tA", k1), ("tB", k2)], W=[kb])
                            if q == 0:
                                I("dve", lambda e, g=g, st=st, bt=bt: e.tensor_tensor_scan(
                                    out=st[:], data0=p["r"][:, g:g + 1].to_broadcast([128, 512]), data1=bt[:], initial=0.0,
                                    op0=ALU.mult, op1=ALU.add),
                                  R=["s5_r", kb], W=[kst])
                            else:
                                I("dve", lambda e, g=g, st=st, bt=bt: e.tensor_tensor_scan(
                                    out=st[:], data0=p["r"][:, g:g + 1].to_broadcast([128, 512]), data1=bt[:],
                                    initial=state[:, g:g + 1], op0=ALU.mult, op1=ALU.add),
                                  R=["s5_r", kb, ("s5_state", g)], W=[kst])
                            I("dve", lambda e, g=g, st=st: e.tensor_copy(out=state[:, g:g + 1], in_=st[:, 511:512]),
                              R=[kst], W=[("s5_state", g)])
                            I("dve", lambda e, mcos=mcos, st=st, P1=P1: e.tensor_tensor(out=P1[:], in0=mcos[:], in1=st[:], op=ALU.mult),
                              R=[kc_, kst], W=[k1_])
                            I("dve", lambda e, msin=msin, st=st, P2=P2: e.tensor_tensor(out=P2[:], in0=msin[:], in1=st[:], op=ALU.mult),
                              R=[ks, kst], W=[k2_])
                            I("pe", lambda e, g_=g_, P1=P1: e.matmul(py[:], lhsT=W1[:, g_, :], rhs=P1[:], start=(g_ == 0), stop=False),
                              R=["s5_W1", k1_], W=["pb5"])
                            I("pe", lambda e, g_=g_, P2=P2: e.matmul(py[:], lhsT=W2[:, g_, :], rhs=P2[:], start=False, stop=(g_ == 7)),
                              R=["s5_W2", k2_], W=["pb5"])
                        I("dve", lambda e: e.scalar_tensor_tensor(out=yv[:], in0=hT[:, ct, ts], scalar=dv[:, ct:ct + 1], in1=py[:],
                                                                 op0=ALU.mult, op1=ALU.add),
                          R=[("hT", ct, q), "s5_dv", "pb5"], W=["s5_yv"])
                        I("act", lambda e: e.activation(out=zz[:], in_=yv[:], func=AF.Square, scale=math.sqrt(GELU_C1)),
                          R=["s5_yv"], W=["s5_zz"])
                        I("dve", lambda e: e.scalar_tensor_tensor(out=zz[:], in0=zz[:], scalar=1.0, in1=yv[:], op0=ALU.add, op1=ALU.mult),
                          R=["s5_zz", "s5_yv"], W=["s5_zz"])
                        I("act", lambda e: e.activation(out=zz[:], in_=zz[:], func=AF.Sigmoid, scale=2.0 * GELU_C0),
                          R=["s5_zz"], W=["s5_zz"])
                        I("dve", lambda e: e.tensor_tensor(out=gT[:, ct, ts], in0=yv[:], in1=zz[:], op=ALU.mult),
                          R=["s5_zz", "s5_yv"], W=[("s5_gT", ct, q)])
                for oc in range(DC):
                    s = load_w256(s5w_d[j5, oc])
                    for tt in range(NT):
                        ts = slice(tt * 512, (tt + 1) * 512)
                        pv, pgt, kv, kg = mm_pair(s, gT, lambda k, tt: ("s5_gT", k, tt), tt)
                        k2 = nxt("tB")
                        I("act", lambda e, k2=k2: e.activation(out=tB[k2][:], in_=pgt[:], func=AF.Sigmoid, bias=dv[:, 16 + oc:17 + oc]),
                          R=[kg, "s5_dv"], W=[("tB", k2)])
                        k1 = nxt("tA")
                        I("dve", lambda e, k1=k1, k2=k2: e.scalar_tensor_tensor(out=tA[k1][:], in0=pv[:], scalar=dv[:, 8 + oc:9 + oc],
                                                                               in1=tB[k2][:], op0=ALU.add, op1=ALU.mult),
                          R=[kv, "s5_dv", ("tB", k2)], W=[("tA", k1)])
                        I("dve", lambda e, k1=k1: e.tensor_tensor(out=xT[:, oc, ts], in0=xT[:, oc, ts], in1=tA[k1][:], op=ALU.add),
                          R=[("tA", k1), ("xT", oc, tt)], W=[("xT", oc, tt)])
                P.release([k for k in list(P.res) if (k[0] if isinstance(k, tuple) else k).startswith("s5_")])

        for sq_i in range(NS):
            with ExitStack() as sc:
                xin = [sb(nc, sc, "xin%d" % i, [128, D], F32) for i in range(2)]
                for tb in range(NB):
                    s = nxt("xin")
                    I("sp", lambda e, s=s, tb=tb: e.dma_start(out=xin[s][:], in_=x_d[sq_i, tb * 128:(tb + 1) * 128, :]),
                      W=[("xin", s)], dma=("xin", s))
                    for half in range(2):
                        pt = pb[6]
                        for cc in range(4):
                            c = half * 4 + cc
                            I("pe", lambda e, s=s, c=c, cc=cc: e.transpose(
                                out=pt[:, cc * 128:(cc + 1) * 128], in_=xin[s][:, c * 128:(c + 1) * 128],
                                identity=ident[:]),
                              R=[("xin", s), "ident"], W=["pb6"])
                        I("act", lambda e, half=half, tb=tb: e.activation(
                            out=xT[:, half * 4:half * 4 + 4, tb * 128:(tb + 1) * 128],
                            in_=pt[:].rearrange("p (c t) -> p c t", c=4), func=AF.Copy),
                          R=["pb6"], W=[("xT", half * 4 + cc, tb // 4) for cc in range(4)])
                P.release_prefix({"xin"})
            i5 = 0
            i0 = 0
            i2 = 0
            for li, kind in enumerate(cfg.layers):
                if cfg.do_ffn:
                    ffn(2 * li, 3 * li)
                if kind == 1:
                    s5_mixer(i5, 3 * li + 1)
                    i5 += 1
                if kind == 0:
                    ssd_mixer(i0, 3 * li + 1)
                    i0 += 1
                if kind == 2:
                    dsa_mixer(i2, 3 * li + 1)
                    i2 += 1
                if cfg.do_ffn:
                    ffn(2 * li + 1, 3 * li + 2)
            with ExitStack() as sc:
                xin = [sb(nc, sc, "xin%d" % i, [128, D], F32) for i in range(2)]
                for tt in range(NT):
                    ts = slice(tt * 512, (tt + 1) * 512)
                    rmsnorm_to(lambda c, ts=ts: xT[:, c, ts], 3 * depth, lambda c, tt=tt: ("xT", c, tt), tt)
                    for q in range(4):
                        tb = tt * 4 + q
                        s = nxt("xin")
                        for half in range(2):
                            pt = pb[6]
                            for cc in range(4):
                                c = half * 4 + cc
                                I("pe", lambda e, c=c, cc=cc, tb=tb: e.transpose(
                                    out=pt[:, cc * 128:(cc + 1) * 128], in_=xT[:, c, tb * 128:(tb + 1) * 128],
                                    identity=ident[:]),
                                  R=[("xT", c, tt), "ident"], W=["pb6"])
                            I("act", lambda e, half=half, s=s: e.activation(
                                out=xin[s][:, half * 512:(half + 1) * 512], in_=pt[:], func=AF.Copy),
                              R=["pb6"], W=[("xin", s)])
                        I("sp", lambda e, s=s, tb=tb: e.dma_start(out=y_d[sq_i, tb * 128:(tb + 1) * 128, :], in_=xin[s][:]),
                          R=[("xin", s)], dma=("yout", s))
                P.release_prefix({"xin"})
        P.finish()
        print("instructions:", P.ninst, "sems:", P.nsem)
    return nc


def w256_layout(w, ncol_blocks, col_a, col_b):
    out = np.empty((ncol_blocks, 128, DC, 256), np.float32)
    wk = w.reshape(DC, 128, -1)
    for b in range(ncol_blocks):
        out[b, :, :, 0:128] = wk[:, :, col_a + b * 128: col_a + (b + 1) * 128].transpose(1, 0, 2)
        out[b, :, :, 128:256] = wk[:, :, col_b + b * 128: col_b + (b + 1) * 128].transpose(1, 0, 2)
    return out.reshape(ncol_blocks, 128, DC * 256)


def prep_ssd(inp, cfg, m):
    f32 = np.float32
    n0 = cfg.n_ssd
    win = np.empty((n0, 24, 128, DC, 256), f32)
    wdt = np.empty((n0, 128, DC * 32), f32)
    wout = np.empty((n0, DC, 128, 16 * 128), f32)
    cwb = np.empty((n0, 128, 32, 5), f32)
    hp = np.empty((n0, 128, 112), f32)
    for j in range(n0):
        W = inp["ssd_in_proj"][j]
        wk = W.reshape(DC, 128, -1)
        for b in range(24):
            win[j, b] = wk[:, :, 256 * b:256 * (b + 1)].transpose(1, 0, 2)
        wdt[j] = wk[:, :, 6144:6176].transpose(1, 0, 2).reshape(128, DC * 32)
        Wo = inp["ssd_out_proj"][j].reshape(16, 128, DC, 128)
        wout[j] = Wo.transpose(2, 1, 0, 3).reshape(DC, 128, 16 * 128)
        cw = inp["ssd_conv_w"][j].reshape(4, 32, 128)
        cwb[j, :, :, 0:4] = cw.transpose(2, 1, 0)
        cwb[j, :, :, 4] = inp["ssd_conv_b"][j].reshape(32, 128).T
        hp[j, :, 0:32] = np.broadcast_to(inp["ssd_dt_bias"][j][None, :], (128, 32))
        hp[j, :, 32:64] = np.broadcast_to(inp["ssd_a_log"][j][None, :], (128, 32))
        hp[j, :, 64:96] = np.broadcast_to(inp["ssd_d"][j][None, :], (128, 32))
        hp[j, :, 96:112] = inp["ssd_gate_norm"][j].reshape(16, 128).T
    kc = np.zeros((128, 768), f32)
    tri = (np.arange(128)[:, None] <= np.arange(128)[None, :])
    kc[:, 0:128] = tri.astype(f32)
    cb = np.where(np.arange(128)[:, None] > np.arange(128)[None, :], -30000.0, 0.0).astype(f32)
    kc[:, 128:640] = np.tile(cb, (1, 4))
    kc[:, 640:768] = 1.0
    m.update({"ssd_win": win.reshape(n0, 24, 128, DC * 256), "ssd_wdt": wdt, "ssd_wout": wout,
              "ssd_cw": cwb.reshape(n0, 128, 160), "ssd_hp": hp, "ssd_k": kc})


def w256_pairs(wa, wb):
    out = np.empty((128, DC, 256), np.float32)
    out[:, :, 0:128] = wa.reshape(DC, 128, 128).transpose(1, 0, 2)
    out[:, :, 128:256] = wb.reshape(DC, 128, 128).transpose(1, 0, 2)
    return out.reshape(128, DC * 256)


def prep_dsa(inp, cfg, m):
    f32 = np.float32
    n2 = cfg.n_dsa
    L = cfg.L
    win = np.empty((n2, 14, 128, DC * 256), f32)
    wvw = np.empty((n2, 128, DC * 72), f32)
    wo = np.empty((n2, 4, 128, DC * 256), f32)
    perm = np.concatenate([np.arange(32, 64), np.arange(0, 32)])
    perm128 = np.concatenate([perm, 64 + perm])
    for j in range(n2):
        W = inp["dsa_in_proj"][j]
        Wq, Wk, Wv = W[:, 0:1024], W[:, 1024:1088], W[:, 1088:1152]
        Wqi, Wki, Wwi = W[:, 1152:1664], W[:, 1664:1728], W[:, 1728:1736]
        blocks = [Wq[:, c * 128:(c + 1) * 128] for c in range(8)]
        blocks.append(np.concatenate([Wk, Wk], 1))
        blocks += [Wqi[:, c * 128:(c + 1) * 128] for c in range(4)]
        blocks.append(np.concatenate([Wki, Wki], 1))
        for b, A in enumerate(blocks):
            win[j, b] = w256_pairs(A, A[:, perm128])
        vw = np.concatenate([Wv, Wwi], 1)
        wvw[j] = vw.reshape(DC, 128, 72).transpose(1, 0, 2).reshape(128, DC * 72)
        Wo = inp["dsa_out_proj"][j]
        for b in range(4):
            wo[j, b] = w256_pairs(Wo[:, 256 * b:256 * b + 128], Wo[:, 256 * b + 128:256 * b + 256])
    inv = (10000.0 ** (-np.arange(32, dtype=np.float64) / 32.0))
    ang = np.arange(L, dtype=np.float64)[None, :] * inv[np.arange(128) % 32][:, None]
    sgn = np.where((np.arange(128) % 64) < 32, -1.0, 1.0)[:, None]
    rope = np.stack([np.cos(ang), np.sin(ang) * sgn], 0).astype(f32)
    kd = np.zeros((128, 640), f32)
    kd[:, 0:128] = np.where(np.arange(128)[None, :] > np.arange(128)[:, None], -3.0e30, 0.0)
    kd[:, 128:640] = np.tile(np.eye(128, dtype=f32), (1, 4))
    m.update({"dsa_win": win, "dsa_wvw": wvw, "dsa_wo": wo, "dsa_rope": rope, "dsa_k": kd})


def prep_common(inp, cfg):
    depth = cfg.depth
    f32 = np.float32
    g = []
    for i in range(depth):
        g += [inp["ffn1_norm"][i], inp["mix_norm"][i], inp["ffn2_norm"][i]]
    g.append(inp["final_norm"])
    g = np.stack(g, 0).astype(f32)
    gains = np.ascontiguousarray(g.reshape(-1, DC, 128).transpose(2, 0, 1))
    wi = np.empty((2 * depth, FC, 128, DC * 256), f32)
    wo = np.empty((2 * depth, 2, DC, 128, 11 * 128), f32)
    for i in range(depth):
        for which, (kin, kout) in enumerate((("ffn1_w_in", "ffn1_w_out"), ("ffn2_w_in", "ffn2_w_out"))):
            w_in = inp[kin][i]
            w_out = inp[kout][i]
            wi[2 * i + which] = w256_layout(w_in, FC, 0, FFN)
            b = w_out.reshape(2, 11, 128, DC, 128)
            b = b.transpose(0, 3, 2, 1, 4)
            wo[2 * i + which] = b.reshape(2, DC, 128, 11 * 128)
    m = {"gains": gains, "ident": np.eye(128, dtype=f32), "ffn_wi": wi, "ffn_wo": wo}
    if cfg.n_s5:
        n5 = cfg.n_s5
        par = np.empty((n5, 3, 128, 64), f32)
        sb_ = np.empty((n5, 2, 64, 1024), f32)
        sc_ = np.empty((n5, 2, 128, 1024), f32)
        sd_ = np.empty((n5, 128, 24), f32)
        sw_ = np.empty((n5, DC, 128, DC * 256), f32)
        for j in range(n5):
            lre = inp["s5_lam_re"][j].T
            lim = inp["s5_lam_im"][j].T
            lst = np.broadcast_to(inp["s5_log_step"][j][None, :], (64, 64))
            for a, t in enumerate((lre, lim, lst)):
                par[j, a, 0:64] = t
                par[j, a, 64:128] = t
            sb_[j, 0] = inp["s5_b_re"][j].transpose(1, 0, 2).reshape(64, 1024)
            sb_[j, 1] = inp["s5_b_im"][j].transpose(1, 0, 2).reshape(64, 1024)
            cre = inp["s5_c_re"][j].transpose(2, 0, 1).reshape(64, 1024)
            cim = inp["s5_c_im"][j].transpose(2, 0, 1).reshape(64, 1024)
            sc_[j, 0, 0:64] = cre
            sc_[j, 0, 64:128] = cim
            sc_[j, 1, 0:64] = cim
            sc_[j, 1, 64:128] = cre
            sd_[j, :, 0:8] = inp["s5_d"][j].reshape(DC, 128).T
            sd_[j, :, 8:24] = inp["s5_glu_b"][j].reshape(16, 128).T
            sw_[j] = w256_layout(inp["s5_glu_w"][j], DC, 0, D)
        kk = np.zeros((128, 16), f32)
        kk[0:64, 0] = 1.0
        kk[64:128, 0] = -1.0
        kk[:, 1] = -1.0
        for g_ in range(8):
            kk[16 * g_:16 * g_ + 16, 2 + g_] = 1.0
        m.update({"s5_par": par, "s5_b": sb_, "s5_c": sc_, "s5_k": kk, "s5_dv": sd_, "s5_glu": sw_,
                  "iota16": np.broadcast_to(np.arange(cfg.L, dtype=np.int16)[None, :], (128, cfg.L)).copy()})
    if cfg.n_ssd:
        prep_ssd(inp, cfg, m)
    if cfg.n_dsa:
        prep_dsa(inp, cfg, m)
    return m


def kernel(**inputs):
    cfg = Cfg()
    inp = {k: np.asarray(v) for k, v in inputs.items()}
    common = prep_common(inp, cfg)
    x = inp["x"].astype(np.float32)
    B = x.shape[0]
    per = B // N_CORES
    nc = build(cfg)
    in_maps = []
    for c in range(N_CORES):
        m = dict(common)
        m["x"] = np.ascontiguousarray(x[c * per:(c + 1) * per])
        in_maps.append(m)
    res = run_bass_kernel_spmd(nc, in_maps, core_ids=list(range(N_CORES)))
    out = np.concatenate([r["y"] for r in res.results], axis=0)
    return out.astype(np.float32)
```

```python
import math
import os
from contextlib import ExitStack

import numpy as np
import concourse.bass as bass
import concourse.mybir as mybir
from concourse.bass_utils import run_bass_kernel_spmd

F32 = mybir.dt.float32
BF16 = mybir.dt.bfloat16
AF = mybir.ActivationFunctionType
ALU = mybir.AluOpType
AX = mybir.AxisListType

D = 1024
DC = 8
FFN = 2816
FC = 22
EPS = 1e-6
N_CORES = 8

ENGS = ("pe", "act", "dve", "pool", "sp")
EPOCH = 12000


class Prog:
    def __init__(self, nc, es):
        self.nc = nc
        self.es = es
        self.eng = {"pe": nc.tensor, "act": nc.scalar, "dve": nc.vector,
                    "pool": nc.gpsimd, "sp": nc.sync}
        self.nsem = 0
        self.esem = {}
        self.ecnt = {}
        self.eepoch = {}
        for e in ENGS:
            self.eepoch[e] = 0
            self.ecnt[e] = 0
            self.esem[e] = self._newsem()
        self.seen = {e: {} for e in ENGS}
        self.res = {}
        self.dsem = {}
        self.free_ev = {}
        self.ninst = 0

    def _newsem(self):
        self.nsem += 1
        return self.es.enter_context(self.nc.semaphore("s%d" % self.nsem))

    def _need(self, eng, deps):
        out = {}
        for ev in deps:
            if ev is None:
                continue
            sem, val, kind = ev
            if kind == "pe" and eng == "pe":
                continue
            nm = sem.name
            if self.seen[eng].get(nm, 0) >= val:
                continue
            if nm not in out or out[nm][1] < val:
                out[nm] = ev
        return list(out.values())

    def I(self, eng, fn, R=(), W=(), dma=None):
        pr = [r for r in R if isinstance(r, str) and r.startswith("pb")]
        if pr:
            R = [r for r in R if r not in pr]
            W = list(W) + [r for r in pr if r not in W]
        deps = []
        for r in R:
            ent = self.res.get(r)
            if ent is not None:
                deps.append(ent[0])
        for w in W:
            ent = self.res.get(w)
            if ent is not None:
                deps.append(ent[0])
                deps.extend(ent[1].values())
            else:
                deps.extend(self.free_ev.values())
        need = self._need(eng, deps)
        e = self.eng[eng]
        for sem, val, kind in need:
            e.wait_ge(sem, val)
            self.seen[eng][sem.name] = val
        ins = fn(e)
        if dma is not None:
            ent = self.dsem.get(dma)
            if ent is None:
                ent = [self._newsem(), 0]
                self.dsem[dma] = ent
            ent[1] += 16
            ins.then_inc(ent[0], 16)
            ev = (ent[0], ent[1], "dma")
        else:
            if self.ecnt[eng] >= EPOCH:
                self.esem[eng] = self._newsem()
                self.ecnt[eng] = 0
                self.eepoch[eng] += 1
            self.ecnt[eng] += 1
            ins.then_inc(self.esem[eng], 1)
            ev = (self.esem[eng], self.ecnt[eng], eng)
        for r in R:
            ent = self.res.get(r)
            if ent is None:
                ent = [None, {}]
                self.res[r] = ent
            ent[1][ev[0].name] = ev
        for w in W:
            self.res[w] = [ev, {}]
        self.ninst += 1
        return ev

    def release(self, keys):
        for k in keys:
            ent = self.res.pop(k, None)
            if ent is None:
                continue
            for ev in [ent[0]] + list(ent[1].values()):
                if ev is None:
                    continue
                nm = ev[0].name
                if nm not in self.free_ev or self.free_ev[nm][1] < ev[1]:
                    self.free_ev[nm] = ev

    def release_prefix(self, prefixes):
        ks = [k for k in self.res if (k[0] if isinstance(k, tuple) else k) in prefixes]
        self.release(ks)

    def finish(self):
        allev = list(self.free_ev.values())
        for ent in self.res.values():
            allev.append(ent[0])
            allev.extend(ent[1].values())
        for eng in ENGS:
            for sem, val, kind in self._need(eng, allev):
                if kind == eng and eng != "sp":
                    pass
                self.eng[eng].wait_ge(sem, val)
                self.seen[eng][sem.name] = val


_UID = [0]


def sb(nc, es, name, shape, dt):
    _UID[0] += 1
    return es.enter_context(nc.sbuf_tensor("%s_u%d" % (name, _UID[0]), shape, dt))


def ps(nc, es, name, shape, dt):
    return es.enter_context(nc.psum_tensor(name, shape, dt))


class Cfg:
    def __init__(self, nseq=4, L=2048, layers=(0, 1, 2, 0), do_ffn=True):
        self.nseq = nseq
        self.L = L
        self.layers = tuple(layers)
        self.do_ffn = do_ffn
        self.depth = len(self.layers)
        self.n_ssd = sum(1 for k in self.layers if k == 0)
        self.n_s5 = sum(1 for k in self.layers if k == 1)
        self.n_dsa = sum(1 for k in self.layers if k == 2)


TWO_PI = 2.0 * math.pi
GELU_C0 = math.sqrt(2.0 / math.pi)
GELU_C1 = 0.044715
MAGIC = 12582912.0


def build(cfg):
    nc = bass.Bass("TRN2", target_bir_lowering=False)
    L = cfg.L
    NS = cfg.nseq
    NT = L // 512
    NB = L // 128
    depth = cfg.depth

    def din(name, shape, dt=F32):
        return nc.dram_tensor(name, list(shape), dt, kind="ExternalInput").ap()

    x_d = din("x", [NS, L, D])
    y_d = nc.dram_tensor("y", [NS, L, D], F32, kind="ExternalOutput").ap()
    gains_d = din("gains", [128, 3 * depth + 1, DC])
    ident_d = din("ident", [128, 128])
    wi_d = din("ffn_wi", [2 * depth, FC, 128, DC * 256])
    wo_d = din("ffn_wo", [2 * depth, 2, DC, 128, 11 * 128])
    if cfg.n_s5:
        n5 = cfg.n_s5
        s5p_d = din("s5_par", [n5, 3, 128, 64])
        s5b_d = din("s5_b", [n5, 2, 64, 1024])
        s5c_d = din("s5_c", [n5, 2, 128, 1024])
        s5k_d = din("s5_k", [128, 16])
        s5d_d = din("s5_dv", [n5, 128, 24])
        s5w_d = din("s5_glu", [n5, DC, 128, DC * 256])
        iota_d = din("iota16", [128, L], mybir.dt.int16)

    if cfg.n_ssd:
        n0 = cfg.n_ssd
        ssd_win_d = din("ssd_win", [n0, 24, 128, DC * 256])
        ssd_wdt_d = din("ssd_wdt", [n0, 128, DC * 32])
        ssd_wout_d = din("ssd_wout", [n0, DC, 128, 16 * 128])
        ssd_cw_d = din("ssd_cw", [n0, 128, 32 * 5])
        ssd_hp_d = din("ssd_hp", [n0, 128, 3 * 32 + 16])
        ssd_k_d = din("ssd_k", [128, 128 + 512 + 128])

    if cfg.n_dsa:
        n2 = cfg.n_dsa
        dsa_win_d = din("dsa_win", [n2, 14, 128, DC * 256])
        dsa_wvw_d = din("dsa_wvw", [n2, 128, DC * 72])
        dsa_wo_d = din("dsa_wo", [n2, 4, 128, DC * 256])
        dsa_rope_d = din("dsa_rope", [2, 128, L])
        dsa_k_d = din("dsa_k", [128, 640])

    with ExitStack() as es:
        P = Prog(nc, es)
        I = P.I
        xT = sb(nc, es, "xT", [128, DC, L], F32)
        hT = sb(nc, es, "hT", [128, DC, L], BF16)
        gains = sb(nc, es, "gains_sb", [128, 3 * depth + 1, DC], F32)
        ident = sb(nc, es, "ident_sb", [128, 128], F32)
        ones_bf = sb(nc, es, "ones_bf", [128, 128], BF16)
        epsb = sb(nc, es, "epsb", [128, 1], F32)
        negpi = sb(nc, es, "negpi", [128, 1], F32)
        sq = [sb(nc, es, "sq%d" % i, [128, 512], BF16) for i in range(2)]
        tA = [sb(nc, es, "tA%d" % i, [128, 512], F32) for i in range(2)]
        tB = [sb(nc, es, "tB%d" % i, [128, 512], F32) for i in range(2)]
        NWI = 2
        wi = [sb(nc, es, "wi%d" % i, [128, DC, 256], BF16) for i in range(NWI)]
        pb = [ps(nc, es, "pb%d" % i, [128, 512], F32) for i in range(7)]
        pbh = ps(nc, es, "pbh", [128, 1024], BF16)
        ident_bf = sb(nc, es, "ident_bf", [128, 128], BF16)

        cnt = {"wi": 0, "wo": 0, "sq": 0, "tA": 0, "tB": 0, "xin": 0, "pg": 0, "po": 0}

        I("sp", lambda e: e.dma_start(out=gains[:], in_=gains_d), W=["gains"], dma="gains")
        I("sp", lambda e: e.dma_start(out=ident[:], in_=ident_d), W=["ident"], dma="ident")
        I("dve", lambda e: e.memset(ones_bf[:], 1.0), W=["ones"])
        I("dve", lambda e: e.tensor_copy(out=ident_bf[:], in_=ident[:]), R=["ident"], W=["ident_bf"])
        oneb = sb(nc, es, "oneb", [128, 1], F32)
        I("dve", lambda e: e.memset(oneb[:], 1.0), W=["oneb"])
        I("dve", lambda e: e.memset(epsb[:], EPS), W=["epsb"])
        I("dve", lambda e: e.memset(negpi[:], -math.pi), W=["negpi"])
        halfpi = sb(nc, es, "halfpi", [128, 1], F32)
        I("dve", lambda e: e.memset(halfpi[:], math.pi / 2), W=["negpi"])

        def nxt(name, n=2):
            k = cnt[name] % n
            cnt[name] += 1
            return k

        def rmsnorm_to(dst_fn, gidx, dst_key_fn, tt):
            ts = slice(tt * 512, (tt + 1) * 512)
            pn = pb[6]
            for c in range(DC):
                k = nxt("sq")
                I("act", lambda e, c=c, k=k: e.activation(out=sq[k][:], in_=xT[:, c, ts], func=AF.Square),
                  R=[("xT", c, tt)], W=[("sq", k)])
                I("pe", lambda e, c=c, k=k: e.matmul(pn[:], lhsT=ones_bf[:], rhs=sq[k][:],
                                                      start=(c == 0), stop=(c == DC - 1)),
                  R=[("sq", k), "ones"], W=["pb6"])
            k = nxt("tA")
            I("act", lambda e, k=k: e.activation(out=tA[k][:], in_=pn[:], func=AF.Sqrt,
                                                 scale=1.0 / D, bias=epsb[:]),
              R=["pb6", "epsb"], W=[("tA", k)])
            I("dve", lambda e, k=k: e.reciprocal(out=tA[k][:], in_=tA[k][:]),
              R=[("tA", k)], W=[("tA", k)])
            for c in range(DC):
                I("dve", lambda e, c=c, k=k: e.scalar_tensor_tensor(
                    out=dst_fn(c), in0=xT[:, c, ts], scalar=gains[:, gidx, c:c + 1], in1=tA[k][:],
                    op0=ALU.mult, op1=ALU.mult),
                  R=[("xT", c, tt), ("tA", k), "gains"], W=[dst_key_fn(c)])

        def norm_to_hT(gidx):
            for tt in range(NT):
                ts = slice(tt * 512, (tt + 1) * 512)
                rmsnorm_to(lambda c, ts=ts: hT[:, c, ts], gidx, lambda c, tt=tt: ("hT", c, tt), tt)

        def load_w256(src_ap, key="wi"):
            s = nxt("wi", NWI)
            I("pool", lambda e, s=s: e.dma_start(out=wi[s][:], in_=src_ap.rearrange("p (k f) -> p k f", k=DC)),
              W=[("wi", s)], dma=("wi", s))
            return s

        def mm_pair(s, src, src_key_fn, tt):
            ts = slice(tt * 512, (tt + 1) * 512)
            b = nxt("pg")
            pg, pu = pb[2 * b], pb[2 * b + 1]
            for k in range(DC):
                I("pe", lambda e, k=k: e.matmul(pg[:], lhsT=wi[s][:, k, 0:128], rhs=src[:, k, ts],
                                                start=(k == 0), stop=(k == DC - 1)),
                  R=[("wi", s), src_key_fn(k, tt)], W=["pb%d" % (2 * b)])
            for k in range(DC):
                I("pe", lambda e, k=k: e.matmul(pu[:], lhsT=wi[s][:, k, 128:256], rhs=src[:, k, ts],
                                                start=(k == 0), stop=(k == DC - 1)),
                  R=[("wi", s), src_key_fn(k, tt)], W=["pb%d" % (2 * b + 1)])
            return pg, pu, "pb%d" % (2 * b), "pb%d" % (2 * b + 1)

        def ffn(fidx, gidx):
            def norm_tile(tt):
                ts = slice(tt * 512, (tt + 1) * 512)
                rmsnorm_to(lambda c, ts=ts: hT[:, c, ts], gidx, lambda c, tt=tt: ("hT", c, tt), tt)
            with ExitStack() as sc:
                aT = sb(nc, sc, "aT", [128, 11, L], BF16)
                NWO = 3
                wo = [sb(nc, sc, "wo%d" % i, [128, 11, 128], BF16) for i in range(NWO)]
                NWF = 4
                wf = [sb(nc, sc, "wf%d" % i, [128, DC, 256], BF16) for i in range(NWF)]
                fcn = [0]

                def load_wf(src_ap):
                    s = fcn[0] % NWF
                    fcn[0] += 1
                    I("pool", lambda e, s=s: e.dma_start(out=wf[s][:], in_=src_ap.rearrange("p (k f) -> p k f", k=DC)),
                      W=[("wf", s)], dma=("wf", s))
                    return s
                pre = [load_wf(wi_d[fidx, fc]) for fc in range(min(NWF - 1, FC))]
                NP = max(NT // 2, 1)
                TW = NT // NP
                for t0 in range(TW):
                    norm_tile(t0)
                bp = [0]

                def next_banks():
                    k0 = (bp[0] % 3) * 2
                    bp[0] += 1
                    return [k0 + i for i in range(TW)]
                for hf in range(2):
                    for j in range(11):
                        fc = hf * 11 + j
                        s = pre[fc] if fc < len(pre) else load_wf(wi_d[fidx, fc])
                        for tp in range(NP):
                            tts = [tp * TW + i for i in range(TW)]
                            gb = next_banks()
                            ub = next_banks()
                            for half, banks in ((0, gb), (1, ub)):
                                for k in range(DC):
                                    for i, tt in enumerate(tts):
                                        ts = slice(tt * 512, (tt + 1) * 512)
                                        I("pe", lambda e, k=k, bk=banks[i], s=s, ts=ts, half=half: e.matmul(
                                            pb[bk][:], lhsT=wf[s][:, k, half * 128:(half + 1) * 128], rhs=hT[:, k, ts],
                                            start=(k == 0), stop=(k == DC - 1)),
                                          R=[("wf", s), ("hT", k, tt)], W=["pb%d" % banks[i]])
                            if fc == 0 and tp + 1 < NP:
                                for i in range(TW):
                                    norm_tile((tp + 1) * TW + i)
                            for i, tt in enumerate(tts):
                                ts = slice(tt * 512, (tt + 1) * 512)
                                g = nxt("tB")
                                I("act", lambda e, g=g, bk=gb[i]: e.activation(out=tB[g][:], in_=pb[bk][:], func=AF.Silu),
                                  R=["pb%d" % gb[i]], W=[("tB", g)])
                                I("dve", lambda e, g=g, bk=ub[i], j=j, ts=ts: e.tensor_tensor(
                                    out=aT[:, j, ts], in0=tB[g][:], in1=pb[bk][:], op=ALU.mult),
                                  R=[("tB", g), "pb%d" % ub[i]], W=[("aT", j, tt)])
                    for c in range(DC):
                        s = nxt("wo", NWO)
                        I("pool", lambda e, s=s, c=c, hf=hf: e.dma_start(
                            out=wo[s][:], in_=wo_d[fidx, hf, c].rearrange("p (j d) -> p j d", j=11)),
                          W=[("wo", s)], dma=("wo", s))
                        for tp in range(NP):
                            tts = [tp * TW + i for i in range(TW)]
                            ob = next_banks()
                            for j in range(11):
                                for i, tt in enumerate(tts):
                                    ts = slice(tt * 512, (tt + 1) * 512)
                                    I("pe", lambda e, s=s, j=j, ts=ts, bk=ob[i]: e.matmul(
                                        pb[bk][:], lhsT=wo[s][:, j, :], rhs=aT[:, j, ts],
                                        start=(j == 0), stop=(j == 10)),
                                      R=[("wo", s), ("aT", j, tt)], W=["pb%d" % ob[i]])
                            for i, tt in enumerate(tts):
                                ts = slice(tt * 512, (tt + 1) * 512)
                                I("dve", lambda e, bk=ob[i], c=c, ts=ts: e.scalar_tensor_tensor(
                                    out=xT[:, c, ts], in0=pb[bk][:], scalar=0.5, in1=xT[:, c, ts],
                                    op0=ALU.mult, op1=ALU.add),
                                  R=["pb%d" % ob[i], ("xT", c, tt)], W=[("xT", c, tt)])
                P.release_prefix({"aT", "wo", "wf"})

        def ssd_mixer(j0, gidx):
            norm_to_hT(gidx)
            TT2 = 256
            with ExitStack() as sc:
                def T(name, shape, dt=F32):
                    return sb(nc, sc, "ssd_" + name, shape, dt)
                xc = T("xc", [128, 32, TT2], BF16)
                zT = T("zT", [128, 16, TT2], BF16)
                ynT = T("ynT", [128, 16, TT2], BF16)
                halo = T("halo", [128, 32, 3])
                xp = [T("xp%d" % i, [128, 3 + TT2]) for i in range(2)]
                acc = [T("acc%d" % i, [128, TT2]) for i in range(2)]
                wdt = T("wdt", [128, DC, 32], BF16)
                cw = T("cw", [128, 32, 5])
                hp = T("hp", [128, 3 * 32 + 16])
                kc = T("kc", [128, 128 + 512 + 128])
                Aneg = T("Aneg", [128, 32])
                dtr = T("dtr", [128, 32])
                dtv = T("dtv", [128, 32])
                av = T("av", [128, 32])
                ncs = T("ncs", [128, 32])
                ecs = T("ecs", [128, 32])
                cl = T("cl", [128, 32])
                wl = T("wl", [128, 32])
                dec = T("dec", [128, 32])
                xs_tok = T("xs_tok", [128, 32, 64], BF16)
                dtx = T("dtx", [128, 32, 64], BF16)
                Btok = T("Btok", [128, 8, 128], BF16)
                aU = [[T("aU%d_%d" % (i, j), [128, 4, 128], BF16) for j in range(3)] for i in range(1)]
                asp = [T("asp%d" % j, [128, 32], BF16) for j in range(3)]
                ares = [T("ares%d" % j, [128, 32]) for j in range(2)]
                kcb = T("kcb", [128, 640], BF16)
                Lt = [T("Lt%d" % i, [128, 4, 128], BF16) for i in range(1)]
                Mt = [T("Mt%d" % i, [128, 4, 128], BF16) for i in range(2)]
                CBs = [T("CBs%d" % i, [128, 128], BF16) for i in range(1)]
                tm1 = [tA[i][:, 0:256] for i in range(2)]
                tm2 = [tB[i][:, 0:256] for i in range(2)]
                Y = T("Y", [128, 2048])
                sz = T("sz", [128, 2048])
                Ynb = T("Ynb", [128, 2048], BF16)
                ss = T("ss", [128, 8])
                S = T("S", [128, 8, 256])
                Sbf = T("Sbf", [128, 8, 256], BF16)
                U = kc[:, 0:128]
                causal4 = kc[:, 128:640]
                ones_f = kc[:, 640:768]
                dtb, alog, Dbc, gn = hp[:, 0:32], hp[:, 32:64], hp[:, 64:96], hp[:, 96:112]

                def K_(n):
                    return "ssd_" + n

                def ld(dst, src, key, eng="sp"):
                    I(eng, lambda e: e.dma_start(out=dst, in_=src), W=[key], dma=key)
                ld(wdt[:], ssd_wdt_d[j0].rearrange("p (k f) -> p k f", k=DC), K_("wdt"), "pool")
                ld(cw[:], ssd_cw_d[j0].rearrange("p (c f) -> p c f", c=32), K_("cw"))
                ld(hp[:], ssd_hp_d[j0], K_("hp"))
                ld(kc[:], ssd_k_d, K_("kc"))
                I("dve", lambda e: e.tensor_copy(out=kcb[:], in_=kc[:, 0:640]), R=[K_("kc")], W=[K_("kcb")])
                Ub = kcb[:, 0:128]
                c4b = kcb[:, 128:640]
                I("act", lambda e: e.activation(out=Aneg[:], in_=alog, func=AF.Exp), R=[K_("hp")], W=[K_("Aneg")])
                I("dve", lambda e: e.tensor_scalar(out=Aneg[:], in0=Aneg[:], scalar1=-1.0, scalar2=None, op0=ALU.mult),
                  R=[K_("Aneg")], W=[K_("Aneg")])
                I("dve", lambda e: e.memset(S[:], 0.0), W=[K_("S")])
                I("dve", lambda e: e.memset(Sbf[:], 0.0), W=[K_("Sbf")])
                I("dve", lambda e: e.memset(halo[:], 0.0), W=[K_("halo")])
                cn = {"xp": 0, "g": 0}
                print("ssd sbuf remaining", nc.sbuf_bytes_remaining)

                for t2 in range(L // TT2):
                    tsl = slice(t2 * TT2, (t2 + 1) * TT2)
                    tt = (t2 * TT2) // 512
                    for blk in range(24):
                        s = load_w256(ssd_win_d[j0, blk])
                        b = nxt("pg")
                        pg, pu = pb[2 * b], pb[2 * b + 1]
                        for hh, (pp, kp) in enumerate(((pg, "pb%d" % (2 * b)), (pu, "pb%d" % (2 * b + 1)))):
                            for k in range(DC):
                                I("pe", lambda e, k=k, pp=pp, hh=hh: e.matmul(pp[:, 0:TT2], lhsT=wi[s][:, k, hh * 128:(hh + 1) * 128],
                                                                             rhs=hT[:, k, tsl], start=(k == 0), stop=(k == DC - 1)),
                                  R=[("wi", s), ("hT", k, tt)], W=[kp])
                            ch = 2 * blk + hh
                            if ch < 16:
                                I("act", lambda e, pp=pp, ch=ch: e.activation(out=zT[:, ch, :], in_=pp[:, 0:TT2], func=AF.Copy),
                                  R=[kp], W=[K_("zT")])
                            else:
                                xch = ch - 16
                                r = cn["xp"] % 2
                                cn["xp"] += 1
                                I("act", lambda e, pp=pp, r=r: e.activation(out=xp[r][:, 3:3 + TT2], in_=pp[:, 0:TT2], func=AF.Copy),
                                  R=[kp], W=[(K_("xp"), r)])
                                I("act", lambda e, r=r, xch=xch: e.activation(out=xp[r][:, 0:3], in_=halo[:, xch, :], func=AF.Copy),
                                  R=[(K_("halo"), xch), K_("halo")], W=[(K_("xp"), r)])
                                I("act", lambda e, r=r, xch=xch: e.activation(out=halo[:, xch, :], in_=xp[r][:, TT2:TT2 + 3], func=AF.Copy),
                                  R=[(K_("xp"), r)], W=[(K_("halo"), xch)])
                                I("dve", lambda e, r=r, xch=xch: e.tensor_scalar(out=acc[r][:], in0=xp[r][:, 0:TT2], scalar1=cw[:, xch, 0:1],
                                                                                scalar2=None, op0=ALU.mult),
                                  R=[(K_("xp"), r), K_("cw")], W=[(K_("acc"), r)])
                                for tap in range(1, 4):
                                    I("dve", lambda e, r=r, xch=xch, tap=tap: e.scalar_tensor_tensor(
                                        out=acc[r][:], in0=xp[r][:, tap:tap + TT2], scalar=cw[:, xch, tap:tap + 1], in1=acc[r][:],
                                        op0=ALU.mult, op1=ALU.add),
                                      R=[(K_("xp"), r), K_("cw"), (K_("acc"), r)], W=[(K_("acc"), r)])
                                I("act", lambda e, r=r, xch=xch: e.activation(out=xc[:, xch, :], in_=acc[r][:], func=AF.Silu,
                                                                             bias=cw[:, xch, 4:5]),
                                  R=[(K_("acc"), r), K_("cw")], W=[(K_("xc"), xch)])
                    STG = int(os.environ.get('DBG_SSD', 99))
                    for ck in range(TT2 // 128 if STG >= 2 else 0):
                        lsl = slice(ck * 128, (ck + 1) * 128)
                        asl = slice(t2 * TT2 + ck * 128, t2 * TT2 + (ck + 1) * 128)
                        pdt = pb[3][:, 0:32]
                        pcs = pb[3][:, 64:96]
                        for k in range(DC):
                            I("pe", lambda e, k=k: e.matmul(pdt, lhsT=hT[:, k, asl], rhs=wdt[:, k, :], start=(k == 0), stop=(k == DC - 1)),
                              R=[("hT", k, tt), K_("wdt")], W=["pb3"])
                        SUB = int(os.environ.get('DBG_SUB', 99))
                        if SUB < 2:
                            continue
                        I("dve", lambda e: e.tensor_tensor(out=dtr[:], in0=pdt, in1=dtb, op=ALU.add), R=["pb3", K_("hp")], W=[K_("dtr")])
                        if SUB < 3:
                            continue
                        I("act", lambda e: e.activation(out=dtr[:], in_=dtr[:], func=AF.Exp), R=[K_("dtr")], W=[K_("dtr")])
                        I("act", lambda e: e.activation(out=dtv[:], in_=dtr[:], func=AF.Ln, bias=oneb[:]), R=[K_("dtr"), "oneb"], W=[K_("dtv")])
                        if SUB < 4:
                            continue
                        I("dve", lambda e: e.tensor_tensor(out=av[:], in0=dtv[:], in1=Aneg[:], op=ALU.mult), R=[K_("dtv"), K_("Aneg")], W=[K_("av")])
                        if SUB < 5:
                            continue
                        I("dve", lambda e: e.tensor_copy(out=asp[0][:], in_=av[:]), R=[K_("av")], W=[(K_("asp"), 0)])
                        I("dve", lambda e: e.tensor_tensor(out=ares[0][:], in0=av[:], in1=asp[0][:], op=ALU.subtract),
                          R=[K_("av"), (K_("asp"), 0)], W=[(K_("ares"), 0)])
                        I("dve", lambda e: e.tensor_copy(out=asp[1][:], in_=ares[0][:]), R=[(K_("ares"), 0)], W=[(K_("asp"), 1)])
                        I("dve", lambda e: e.tensor_tensor(out=ares[1][:], in0=ares[0][:], in1=asp[1][:], op=ALU.subtract),
                          R=[(K_("ares"), 0), (K_("asp"), 1)], W=[(K_("ares"), 1)])
                        I("dve", lambda e: e.tensor_copy(out=asp[2][:], in_=ares[1][:]), R=[(K_("ares"), 1)], W=[(K_("asp"), 2)])
                        for j3 in range(3):
                            I("pe", lambda e, j3=j3: e.matmul(pcs, lhsT=Ub, rhs=asp[j3][:], start=(j3 == 0), stop=(j3 == 2)),
                              R=[K_("kcb"), (K_("asp"), j3)], W=["pb3"])
                        I("dve", lambda e: e.tensor_scalar(out=ncs[:], in0=pcs, scalar1=-1.0, scalar2=None, op0=ALU.mult),
                          R=["pb3"], W=[K_("ncs")])
                        I("act", lambda e: e.activation(out=ecs[:], in_=pcs, func=AF.Exp), R=["pb3"], W=[K_("ecs")])
                        if STG < 3:
                            continue
                        for r in range(2):
                            for q in range(8):
                                I("pe", lambda e, r=r, q=q: e.transpose(out=pbh[:, q * 128:(q + 1) * 128], in_=xc[:, 8 * r + q, lsl],
                                                                       identity=ident_bf[:]),
                                  R=[(K_("xc"), 8 * r + q), "ident_bf"], W=["pbh"])
                            I("dve", lambda e, r=r: e.tensor_tensor(out=xs_tok[:, 16 * r:16 * r + 16, :],
                                                                    in0=pbh[:].rearrange("p (h d) -> p h d", h=16),
                                                                    in1=Dbc[:, 16 * r:16 * r + 16].unsqueeze(2).to_broadcast([128, 16, 64]),
                                                                    op=ALU.mult),
                              R=["pbh", K_("hp")], W=[K_("xs_tok")])
                            I("dve", lambda e, r=r: e.tensor_tensor(out=dtx[:, 16 * r:16 * r + 16, :],
                                                                    in0=pbh[:].rearrange("p (h d) -> p h d", h=16),
                                                                    in1=dtv[:, 16 * r:16 * r + 16].unsqueeze(2).to_broadcast([128, 16, 64]),
                                                                    op=ALU.mult),
                              R=["pbh", K_("dtv")], W=[K_("dtx")])
                        for q in range(8):
                            I("pe", lambda e, q=q: e.transpose(out=pbh[:, q * 128:(q + 1) * 128], in_=xc[:, 16 + q, lsl], identity=ident_bf[:]),
                              R=[(K_("xc"), 16 + q), "ident_bf"], W=["pbh"])
                        I("act", lambda e: e.activation(out=Btok[:], in_=pbh[:].rearrange("p (g n) -> p g n", g=8), func=AF.Copy),
                          R=["pbh"], W=[K_("Btok")])
                        if STG < 4:
                            continue
                        def GA(g):
                            i2 = g % 2
                            X = pb[i2]
                            kX = "pb%d" % i2
                            for j3 in range(3):
                                I("dve", lambda e, g=g, i2=i2, j3=j3: e.tensor_tensor(
                                    out=aU[0][j3][:], in0=Ub.unsqueeze(1).to_broadcast([128, 4, 128]),
                                    in1=asp[j3][:, 4 * g:4 * g + 4].unsqueeze(2).to_broadcast([128, 4, 128]), op=ALU.mult),
                                  R=[K_("kcb"), (K_("asp"), j3)], W=[(K_("aU"), 0, j3)])
                                I("pe", lambda e, i2=i2, X=X, j3=j3: e.matmul(X[:], lhsT=ones_bf[:], rhs=aU[0][j3][:].rearrange("p h t -> p (h t)"),
                                                                          start=(j3 == 0), stop=False),
                                  R=["ones", (K_("aU"), 0, j3)], W=[kX])
                            I("pe", lambda e, X=X: e.matmul(X[:], lhsT=ident_bf[:], rhs=c4b, start=False, stop=True),
                              R=[K_("kcb"), "ident_bf"], W=[kX])
                            for h in range(4):
                                I("act", lambda e, h=h, g=g, i2=i2, X=X: e.activation(
                                    out=Lt[0][:, h, :], in_=X[:, h * 128:(h + 1) * 128], func=AF.Exp,
                                    bias=ncs[:, 4 * g + h:4 * g + h + 1]),
                                  R=[kX, K_("ncs")], W=[(K_("Lt"), 0)])
                            I("dve", lambda e, g=g, X=X: e.tensor_copy(
                                out=cl[:, 4 * g:4 * g + 4].unsqueeze(2),
                                in_=X[:].rearrange("p (h t) -> p h t", h=4)[:, :, 127:128]),
                              R=[kX], W=[(K_("cl"), g)])
                            CBp = pb[2][:, 0:128]
                            I("pe", lambda e, g=g, CBp=CBp: e.matmul(CBp, lhsT=xc[:, 16 + g, lsl], rhs=xc[:, 24 + g, lsl], start=True, stop=True),
                              R=[(K_("xc"), 16 + g), (K_("xc"), 24 + g)], W=["pb2"])
                            I("act", lambda e, i2=i2, CBp=CBp: e.activation(out=CBs[0][:], in_=CBp, func=AF.Copy),
                              R=["pb2"], W=[(K_("CBs"), 0)])
                            I("dve", lambda e, i2=i2: e.tensor_tensor(out=Mt[i2][:], in0=Lt[0][:],
                                                                      in1=CBs[0][:].unsqueeze(1).to_broadcast([128, 4, 128]), op=ALU.mult),
                              R=[(K_("Lt"), 0), (K_("CBs"), 0)], W=[(K_("Mt"), i2)])
                        def GB(g):
                            i2 = g % 2
                            pY = pb[5 + i2]
                            kY = "pb%d" % (5 + i2)
                            for h in range(4):
                                I("pe", lambda e, h=h, g=g, i2=i2: e.matmul(pY[:, 256 + h * 64:256 + (h + 1) * 64], lhsT=Mt[i2][:, h, :],
                                                                           rhs=dtx[:, 4 * g + h, :], start=True, stop=True),
                                  R=[(K_("Mt"), i2), K_("dtx")], W=[kY])
                            I("pe", lambda e, g=g: e.matmul(pY[:, 0:256], lhsT=xc[:, 24 + g, lsl], rhs=Sbf[:, g, :], start=True, stop=True),
                              R=[(K_("xc"), 24 + g), (K_("Sbf"), g)], W=[kY])
                            I("dve", lambda e, g=g, i2=i2: e.tensor_tensor(
                                out=tm1[i2].rearrange("p (h d) -> p h d", h=4), in0=pY[:, 0:256].rearrange("p (h d) -> p h d", h=4),
                                in1=ecs[:, 4 * g:4 * g + 4].unsqueeze(2).to_broadcast([128, 4, 64]), op=ALU.mult),
                              R=[kY, K_("ecs")], W=[("tA", i2)])
                            I("dve", lambda e, i2=i2: e.tensor_tensor(out=tm1[i2], in0=tm1[i2], in1=pY[:, 256:512], op=ALU.add),
                              R=[kY, ("tA", i2)], W=[("tA", i2)])
                            I("dve", lambda e, g=g, i2=i2: e.tensor_tensor(out=Y[:, g * 256:(g + 1) * 256], in0=tm1[i2],
                                                                         in1=xs_tok[:, 4 * g:4 * g + 4, :].rearrange("p h d -> p (h d)"), op=ALU.add),
                              R=[("tA", i2), K_("xs_tok")], W=[(K_("Y"), g)])
                        GA(0)
                        for g in range(8):
                            if g + 1 < 8:
                                GA(g + 1)
                            GB(g)
                        I("dve", lambda e: e.tensor_tensor(out=wl[:], in0=cl[:], in1=ncs[:], op=ALU.add),
                          R=[(K_("cl"), g) for g in range(8)] + [K_("ncs")], W=[K_("wl")])
                        I("act", lambda e: e.activation(out=wl[:], in_=wl[:], func=AF.Exp), R=[K_("wl")], W=[K_("wl")])
                        I("act", lambda e: e.activation(out=dec[:], in_=cl[:], func=AF.Exp), R=[(K_("cl"), g) for g in range(8)], W=[K_("dec")])
                        I("dve", lambda e: e.tensor_tensor(out=dtx[:], in0=dtx[:], in1=wl[:].unsqueeze(2).to_broadcast([128, 32, 64]), op=ALU.mult),
                          R=[K_("dtx"), K_("wl")], W=[K_("dtx")])
                        for g in range(8):
                            i2 = g % 2
                            SU = pb[4][:, 0:256]
                            I("pe", lambda e, g=g, SU=SU: e.matmul(SU, lhsT=Btok[:, g, :], rhs=dtx[:, 4 * g:4 * g + 4, :].rearrange("p h d -> p (h d)"),
                                                                  start=True, stop=True),
                              R=[K_("Btok"), K_("dtx")], W=["pb4"])
                            I("dve", lambda e, g=g: e.tensor_tensor(
                                out=S[:, g, :].rearrange("p (h d) -> p h d", h=4), in0=S[:, g, :].rearrange("p (h d) -> p h d", h=4),
                                in1=dec[:, 4 * g:4 * g + 4].unsqueeze(2).to_broadcast([128, 4, 64]), op=ALU.mult),
                              R=[(K_("S"), g), K_("S"), K_("dec")], W=[(K_("S"), g)])
                            I("dve", lambda e, g=g, SU=SU: e.tensor_tensor(out=S[:, g, :], in0=S[:, g, :], in1=SU, op=ALU.add),
                              R=[(K_("S"), g), "pb4"], W=[(K_("S"), g)])
                            I("act", lambda e, g=g: e.activation(out=Sbf[:, g, :], in_=S[:, g, :], func=AF.Copy),
                              R=[(K_("S"), g), K_("Sbf")], W=[(K_("Sbf"), g)])
                        if STG < 6:
                            continue
                        for r in range(2):
                            for q in range(8):
                                I("pe", lambda e, r=r, q=q: e.transpose(out=pbh[:, q * 128:(q + 1) * 128], in_=zT[:, 8 * r + q, lsl],
                                                                       identity=ident_bf[:]),
                                  R=[K_("zT"), "ident_bf"], W=["pbh"])
                            I("act", lambda e, r=r: e.activation(out=sz[:, r * 1024:(r + 1) * 1024], in_=pbh[:], func=AF.Silu),
                              R=["pbh"], W=[K_("sz")])
                        I("dve", lambda e: e.tensor_tensor(out=Y[:], in0=Y[:], in1=sz[:], op=ALU.mult),
                          R=[(K_("Y"), g) for g in range(8)] + [K_("sz")], W=[K_("Y2")])
                        I("act", lambda e: e.activation(out=sz[:], in_=Y[:], func=AF.Square), R=[K_("Y2")], W=[K_("sz")])
                        I("dve", lambda e: e.tensor_reduce(out=ss[:], in_=sz[:].rearrange("p (g c) -> p g c", g=8), axis=AX.X, op=ALU.add),
                          R=[K_("sz")], W=[K_("ss")])
                        I("act", lambda e: e.activation(out=ss[:], in_=ss[:], func=AF.Sqrt, scale=1.0 / 256.0, bias=epsb[:]),
                          R=[K_("ss"), "epsb"], W=[K_("ss")])
                        I("dve", lambda e: e.reciprocal(out=ss[:], in_=ss[:]), R=[K_("ss")], W=[K_("ss")])
                        I("dve", lambda e: e.tensor_tensor(out=Ynb[:].rearrange("p (g c) -> p g c", g=8), in0=Y[:].rearrange("p (g c) -> p g c", g=8),
                                                           in1=ss[:].unsqueeze(2).to_broadcast([128, 8, 256]), op=ALU.mult),
                          R=[K_("Y2"), K_("ss")], W=[K_("Ynb")] + [(K_("Y"), g) for g in range(8)])
                        for r in range(2):
                            for q in range(8):
                                I("pe", lambda e, r=r, q=q: e.transpose(out=pbh[:, q * 128:(q + 1) * 128],
                                                                       in_=Ynb[:, (8 * r + q) * 128:(8 * r + q + 1) * 128], identity=ident_bf[:]),
                                  R=[K_("Ynb"), "ident_bf"], W=["pbh"])
                            I("dve", lambda e, r=r: e.tensor_tensor(out=ynT[:, 8 * r:8 * r + 8, lsl], in0=pbh[:].rearrange("p (c t) -> p c t", c=8),
                                                                    in1=gn[:, 8 * r:8 * r + 8].unsqueeze(2).to_broadcast([128, 8, 128]), op=ALU.mult),
                              R=["pbh", K_("hp")], W=[K_("ynT")])
                    for dc in range(DC if STG >= 7 else 0):
                        s = load_w256(ssd_wout_d[j0, dc])
                        wv = wi[s][:].rearrange("p k (a f) -> p (k a) f", a=2)
                        b = nxt("po")
                        po = pb[4 + b] if False else pb[6]
                        for c in range(16):
                            I("pe", lambda e, c=c, wv=wv: e.matmul(pb[6][:, 0:TT2], lhsT=wv[:, c, :], rhs=ynT[:, c, :], start=(c == 0), stop=(c == 15)),
                              R=[("wi", s), K_("ynT")], W=["pb6"])
                        I("dve", lambda e, dc=dc: e.tensor_tensor(out=xT[:, dc, tsl], in0=xT[:, dc, tsl], in1=pb[6][:, 0:TT2], op=ALU.add),
                          R=["pb6", ("xT", dc, tt)], W=[("xT", dc, tt)])
                P.release([k for k in list(P.res) if (k[0] if isinstance(k, tuple) else k).startswith("ssd_")])


        def dsa_mixer(j2, gidx):
            norm_to_hT(gidx)
            NEG = -1.0e30
            TOPK = min(256, L // 4)
            NR = TOPK // 8
            with ExitStack() as sc:
                def T(name, shape, dt=F32):
                    return sb(nc, sc, "dsa_" + name, shape, dt)

                def K_(n):
                    return "dsa_" + n
                qT = T("qT", [128, 8, L], BF16)
                K2T = T("K2T", [128, L], BF16)
                qiT = T("qiT", [128, 4, L], BF16)
                ki2T = T("ki2T", [128, L], BF16)
                Vaug = T("Vaug", [128, NB, 65], BF16)
                witok = T("witok", [128, NB, 8])
                I("dve", lambda e: e.memset(Vaug[:], 1.0), W=[K_("Vaug")])
                with ExitStack() as sc2:
                    wvw = sb(nc, sc2, "dsa_wvw", [128, DC, 72], BF16)
                    I("pool", lambda e: e.dma_start(out=wvw[:], in_=dsa_wvw_d[j2].rearrange("p (k f) -> p k f", k=DC)),
                      W=[K_("wvw")], dma=K_("wvw"))
                    cosT = sb(nc, sc2, "dsa_cos", [128, L], F32)
                    sinS = sb(nc, sc2, "dsa_sin", [128, L], F32)
                    I("sp", lambda e: e.dma_start(out=cosT[:], in_=dsa_rope_d[0]), W=[K_("cos")], dma=K_("cos"))
                    I("sp", lambda e: e.dma_start(out=sinS[:], in_=dsa_rope_d[1]), W=[K_("sin")], dma=K_("sin"))
                    for blk in range(14):
                        s = load_w256(dsa_win_d[j2, blk])
                        if blk < 8:
                            dst, dkey, scl = (lambda ts, blk=blk: qT[:, blk, ts]), K_("qT"), 0.125
                        elif blk == 8:
                            dst, dkey, scl = (lambda ts: K2T[:, ts]), K_("K2T"), 1.0
                        elif blk < 13:
                            dst, dkey, scl = (lambda ts, blk=blk: qiT[:, blk - 9, ts]), K_("qiT"), 1.0
                        else:
                            dst, dkey, scl = (lambda ts: ki2T[:, ts]), K_("ki2T"), 1.0
                        for tt in range(NT):
                            ts = slice(tt * 512, (tt + 1) * 512)
                            pg, pu, kg, ku = mm_pair(s, hT, lambda k, tt: ("hT", k, tt), tt)
                            k1 = nxt("tA")
                            k2 = nxt("tB")
                            I("dve", lambda e, k1=k1, pg=pg, ts=ts, scl=scl: e.scalar_tensor_tensor(
                                out=tA[k1][:], in0=pg[:], scalar=scl, in1=cosT[:, ts], op0=ALU.mult, op1=ALU.mult),
                              R=[kg, K_("cos")], W=[("tA", k1)])
                            I("dve", lambda e, k2=k2, pu=pu, ts=ts, scl=scl: e.scalar_tensor_tensor(
                                out=tB[k2][:], in0=pu[:], scalar=scl, in1=sinS[:, ts], op0=ALU.mult, op1=ALU.mult),
                              R=[ku, K_("sin")], W=[("tB", k2)])
                            I("pool", lambda e, k1=k1, k2=k2, ts=ts, dst=dst: e.tensor_tensor(out=dst(ts), in0=tA[k1][:], in1=tB[k2][:], op=ALU.add),
                              R=[("tA", k1), ("tB", k2)], W=[(dkey, tt)])
                    for tb in range(NB):
                        bsl = slice(tb * 128, (tb + 1) * 128)
                        pv = pb[4][:, 0:72]
                        for k in range(DC):
                            I("pe", lambda e, k=k: e.matmul(pv, lhsT=hT[:, k, bsl], rhs=wvw[:, k, :], start=(k == 0), stop=(k == DC - 1)),
                              R=[("hT", k, tb // 4), K_("wvw")], W=["pb4"])
                        I("act", lambda e, tb=tb: e.activation(out=Vaug[:, tb, 0:64], in_=pb[4][:, 0:64], func=AF.Copy),
                          R=["pb4", K_("Vaug")], W=[(K_("Vaug"), tb)])
                        I("dve", lambda e, tb=tb: e.tensor_scalar(out=witok[:, tb, :], in0=pb[4][:, 64:72], scalar1=8.0 ** -0.5 * 64.0 ** -0.5,
                                                                  scalar2=None, op0=ALU.mult),
                          R=["pb4"], W=[(K_("witok"), tb)])
                    P.release([K_("cos"), K_("sin"), K_("wvw")])
                with ExitStack() as sc3:
                    def T3(name, shape, dt=F32):
                        return sb(nc, sc3, "dsa_" + name, shape, dt)
                    idx_ = [T3("idx%d" % i, [128, L]) for i in range(2)]
                    mb_ = [T3("mb%d" % i, [128, L], BF16) for i in range(2)]
                    Pt = [T3("Pt%d" % i, [128, 512], BF16) for i in range(2)]
                    Otok = T3("Otok", [128, 16, 64], BF16)
                    OT = T3("OT", [128, DC, 128], BF16)
                    kd = T3("kd", [128, 128])
                    irep = T3("irep", [128, 512], BF16)
                    m8 = T3("m8", [128, 8])
                    rcp = T3("rcp", [128, 16, 1])
                    zer = T3("zer", [128, 512], BF16)
                    I("dve", lambda e: e.memset(zer[:], 0.0), W=[K_("zer")])
                    I("sp", lambda e: e.dma_start(out=kd[:], in_=dsa_k_d[:, 0:128]), W=[K_("kd")], dma=K_("kd"))
                    I("pool", lambda e: e.dma_start(out=irep[:], in_=dsa_k_d[:, 128:640]), W=[K_("irep")], dma=K_("irep"))
                    negtri = kd[:, 0:128]
                    cn = {"pi": 0, "ps": 0}
                    def stageA(qb):
                        qsl = slice(qb * 128, (qb + 1) * 128)
                        S = 128 * (qb + 1)
                        bi = qb % 2
                        idx, mb = idx_[bi], mb_[bi]
                        npc = (S + 511) // 512
                        for hi in range(8):
                            c, half = hi // 2, hi % 2
                            hs = slice(half * 64, (half + 1) * 64)
                            for pc in range(npc):
                                cols = min(512, S - 512 * pc)
                                csl = slice(512 * pc, 512 * pc + cols)
                                ib = cn["pi"] % 2
                                cn["pi"] += 1
                                pI = pb[5 + ib]
                                I("pe", lambda e, c=c, hs=hs, csl=csl, cols=cols, pI=pI: e.matmul(
                                    pI[:, 0:cols], lhsT=qiT[hs, c, qsl], rhs=ki2T[hs, csl], start=True, stop=True),
                                  R=[(K_("qiT"), qb // 4), (K_("ki2T"), pc)], W=["pb%d" % (5 + ib)])
                                k1 = nxt("tA")
                                I("act", lambda e, k1=k1, cols=cols, pI=pI: e.activation(out=tA[k1][:, 0:cols], in_=pI[:, 0:cols], func=AF.Relu),
                                  R=["pb%d" % (5 + ib)], W=[("tA", k1)])
                                wcol = witok[:, qb, hi:hi + 1]
                                if hi == 0:
                                    I("dve", lambda e, k1=k1, cols=cols, csl=csl, wcol=wcol, idx=idx: e.tensor_scalar(
                                        out=idx[:, csl], in0=tA[k1][:, 0:cols], scalar1=wcol, scalar2=None, op0=ALU.mult),
                                      R=[("tA", k1), (K_("witok"), qb)], W=[(K_("idx"), bi, pc)])
                                else:
                                    I("dve", lambda e, k1=k1, cols=cols, csl=csl, wcol=wcol, idx=idx: e.scalar_tensor_tensor(
                                        out=idx[:, csl], in0=tA[k1][:, 0:cols], scalar=wcol, in1=idx[:, csl], op0=ALU.mult, op1=ALU.add),
                                      R=[("tA", k1), (K_("witok"), qb), (K_("idx"), bi, pc)], W=[(K_("idx"), bi, pc)])
                        allidx = [(K_("idx"), bi, pc) for pc in range(npc)]
                        I("dve", lambda e, S=S, idx=idx: e.tensor_tensor(out=idx[:, S - 128:S], in0=idx[:, S - 128:S], in1=negtri, op=ALU.add),
                          R=allidx + [K_("kd")], W=allidx)
                        if qb >= TOPK // 128:
                            for rd in range(NR):
                                I("dve", lambda e, S=S, idx=idx: e.max(out=m8[:], in_=idx[:, 0:S]), R=allidx, W=[K_("m8")])
                                I("dve", lambda e, S=S, idx=idx: e.match_replace(out=idx[:, 0:S], in_to_replace=m8[:], in_values=idx[:, 0:S],
                                                                                imm_value=NEG),
                                  R=[K_("m8")], W=allidx)
                            I("dve", lambda e, S=S, idx=idx, mb=mb: e.tensor_scalar(out=mb[:, 0:S], in0=idx[:, 0:S], scalar1=NEG, scalar2=-30000.0,
                                                                                  op0=ALU.not_equal, op1=ALU.mult),
                              R=allidx, W=[(K_("mb"), bi)])
                        else:
                            I("dve", lambda e, S=S, idx=idx, mb=mb: e.tensor_scalar(out=mb[:, 0:S], in0=idx[:, 0:S], scalar1=-1.0e29, scalar2=-30000.0,
                                                                                  op0=ALU.is_lt, op1=ALU.mult),
                              R=allidx, W=[(K_("mb"), bi)])
                    def stageB(qb):
                        qsl = slice(qb * 128, (qb + 1) * 128)
                        bi = qb % 2
                        mb = mb_[bi]
                        for bk, nh in ((2, 7), (3, 7), (4, 2)):
                            I("pe", lambda e, bk=bk, nh=nh: e.matmul(pb[bk][:, 0:nh * 65], lhsT=zer[:, 0:128], rhs=zer[:, 0:nh * 65],
                                                                    start=True, stop=False),
                              R=[K_("zer")], W=["pb%d" % bk])
                        for sbk in range(qb + 1):
                            ssl = slice(sbk * 128, (sbk + 1) * 128)
                            for half in range(2):
                                hs = slice(half * 64, (half + 1) * 64)
                                for cg in range(2):
                                    ib = cn["ps"] % 2
                                    cn["ps"] += 1
                                    pS = pb[ib]
                                    I("pe", lambda e, hs=hs, cg=cg, ssl=ssl, pS=pS: e.matmul(
                                        pS[:], lhsT=K2T[hs, ssl], rhs=qT[hs, 4 * cg:4 * cg + 4, qsl], start=True, stop=False),
                                      R=[(K_("K2T"), sbk // 4), (K_("qT"), qb // 4)], W=["pb%d" % ib])
                                    I("pe", lambda e, ssl=ssl, pS=pS, mb=mb: e.matmul(pS[:], lhsT=mb[:, ssl], rhs=irep[:], start=False, stop=True),
                                      R=[(K_("mb"), bi), K_("irep")], W=["pb%d" % ib])
                                    I("act", lambda e, ib=ib, pS=pS: e.activation(out=Pt[ib][:], in_=pS[:], func=AF.Exp),
                                      R=["pb%d" % ib], W=[(K_("Pt"), ib)])
                                    for j in range(4):
                                        head = 2 * (4 * cg + j) + half
                                        bk, off = 2 + head // 7, (head % 7) * 65
                                        I("pe", lambda e, ib=ib, j=j, bk=bk, off=off, sbk=sbk: e.matmul(
                                            pb[bk][:, off:off + 65], lhsT=Pt[ib][:, j * 128:(j + 1) * 128], rhs=Vaug[:, sbk, :],
                                            start=False, stop=False),
                                          R=[(K_("Pt"), ib), (K_("Vaug"), sbk), K_("Vaug")], W=["pb%d" % bk])
                        for bk, nh in ((2, 7), (3, 7), (4, 2)):
                            I("pe", lambda e, bk=bk, nh=nh: e.matmul(pb[bk][:, 0:nh * 65], lhsT=zer[:, 0:128], rhs=zer[:, 0:nh * 65],
                                                                    start=False, stop=True),
                              R=[K_("zer")], W=["pb%d" % bk])
                        for bk, h0, nh in ((2, 0, 7), (3, 7, 7), (4, 14, 2)):
                            pv3 = pb[bk][:, 0:nh * 65].rearrange("p (h e) -> p h e", e=65)
                            I("dve", lambda e, pv3=pv3, h0=h0, nh=nh: e.reciprocal(out=rcp[:, h0:h0 + nh, :], in_=pv3[:, :, 64:65]),
                              R=["pb%d" % bk], W=[(K_("rcp"), bk)])
                            I("dve", lambda e, pv3=pv3, h0=h0, nh=nh: e.tensor_tensor(
                                out=Otok[:, h0:h0 + nh, :], in0=pv3[:, :, 0:64], in1=rcp[:, h0:h0 + nh, :].to_broadcast([128, nh, 64]), op=ALU.mult),
                              R=["pb%d" % bk, (K_("rcp"), bk)], W=[(K_("Otok"), bk)])
                        lq = 0
                        for c in range(8):
                            I("pe", lambda e, c=c: e.transpose(out=pbh[:, c * 128:(c + 1) * 128],
                                                               in_=Otok[:, 2 * c:2 * c + 2, :].rearrange("p h d -> p (h d)"), identity=ident_bf[:]),
                              R=[(K_("Otok"), 2), (K_("Otok"), 3), (K_("Otok"), 4), "ident_bf"], W=["pbh"])
                        I("act", lambda e, lq=lq: e.activation(out=OT[:, :, lq * 128:(lq + 1) * 128],
                                                               in_=pbh[:].rearrange("p (c t) -> p c t", c=8), func=AF.Copy),
                          R=["pbh"], W=[K_("OT")])
                        if True:
                            osl = slice(qb * 128, (qb + 1) * 128)
                            tt = qb // 4
                            for b4 in range(4):
                                s = load_w256(dsa_wo_d[j2, b4])
                                for hh in range(2):
                                    dc = 2 * b4 + hh
                                    po = pb[5 + hh]
                                    for k in range(DC):
                                        I("pe", lambda e, k=k, hh=hh, po=po: e.matmul(po[:, 0:128], lhsT=wi[s][:, k, hh * 128:(hh + 1) * 128],
                                                                                     rhs=OT[:, k, :], start=(k == 0), stop=(k == DC - 1)),
                                          R=[("wi", s), K_("OT")], W=["pb%d" % (5 + hh)])
                                    I("dve", lambda e, dc=dc, po=po: e.tensor_tensor(out=xT[:, dc, osl], in0=xT[:, dc, osl], in1=po[:, 0:128], op=ALU.add),
                                      R=["pb%d" % (5 + hh), ("xT", dc, tt)], W=[("xT", dc, tt)])
                    stageA(0)
                    for qb in range(NB):
                        if qb + 1 < NB:
                            stageA(qb + 1)
                        stageB(qb)
                P.release([k for k in list(P.res) if (k[0] if isinstance(k, tuple) else k).startswith("dsa_")])

        def s5_mixer(j5, gidx):
            norm_to_hT(gidx)
            with ExitStack() as sc:
                def T(name, shape, dt=F32):
                    return sb(nc, sc, "s5_" + name, shape, dt)
                gT = T("gT", [128, DC, L], BF16)
                iota = T("iota", [128, L], mybir.dt.int16)
                par = T("par", [128, 3, 64])
                bre2 = [T("bre%d" % i, [64, 128]) for i in range(2)]
                bim2 = [T("bim%d" % i, [64, 128]) for i in range(2)]
                cst12 = [T("cst1%d" % i, [128, 128]) for i in range(2)]
                cst22 = [T("cst2%d" % i, [128, 128]) for i in range(2)]
                magic = T("magic", [128, 1])
                kk = T("kk", [128, 16])
                dv = T("dv", [128, 24])
                names = ["step", "lr", "th", "r", "f", "raw", "fs", "fc", "ms0", "mc0", "nr", "ni",
                         "den", "inv", "cr", "ci", "u1", "u2"]
                pt_ = {n: T(n, [128, 64]) for n in names}
                state = T("state", [128, 64])
                z = {n: T(n, [64, 128]) for n in ["t1", "t2", "zre", "zim", "nzre"]}
                wA = T("wA", [128, 8, 128], BF16)
                wA2 = T("wA2", [128, 8, 128], BF16)
                W1 = T("W1", [128, 8, 128], BF16)
                W2 = T("W2", [128, 8, 128], BF16)
                raw_ = [T("rawt%d" % i, [128, 512]) for i in range(2)]
                msin_ = [T("msin%d" % i, [128, 512]) for i in range(2)]
                mcos_ = [T("mcos%d" % i, [128, 512]) for i in range(2)]
                bt_ = [T("bt%d" % i, [128, 512]) for i in range(2)]
                st_ = [T("st%d" % i, [128, 512]) for i in range(2)]
                P1_ = [T("P1%d" % i, [128, 512], BF16) for i in range(2)]
                P2_ = [T("P2%d" % i, [128, 512], BF16) for i in range(2)]
                I("dve", lambda e: e.memset(magic[:], MAGIC), W=["s5_magic"])
                itc = [0]
                yv = T("yv", [128, 512])
                zz = T("zz", [128, 512])

                def ld(dst, src, key):
                    I("sp", lambda e: e.dma_start(out=dst, in_=src), W=[key], dma=key)
                ld(iota[:], iota_d, "s5_iota")
                ld(par[:], s5p_d[j5].rearrange("a p g -> p a g"), "s5_par")
                ld(kk[:], s5k_d, "s5_kk")
                ld(dv[:], s5d_d[j5], "s5_dv")

                def dv_(fn, R, W):
                    I("dve", fn, R=["s5_" + r for r in R], W=["s5_" + w for w in W])

                def po_(fn, R, W):
                    I("pool", fn, R=["s5_" + r for r in R], W=["s5_" + w for w in W])

                def ac_(fn, R, W):
                    I("act", fn, R=["s5_" + r for r in R] + ["negpi"], W=["s5_" + w for w in W])
                p = pt_
                lre, lim, lst = par[:, 0, :], par[:, 1, :], par[:, 2, :]
                ac_(lambda e: e.activation(out=p["step"][:], in_=lst, func=AF.Exp), ["par"], ["step"])
                dv_(lambda e: e.tensor_tensor(out=p["lr"][:], in0=lre, in1=p["step"][:], op=ALU.mult), ["par", "step"], ["lr"])
                dv_(lambda e: e.tensor_tensor(out=p["th"][:], in0=lim, in1=p["step"][:], op=ALU.mult), ["par", "step"], ["th"])
                ac_(lambda e: e.activation(out=p["r"][:], in_=p["lr"][:], func=AF.Exp), ["lr"], ["r"])
                dv_(lambda e: e.tensor_scalar(out=p["f"][:], in0=p["th"][:], scalar1=1.0 / TWO_PI, scalar2=None, op0=ALU.mult), ["th"], ["f"])
                dv_(lambda e: e.tensor_scalar(out=p["raw"][:], in0=p["f"][:], scalar1=MAGIC, scalar2=None, op0=ALU.add), ["f"], ["raw"])
                dv_(lambda e: e.scalar_tensor_tensor(out=p["fs"][:], in0=p["raw"][:], scalar=MAGIC, in1=p["f"][:], op0=ALU.subtract, op1=ALU.subtract), ["raw", "f"], ["fs"])
                dv_(lambda e: e.scalar_tensor_tensor(out=p["fc"][:], in0=p["fs"][:], scalar=-1.0, in1=p["fs"][:], op0=ALU.mult, op1=ALU.max), ["fs"], ["fc"])
                ac_(lambda e: e.activation(out=p["ms0"][:], in_=p["fs"][:], func=AF.Sin, scale=-TWO_PI), ["fs"], ["ms0"])
                ac_(lambda e: e.activation(out=p["mc0"][:], in_=p["fc"][:], func=AF.Sin, scale=-TWO_PI, bias=halfpi[:]), ["fc"], ["mc0"])
                dv_(lambda e: e.tensor_tensor(out=p["u1"][:], in0=p["mc0"][:], in1=p["r"][:], op=ALU.mult), ["mc0", "r"], ["u1"])
                dv_(lambda e: e.tensor_scalar(out=p["nr"][:], in0=p["u1"][:], scalar1=-1.0, scalar2=None, op0=ALU.add), ["u1"], ["nr"])
                dv_(lambda e: e.tensor_tensor(out=p["u2"][:], in0=p["ms0"][:], in1=p["r"][:], op=ALU.mult), ["ms0", "r"], ["u2"])
                dv_(lambda e: e.tensor_copy(out=p["ni"][:], in_=p["u2"][:]), ["u2"], ["ni"])
                dv_(lambda e: e.tensor_tensor(out=p["u1"][:], in0=lre, in1=lre, op=ALU.mult), ["par", "nr"], ["u1"])
                dv_(lambda e: e.tensor_tensor(out=p["u2"][:], in0=lim, in1=lim, op=ALU.mult), ["par", "ni"], ["u2"])
                dv_(lambda e: e.tensor_tensor(out=p["den"][:], in0=p["u1"][:], in1=p["u2"][:], op=ALU.add), ["u1", "u2"], ["den"])
                dv_(lambda e: e.reciprocal(out=p["inv"][:], in_=p["den"][:]), ["den"], ["inv"])
                dv_(lambda e: e.tensor_tensor(out=p["u1"][:], in0=p["nr"][:], in1=lre, op=ALU.mult), ["nr", "par", "den"], ["u1"])
                dv_(lambda e: e.tensor_tensor(out=p["u2"][:], in0=p["ni"][:], in1=lim, op=ALU.mult), ["ni", "par", "den"], ["u2"])
                dv_(lambda e: e.tensor_tensor(out=p["cr"][:], in0=p["u1"][:], in1=p["u2"][:], op=ALU.add), ["u1", "u2"], ["cr"])
                dv_(lambda e: e.tensor_tensor(out=p["cr"][:], in0=p["cr"][:], in1=p["inv"][:], op=ALU.mult), ["cr", "inv"], ["cr"])
                dv_(lambda e: e.tensor_tensor(out=p["u1"][:], in0=p["ni"][:], in1=lre, op=ALU.mult), ["ni", "par", "cr"], ["u1"])
                dv_(lambda e: e.tensor_tensor(out=p["u2"][:], in0=p["nr"][:], in1=lim, op=ALU.mult), ["nr", "par", "cr"], ["u2"])
                dv_(lambda e: e.tensor_tensor(out=p["ci"][:], in0=p["u1"][:], in1=p["u2"][:], op=ALU.subtract), ["u1", "u2"], ["ci"])
                dv_(lambda e: e.tensor_tensor(out=p["ci"][:], in0=p["ci"][:], in1=p["inv"][:], op=ALU.mult), ["ci", "inv"], ["ci"])

                def s5_tabA(q, g, si):
                    ts = slice(q * 512, (q + 1) * 512)
                    fcol = p["f"][:, g:g + 1]
                    raw, msin, mcos = raw_[si], msin_[si], mcos_[si]
                    kr, ks, kc_ = [("s5_" + n, si) for n in ("rawt", "msin", "mcos")]
                    I("act", lambda e, fcol=fcol, raw=raw: e.activation(out=raw[:], in_=iota[:, ts], func=AF.Identity, scale=fcol),
                      R=["s5_iota", "s5_f"], W=[kr])
                    I("act", lambda e, fcol=fcol, msin=msin: e.activation(out=msin[:], in_=iota[:, ts], func=AF.Identity, scale=fcol,
                                                                       bias=magic[:]),
                      R=["s5_iota", "s5_f", "s5_magic"], W=[ks])
                def s5_tabB(q, g, si):
                    ts = slice(q * 512, (q + 1) * 512)
                    fcol = p["f"][:, g:g + 1]
                    raw, msin, mcos = raw_[si], msin_[si], mcos_[si]
                    kr, ks, kc_ = [("s5_" + n, si) for n in ("rawt", "msin", "mcos")]
                    I("dve", lambda e, msin=msin, raw=raw: e.scalar_tensor_tensor(out=msin[:], in0=msin[:], scalar=MAGIC, in1=raw[:],
                                                                             op0=ALU.subtract, op1=ALU.subtract),
                      R=[kr, ks], W=[ks])
                    I("dve", lambda e, msin=msin, mcos=mcos: e.scalar_tensor_tensor(out=mcos[:], in0=msin[:], scalar=-1.0, in1=msin[:],
                                                                               op0=ALU.mult, op1=ALU.max),
                      R=[ks], W=[kc_])
                def s5_tabC(q, g, si):
                    ts = slice(q * 512, (q + 1) * 512)
                    fcol = p["f"][:, g:g + 1]
                    raw, msin, mcos = raw_[si], msin_[si], mcos_[si]
                    kr, ks, kc_ = [("s5_" + n, si) for n in ("rawt", "msin", "mcos")]
                    I("act", lambda e, msin=msin: e.activation(out=msin[:], in_=msin[:], func=AF.Sin, scale=-TWO_PI),
                      R=[ks], W=[ks])
                    I("act", lambda e, mcos=mcos: e.activation(out=mcos[:], in_=mcos[:], func=AF.Sin, scale=-TWO_PI, bias=halfpi[:]),
                      R=[kc_, "negpi"], W=[kc_])

                def s5_tab(q, g, si):
                    s5_tabA(q, g, si)
                    s5_tabB(q, g, si)
                    s5_tabC(q, g, si)
                order = [(q_, 8 * ct_ + g__) for ct_ in range(DC) for q_ in range(NT) for g__ in range(8)]
                s5_tab(order[0][0], order[0][1], 0)
                for ct in range(DC):
                    gs = slice(8 * ct, 8 * ct + 8)
                    cs_ = slice(128 * ct, 128 * ct + 128)
                    cb_ = ct % 2
                    bre, bim, cst1, cst2 = bre2[cb_], bim2[cb_], cst12[cb_], cst22[cb_]
                    I("sp", lambda e: e.dma_start(out=bre[:], in_=s5b_d[j5, 0][:, cs_]), W=["s5_bre"], dma=("s5_bre", cb_))
                    I("sp", lambda e: e.dma_start(out=bim[:], in_=s5b_d[j5, 1][:, cs_]), W=["s5_bim"], dma=("s5_bim", cb_))
                    I("sp", lambda e: e.dma_start(out=cst1[:], in_=s5c_d[j5, 0][:, cs_]), W=["s5_cst1"], dma=("s5_cst1", cb_))
                    I("sp", lambda e: e.dma_start(out=cst2[:], in_=s5c_d[j5, 1][:, cs_]), W=["s5_cst2"], dma=("s5_cst2", cb_))

                    def bc(t):
                        return t[0:64, gs].unsqueeze(2).to_broadcast([64, 8, 16])

                    def v3(t):
                        return t[:].rearrange("p (g c) -> p g c", g=8)
                    b_re = bre[:].rearrange("p (g c) -> p g c", g=8)
                    b_im = bim[:].rearrange("p (g c) -> p g c", g=8)
                    dv_(lambda e: e.tensor_tensor(out=v3(z["t1"]), in0=b_re, in1=bc(p["cr"]), op=ALU.mult), ["bre", "cr"], ["t1"])
                    dv_(lambda e: e.tensor_tensor(out=v3(z["t2"]), in0=b_im, in1=bc(p["ci"]), op=ALU.mult), ["bim", "ci"], ["t2"])
                    dv_(lambda e: e.tensor_tensor(out=z["zre"][:], in0=z["t1"][:], in1=z["t2"][:], op=ALU.subtract), ["t1", "t2", "wA", "wA2"], ["zre"])
                    dv_(lambda e: e.tensor_scalar(out=z["nzre"][:], in0=z["zre"][:], scalar1=-1.0, scalar2=None, op0=ALU.mult), ["zre", "wA2"], ["nzre"])
                    dv_(lambda e: e.tensor_tensor(out=v3(z["t1"]), in0=b_im, in1=bc(p["cr"]), op=ALU.mult), ["bim", "cr", "zre"], ["t1"])
                    dv_(lambda e: e.tensor_tensor(out=v3(z["t2"]), in0=b_re, in1=bc(p["ci"]), op=ALU.mult), ["bre", "ci", "zre"], ["t2"])
                    dv_(lambda e: e.tensor_tensor(out=z["zim"][:], in0=z["t1"][:], in1=z["t2"][:], op=ALU.add), ["t1", "t2", "wA", "wA2"], ["zim"])
                    pT = pb[6]
                    id64 = ident[0:64, 0:64]
                    for q, src in enumerate(["zre", "zim", "zim", "nzre"]):
                        I("pe", lambda e, q=q, src=src: e.transpose(out=pT[:, q * 64:(q + 1) * 64], in_=z[src][:], identity=id64),
                          R=["s5_" + src, "ident"], W=["pb6"])
                    for g_ in range(8):
                        I("dve", lambda e, g_=g_: e.tensor_scalar(out=wA[:, g_, :], in0=pT[:, 0:128], scalar1=kk[:, 2 + g_:3 + g_],
                                                                 scalar2=None, op0=ALU.mult),
                          R=["pb6", "s5_kk"], W=["s5_wA"])
                        I("dve", lambda e, g_=g_: e.tensor_scalar(out=wA2[:, g_, :], in0=pT[:, 128:256], scalar1=kk[:, 2 + g_:3 + g_],
                                                                 scalar2=None, op0=ALU.mult),
                          R=["pb6", "s5_kk"], W=["s5_wA2"])
                    I("dve", lambda e: e.memset(W1[:], 0.0), W=["s5_W1"])
                    I("dve", lambda e: e.memset(W2[:], 0.0), W=["s5_W2"])
                    for g_ in range(8):
                        gcs = slice(16 * g_, 16 * g_ + 16)
                        I("dve", lambda e, g_=g_, gcs=gcs: e.tensor_scalar(out=W1[:, g_, 16 * g_:16 * g_ + 16], in0=cst1[:, gcs],
                                                                          scalar1=kk[:, 0:1], scalar2=None, op0=ALU.mult),
                          R=["s5_cst1", "s5_kk"], W=["s5_W1"])
                        I("dve", lambda e, g_=g_, gcs=gcs: e.tensor_scalar(out=W2[:, g_, 16 * g_:16 * g_ + 16], in0=cst2[:, gcs],
                                                                          scalar1=kk[:, 1:2], scalar2=None, op0=ALU.mult),
                          R=["s5_cst2", "s5_kk"], W=["s5_W2"])
                    for q in range(NT):
                        ts = slice(q * 512, (q + 1) * 512)
                        py = pb[5]
                        for g_ in range(8):
                            g = 8 * ct + g_
                            n_it = itc[0]
                            si = n_it % 2
                            itc[0] += 1
                            if n_it + 1 < len(order):
                                s5_tabA(order[n_it + 1][0], order[n_it + 1][1], (n_it + 1) % 2)
                            raw, msin, mcos, bt, st, P1, P2 = raw_[si], msin_[si], mcos_[si], bt_[si], st_[si], P1_[si], P2_[si]
                            kr, ks, kc_, kb, kst, k1_, k2_ = [("s5_" + n, si) for n in ("rawt", "msin", "mcos", "bt", "st", "P1", "P2")]
                            pA, pA2 = pb[2 * si], pb[2 * si + 1]
                            kpA, kpA2 = "pb%d" % (2 * si), "pb%d" % (2 * si + 1)
                            I("pe", lambda e, g_=g_, pA=pA: e.matmul(pA[:], lhsT=wA[:, g_, :], rhs=hT[:, ct, ts], start=True, stop=True),
                              R=["s5_wA", ("hT", ct, q)], W=[kpA])
                            I("pe", lambda e, g_=g_, pA2=pA2: e.matmul(pA2[:], lhsT=wA2[:, g_, :], rhs=hT[:, ct, ts], start=True, stop=True),
                              R=["s5_wA2", ("hT", ct, q)], W=[kpA2])
                            k1 = nxt("tA")
                            k2 = nxt("tB")
                            I("dve", lambda e, k1=k1, mcos=mcos, pA=pA: e.tensor_tensor(out=tA[k1][:], in0=mcos[:], in1=pA[:], op=ALU.mult),
                              R=[kc_, kpA], W=[("tA", k1)])
                            I("dve", lambda e, k2=k2, msin=msin, pA2=pA2: e.tensor_tensor(out=tB[k2][:], in0=msin[:], in1=pA2[:], op=ALU.mult),
                              R=[ks, kpA2], W=[("tB", k2)])
                            I("dve", lambda e, k1=k1, k2=k2, bt=bt: e.tensor_tensor(out=bt[:], in0=tA[k1][:], in1=tB[k2][:], op=ALU.add),
                              R=[("tA", k1), ("tB", k2)], W=[kb])
                            if n_it + 1 < len(order):
                                s5_tabB(order[n_it + 1][0], order[n_it + 1][1], (n_it + 1) % 2)
                                s5_tabC(order[n_it + 1][0], order[n_it + 1][1], (n_it + 1) % 2)
                            if q == 0:
                                I("dve", lambda e, g=g, st=st, bt=bt: e.tensor_tensor_scan(
                                    out=st[:], data0=p["r"][:, g:g + 1].to_broadcast([128, 512]), data1=bt[:], initial=0.0,
                                    op0=ALU.mult, op1=ALU.add),
                                  R=["s5_r", kb], W=[kst])
                            else:
                                I("dve", lambda e, g=g, st=st, bt=bt: e.tensor_tensor_scan(
                                    out=st[:], data0=p["r"][:, g:g + 1].to_broadcast([128, 512]), data1=bt[:],
                                    initial=state[:, g:g + 1], op0=ALU.mult, op1=ALU.add),
                                  R=["s5_r", kb, ("s5_state", g)], W=[kst])
                            I("dve", lambda e, g=g, st=st: e.tensor_copy(out=state[:, g:g + 1], in_=st[:, 511:512]),
                              R=[kst], W=[("s5_state", g)])
                            I("dve", lambda e, mcos=mcos, st=st, P1=P1: e.tensor_tensor(out=P1[:], in0=mcos[:], in1=st[:], op=ALU.mult),
                              R=[kc_, kst], W=[k1_])
                            I("dve", lambda e, msin=msin, st=st, P2=P2: e.tensor_tensor(out=P2[:], in0=msin[:], in1=st[:], op=ALU.mult),
                              R=[ks, kst], W=[k2_])
                            I("pe", lambda e, g_=g_, P1=P1: e.matmul(py[:], lhsT=W1[:, g_, :], rhs=P1[:], start=(g_ == 0), stop=False),
                              R=["s5_W1", k1_], W=["pb5"])
                            I("pe", lambda e, g_=g_, P2=P2: e.matmul(py[:], lhsT=W2[:, g_, :], rhs=P2[:], start=False, stop=(g_ == 7)),
                              R=["s5_W2", k2_], W=["pb5"])
                        I("dve", lambda e: e.scalar_tensor_tensor(out=yv[:], in0=hT[:, ct, ts], scalar=dv[:, ct:ct + 1], in1=py[:],
                                                                 op0=ALU.mult, op1=ALU.add),
                          R=[("hT", ct, q), "s5_dv", "pb5"], W=["s5_yv"])
                        I("act", lambda e: e.activation(out=zz[:], in_=yv[:], func=AF.Square, scale=math.sqrt(GELU_C1)),
                          R=["s5_yv"], W=["s5_zz"])
                        I("dve", lambda e: e.scalar_tensor_tensor(out=zz[:], in0=zz[:], scalar=1.0, in1=yv[:], op0=ALU.add, op1=ALU.mult),
                          R=["s5_zz", "s5_yv"], W=["s5_zz"])
                        I("act", lambda e: e.activation(out=zz[:], in_=zz[:], func=AF.Sigmoid, scale=2.0 * GELU_C0),
                          R=["s5_zz"], W=["s5_zz"])
                        I("dve", lambda e: e.tensor_tensor(out=gT[:, ct, ts], in0=yv[:], in1=zz[:], op=ALU.mult),
                          R=["s5_zz", "s5_yv"], W=[("s5_gT", ct, q)])
                for oc in range(DC):
                    s = load_w256(s5w_d[j5, oc])
                    for tt in range(NT):
                        ts = slice(tt * 512, (tt + 1) * 512)
                        pv, pgt, kv, kg = mm_pair(s, gT, lambda k, tt: ("s5_gT", k, tt), tt)
                        k2 = nxt("tB")
                        I("act", lambda e, k2=k2: e.activation(out=tB[k2][:], in_=pgt[:], func=AF.Sigmoid, bias=dv[:, 16 + oc:17 + oc]),
                          R=[kg, "s5_dv"], W=[("tB", k2)])
                        k1 = nxt("tA")
                        I("dve", lambda e, k1=k1, k2=k2: e.scalar_tensor_tensor(out=tA[k1][:], in0=pv[:], scalar=dv[:, 8 + oc:9 + oc],
                                                                               in1=tB[k2][:], op0=ALU.add, op1=ALU.mult),
                          R=[kv, "s5_dv", ("tB", k2)], W=[("tA", k1)])
                        I("dve", lambda e, k1=k1: e.tensor_tensor(out=xT[:, oc, ts], in0=xT[:, oc, ts], in1=tA[k1][:], op=ALU.add),
                          R=[("tA", k1), ("xT", oc, tt)], W=[("xT", oc, tt)])
                P.release([k for k in list(P.res) if (k[0] if isinstance(k, tuple) else k).startswith("s5_")])

        for sq_i in range(NS):
            with ExitStack() as sc:
                xin = [sb(nc, sc, "xin%d" % i, [128, D], F32) for i in range(2)]
                for tb in range(NB):
                    s = nxt("xin")
                    I("sp", lambda e, s=s, tb=tb: e.dma_start(out=xin[s][:], in_=x_d[sq_i, tb * 128:(tb + 1) * 128, :]),
                      W=[("xin", s)], dma=("xin", s))
                    for half in range(2):
                        pt = pb[6]
                        for cc in range(4):
                            c = half * 4 + cc
                            I("pe", lambda e, s=s, c=c, cc=cc: e.transpose(
                                out=pt[:, cc * 128:(cc + 1) * 128], in_=xin[s][:, c * 128:(c + 1) * 128],
                                identity=ident[:]),
                              R=[("xin", s), "ident"], W=["pb6"])
                        I("act", lambda e, half=half, tb=tb: e.activation(
                            out=xT[:, half * 4:half * 4 + 4, tb * 128:(tb + 1) * 128],
                            in_=pt[:].rearrange("p (c t) -> p c t", c=4), func=AF.Copy),
                          R=["pb6"], W=[("xT", half * 4 + cc, tb // 4) for cc in range(4)])
                P.release_prefix({"xin"})
            i5 = 0
            i0 = 0
            i2 = 0
            for li, kind in enumerate(cfg.layers):
                if cfg.do_ffn:
                    ffn(2 * li, 3 * li)
                if kind == 1:
                    s5_mixer(i5, 3 * li + 1)
                    i5 += 1
                if kind == 0:
                    ssd_mixer(i0, 3 * li + 1)
                    i0 += 1
                if kind == 2:
                    dsa_mixer(i2, 3 * li + 1)
                    i2 += 1
                if cfg.do_ffn:
                    ffn(2 * li + 1, 3 * li + 2)
            with ExitStack() as sc:
                xin = [sb(nc, sc, "xin%d" % i, [128, D], F32) for i in range(2)]
                for tt in range(NT):
                    ts = slice(tt * 512, (tt + 1) * 512)
                    rmsnorm_to(lambda c, ts=ts: xT[:, c, ts], 3 * depth, lambda c, tt=tt: ("xT", c, tt), tt)
                    for q in range(4):
                        tb = tt * 4 + q
                        s = nxt("xin")
                        for half in range(2):
                            pt = pb[6]
                            for cc in range(4):
                                c = half * 4 + cc
                                I("pe", lambda e, c=c, cc=cc, tb=tb: e.transpose(
                                    out=pt[:, cc * 128:(cc + 1) * 128], in_=xT[:, c, tb * 128:(tb + 1) * 128],
                                    identity=ident[:]),
                                  R=[("xT", c, tt), "ident"], W=["pb6"])
                            I("act", lambda e, half=half, s=s: e.activation(
                                out=xin[s][:, half * 512:(half + 1) * 512], in_=pt[:], func=AF.Copy),
                              R=["pb6"], W=[("xin", s)])
                        I("sp", lambda e, s=s, tb=tb: e.dma_start(out=y_d[sq_i, tb * 128:(tb + 1) * 128, :], in_=xin[s][:]),
                          R=[("xin", s)], dma=("yout", s))
                P.release_prefix({"xin"})
        P.finish()
        print("instructions:", P.ninst, "sems:", P.nsem)
    return nc


def w256_layout(w, ncol_blocks, col_a, col_b):
    out = np.empty((ncol_blocks, 128, DC, 256), np.float32)
    wk = w.reshape(DC, 128, -1)
    for b in range(ncol_blocks):
        out[b, :, :, 0:128] = wk[:, :, col_a + b * 128: col_a + (b + 1) * 128].transpose(1, 0, 2)
        out[b, :, :, 128:256] = wk[:, :, col_b + b * 128: col_b + (b + 1) * 128].transpose(1, 0, 2)
    return out.reshape(ncol_blocks, 128, DC * 256)


def prep_ssd(inp, cfg, m):
    f32 = np.float32
    n0 = cfg.n_ssd
    win = np.empty((n0, 24, 128, DC, 256), f32)
    wdt = np.empty((n0, 128, DC * 32), f32)
    wout = np.empty((n0, DC, 128, 16 * 128), f32)
    cwb = np.empty((n0, 128, 32, 5), f32)
    hp = np.empty((n0, 128, 112), f32)
    for j in range(n0):
        W = inp["ssd_in_proj"][j]
        wk = W.reshape(DC, 128, -1)
        for b in range(24):
            win[j, b] = wk[:, :, 256 * b:256 * (b + 1)].transpose(1, 0, 2)
        wdt[j] = wk[:, :, 6144:6176].transpose(1, 0, 2).reshape(128, DC * 32)
        Wo = inp["ssd_out_proj"][j].reshape(16, 128, DC, 128)
        wout[j] = Wo.transpose(2, 1, 0, 3).reshape(DC, 128, 16 * 128)
        cw = inp["ssd_conv_w"][j].reshape(4, 32, 128)
        cwb[j, :, :, 0:4] = cw.transpose(2, 1, 0)
        cwb[j, :, :, 4] = inp["ssd_conv_b"][j].reshape(32, 128).T
        hp[j, :, 0:32] = np.broadcast_to(inp["ssd_dt_bias"][j][None, :], (128, 32))
        hp[j, :, 32:64] = np.broadcast_to(inp["ssd_a_log"][j][None, :], (128, 32))
        hp[j, :, 64:96] = np.broadcast_to(inp["ssd_d"][j][None, :], (128, 32))
        hp[j, :, 96:112] = inp["ssd_gate_norm"][j].reshape(16, 128).T
    kc = np.zeros((128, 768), f32)
    tri = (np.arange(128)[:, None] <= np.arange(128)[None, :])
    kc[:, 0:128] = tri.astype(f32)
    cb = np.where(np.arange(128)[:, None] > np.arange(128)[None, :], -30000.0, 0.0).astype(f32)
    kc[:, 128:640] = np.tile(cb, (1, 4))
    kc[:, 640:768] = 1.0
    m.update({"ssd_win": win.reshape(n0, 24, 128, DC * 256), "ssd_wdt": wdt, "ssd_wout": wout,
              "ssd_cw": cwb.reshape(n0, 128, 160), "ssd_hp": hp, "ssd_k": kc})


def w256_pairs(wa, wb):
    out = np.empty((128, DC, 256), np.float32)
    out[:, :, 0:128] = wa.reshape(DC, 128, 128).transpose(1, 0, 2)
    out[:, :, 128:256] = wb.reshape(DC, 128, 128).transpose(1, 0, 2)
    return out.reshape(128, DC * 256)


def prep_dsa(inp, cfg, m):
    f32 = np.float32
    n2 = cfg.n_dsa
    L = cfg.L
    win = np.empty((n2, 14, 128, DC * 256), f32)
    wvw = np.empty((n2, 128, DC * 72), f32)
    wo = np.empty((n2, 4, 128, DC * 256), f32)
    perm = np.concatenate([np.arange(32, 64), np.arange(0, 32)])
    perm128 = np.concatenate([perm, 64 + perm])
    for j in range(n2):
        W = inp["dsa_in_proj"][j]
        Wq, Wk, Wv = W[:, 0:1024], W[:, 1024:1088], W[:, 1088:1152]
        Wqi, Wki, Wwi = W[:, 1152:1664], W[:, 1664:1728], W[:, 1728:1736]
        blocks = [Wq[:, c * 128:(c + 1) * 128] for c in range(8)]
        blocks.append(np.concatenate([Wk, Wk], 1))
        blocks += [Wqi[:, c * 128:(c + 1) * 128] for c in range(4)]
        blocks.append(np.concatenate([Wki, Wki], 1))
        for b, A in enumerate(blocks):
            win[j, b] = w256_pairs(A, A[:, perm128])
        vw = np.concatenate([Wv, Wwi], 1)
        wvw[j] = vw.reshape(DC, 128, 72).transpose(1, 0, 2).reshape(128, DC * 72)
        Wo = inp["dsa_out_proj"][j]
        for b in range(4):
            wo[j, b] = w256_pairs(Wo[:, 256 * b:256 * b + 128], Wo[:, 256 * b + 128:256 * b + 256])
    inv = (10000.0 ** (-np.arange(32, dtype=np.float64) / 32.0))
    ang = np.arange(L, dtype=np.float64)[None, :] * inv[np.arange(128) % 32][:, None]
    sgn = np.where((np.arange(128) % 64) < 32, -1.0, 1.0)[:, None]
    rope = np.stack([np.cos(ang), np.sin(ang) * sgn], 0).astype(f32)
    kd = np.zeros((128, 640), f32)
    kd[:, 0:128] = np.where(np.arange(128)[None, :] > np.arange(128)[:, None], -3.0e30, 0.0)
    kd[:, 128:640] = np.tile(np.eye(128, dtype=f32), (1, 4))
    m.update({"dsa_win": win, "dsa_wvw": wvw, "dsa_wo": wo, "dsa_rope": rope, "dsa_k": kd})


def prep_common(inp, cfg):
    depth = cfg.depth
    f32 = np.float32
    g = []
    for i in range(depth):
        g += [inp["ffn1_norm"][i], inp["mix_norm"][i], inp["ffn2_norm"][i]]
    g.append(inp["final_norm"])
    g = np.stack(g, 0).astype(f32)
    gains = np.ascontiguousarray(g.reshape(-1, DC, 128).transpose(2, 0, 1))
    wi = np.empty((2 * depth, FC, 128, DC * 256), f32)
    wo = np.empty((2 * depth, 2, DC, 128, 11 * 128), f32)
    for i in range(depth):
        for which, (kin, kout) in enumerate((("ffn1_w_in", "ffn1_w_out"), ("ffn2_w_in", "ffn2_w_out"))):
            w_in = inp[kin][i]
            w_out = inp[kout][i]
            wi[2 * i + which] = w256_layout(w_in, FC, 0, FFN)
            b = w_out.reshape(2, 11, 128, DC, 128)
            b = b.transpose(0, 3, 2, 1, 4)
            wo[2 * i + which] = b.reshape(2, DC, 128, 11 * 128)
    m = {"gains": gains, "ident": np.eye(128, dtype=f32), "ffn_wi": wi, "ffn_wo": wo}
    if cfg.n_s5:
        n5 = cfg.n_s5
        par = np.empty((n5, 3, 128, 64), f32)
        sb_ = np.empty((n5, 2, 64, 1024), f32)
        sc_ = np.empty((n5, 2, 128, 1024), f32)
        sd_ = np.empty((n5, 128, 24), f32)
        sw_ = np.empty((n5, DC, 128, DC * 256), f32)
        for j in range(n5):
            lre = inp["s5_lam_re"][j].T
            lim = inp["s5_lam_im"][j].T
            lst = np.broadcast_to(inp["s5_log_step"][j][None, :], (64, 64))
            for a, t in enumerate((lre, lim, lst)):
                par[j, a, 0:64] = t
                par[j, a, 64:128] = t
            sb_[j, 0] = inp["s5_b_re"][j].transpose(1, 0, 2).reshape(64, 1024)
            sb_[j, 1] = inp["s5_b_im"][j].transpose(1, 0, 2).reshape(64, 1024)
            cre = inp["s5_c_re"][j].transpose(2, 0, 1).reshape(64, 1024)
            cim = inp["s5_c_im"][j].transpose(2, 0, 1).reshape(64, 1024)
            sc_[j, 0, 0:64] = cre
            sc_[j, 0, 64:128] = cim
            sc_[j, 1, 0:64] = cim
            sc_[j, 1, 64:128] = cre
            sd_[j, :, 0:8] = inp["s5_d"][j].reshape(DC, 128).T
            sd_[j, :, 8:24] = inp["s5_glu_b"][j].reshape(16, 128).T
            sw_[j] = w256_layout(inp["s5_glu_w"][j], DC, 0, D)
        kk = np.zeros((128, 16), f32)
        kk[0:64, 0] = 1.0
        kk[64:128, 0] = -1.0
        kk[:, 1] = -1.0
        for g_ in range(8):
            kk[16 * g_:16 * g_ + 16, 2 + g_] = 1.0
        m.update({"s5_par": par, "s5_b": sb_, "s5_c": sc_, "s5_k": kk, "s5_dv": sd_, "s5_glu": sw_,
                  "iota16": np.broadcast_to(np.arange(cfg.L, dtype=np.int16)[None, :], (128, cfg.L)).copy()})
    if cfg.n_ssd:
        prep_ssd(inp, cfg, m)
    if cfg.n_dsa:
        prep_dsa(inp, cfg, m)
    return m


def kernel(**inputs):
    cfg = Cfg()
    inp = {k: np.asarray(v) for k, v in inputs.items()}
    common = prep_common(inp, cfg)
    x = inp["x"].astype(np.float32)
    B = x.shape[0]
    per = B // N_CORES
    nc = build(cfg)
    in_maps = []
    for c in range(N_CORES):
        m = dict(common)
        m["x"] = np.ascontiguousarray(x[c * per:(c + 1) * per])
        in_maps.append(m)
    res = run_bass_kernel_spmd(nc, in_maps, core_ids=list(range(N_CORES)))
    out = np.concatenate([r["y"] for r in res.results], axis=0)
    return out.astype(np.float32)
```

```python
import math
import os
from contextlib import ExitStack

import numpy as np
import concourse.bass as bass
import concourse.mybir as mybir
from concourse.bass_utils import run_bass_kernel_spmd

F32 = mybir.dt.float32
BF16 = mybir.dt.bfloat16
AF = mybir.ActivationFunctionType
ALU = mybir.AluOpType
AX = mybir.AxisListType

D = 1024
DC = 8
FFN = 2816
FC = 22
EPS = 1e-6
N_CORES = 8

ENGS = ("pe", "act", "dve", "pool", "sp")
EPOCH = 12000


class Prog:
    def __init__(self, nc, es):
        self.nc = nc
        self.es = es
        self.eng = {"pe": nc.tensor, "act": nc.scalar, "dve": nc.vector,
                    "pool": nc.gpsimd, "sp": nc.sync}
        self.nsem = 0
        self.esem = {}
        self.ecnt = {}
        self.eepoch = {}
        for e in ENGS:
            self.eepoch[e] = 0
            self.ecnt[e] = 0
            self.esem[e] = self._newsem()
        self.seen = {e: {} for e in ENGS}
        self.res = {}
        self.dsem = {}
        self.free_ev = {}
        self.ninst = 0

    def _newsem(self):
        self.nsem += 1
        return self.es.enter_context(self.nc.semaphore("s%d" % self.nsem))

    def _need(self, eng, deps):
        out = {}
        for ev in deps:
            if ev is None:
                continue
            sem, val, kind = ev
            if kind == "pe" and eng == "pe":
                continue
            nm = sem.name
            if self.seen[eng].get(nm, 0) >= val:
                continue
            if nm not in out or out[nm][1] < val:
                out[nm] = ev
        return list(out.values())

    def I(self, eng, fn, R=(), W=(), dma=None):
        pr = [r for r in R if isinstance(r, str) and r.startswith("pb")]
        if pr:
            R = [r for r in R if r not in pr]
            W = list(W) + [r for r in pr if r not in W]
        deps = []
        for r in R:
            ent = self.res.get(r)
            if ent is not None:
                deps.append(ent[0])
        for w in W:
            ent = self.res.get(w)
            if ent is not None:
                deps.append(ent[0])
                deps.extend(ent[1].values())
            else:
                deps.extend(self.free_ev.values())
        need = self._need(eng, deps)
        e = self.eng[eng]
        for sem, val, kind in need:
            e.wait_ge(sem, val)
            self.seen[eng][sem.name] = val
        ins = fn(e)
        if dma is not None:
            ent = self.dsem.get(dma)
            if ent is None:
                ent = [self._newsem(), 0]
                self.dsem[dma] = ent
            ent[1] += 16
            ins.then_inc(ent[0], 16)
            ev = (ent[0], ent[1], "dma")
        else:
            if self.ecnt[eng] >= EPOCH:
                self.esem[eng] = self._newsem()
                self.ecnt[eng] = 0
                self.eepoch[eng] += 1
            self.ecnt[eng] += 1
            ins.then_inc(self.esem[eng], 1)
            ev = (self.esem[eng], self.ecnt[eng], eng)
        for r in R:
            ent = self.res.get(r)
            if ent is None:
                ent = [None, {}]
                self.res[r] = ent
            ent[1][ev[0].name] = ev
        for w in W:
            self.res[w] = [ev, {}]
        self.ninst += 1
        return ev

    def release(self, keys):
        for k in keys:
            ent = self.res.pop(k, None)
            if ent is None:
                continue
            for ev in [ent[0]] + list(ent[1].values()):
                if ev is None:
                    continue
                nm = ev[0].name
                if nm not in self.free_ev or self.free_ev[nm][1] < ev[1]:
                    self.free_ev[nm] = ev

    def release_prefix(self, prefixes):
        ks = [k for k in self.res if (k[0] if isinstance(k, tuple) else k) in prefixes]
        self.release(ks)

    def finish(self):
        allev = list(self.free_ev.values())
        for ent in self.res.values():
            allev.append(ent[0])
            allev.extend(ent[1].values())
        for eng in ENGS:
            for sem, val, kind in self._need(eng, allev):
                if kind == eng and eng != "sp":
                    pass
                self.eng[eng].wait_ge(sem, val)
                self.seen[eng][sem.name] = val


_UID = [0]


def sb(nc, es, name, shape, dt):
    _UID[0] += 1
    return es.enter_context(nc.sbuf_tensor("%s_u%d" % (name, _UID[0]), shape, dt))


def ps(nc, es, name, shape, dt):
    return es.enter_context(nc.psum_tensor(name, shape, dt))


class Cfg:
    def __init__(self, nseq=4, L=2048, layers=(0, 1, 2, 0), do_ffn=True):
        self.nseq = nseq
        self.L = L
        self.layers = tuple(layers)
        self.do_ffn = do_ffn
        self.depth = len(self.layers)
        self.n_ssd = sum(1 for k in self.layers if k == 0)
        self.n_s5 = sum(1 for k in self.layers if k == 1)
        self.n_dsa = sum(1 for k in self.layers if k == 2)


TWO_PI = 2.0 * math.pi
GELU_C0 = math.sqrt(2.0 / math.pi)
GELU_C1 = 0.044715
MAGIC = 12582912.0


def build(cfg):
    nc = bass.Bass("TRN2", target_bir_lowering=False)
    L = cfg.L
    NS = cfg.nseq
    NT = L // 512
    NB = L // 128
    depth = cfg.depth

    def din(name, shape, dt=F32):
        return nc.dram_tensor(name, list(shape), dt, kind="ExternalInput").ap()

    x_d = din("x", [NS, L, D])
    y_d = nc.dram_tensor("y", [NS, L, D], F32, kind="ExternalOutput").ap()
    gains_d = din("gains", [128, 3 * depth + 1, DC])
    ident_d = din("ident", [128, 128])
    wi_d = din("ffn_wi", [2 * depth, FC, 128, DC * 256])
    wo_d = din("ffn_wo", [2 * depth, 2, DC, 128, 11 * 128])
    if cfg.n_s5:
        n5 = cfg.n_s5
        s5p_d = din("s5_par", [n5, 3, 128, 64])
        s5b_d = din("s5_b", [n5, 2, 64, 1024])
        s5c_d = din("s5_c", [n5, 2, 128, 1024])
        s5k_d = din("s5_k", [128, 16])
        s5d_d = din("s5_dv", [n5, 128, 24])
        s5w_d = din("s5_glu", [n5, DC, 128, DC * 256])
        iota_d = din("iota16", [128, L], mybir.dt.int16)

    if cfg.n_ssd:
        n0 = cfg.n_ssd
        ssd_win_d = din("ssd_win", [n0, 24, 128, DC * 256])
        ssd_wdt_d = din("ssd_wdt", [n0, 128, DC * 32])
        ssd_wout_d = din("ssd_wout", [n0, DC, 128, 16 * 128])
        ssd_cw_d = din("ssd_cw", [n0, 128, 32 * 5])
        ssd_hp_d = din("ssd_hp", [n0, 128, 3 * 32 + 16])
        ssd_k_d = din("ssd_k", [128, 128 + 512 + 128])

    if cfg.n_dsa:
        n2 = cfg.n_dsa
        dsa_win_d = din("dsa_win", [n2, 14, 128, DC * 256])
        dsa_wvw_d = din("dsa_wvw", [n2, 128, DC * 72])
        dsa_wo_d = din("dsa_wo", [n2, 4, 128, DC * 256])
        dsa_rope_d = din("dsa_rope", [2, 128, L])
        dsa_k_d = din("dsa_k", [128, 640])

    with ExitStack() as es:
        P = Prog(nc, es)
        I = P.I
        xT = sb(nc, es, "xT", [128, DC, L], F32)
        hT = sb(nc, es, "hT", [128, DC, L], BF16)
        gains = sb(nc, es, "gains_sb", [128, 3 * depth + 1, DC], F32)
        ident = sb(nc, es, "ident_sb", [128, 128], F32)
        ones_bf = sb(nc, es, "ones_bf", [128, 128], BF16)
        epsb = sb(nc, es, "epsb", [128, 1], F32)
        negpi = sb(nc, es, "negpi", [128, 1], F32)
        sq = [sb(nc, es, "sq%d" % i, [128, 512], BF16) for i in range(2)]
        tA = [sb(nc, es, "tA%d" % i, [128, 512], F32) for i in range(2)]
        tB = [sb(nc, es, "tB%d" % i, [128, 512], F32) for i in range(2)]
        NWI = 2
        wi = [sb(nc, es, "wi%d" % i, [128, DC, 256], BF16) for i in range(NWI)]
        pb = [ps(nc, es, "pb%d" % i, [128, 512], F32) for i in range(7)]
        pbh = ps(nc, es, "pbh", [128, 1024], BF16)
        ident_bf = sb(nc, es, "ident_bf", [128, 128], BF16)

        cnt = {"wi": 0, "wo": 0, "sq": 0, "tA": 0, "tB": 0, "xin": 0, "pg": 0, "po": 0}

        I("sp", lambda e: e.dma_start(out=gains[:], in_=gains_d), W=["gains"], dma="gains")
        I("sp", lambda e: e.dma_start(out=ident[:], in_=ident_d), W=["ident"], dma="ident")
        I("dve", lambda e: e.memset(ones_bf[:], 1.0), W=["ones"])
        I("dve", lambda e: e.tensor_copy(out=ident_bf[:], in_=ident[:]), R=["ident"], W=["ident_bf"])
        oneb = sb(nc, es, "oneb", [128, 1], F32)
        I("dve", lambda e: e.memset(oneb[:], 1.0), W=["oneb"])
        I("dve", lambda e: e.memset(epsb[:], EPS), W=["epsb"])
        I("dve", lambda e: e.memset(negpi[:], -math.pi), W=["negpi"])
        halfpi = sb(nc, es, "halfpi", [128, 1], F32)
        I("dve", lambda e: e.memset(halfpi[:], math.pi / 2), W=["negpi"])

        def nxt(name, n=2):
            k = cnt[name] % n
            cnt[name] += 1
            return k

        def rmsnorm_to(dst_fn, gidx, dst_key_fn, tt):
            ts = slice(tt * 512, (tt + 1) * 512)
            pn = pb[6]
            for c in range(DC):
                k = nxt("sq")
                I("act", lambda e, c=c, k=k: e.activation(out=sq[k][:], in_=xT[:, c, ts], func=AF.Square),
                  R=[("xT", c, tt)], W=[("sq", k)])
                I("pe", lambda e, c=c, k=k: e.matmul(pn[:], lhsT=ones_bf[:], rhs=sq[k][:],
                                                      start=(c == 0), stop=(c == DC - 1)),
                  R=[("sq", k), "ones"], W=["pb6"])
            k = nxt("tA")
            I("act", lambda e, k=k: e.activation(out=tA[k][:], in_=pn[:], func=AF.Sqrt,
                                                 scale=1.0 / D, bias=epsb[:]),
              R=["pb6", "epsb"], W=[("tA", k)])
            I("dve", lambda e, k=k: e.reciprocal(out=tA[k][:], in_=tA[k][:]),
              R=[("tA", k)], W=[("tA", k)])
            for c in range(DC):
                I("dve", lambda e, c=c, k=k: e.scalar_tensor_tensor(
                    out=dst_fn(c), in0=xT[:, c, ts], scalar=gains[:, gidx, c:c + 1], in1=tA[k][:],
                    op0=ALU.mult, op1=ALU.mult),
                  R=[("xT", c, tt), ("tA", k), "gains"], W=[dst_key_fn(c)])

        def norm_to_hT(gidx):
            for tt in range(NT):
                ts = slice(tt * 512, (tt + 1) * 512)
                rmsnorm_to(lambda c, ts=ts: hT[:, c, ts], gidx, lambda c, tt=tt: ("hT", c, tt), tt)

        def load_w256(src_ap, key="wi"):
            s = nxt("wi", NWI)
            I("pool", lambda e, s=s: e.dma_start(out=wi[s][:], in_=src_ap.rearrange("p (k f) -> p k f", k=DC)),
              W=[("wi", s)], dma=("wi", s))
            return s

        def mm_pair(s, src, src_key_fn, tt):
            ts = slice(tt * 512, (tt + 1) * 512)
            b = nxt("pg")
            pg, pu = pb[2 * b], pb[2 * b + 1]
            for k in range(DC):
                I("pe", lambda e, k=k: e.matmul(pg[:], lhsT=wi[s][:, k, 0:128], rhs=src[:, k, ts],
                                                start=(k == 0), stop=(k == DC - 1)),
                  R=[("wi", s), src_key_fn(k, tt)], W=["pb%d" % (2 * b)])
            for k in range(DC):
                I("pe", lambda e, k=k: e.matmul(pu[:], lhsT=wi[s][:, k, 128:256], rhs=src[:, k, ts],
                                                start=(k == 0), stop=(k == DC - 1)),
                  R=[("wi", s), src_key_fn(k, tt)], W=["pb%d" % (2 * b + 1)])
            return pg, pu, "pb%d" % (2 * b), "pb%d" % (2 * b + 1)

        def ffn(fidx, gidx):
            def norm_tile(tt):
                ts = slice(tt * 512, (tt + 1) * 512)
                rmsnorm_to(lambda c, ts=ts: hT[:, c, ts], gidx, lambda c, tt=tt: ("hT", c, tt), tt)
            with ExitStack() as sc:
                aT = sb(nc, sc, "aT", [128, 11, L], BF16)
                NWO = 3
                wo = [sb(nc, sc, "wo%d" % i, [128, 11, 128], BF16) for i in range(NWO)]
                NWF = 4
                wf = [sb(nc, sc, "wf%d" % i, [128, DC, 256], BF16) for i in range(NWF)]
                fcn = [0]

                def load_wf(src_ap):
                    s = fcn[0] % NWF
                    fcn[0] += 1
                    I("pool", lambda e, s=s: e.dma_start(out=wf[s][:], in_=src_ap.rearrange("p (k f) -> p k f", k=DC)),
                      W=[("wf", s)], dma=("wf", s))
                    return s
                pre = [load_wf(wi_d[fidx, fc]) for fc in range(min(NWF - 1, FC))]
                NP = max(NT // 2, 1)
                TW = NT // NP
                for t0 in range(TW):
                    norm_tile(t0)
                bp = [0]

                def next_banks():
                    k0 = (bp[0] % 3) * 2
                    bp[0] += 1
                    return [k0 + i for i in range(TW)]
                for hf in range(2):
                    for j in range(11):
                        fc = hf * 11 + j
                        s = pre[fc] if fc < len(pre) else load_wf(wi_d[fidx, fc])
                        for tp in range(NP):
                            tts = [tp * TW + i for i in range(TW)]
                            gb = next_banks()
                            ub = next_banks()
                            for half, banks in ((0, gb), (1, ub)):
                                for k in range(DC):
                                    for i, tt in enumerate(tts):
                                        ts = slice(tt * 512, (tt + 1) * 512)
                                        I("pe", lambda e, k=k, bk=banks[i], s=s, ts=ts, half=half: e.matmul(
                                            pb[bk][:], lhsT=wf[s][:, k, half * 128:(half + 1) * 128], rhs=hT[:, k, ts],
                                            start=(k == 0), stop=(k == DC - 1)),
                                          R=[("wf", s), ("hT", k, tt)], W=["pb%d" % banks[i]])
                            if fc == 0 and tp + 1 < NP:
                                for i in range(TW):
                                    norm_tile((tp + 1) * TW + i)
                            for i, tt in enumerate(tts):
                                ts = slice(tt * 512, (tt + 1) * 512)
                                g = nxt("tB")
                                I("act", lambda e, g=g, bk=gb[i]: e.activation(out=tB[g][:], in_=pb[bk][:], func=AF.Silu),
                                  R=["pb%d" % gb[i]], W=[("tB", g)])
                                I("dve", lambda e, g=g, bk=ub[i], j=j, ts=ts: e.tensor_tensor(
                                    out=aT[:, j, ts], in0=tB[g][:], in1=pb[bk][:], op=ALU.mult),
                                  R=[("tB", g), "pb%d" % ub[i]], W=[("aT", j, tt)])
                    for c in range(DC):
                        s = nxt("wo", NWO)
                        I("pool", lambda e, s=s, c=c, hf=hf: e.dma_start(
                            out=wo[s][:], in_=wo_d[fidx, hf, c].rearrange("p (j d) -> p j d", j=11)),
                          W=[("wo", s)], dma=("wo", s))
                        for tp in range(NP):
                            tts = [tp * TW + i for i in range(TW)]
                            ob = next_banks()
                            for j in range(11):
                                for i, tt in enumerate(tts):
                                    ts = slice(tt * 512, (tt + 1) * 512)
                                    I("pe", lambda e, s=s, j=j, ts=ts, bk=ob[i]: e.matmul(
                                        pb[bk][:], lhsT=wo[s][:, j, :], rhs=aT[:, j, ts],
                                        start=(j == 0), stop=(j == 10)),
                                      R=[("wo", s), ("aT", j, tt)], W=["pb%d" % ob[i]])
                            for i, tt in enumerate(tts):
                                ts = slice(tt * 512, (tt + 1) * 512)
                                I("dve", lambda e, bk=ob[i], c=c, ts=ts: e.scalar_tensor_tensor(
                                    out=xT[:, c, ts], in0=pb[bk][:], scalar=0.5, in1=xT[:, c, ts],
                                    op0=ALU.mult, op1=ALU.add),
                                  R=["pb%d" % ob[i], ("xT", c, tt)], W=[("xT", c, tt)])
                P.release_prefix({"aT", "wo", "wf"})

        def ssd_mixer(j0, gidx):
            norm_to_hT(gidx)
            TT2 = 256
            with ExitStack() as sc:
                def T(name, shape, dt=F32):
                    return sb(nc, sc, "ssd_" + name, shape, dt)
                xc = T("xc", [128, 32, TT2], BF16)
                zT = T("zT", [128, 16, TT2], BF16)
                ynT = T("ynT", [128, 16, TT2], BF16)
                halo = T("halo", [128, 32, 3])
                xp = [T("xp%d" % i, [128, 3 + TT2]) for i in range(2)]
                acc = [T("acc%d" % i, [128, TT2]) for i in range(2)]
                wdt = T("wdt", [128, DC, 32], BF16)
                cw = T("cw", [128, 32, 5])
                hp = T("hp", [128, 3 * 32 + 16])
                kc = T("kc", [128, 128 + 512 + 128])
                Aneg = T("Aneg", [128, 32])
                dtr = T("dtr", [128, 32])
                dtv = T("dtv", [128, 32])
                av = T("av", [128, 32])
                ncs = T("ncs", [128, 32])
                ecs = T("ecs", [128, 32])
                cl = T("cl", [128, 32])
                wl = T("wl", [128, 32])
                dec = T("dec", [128, 32])
                xs_tok = T("xs_tok", [128, 32, 64], BF16)
                dtx = T("dtx", [128, 32, 64], BF16)
                Btok = T("Btok", [128, 8, 128], BF16)
                aU = [[T("aU%d_%d" % (i, j), [128, 4, 128], BF16) for j in range(3)] for i in range(1)]
                asp = [T("asp%d" % j, [128, 32], BF16) for j in range(3)]
                ares = [T("ares%d" % j, [128, 32]) for j in range(2)]
                kcb = T("kcb", [128, 640], BF16)
                Lt = [T("Lt%d" % i, [128, 4, 128], BF16) for i in range(1)]
                Mt = [T("Mt%d" % i, [128, 4, 128], BF16) for i in range(2)]
                CBs = [T("CBs%d" % i, [128, 128], BF16) for i in range(1)]
                tm1 = [tA[i][:, 0:256] for i in range(2)]
                tm2 = [tB[i][:, 0:256] for i in range(2)]
                Y = T("Y", [128, 2048])
                sz = T("sz", [128, 2048])
                Ynb = T("Ynb", [128, 2048], BF16)
                ss = T("ss", [128, 8])
                S = T("S", [128, 8, 256])
                Sbf = T("Sbf", [128, 8, 256], BF16)
                U = kc[:, 0:128]
                causal4 = kc[:, 128:640]
                ones_f = kc[:, 640:768]
                dtb, alog, Dbc, gn = hp[:, 0:32], hp[:, 32:64], hp[:, 64:96], hp[:, 96:112]

                def K_(n):
                    return "ssd_" + n

                def ld(dst, src, key, eng="sp"):
                    I(eng, lambda e: e.dma_start(out=dst, in_=src), W=[key], dma=key)
                ld(wdt[:], ssd_wdt_d[j0].rearrange("p (k f) -> p k f", k=DC), K_("wdt"), "pool")
                ld(cw[:], ssd_cw_d[j0].rearrange("p (c f) -> p c f", c=32), K_("cw"))
                ld(hp[:], ssd_hp_d[j0], K_("hp"))
                ld(kc[:], ssd_k_d, K_("kc"))
                I("dve", lambda e: e.tensor_copy(out=kcb[:], in_=kc[:, 0:640]), R=[K_("kc")], W=[K_("kcb")])
                Ub = kcb[:, 0:128]
                c4b = kcb[:, 128:640]
                I("act", lambda e: e.activation(out=Aneg[:], in_=alog, func=AF.Exp), R=[K_("hp")], W=[K_("Aneg")])
                I("dve", lambda e: e.tensor_scalar(out=Aneg[:], in0=Aneg[:], scalar1=-1.0, scalar2=None, op0=ALU.mult),
                  R=[K_("Aneg")], W=[K_("Aneg")])
                I("dve", lambda e: e.memset(S[:], 0.0), W=[K_("S")])
                I("dve", lambda e: e.memset(Sbf[:], 0.0), W=[K_("Sbf")])
                I("dve", lambda e: e.memset(halo[:], 0.0), W=[K_("halo")])
                cn = {"xp": 0, "g": 0}
                pend = []
                print("ssd sbuf remaining", nc.sbuf_bytes_remaining)

                for t2 in range(L // TT2):
                    tsl = slice(t2 * TT2, (t2 + 1) * TT2)
                    tt = (t2 * TT2) // 512
                    for blk in range(24):
                        s = load_w256(ssd_win_d[j0, blk])
                        b = nxt("pg")
                        pg, pu = pb[2 * b], pb[2 * b + 1]
                        for hh, (pp, kp) in enumerate(((pg, "pb%d" % (2 * b)), (pu, "pb%d" % (2 * b + 1)))):
                            for k in range(DC):
                                I("pe", lambda e, k=k, pp=pp, hh=hh: e.matmul(pp[:, 0:TT2], lhsT=wi[s][:, k, hh * 128:(hh + 1) * 128],
                                                                             rhs=hT[:, k, tsl], start=(k == 0), stop=(k == DC - 1)),
                                  R=[("wi", s), ("hT", k, tt)], W=[kp])
                            ch = 2 * blk + hh
                            if ch < 16:
                                I("act", lambda e, pp=pp, ch=ch: e.activation(out=zT[:, ch, :], in_=pp[:, 0:TT2], func=AF.Copy),
                                  R=[kp], W=[K_("zT")])
                            else:
                                xch = ch - 16
                                r = cn["xp"] % 2
                                cn["xp"] += 1
                                I("act", lambda e, pp=pp, r=r: e.activation(out=xp[r][:, 3:3 + TT2], in_=pp[:, 0:TT2], func=AF.Copy),
                                  R=[kp], W=[(K_("xp"), r)])
                                I("act", lambda e, r=r, xch=xch: e.activation(out=xp[r][:, 0:3], in_=halo[:, xch, :], func=AF.Copy),
                                  R=[(K_("halo"), xch), K_("halo")], W=[(K_("xp"), r)])
                                I("act", lambda e, r=r, xch=xch: e.activation(out=halo[:, xch, :], in_=xp[r][:, TT2:TT2 + 3], func=AF.Copy),
                                  R=[(K_("xp"), r)], W=[(K_("halo"), xch)])
                                while len(pend) > 0:
                                    pend.pop(0)()
                                I("dve", lambda e, r=r, xch=xch: e.tensor_scalar(out=acc[r][:], in0=xp[r][:, 0:TT2], scalar1=cw[:, xch, 0:1],
                                                                                scalar2=None, op0=ALU.mult),
                                  R=[(K_("xp"), r), K_("cw")], W=[(K_("acc"), r)])
                                for tap in range(1, 4):
                                    I("dve", lambda e, r=r, xch=xch, tap=tap: e.scalar_tensor_tensor(
                                        out=acc[r][:], in0=xp[r][:, tap:tap + TT2], scalar=cw[:, xch, tap:tap + 1], in1=acc[r][:],
                                        op0=ALU.mult, op1=ALU.add),
                                      R=[(K_("xp"), r), K_("cw"), (K_("acc"), r)], W=[(K_("acc"), r)])
                                def silu_later(r=r, xch=xch):
                                    I("act", lambda e: e.activation(out=xc[:, xch, :], in_=acc[r][:], func=AF.Silu,
                                                                    bias=cw[:, xch, 4:5]),
                                      R=[(K_("acc"), r), K_("cw")], W=[(K_("xc"), xch)])
                                pend.append(silu_later)
                    while len(pend) > 0:
                        pend.pop(0)()
                    STG = int(os.environ.get('DBG_SSD', 99))
                    for ck in range(TT2 // 128 if STG >= 2 else 0):
                        lsl = slice(ck * 128, (ck + 1) * 128)
                        asl = slice(t2 * TT2 + ck * 128, t2 * TT2 + (ck + 1) * 128)
                        pdt = pb[3][:, 0:32]
                        pcs = pb[3][:, 64:96]
                        for k in range(DC):
                            I("pe", lambda e, k=k: e.matmul(pdt, lhsT=hT[:, k, asl], rhs=wdt[:, k, :], start=(k == 0), stop=(k == DC - 1)),
                              R=[("hT", k, tt), K_("wdt")], W=["pb3"])
                        SUB = int(os.environ.get('DBG_SUB', 99))
                        if SUB < 2:
                            continue
                        I("dve", lambda e: e.tensor_tensor(out=dtr[:], in0=pdt, in1=dtb, op=ALU.add), R=["pb3", K_("hp")], W=[K_("dtr")])
                        if SUB < 3:
                            continue
                        I("act", lambda e: e.activation(out=dtr[:], in_=dtr[:], func=AF.Exp), R=[K_("dtr")], W=[K_("dtr")])
                        I("act", lambda e: e.activation(out=dtv[:], in_=dtr[:], func=AF.Ln, bias=oneb[:]), R=[K_("dtr"), "oneb"], W=[K_("dtv")])
                        if SUB < 4:
                            continue
                        I("dve", lambda e: e.tensor_tensor(out=av[:], in0=dtv[:], in1=Aneg[:], op=ALU.mult), R=[K_("dtv"), K_("Aneg")], W=[K_("av")])
                        if SUB < 5:
                            continue
                        I("dve", lambda e: e.tensor_copy(out=asp[0][:], in_=av[:]), R=[K_("av")], W=[(K_("asp"), 0)])
                        I("dve", lambda e: e.tensor_tensor(out=ares[0][:], in0=av[:], in1=asp[0][:], op=ALU.subtract),
                          R=[K_("av"), (K_("asp"), 0)], W=[(K_("ares"), 0)])
                        I("dve", lambda e: e.tensor_copy(out=asp[1][:], in_=ares[0][:]), R=[(K_("ares"), 0)], W=[(K_("asp"), 1)])
                        I("dve", lambda e: e.tensor_tensor(out=ares[1][:], in0=ares[0][:], in1=asp[1][:], op=ALU.subtract),
                          R=[(K_("ares"), 0), (K_("asp"), 1)], W=[(K_("ares"), 1)])
                        I("dve", lambda e: e.tensor_copy(out=asp[2][:], in_=ares[1][:]), R=[(K_("ares"), 1)], W=[(K_("asp"), 2)])
                        for j3 in range(3):
                            I("pe", lambda e, j3=j3: e.matmul(pcs, lhsT=Ub, rhs=asp[j3][:], start=(j3 == 0), stop=(j3 == 2)),
                              R=[K_("kcb"), (K_("asp"), j3)], W=["pb3"])
                        I("dve", lambda e: e.tensor_scalar(out=ncs[:], in0=pcs, scalar1=-1.0, scalar2=None, op0=ALU.mult),
                          R=["pb3"], W=[K_("ncs")])
                        I("act", lambda e: e.activation(out=ecs[:], in_=pcs, func=AF.Exp), R=["pb3"], W=[K_("ecs")])
                        if STG < 3:
                            continue
                        for r in range(2):
                            for q in range(8):
                                I("pe", lambda e, r=r, q=q: e.transpose(out=pbh[:, q * 128:(q + 1) * 128], in_=xc[:, 8 * r + q, lsl],
                                                                       identity=ident_bf[:]),
                                  R=[(K_("xc"), 8 * r + q), "ident_bf"], W=["pbh"])
                            I("dve", lambda e, r=r: e.tensor_tensor(out=xs_tok[:, 16 * r:16 * r + 16, :],
                                                                    in0=pbh[:].rearrange("p (h d) -> p h d", h=16),
                                                                    in1=Dbc[:, 16 * r:16 * r + 16].unsqueeze(2).to_broadcast([128, 16, 64]),
                                                                    op=ALU.mult),
                              R=["pbh", K_("hp")], W=[K_("xs_tok")])
                            I("dve", lambda e, r=r: e.tensor_tensor(out=dtx[:, 16 * r:16 * r + 16, :],
                                                                    in0=pbh[:].rearrange("p (h d) -> p h d", h=16),
                                                                    in1=dtv[:, 16 * r:16 * r + 16].unsqueeze(2).to_broadcast([128, 16, 64]),
                                                                    op=ALU.mult),
                              R=["pbh", K_("dtv")], W=[K_("dtx")])
                        for q in range(8):
                            I("pe", lambda e, q=q: e.transpose(out=pbh[:, q * 128:(q + 1) * 128], in_=xc[:, 16 + q, lsl], identity=ident_bf[:]),
                              R=[(K_("xc"), 16 + q), "ident_bf"], W=["pbh"])
                        I("act", lambda e: e.activation(out=Btok[:], in_=pbh[:].rearrange("p (g n) -> p g n", g=8), func=AF.Copy),
                          R=["pbh"], W=[K_("Btok")])
                        if STG < 4:
                            continue
                        def GA(g):
                            i2 = g % 2
                            X = pb[i2]
                            kX = "pb%d" % i2
                            for j3 in range(3):
                                I("dve", lambda e, g=g, i2=i2, j3=j3: e.tensor_tensor(
                                    out=aU[0][j3][:], in0=Ub.unsqueeze(1).to_broadcast([128, 4, 128]),
                                    in1=asp[j3][:, 4 * g:4 * g + 4].unsqueeze(2).to_broadcast([128, 4, 128]), op=ALU.mult),
                                  R=[K_("kcb"), (K_("asp"), j3)], W=[(K_("aU"), 0, j3)])
                                I("pe", lambda e, i2=i2, X=X, j3=j3: e.matmul(X[:], lhsT=ones_bf[:], rhs=aU[0][j3][:].rearrange("p h t -> p (h t)"),
                                                                          start=(j3 == 0), stop=False),
                                  R=["ones", (K_("aU"), 0, j3)], W=[kX])
                            I("pe", lambda e, X=X: e.matmul(X[:], lhsT=ident_bf[:], rhs=c4b, start=False, stop=True),
                              R=[K_("kcb"), "ident_bf"], W=[kX])
                            for h in range(4):
                                I("act", lambda e, h=h, g=g, i2=i2, X=X: e.activation(
                                    out=Lt[0][:, h, :], in_=X[:, h * 128:(h + 1) * 128], func=AF.Exp,
                                    bias=ncs[:, 4 * g + h:4 * g + h + 1]),
                                  R=[kX, K_("ncs")], W=[(K_("Lt"), 0)])
                            I("dve", lambda e, g=g, X=X: e.tensor_copy(
                                out=cl[:, 4 * g:4 * g + 4].unsqueeze(2),
                                in_=X[:].rearrange("p (h t) -> p h t", h=4)[:, :, 127:128]),
                              R=[kX], W=[(K_("cl"), g)])
                            CBp = pb[2][:, 0:128]
                            I("pe", lambda e, g=g, CBp=CBp: e.matmul(CBp, lhsT=xc[:, 16 + g, lsl], rhs=xc[:, 24 + g, lsl], start=True, stop=True),
                              R=[(K_("xc"), 16 + g), (K_("xc"), 24 + g)], W=["pb2"])
                            I("act", lambda e, i2=i2, CBp=CBp: e.activation(out=CBs[0][:], in_=CBp, func=AF.Copy),
                              R=["pb2"], W=[(K_("CBs"), 0)])
                            I("dve", lambda e, i2=i2: e.tensor_tensor(out=Mt[i2][:], in0=Lt[0][:],
                                                                      in1=CBs[0][:].unsqueeze(1).to_broadcast([128, 4, 128]), op=ALU.mult),
                              R=[(K_("Lt"), 0), (K_("CBs"), 0)], W=[(K_("Mt"), i2)])
                        def GB(g):
                            i2 = g % 2
                            pY = pb[5 + i2]
                            kY = "pb%d" % (5 + i2)
                            for h in range(4):
                                I("pe", lambda e, h=h, g=g, i2=i2: e.matmul(pY[:, 256 + h * 64:256 + (h + 1) * 64], lhsT=Mt[i2][:, h, :],
                                                                           rhs=dtx[:, 4 * g + h, :], start=True, stop=True),
                                  R=[(K_("Mt"), i2), K_("dtx")], W=[kY])
                            I("pe", lambda e, g=g: e.matmul(pY[:, 0:256], lhsT=xc[:, 24 + g, lsl], rhs=Sbf[:, g, :], start=True, stop=True),
                              R=[(K_("xc"), 24 + g), (K_("Sbf"), g)], W=[kY])
                            I("dve", lambda e, g=g, i2=i2: e.tensor_tensor(
                                out=tm1[i2].rearrange("p (h d) -> p h d", h=4), in0=pY[:, 0:256].rearrange("p (h d) -> p h d", h=4),
                                in1=ecs[:, 4 * g:4 * g + 4].unsqueeze(2).to_broadcast([128, 4, 64]), op=ALU.mult),
                              R=[kY, K_("ecs")], W=[("tA", i2)])
                            I("dve", lambda e, i2=i2: e.tensor_tensor(out=tm1[i2], in0=tm1[i2], in1=pY[:, 256:512], op=ALU.add),
                              R=[kY, ("tA", i2)], W=[("tA", i2)])
                            I("dve", lambda e, g=g, i2=i2: e.tensor_tensor(out=Y[:, g * 256:(g + 1) * 256], in0=tm1[i2],
                                                                         in1=xs_tok[:, 4 * g:4 * g + 4, :].rearrange("p h d -> p (h d)"), op=ALU.add),
                              R=[("tA", i2), K_("xs_tok")], W=[(K_("Y"), g)])
                        GA(0)
                        for g in range(8):
                            if g + 1 < 8:
                                GA(g + 1)
                            GB(g)
                        I("dve", lambda e: e.tensor_tensor(out=wl[:], in0=cl[:], in1=ncs[:], op=ALU.add),
                          R=[(K_("cl"), g) for g in range(8)] + [K_("ncs")], W=[K_("wl")])
                        I("act", lambda e: e.activation(out=wl[:], in_=wl[:], func=AF.Exp), R=[K_("wl")], W=[K_("wl")])
                        I("act", lambda e: e.activation(out=dec[:], in_=cl[:], func=AF.Exp), R=[(K_("cl"), g) for g in range(8)], W=[K_("dec")])
                        I("dve", lambda e: e.tensor_tensor(out=dtx[:], in0=dtx[:], in1=wl[:].unsqueeze(2).to_broadcast([128, 32, 64]), op=ALU.mult),
                          R=[K_("dtx"), K_("wl")], W=[K_("dtx")])
                        for g in range(8):
                            i2 = g % 2
                            SU = pb[4][:, 0:256]
                            I("pe", lambda e, g=g, SU=SU: e.matmul(SU, lhsT=Btok[:, g, :], rhs=dtx[:, 4 * g:4 * g + 4, :].rearrange("p h d -> p (h d)"),
                                                                  start=True, stop=True),
                              R=[K_("Btok"), K_("dtx")], W=["pb4"])
                            I("dve", lambda e, g=g: e.tensor_tensor(
                                out=S[:, g, :].rearrange("p (h d) -> p h d", h=4), in0=S[:, g, :].rearrange("p (h d) -> p h d", h=4),
                                in1=dec[:, 4 * g:4 * g + 4].unsqueeze(2).to_broadcast([128, 4, 64]), op=ALU.mult),
                              R=[(K_("S"), g), K_("S"), K_("dec")], W=[(K_("S"), g)])
                            I("dve", lambda e, g=g, SU=SU: e.tensor_tensor(out=S[:, g, :], in0=S[:, g, :], in1=SU, op=ALU.add),
                              R=[(K_("S"), g), "pb4"], W=[(K_("S"), g)])
                            I("act", lambda e, g=g: e.activation(out=Sbf[:, g, :], in_=S[:, g, :], func=AF.Copy),
                              R=[(K_("S"), g), K_("Sbf")], W=[(K_("Sbf"), g)])
                        if STG < 6:
                            continue
                        for r in range(2):
                            for q in range(8):
                                I("pe", lambda e, r=r, q=q: e.transpose(out=pbh[:, q * 128:(q + 1) * 128], in_=zT[:, 8 * r + q, lsl],
                                                                       identity=ident_bf[:]),
                                  R=[K_("zT"), "ident_bf"], W=["pbh"])
                            I("act", lambda e, r=r: e.activation(out=sz[:, r * 1024:(r + 1) * 1024], in_=pbh[:], func=AF.Silu),
                              R=["pbh"], W=[K_("sz")])
                        I("dve", lambda e: e.tensor_tensor(out=Y[:], in0=Y[:], in1=sz[:], op=ALU.mult),
                          R=[(K_("Y"), g) for g in range(8)] + [K_("sz")], W=[K_("Y2")])
                        I("act", lambda e: e.activation(out=sz[:], in_=Y[:], func=AF.Square), R=[K_("Y2")], W=[K_("sz")])
                        I("dve", lambda e: e.tensor_reduce(out=ss[:], in_=sz[:].rearrange("p (g c) -> p g c", g=8), axis=AX.X, op=ALU.add),
                          R=[K_("sz")], W=[K_("ss")])
                        I("act", lambda e: e.activation(out=ss[:], in_=ss[:], func=AF.Sqrt, scale=1.0 / 256.0, bias=epsb[:]),
                          R=[K_("ss"), "epsb"], W=[K_("ss")])
                        I("dve", lambda e: e.reciprocal(out=ss[:], in_=ss[:]), R=[K_("ss")], W=[K_("ss")])
                        I("dve", lambda e: e.tensor_tensor(out=Ynb[:].rearrange("p (g c) -> p g c", g=8), in0=Y[:].rearrange("p (g c) -> p g c", g=8),
                                                           in1=ss[:].unsqueeze(2).to_broadcast([128, 8, 256]), op=ALU.mult),
                          R=[K_("Y2"), K_("ss")], W=[K_("Ynb")] + [(K_("Y"), g) for g in range(8)])
                        for r in range(2):
                            for q in range(8):
                                I("pe", lambda e, r=r, q=q: e.transpose(out=pbh[:, q * 128:(q + 1) * 128],
                                                                       in_=Ynb[:, (8 * r + q) * 128:(8 * r + q + 1) * 128], identity=ident_bf[:]),
                                  R=[K_("Ynb"), "ident_bf"], W=["pbh"])
                            I("dve", lambda e, r=r: e.tensor_tensor(out=ynT[:, 8 * r:8 * r + 8, lsl], in0=pbh[:].rearrange("p (c t) -> p c t", c=8),
                                                                    in1=gn[:, 8 * r:8 * r + 8].unsqueeze(2).to_broadcast([128, 8, 128]), op=ALU.mult),
                              R=["pbh", K_("hp")], W=[K_("ynT")])
                    for dc in range(DC if STG >= 7 else 0):
                        s = load_w256(ssd_wout_d[j0, dc])
                        wv = wi[s][:].rearrange("p k (a f) -> p (k a) f", a=2)
                        b = nxt("po")
                        po = pb[4 + b] if False else pb[6]
                        for c in range(16):
                            I("pe", lambda e, c=c, wv=wv: e.matmul(pb[6][:, 0:TT2], lhsT=wv[:, c, :], rhs=ynT[:, c, :], start=(c == 0), stop=(c == 15)),
                              R=[("wi", s), K_("ynT")], W=["pb6"])
                        I("dve", lambda e, dc=dc: e.tensor_tensor(out=xT[:, dc, tsl], in0=xT[:, dc, tsl], in1=pb[6][:, 0:TT2], op=ALU.add),
                          R=["pb6", ("xT", dc, tt)], W=[("xT", dc, tt)])
                P.release([k for k in list(P.res) if (k[0] if isinstance(k, tuple) else k).startswith("ssd_")])


        def dsa_mixer(j2, gidx):
            norm_to_hT(gidx)
            NEG = -1.0e30
            TOPK = min(256, L // 4)
            NR = TOPK // 8
            with ExitStack() as sc:
                def T(name, shape, dt=F32):
                    return sb(nc, sc, "dsa_" + name, shape, dt)

                def K_(n):
                    return "dsa_" + n
                qT = T("qT", [128, 8, L], BF16)
                K2T = T("K2T", [128, L], BF16)
                qiT = T("qiT", [128, 4, L], BF16)
                ki2T = T("ki2T", [128, L], BF16)
                Vaug = T("Vaug", [128, NB, 65], BF16)
                witok = T("witok", [128, NB, 8])
                I("dve", lambda e: e.memset(Vaug[:], 1.0), W=[K_("Vaug")])
                with ExitStack() as sc2:
                    wvw = sb(nc, sc2, "dsa_wvw", [128, DC, 72], BF16)
                    I("pool", lambda e: e.dma_start(out=wvw[:], in_=dsa_wvw_d[j2].rearrange("p (k f) -> p k f", k=DC)),
                      W=[K_("wvw")], dma=K_("wvw"))
                    cosT = sb(nc, sc2, "dsa_cos", [128, L], F32)
                    sinS = sb(nc, sc2, "dsa_sin", [128, L], F32)
                    I("sp", lambda e: e.dma_start(out=cosT[:], in_=dsa_rope_d[0]), W=[K_("cos")], dma=K_("cos"))
                    I("sp", lambda e: e.dma_start(out=sinS[:], in_=dsa_rope_d[1]), W=[K_("sin")], dma=K_("sin"))
                    for blk in range(14):
                        s = load_w256(dsa_win_d[j2, blk])
                        if blk < 8:
                            dst, dkey, scl = (lambda ts, blk=blk: qT[:, blk, ts]), K_("qT"), 0.125
                        elif blk == 8:
                            dst, dkey, scl = (lambda ts: K2T[:, ts]), K_("K2T"), 1.0
                        elif blk < 13:
                            dst, dkey, scl = (lambda ts, blk=blk: qiT[:, blk - 9, ts]), K_("qiT"), 1.0
                        else:
                            dst, dkey, scl = (lambda ts: ki2T[:, ts]), K_("ki2T"), 1.0
                        for tt in range(NT):
                            ts = slice(tt * 512, (tt + 1) * 512)
                            pg, pu, kg, ku = mm_pair(s, hT, lambda k, tt: ("hT", k, tt), tt)
                            k1 = nxt("tA")
                            k2 = nxt("tB")
                            I("dve", lambda e, k1=k1, pg=pg, ts=ts, scl=scl: e.scalar_tensor_tensor(
                                out=tA[k1][:], in0=pg[:], scalar=scl, in1=cosT[:, ts], op0=ALU.mult, op1=ALU.mult),
                              R=[kg, K_("cos")], W=[("tA", k1)])
                            I("dve", lambda e, k2=k2, pu=pu, ts=ts, scl=scl: e.scalar_tensor_tensor(
                                out=tB[k2][:], in0=pu[:], scalar=scl, in1=sinS[:, ts], op0=ALU.mult, op1=ALU.mult),
                              R=[ku, K_("sin")], W=[("tB", k2)])
                            I("pool", lambda e, k1=k1, k2=k2, ts=ts, dst=dst: e.tensor_tensor(out=dst(ts), in0=tA[k1][:], in1=tB[k2][:], op=ALU.add),
                              R=[("tA", k1), ("tB", k2)], W=[(dkey, tt)])
                    for tb in range(NB):
                        bsl = slice(tb * 128, (tb + 1) * 128)
                        pv = pb[4][:, 0:72]
                        for k in range(DC):
                            I("pe", lambda e, k=k: e.matmul(pv, lhsT=hT[:, k, bsl], rhs=wvw[:, k, :], start=(k == 0), stop=(k == DC - 1)),
                              R=[("hT", k, tb // 4), K_("wvw")], W=["pb4"])
                        I("act", lambda e, tb=tb: e.activation(out=Vaug[:, tb, 0:64], in_=pb[4][:, 0:64], func=AF.Copy),
                          R=["pb4", K_("Vaug")], W=[(K_("Vaug"), tb)])
                        I("dve", lambda e, tb=tb: e.tensor_scalar(out=witok[:, tb, :], in0=pb[4][:, 64:72], scalar1=8.0 ** -0.5 * 64.0 ** -0.5,
                                                                  scalar2=None, op0=ALU.mult),
                          R=["pb4"], W=[(K_("witok"), tb)])
                    P.release([K_("cos"), K_("sin"), K_("wvw")])
                with ExitStack() as sc3:
                    def T3(name, shape, dt=F32):
                        return sb(nc, sc3, "dsa_" + name, shape, dt)
                    idx_ = [T3("idx%d" % i, [128, L]) for i in range(2)]
                    mb_ = [T3("mb%d" % i, [128, L], BF16) for i in range(2)]
                    Pt = [T3("Pt%d" % i, [128, 512], BF16) for i in range(2)]
                    Otok = T3("Otok", [128, 16, 64], BF16)
                    OT = T3("OT", [128, DC, 128], BF16)
                    kd = T3("kd", [128, 128])
                    irep = T3("irep", [128, 512], BF16)
                    m8 = T3("m8", [128, 8])
                    rcp = T3("rcp", [128, 16, 1])
                    zer = T3("zer", [128, 512], BF16)
                    I("dve", lambda e: e.memset(zer[:], 0.0), W=[K_("zer")])
                    I("sp", lambda e: e.dma_start(out=kd[:], in_=dsa_k_d[:, 0:128]), W=[K_("kd")], dma=K_("kd"))
                    I("pool", lambda e: e.dma_start(out=irep[:], in_=dsa_k_d[:, 128:640]), W=[K_("irep")], dma=K_("irep"))
                    negtri = kd[:, 0:128]
                    cn = {"pi": 0, "ps": 0}
                    def stageA(qb):
                        qsl = slice(qb * 128, (qb + 1) * 128)
                        S = 128 * (qb + 1)
                        bi = qb % 2
                        idx, mb = idx_[bi], mb_[bi]
                        npc = (S + 511) // 512
                        for hi in range(8):
                            c, half = hi // 2, hi % 2
                            hs = slice(half * 64, (half + 1) * 64)
                            for pc in range(npc):
                                cols = min(512, S - 512 * pc)
                                csl = slice(512 * pc, 512 * pc + cols)
                                ib = cn["pi"] % 2
                                cn["pi"] += 1
                                pI = pb[5 + ib]
                                I("pe", lambda e, c=c, hs=hs, csl=csl, cols=cols, pI=pI: e.matmul(
                                    pI[:, 0:cols], lhsT=qiT[hs, c, qsl], rhs=ki2T[hs, csl], start=True, stop=True),
                                  R=[(K_("qiT"), qb // 4), (K_("ki2T"), pc)], W=["pb%d" % (5 + ib)])
                                k1 = nxt("tA")
                                I("act", lambda e, k1=k1, cols=cols, pI=pI: e.activation(out=tA[k1][:, 0:cols], in_=pI[:, 0:cols], func=AF.Relu),
                                  R=["pb%d" % (5 + ib)], W=[("tA", k1)])
                                wcol = witok[:, qb, hi:hi + 1]
                                if hi == 0:
                                    I("dve", lambda e, k1=k1, cols=cols, csl=csl, wcol=wcol, idx=idx: e.tensor_scalar(
                                        out=idx[:, csl], in0=tA[k1][:, 0:cols], scalar1=wcol, scalar2=None, op0=ALU.mult),
                                      R=[("tA", k1), (K_("witok"), qb)], W=[(K_("idx"), bi, pc)])
                                else:
                                    I("dve", lambda e, k1=k1, cols=cols, csl=csl, wcol=wcol, idx=idx: e.scalar_tensor_tensor(
                                        out=idx[:, csl], in0=tA[k1][:, 0:cols], scalar=wcol, in1=idx[:, csl], op0=ALU.mult, op1=ALU.add),
                                      R=[("tA", k1), (K_("witok"), qb), (K_("idx"), bi, pc)], W=[(K_("idx"), bi, pc)])
                        allidx = [(K_("idx"), bi, pc) for pc in range(npc)]
                        I("dve", lambda e, S=S, idx=idx: e.tensor_tensor(out=idx[:, S - 128:S], in0=idx[:, S - 128:S], in1=negtri, op=ALU.add),
                          R=allidx + [K_("kd")], W=allidx)
                        if qb >= TOPK // 128:
                            for rd in range(NR):
                                I("dve", lambda e, S=S, idx=idx: e.max(out=m8[:], in_=idx[:, 0:S]), R=allidx, W=[K_("m8")])
                                I("dve", lambda e, S=S, idx=idx: e.match_replace(out=idx[:, 0:S], in_to_replace=m8[:], in_values=idx[:, 0:S],
                                                                                imm_value=NEG),
                                  R=[K_("m8")], W=allidx)
                            I("dve", lambda e, S=S, idx=idx, mb=mb: e.tensor_scalar(out=mb[:, 0:S], in0=idx[:, 0:S], scalar1=NEG, scalar2=-30000.0,
                                                                                  op0=ALU.not_equal, op1=ALU.mult),
                              R=allidx, W=[(K_("mb"), bi)])
                        else:
                            I("dve", lambda e, S=S, idx=idx, mb=mb: e.tensor_scalar(out=mb[:, 0:S], in0=idx[:, 0:S], scalar1=-1.0e29, scalar2=-30000.0,
                                                                                  op0=ALU.is_lt, op1=ALU.mult),
                              R=allidx, W=[(K_("mb"), bi)])
                    def stageB(qb):
                        qsl = slice(qb * 128, (qb + 1) * 128)
                        bi = qb % 2
                        mb = mb_[bi]
                        for bk, nh in ((2, 7), (3, 7), (4, 2)):
                            I("pe", lambda e, bk=bk, nh=nh: e.matmul(pb[bk][:, 0:nh * 65], lhsT=zer[:, 0:128], rhs=zer[:, 0:nh * 65],
                                                                    start=True, stop=False),
                              R=[K_("zer")], W=["pb%d" % bk])
                        for sbk in range(qb + 1):
                            ssl = slice(sbk * 128, (sbk + 1) * 128)
                            for half in range(2):
                                hs = slice(half * 64, (half + 1) * 64)
                                for cg in range(2):
                                    ib = cn["ps"] % 2
                                    cn["ps"] += 1
                                    pS = pb[ib]
                                    I("pe", lambda e, hs=hs, cg=cg, ssl=ssl, pS=pS: e.matmul(
                                        pS[:], lhsT=K2T[hs, ssl], rhs=qT[hs, 4 * cg:4 * cg + 4, qsl], start=True, stop=False),
                                      R=[(K_("K2T"), sbk // 4), (K_("qT"), qb // 4)], W=["pb%d" % ib])
                                    I("pe", lambda e, ssl=ssl, pS=pS, mb=mb: e.matmul(pS[:], lhsT=mb[:, ssl], rhs=irep[:], start=False, stop=True),
                                      R=[(K_("mb"), bi), K_("irep")], W=["pb%d" % ib])
                                    I("act", lambda e, ib=ib, pS=pS: e.activation(out=Pt[ib][:], in_=pS[:], func=AF.Exp),
                                      R=["pb%d" % ib], W=[(K_("Pt"), ib)])
                                    for j in range(4):
                                        head = 2 * (4 * cg + j) + half
                                        bk, off = 2 + head // 7, (head % 7) * 65
                                        I("pe", lambda e, ib=ib, j=j, bk=bk, off=off, sbk=sbk: e.matmul(
                                            pb[bk][:, off:off + 65], lhsT=Pt[ib][:, j * 128:(j + 1) * 128], rhs=Vaug[:, sbk, :],
                                            start=False, stop=False),
                                          R=[(K_("Pt"), ib), (K_("Vaug"), sbk), K_("Vaug")], W=["pb%d" % bk])
                        for bk, nh in ((2, 7), (3, 7), (4, 2)):
                            I("pe", lambda e, bk=bk, nh=nh: e.matmul(pb[bk][:, 0:nh * 65], lhsT=zer[:, 0:128], rhs=zer[:, 0:nh * 65],
                                                                    start=False, stop=True),
                              R=[K_("zer")], W=["pb%d" % bk])
                        for bk, h0, nh in ((2, 0, 7), (3, 7, 7), (4, 14, 2)):
                            pv3 = pb[bk][:, 0:nh * 65].rearrange("p (h e) -> p h e", e=65)
                            I("dve", lambda e, pv3=pv3, h0=h0, nh=nh: e.reciprocal(out=rcp[:, h0:h0 + nh, :], in_=pv3[:, :, 64:65]),
                              R=["pb%d" % bk], W=[(K_("rcp"), bk)])
                            I("dve", lambda e, pv3=pv3, h0=h0, nh=nh: e.tensor_tensor(
                                out=Otok[:, h0:h0 + nh, :], in0=pv3[:, :, 0:64], in1=rcp[:, h0:h0 + nh, :].to_broadcast([128, nh, 64]), op=ALU.mult),
                              R=["pb%d" % bk, (K_("rcp"), bk)], W=[(K_("Otok"), bk)])
                        lq = 0
                        for c in range(8):
                            I("pe", lambda e, c=c: e.transpose(out=pbh[:, c * 128:(c + 1) * 128],
                                                               in_=Otok[:, 2 * c:2 * c + 2, :].rearrange("p h d -> p (h d)"), identity=ident_bf[:]),
                              R=[(K_("Otok"), 2), (K_("Otok"), 3), (K_("Otok"), 4), "ident_bf"], W=["pbh"])
                        I("act", lambda e, lq=lq: e.activation(out=OT[:, :, lq * 128:(lq + 1) * 128],
                                                               in_=pbh[:].rearrange("p (c t) -> p c t", c=8), func=AF.Copy),
                          R=["pbh"], W=[K_("OT")])
                        if True:
                            osl = slice(qb * 128, (qb + 1) * 128)
                            tt = qb // 4
                            for b4 in range(4):
                                s = load_w256(dsa_wo_d[j2, b4])
                                for hh in range(2):
                                    dc = 2 * b4 + hh
                                    po = pb[5 + hh]
                                    for k in range(DC):
                                        I("pe", lambda e, k=k, hh=hh, po=po: e.matmul(po[:, 0:128], lhsT=wi[s][:, k, hh * 128:(hh + 1) * 128],
                                                                                     rhs=OT[:, k, :], start=(k == 0), stop=(k == DC - 1)),
                                          R=[("wi", s), K_("OT")], W=["pb%d" % (5 + hh)])
                                    I("dve", lambda e, dc=dc, po=po: e.tensor_tensor(out=xT[:, dc, osl], in0=xT[:, dc, osl], in1=po[:, 0:128], op=ALU.add),
                                      R=["pb%d" % (5 + hh), ("xT", dc, tt)], W=[("xT", dc, tt)])
                    stageA(0)
                    for qb in range(NB):
                        if qb + 1 < NB:
                            stageA(qb + 1)
                        stageB(qb)
                P.release([k for k in list(P.res) if (k[0] if isinstance(k, tuple) else k).startswith("dsa_")])

        def s5_mixer(j5, gidx):
            norm_to_hT(gidx)
            with ExitStack() as sc:
                def T(name, shape, dt=F32):
                    return sb(nc, sc, "s5_" + name, shape, dt)
                gT = T("gT", [128, DC, L], BF16)
                iota = T("iota", [128, L], mybir.dt.int16)
                par = T("par", [128, 3, 64])
                bre2 = [T("bre%d" % i, [64, 128]) for i in range(2)]
                bim2 = [T("bim%d" % i, [64, 128]) for i in range(2)]
                cst12 = [T("cst1%d" % i, [128, 128]) for i in range(2)]
                cst22 = [T("cst2%d" % i, [128, 128]) for i in range(2)]
                magic = T("magic", [128, 1])
                kk = T("kk", [128, 16])
                dv = T("dv", [128, 24])
                names = ["step", "lr", "th", "r", "f", "raw", "fs", "fc", "ms0", "mc0", "nr", "ni",
                         "den", "inv", "cr", "ci", "u1", "u2"]
                pt_ = {n: T(n, [128, 64]) for n in names}
                state = T("state", [128, 64])
                z = {n: T(n, [64, 128]) for n in ["t1", "t2", "zre", "zim", "nzre"]}
                wA = T("wA", [128, 8, 128], BF16)
                wA2 = T("wA2", [128, 8, 128], BF16)
                W1 = T("W1", [128, 8, 128], BF16)
                W2 = T("W2", [128, 8, 128], BF16)
                raw_ = [T("rawt%d" % i, [128, 512]) for i in range(2)]
                msin_ = [T("msin%d" % i, [128, 512]) for i in range(2)]
                mcos_ = [T("mcos%d" % i, [128, 512]) for i in range(2)]
                bt_ = [T("bt%d" % i, [128, 512]) for i in range(2)]
                st_ = [T("st%d" % i, [128, 512]) for i in range(2)]
                P1_ = [T("P1%d" % i, [128, 512], BF16) for i in range(2)]
                P2_ = [T("P2%d" % i, [128, 512], BF16) for i in range(2)]
                I("dve", lambda e: e.memset(magic[:], MAGIC), W=["s5_magic"])
                itc = [0]
                yv = T("yv", [128, 512])
                zz = T("zz", [128, 512])

                def ld(dst, src, key):
                    I("sp", lambda e: e.dma_start(out=dst, in_=src), W=[key], dma=key)
                ld(iota[:], iota_d, "s5_iota")
                ld(par[:], s5p_d[j5].rearrange("a p g -> p a g"), "s5_par")
                ld(kk[:], s5k_d, "s5_kk")
                ld(dv[:], s5d_d[j5], "s5_dv")

                def dv_(fn, R, W):
                    I("dve", fn, R=["s5_" + r for r in R], W=["s5_" + w for w in W])

                def po_(fn, R, W):
                    I("pool", fn, R=["s5_" + r for r in R], W=["s5_" + w for w in W])

                def ac_(fn, R, W):
                    I("act", fn, R=["s5_" + r for r in R] + ["negpi"], W=["s5_" + w for w in W])
                p = pt_
                lre, lim, lst = par[:, 0, :], par[:, 1, :], par[:, 2, :]
                ac_(lambda e: e.activation(out=p["step"][:], in_=lst, func=AF.Exp), ["par"], ["step"])
                dv_(lambda e: e.tensor_tensor(out=p["lr"][:], in0=lre, in1=p["step"][:], op=ALU.mult), ["par", "step"], ["lr"])
                dv_(lambda e: e.tensor_tensor(out=p["th"][:], in0=lim, in1=p["step"][:], op=ALU.mult), ["par", "step"], ["th"])
                ac_(lambda e: e.activation(out=p["r"][:], in_=p["lr"][:], func=AF.Exp), ["lr"], ["r"])
                dv_(lambda e: e.tensor_scalar(out=p["f"][:], in0=p["th"][:], scalar1=1.0 / TWO_PI, scalar2=None, op0=ALU.mult), ["th"], ["f"])
                dv_(lambda e: e.tensor_scalar(out=p["raw"][:], in0=p["f"][:], scalar1=MAGIC, scalar2=None, op0=ALU.add), ["f"], ["raw"])
                dv_(lambda e: e.scalar_tensor_tensor(out=p["fs"][:], in0=p["raw"][:], scalar=MAGIC, in1=p["f"][:], op0=ALU.subtract, op1=ALU.subtract), ["raw", "f"], ["fs"])
                dv_(lambda e: e.scalar_tensor_tensor(out=p["fc"][:], in0=p["fs"][:], scalar=-1.0, in1=p["fs"][:], op0=ALU.mult, op1=ALU.max), ["fs"], ["fc"])
                ac_(lambda e: e.activation(out=p["ms0"][:], in_=p["fs"][:], func=AF.Sin, scale=-TWO_PI), ["fs"], ["ms0"])
                ac_(lambda e: e.activation(out=p["mc0"][:], in_=p["fc"][:], func=AF.Sin, scale=-TWO_PI, bias=halfpi[:]), ["fc"], ["mc0"])
                dv_(lambda e: e.tensor_tensor(out=p["u1"][:], in0=p["mc0"][:], in1=p["r"][:], op=ALU.mult), ["mc0", "r"], ["u1"])
                dv_(lambda e: e.tensor_scalar(out=p["nr"][:], in0=p["u1"][:], scalar1=-1.0, scalar2=None, op0=ALU.add), ["u1"], ["nr"])
                dv_(lambda e: e.tensor_tensor(out=p["u2"][:], in0=p["ms0"][:], in1=p["r"][:], op=ALU.mult), ["ms0", "r"], ["u2"])
                dv_(lambda e: e.tensor_copy(out=p["ni"][:], in_=p["u2"][:]), ["u2"], ["ni"])
                dv_(lambda e: e.tensor_tensor(out=p["u1"][:], in0=lre, in1=lre, op=ALU.mult), ["par", "nr"], ["u1"])
                dv_(lambda e: e.tensor_tensor(out=p["u2"][:], in0=lim, in1=lim, op=ALU.mult), ["par", "ni"], ["u2"])
                dv_(lambda e: e.tensor_tensor(out=p["den"][:], in0=p["u1"][:], in1=p["u2"][:], op=ALU.add), ["u1", "u2"], ["den"])
                dv_(lambda e: e.reciprocal(out=p["inv"][:], in_=p["den"][:]), ["den"], ["inv"])
                dv_(lambda e: e.tensor_tensor(out=p["u1"][:], in0=p["nr"][:], in1=lre, op=ALU.mult), ["nr", "par", "den"], ["u1"])
                dv_(lambda e: e.tensor_tensor(out=p["u2"][:], in0=p["ni"][:], in1=lim, op=ALU.mult), ["ni", "par", "den"], ["u2"])
                dv_(lambda e: e.tensor_tensor(out=p["cr"][:], in0=p["u1"][:], in1=p["u2"][:], op=ALU.add), ["u1", "u2"], ["cr"])
                dv_(lambda e: e.tensor_tensor(out=p["cr"][:], in0=p["cr"][:], in1=p["inv"][:], op=ALU.mult), ["cr", "inv"], ["cr"])
                dv_(lambda e: e.tensor_tensor(out=p["u1"][:], in0=p["ni"][:], in1=lre, op=ALU.mult), ["ni", "par", "cr"], ["u1"])
                dv_(lambda e: e.tensor_tensor(out=p["u2"][:], in0=p["nr"][:], in1=lim, op=ALU.mult), ["nr", "par", "cr"], ["u2"])
                dv_(lambda e: e.tensor_tensor(out=p["ci"][:], in0=p["u1"][:], in1=p["u2"][:], op=ALU.subtract), ["u1", "u2"], ["ci"])
                dv_(lambda e: e.tensor_tensor(out=p["ci"][:], in0=p["ci"][:], in1=p["inv"][:], op=ALU.mult), ["ci", "inv"], ["ci"])

                def s5_tabA(q, g, si):
                    ts = slice(q * 512, (q + 1) * 512)
                    fcol = p["f"][:, g:g + 1]
                    raw, msin, mcos = raw_[si], msin_[si], mcos_[si]
                    kr, ks, kc_ = [("s5_" + n, si) for n in ("rawt", "msin", "mcos")]
                    I("act", lambda e, fcol=fcol, raw=raw: e.activation(out=raw[:], in_=iota[:, ts], func=AF.Identity, scale=fcol),
                      R=["s5_iota", "s5_f"], W=[kr])
                    I("act", lambda e, fcol=fcol, msin=msin: e.activation(out=msin[:], in_=iota[:, ts], func=AF.Identity, scale=fcol,
                                                                       bias=magic[:]),
                      R=["s5_iota", "s5_f", "s5_magic"], W=[ks])
                def s5_tabB(q, g, si):
                    ts = slice(q * 512, (q + 1) * 512)
                    fcol = p["f"][:, g:g + 1]
                    raw, msin, mcos = raw_[si], msin_[si], mcos_[si]
                    kr, ks, kc_ = [("s5_" + n, si) for n in ("rawt", "msin", "mcos")]
                    I("dve", lambda e, msin=msin, raw=raw: e.scalar_tensor_tensor(out=msin[:], in0=msin[:], scalar=MAGIC, in1=raw[:],
                                                                             op0=ALU.subtract, op1=ALU.subtract),
                      R=[kr, ks], W=[ks])
                    I("dve", lambda e, msin=msin, mcos=mcos: e.scalar_tensor_tensor(out=mcos[:], in0=msin[:], scalar=-1.0, in1=msin[:],
                                                                               op0=ALU.mult, op1=ALU.max),
                      R=[ks], W=[kc_])
                def s5_tabC(q, g, si):
                    ts = slice(q * 512, (q + 1) * 512)
                    fcol = p["f"][:, g:g + 1]
                    raw, msin, mcos = raw_[si], msin_[si], mcos_[si]
                    kr, ks, kc_ = [("s5_" + n, si) for n in ("rawt", "msin", "mcos")]
                    I("act", lambda e, msin=msin: e.activation(out=msin[:], in_=msin[:], func=AF.Sin, scale=-TWO_PI),
                      R=[ks], W=[ks])
                    I("act", lambda e, mcos=mcos: e.activation(out=mcos[:], in_=mcos[:], func=AF.Sin, scale=-TWO_PI, bias=halfpi[:]),
                      R=[kc_, "negpi"], W=[kc_])

                def s5_tab(q, g, si):
                    s5_tabA(q, g, si)
                    s5_tabB(q, g, si)
                    s5_tabC(q, g, si)
                order = [(q_, 8 * ct_ + g__) for ct_ in range(DC) for q_ in range(NT) for g__ in range(8)]
                s5_tab(order[0][0], order[0][1], 0)
                for ct in range(DC):
                    gs = slice(8 * ct, 8 * ct + 8)
                    cs_ = slice(128 * ct, 128 * ct + 128)
                    cb_ = ct % 2
                    bre, bim, cst1, cst2 = bre2[cb_], bim2[cb_], cst12[cb_], cst22[cb_]
                    I("sp", lambda e: e.dma_start(out=bre[:], in_=s5b_d[j5, 0][:, cs_]), W=["s5_bre"], dma=("s5_bre", cb_))
                    I("sp", lambda e: e.dma_start(out=bim[:], in_=s5b_d[j5, 1][:, cs_]), W=["s5_bim"], dma=("s5_bim", cb_))
                    I("sp", lambda e: e.dma_start(out=cst1[:], in_=s5c_d[j5, 0][:, cs_]), W=["s5_cst1"], dma=("s5_cst1", cb_))
                    I("sp", lambda e: e.dma_start(out=cst2[:], in_=s5c_d[j5, 1][:, cs_]), W=["s5_cst2"], dma=("s5_cst2", cb_))

                    def bc(t):
                        return t[0:64, gs].unsqueeze(2).to_broadcast([64, 8, 16])

                    def v3(t):
                        return t[:].rearrange("p (g c) -> p g c", g=8)
                    b_re = bre[:].rearrange("p (g c) -> p g c", g=8)
                    b_im = bim[:].rearrange("p (g c) -> p g c", g=8)
                    dv_(lambda e: e.tensor_tensor(out=v3(z["t1"]), in0=b_re, in1=bc(p["cr"]), op=ALU.mult), ["bre", "cr"], ["t1"])
                    dv_(lambda e: e.tensor_tensor(out=v3(z["t2"]), in0=b_im, in1=bc(p["ci"]), op=ALU.mult), ["bim", "ci"], ["t2"])
                    dv_(lambda e: e.tensor_tensor(out=z["zre"][:], in0=z["t1"][:], in1=z["t2"][:], op=ALU.subtract), ["t1", "t2", "wA", "wA2"], ["zre"])
                    dv_(lambda e: e.tensor_scalar(out=z["nzre"][:], in0=z["zre"][:], scalar1=-1.0, scalar2=None, op0=ALU.mult), ["zre", "wA2"], ["nzre"])
                    dv_(lambda e: e.tensor_tensor(out=v3(z["t1"]), in0=b_im, in1=bc(p["cr"]), op=ALU.mult), ["bim", "cr", "zre"], ["t1"])
                    dv_(lambda e: e.tensor_tensor(out=v3(z["t2"]), in0=b_re, in1=bc(p["ci"]), op=ALU.mult), ["bre", "ci", "zre"], ["t2"])
                    dv_(lambda e: e.tensor_tensor(out=z["zim"][:], in0=z["t1"][:], in1=z["t2"][:], op=ALU.add), ["t1", "t2", "wA", "wA2"], ["zim"])
                    pT = pb[6]
                    id64 = ident[0:64, 0:64]
                    for q, src in enumerate(["zre", "zim", "zim", "nzre"]):
                        I("pe", lambda e, q=q, src=src: e.transpose(out=pT[:, q * 64:(q + 1) * 64], in_=z[src][:], identity=id64),
                          R=["s5_" + src, "ident"], W=["pb6"])
                    for g_ in range(8):
                        I("dve", lambda e, g_=g_: e.tensor_scalar(out=wA[:, g_, :], in0=pT[:, 0:128], scalar1=kk[:, 2 + g_:3 + g_],
                                                                 scalar2=None, op0=ALU.mult),
                          R=["pb6", "s5_kk"], W=["s5_wA"])
                        I("dve", lambda e, g_=g_: e.tensor_scalar(out=wA2[:, g_, :], in0=pT[:, 128:256], scalar1=kk[:, 2 + g_:3 + g_],
                                                                 scalar2=None, op0=ALU.mult),
                          R=["pb6", "s5_kk"], W=["s5_wA2"])
                    I("dve", lambda e: e.memset(W1[:], 0.0), W=["s5_W1"])
                    I("dve", lambda e: e.memset(W2[:], 0.0), W=["s5_W2"])
                    for g_ in range(8):
                        gcs = slice(16 * g_, 16 * g_ + 16)
                        I("dve", lambda e, g_=g_, gcs=gcs: e.tensor_scalar(out=W1[:, g_, 16 * g_:16 * g_ + 16], in0=cst1[:, gcs],
                                                                          scalar1=kk[:, 0:1], scalar2=None, op0=ALU.mult),
                          R=["s5_cst1", "s5_kk"], W=["s5_W1"])
                        I("dve", lambda e, g_=g_, gcs=gcs: e.tensor_scalar(out=W2[:, g_, 16 * g_:16 * g_ + 16], in0=cst2[:, gcs],
                                                                          scalar1=kk[:, 1:2], scalar2=None, op0=ALU.mult),
                          R=["s5_cst2", "s5_kk"], W=["s5_W2"])
                    for q in range(NT):
                        ts = slice(q * 512, (q + 1) * 512)
                        py = pb[5]
                        for g_ in range(8):
                            g = 8 * ct + g_
                            n_it = itc[0]
                            si = n_it % 2
                            itc[0] += 1
                            if n_it + 1 < len(order):
                                s5_tabA(order[n_it + 1][0], order[n_it + 1][1], (n_it + 1) % 2)
                            raw, msin, mcos, bt, st, P1, P2 = raw_[si], msin_[si], mcos_[si], bt_[si], st_[si], P1_[si], P2_[si]
                            kr, ks, kc_, kb, kst, k1_, k2_ = [("s5_" + n, si) for n in ("rawt", "msin", "mcos", "bt", "st", "P1", "P2")]
                            pA, pA2 = pb[2 * si], pb[2 * si + 1]
                            kpA, kpA2 = "pb%d" % (2 * si), "pb%d" % (2 * si + 1)
                            I("pe", lambda e, g_=g_, pA=pA: e.matmul(pA[:], lhsT=wA[:, g_, :], rhs=hT[:, ct, ts], start=True, stop=True),
                              R=["s5_wA", ("hT", ct, q)], W=[kpA])
                            I("pe", lambda e, g_=g_, pA2=pA2: e.matmul(pA2[:], lhsT=wA2[:, g_, :], rhs=hT[:, ct, ts], start=True, stop=True),
                              R=["s5_wA2", ("hT", ct, q)], W=[kpA2])
                            k1 = nxt("tA")
                            k2 = nxt("tB")
                            I("dve", lambda e, k1=k1, mcos=mcos, pA=pA: e.tensor_tensor(out=tA[k1][:], in0=mcos[:], in1=pA[:], op=ALU.mult),
                              R=[kc_, kpA], W=[("tA", k1)])
                            I("dve", lambda e, k2=k2, msin=msin, pA2=pA2: e.tensor_tensor(out=tB[k2][:], in0=msin[:], in1=pA2[:], op=ALU.mult),
                              R=[ks, kpA2], W=[("tB", k2)])
                            I("dve", lambda e, k1=k1, k2=k2, bt=bt: e.tensor_tensor(out=bt[:], in0=tA[k1][:], in1=tB[k2][:], op=ALU.add),
                              R=[("tA", k1), ("tB", k2)], W=[kb])
                            if n_it + 1 < len(order):
                                s5_tabB(order[n_it + 1][0], order[n_it + 1][1], (n_it + 1) % 2)
                                s5_tabC(order[n_it + 1][0], order[n_it + 1][1], (n_it + 1) % 2)
                            if q == 0:
                                I("dve", lambda e, g=g, st=st, bt=bt: e.tensor_tensor_scan(
                                    out=st[:], data0=p["r"][:, g:g + 1].to_broadcast([128, 512]), data1=bt[:], initial=0.0,
                                    op0=ALU.mult, op1=ALU.add),
                                  R=["s5_r", kb], W=[kst])
                            else:
                                I("dve", lambda e, g=g, st=st, bt=bt: e.tensor_tensor_scan(
                                    out=st[:], data0=p["r"][:, g:g + 1].to_broadcast([128, 512]), data1=bt[:],
                                    initial=state[:, g:g + 1], op0=ALU.mult, op1=ALU.add),
                                  R=["s5_r", kb, ("s5_state", g)], W=[kst])
                            I("dve", lambda e, g=g, st=st: e.tensor_copy(out=state[:, g:g + 1], in_=st[:, 511:512]),
                              R=[kst], W=[("s5_state", g)])
                            I("dve", lambda e, mcos=mcos, st=st, P1=P1: e.tensor_tensor(out=P1[:], in0=mcos[:], in1=st[:], op=ALU.mult),
                              R=[kc_, kst], W=[k1_])
                            I("dve", lambda e, msin=msin, st=st, P2=P2: e.tensor_tensor(out=P2[:], in0=msin[:], in1=st[:], op=ALU.mult),
                              R=[ks, kst], W=[k2_])
                            I("pe", lambda e, g_=g_, P1=P1: e.matmul(py[:], lhsT=W1[:, g_, :], rhs=P1[:], start=(g_ == 0), stop=False),
                              R=["s5_W1", k1_], W=["pb5"])
                            I("pe", lambda e, g_=g_, P2=P2: e.matmul(py[:], lhsT=W2[:, g_, :], rhs=P2[:], start=False, stop=(g_ == 7)),
                              R=["s5_W2", k2_], W=["pb5"])
                        I("dve", lambda e: e.scalar_tensor_tensor(out=yv[:], in0=hT[:, ct, ts], scalar=dv[:, ct:ct + 1], in1=py[:],
                                                                 op0=ALU.mult, op1=ALU.add),
                          R=[("hT", ct, q), "s5_dv", "pb5"], W=["s5_yv"])
                        I("act", lambda e: e.activation(out=zz[:], in_=yv[:], func=AF.Square, scale=math.sqrt(GELU_C1)),
                          R=["s5_yv"], W=["s5_zz"])
                        I("dve", lambda e: e.scalar_tensor_tensor(out=zz[:], in0=zz[:], scalar=1.0, in1=yv[:], op0=ALU.add, op1=ALU.mult),
                          R=["s5_zz", "s5_yv"], W=["s5_zz"])
                        I("act", lambda e: e.activation(out=zz[:], in_=zz[:], func=AF.Sigmoid, scale=2.0 * GELU_C0),
                          R=["s5_zz"], W=["s5_zz"])
                        I("dve", lambda e: e.tensor_tensor(out=gT[:, ct, ts], in0=yv[:], in1=zz[:], op=ALU.mult),
                          R=["s5_zz", "s5_yv"], W=[("s5_gT", ct, q)])
                for oc in range(DC):
                    s = load_w256(s5w_d[j5, oc])
                    for tt in range(NT):
                        ts = slice(tt * 512, (tt + 1) * 512)
                        pv, pgt, kv, kg = mm_pair(s, gT, lambda k, tt: ("s5_gT", k, tt), tt)
                        k2 = nxt("tB")
                        I("act", lambda e, k2=k2: e.activation(out=tB[k2][:], in_=pgt[:], func=AF.Sigmoid, bias=dv[:, 16 + oc:17 + oc]),
                          R=[kg, "s5_dv"], W=[("tB", k2)])
                        k1 = nxt("tA")
                        I("dve", lambda e, k1=k1, k2=k2: e.scalar_tensor_tensor(out=tA[k1][:], in0=pv[:], scalar=dv[:, 8 + oc:9 + oc],
                                                                               in1=tB[k2][:], op0=ALU.add, op1=ALU.mult),
                          R=[kv, "s5_dv", ("tB", k2)], W=[("tA", k1)])
                        I("dve", lambda e, k1=k1: e.tensor_tensor(out=xT[:, oc, ts], in0=xT[:, oc, ts], in1=tA[k1][:], op=ALU.add),
                          R=[("tA", k1), ("xT", oc, tt)], W=[("xT", oc, tt)])
                P.release([k for k in list(P.res) if (k[0] if isinstance(k, tuple) else k).startswith("s5_")])

        for sq_i in range(NS):
            with ExitStack() as sc:
                xin = [sb(nc, sc, "xin%d" % i, [128, D], F32) for i in range(2)]
                for tb in range(NB):
                    s = nxt("xin")
                    I("sp", lambda e, s=s, tb=tb: e.dma_start(out=xin[s][:], in_=x_d[sq_i, tb * 128:(tb + 1) * 128, :]),
                      W=[("xin", s)], dma=("xin", s))
                    for half in range(2):
                        pt = pb[6]
                        for cc in range(4):
                            c = half * 4 + cc
                            I("pe", lambda e, s=s, c=c, cc=cc: e.transpose(
                                out=pt[:, cc * 128:(cc + 1) * 128], in_=xin[s][:, c * 128:(c + 1) * 128],
                                identity=ident[:]),
                              R=[("xin", s), "ident"], W=["pb6"])
                        I("act", lambda e, half=half, tb=tb: e.activation(
                            out=xT[:, half * 4:half * 4 + 4, tb * 128:(tb + 1) * 128],
                            in_=pt[:].rearrange("p (c t) -> p c t", c=4), func=AF.Copy),
                          R=["pb6"], W=[("xT", half * 4 + cc, tb // 4) for cc in range(4)])
                P.release_prefix({"xin"})
            i5 = 0
            i0 = 0
            i2 = 0
            for li, kind in enumerate(cfg.layers):
                if cfg.do_ffn:
                    ffn(2 * li, 3 * li)
                if kind == 1:
                    s5_mixer(i5, 3 * li + 1)
                    i5 += 1
                if kind == 0:
                    ssd_mixer(i0, 3 * li + 1)
                    i0 += 1
                if kind == 2:
                    dsa_mixer(i2, 3 * li + 1)
                    i2 += 1
                if cfg.do_ffn:
                    ffn(2 * li + 1, 3 * li + 2)
            with ExitStack() as sc:
                xin = [sb(nc, sc, "xin%d" % i, [128, D], F32) for i in range(2)]
                for tt in range(NT):
                    ts = slice(tt * 512, (tt + 1) * 512)
                    rmsnorm_to(lambda c, ts=ts: xT[:, c, ts], 3 * depth, lambda c, tt=tt: ("xT", c, tt), tt)
                    for q in range(4):
                        tb = tt * 4 + q
                        s = nxt("xin")
                        for half in range(2):
                            pt = pb[6]
                            for cc in range(4):
                                c = half * 4 + cc
                                I("pe", lambda e, c=c, cc=cc, tb=tb: e.transpose(
                                    out=pt[:, cc * 128:(cc + 1) * 128], in_=xT[:, c, tb * 128:(tb + 1) * 128],
                                    identity=ident[:]),
                                  R=[("xT", c, tt), "ident"], W=["pb6"])
                            I("act", lambda e, half=half, s=s: e.activation(
                                out=xin[s][:, half * 512:(half + 1) * 512], in_=pt[:], func=AF.Copy),
                              R=["pb6"], W=[("xin", s)])
                        I("sp", lambda e, s=s, tb=tb: e.dma_start(out=y_d[sq_i, tb * 128:(tb + 1) * 128, :], in_=xin[s][:]),
                          R=[("xin", s)], dma=("yout", s))
                P.release_prefix({"xin"})
        P.finish()
        print("instructions:", P.ninst, "sems:", P.nsem)
    return nc


def w256_layout(w, ncol_blocks, col_a, col_b):
    out = np.empty((ncol_blocks, 128, DC, 256), np.float32)
    wk = w.reshape(DC, 128, -1)
    for b in range(ncol_blocks):
        out[b, :, :, 0:128] = wk[:, :, col_a + b * 128: col_a + (b + 1) * 128].transpose(1, 0, 2)
        out[b, :, :, 128:256] = wk[:, :, col_b + b * 128: col_b + (b + 1) * 128].transpose(1, 0, 2)
    return out.reshape(ncol_blocks, 128, DC * 256)


def prep_ssd(inp, cfg, m):
    f32 = np.float32
    n0 = cfg.n_ssd
    win = np.empty((n0, 24, 128, DC, 256), f32)
    wdt = np.empty((n0, 128, DC * 32), f32)
    wout = np.empty((n0, DC, 128, 16 * 128), f32)
    cwb = np.empty((n0, 128, 32, 5), f32)
    hp = np.empty((n0, 128, 112), f32)
    for j in range(n0):
        W = inp["ssd_in_proj"][j]
        wk = W.reshape(DC, 128, -1)
        for b in range(24):
            win[j, b] = wk[:, :, 256 * b:256 * (b + 1)].transpose(1, 0, 2)
        wdt[j] = wk[:, :, 6144:6176].transpose(1, 0, 2).reshape(128, DC * 32)
        Wo = inp["ssd_out_proj"][j].reshape(16, 128, DC, 128)
        wout[j] = Wo.transpose(2, 1, 0, 3).reshape(DC, 128, 16 * 128)
        cw = inp["ssd_conv_w"][j].reshape(4, 32, 128)
        cwb[j, :, :, 0:4] = cw.transpose(2, 1, 0)
        cwb[j, :, :, 4] = inp["ssd_conv_b"][j].reshape(32, 128).T
        hp[j, :, 0:32] = np.broadcast_to(inp["ssd_dt_bias"][j][None, :], (128, 32))
        hp[j, :, 32:64] = np.broadcast_to(inp["ssd_a_log"][j][None, :], (128, 32))
        hp[j, :, 64:96] = np.broadcast_to(inp["ssd_d"][j][None, :], (128, 32))
        hp[j, :, 96:112] = inp["ssd_gate_norm"][j].reshape(16, 128).T
    kc = np.zeros((128, 768), f32)
    tri = (np.arange(128)[:, None] <= np.arange(128)[None, :])
    kc[:, 0:128] = tri.astype(f32)
    cb = np.where(np.arange(128)[:, None] > np.arange(128)[None, :], -30000.0, 0.0).astype(f32)
    kc[:, 128:640] = np.tile(cb, (1, 4))
    kc[:, 640:768] = 1.0
    m.update({"ssd_win": win.reshape(n0, 24, 128, DC * 256), "ssd_wdt": wdt, "ssd_wout": wout,
              "ssd_cw": cwb.reshape(n0, 128, 160), "ssd_hp": hp, "ssd_k": kc})


def w256_pairs(wa, wb):
    out = np.empty((128, DC, 256), np.float32)
    out[:, :, 0:128] = wa.reshape(DC, 128, 128).transpose(1, 0, 2)
    out[:, :, 128:256] = wb.reshape(DC, 128, 128).transpose(1, 0, 2)
    return out.reshape(128, DC * 256)


def prep_dsa(inp, cfg, m):
    f32 = np.float32
    n2 = cfg.n_dsa
    L = cfg.L
    win = np.empty((n2, 14, 128, DC * 256), f32)
    wvw = np.empty((n2, 128, DC * 72), f32)
    wo = np.empty((n2, 4, 128, DC * 256), f32)
    perm = np.concatenate([np.arange(32, 64), np.arange(0, 32)])
    perm128 = np.concatenate([perm, 64 + perm])
    for j in range(n2):
        W = inp["dsa_in_proj"][j]
        Wq, Wk, Wv = W[:, 0:1024], W[:, 1024:1088], W[:, 1088:1152]
        Wqi, Wki, Wwi = W[:, 1152:1664], W[:, 1664:1728], W[:, 1728:1736]
        blocks = [Wq[:, c * 128:(c + 1) * 128] for c in range(8)]
        blocks.append(np.concatenate([Wk, Wk], 1))
        blocks += [Wqi[:, c * 128:(c + 1) * 128] for c in range(4)]
        blocks.append(np.concatenate([Wki, Wki], 1))
        for b, A in enumerate(blocks):
            win[j, b] = w256_pairs(A, A[:, perm128])
        vw = np.concatenate([Wv, Wwi], 1)
        wvw[j] = vw.reshape(DC, 128, 72).transpose(1, 0, 2).reshape(128, DC * 72)
        Wo = inp["dsa_out_proj"][j]
        for b in range(4):
            wo[j, b] = w256_pairs(Wo[:, 256 * b:256 * b + 128], Wo[:, 256 * b + 128:256 * b + 256])
    inv = (10000.0 ** (-np.arange(32, dtype=np.float64) / 32.0))
    ang = np.arange(L, dtype=np.float64)[None, :] * inv[np.arange(128) % 32][:, None]
    sgn = np.where((np.arange(128) % 64) < 32, -1.0, 1.0)[:, None]
    rope = np.stack([np.cos(ang), np.sin(ang) * sgn], 0).astype(f32)
    kd = np.zeros((128, 640), f32)
    kd[:, 0:128] = np.where(np.arange(128)[None, :] > np.arange(128)[:, None], -3.0e30, 0.0)
    kd[:, 128:640] = np.tile(np.eye(128, dtype=f32), (1, 4))
    m.update({"dsa_win": win, "dsa_wvw": wvw, "dsa_wo": wo, "dsa_rope": rope, "dsa_k": kd})


def prep_common(inp, cfg):
    depth = cfg.depth
    f32 = np.float32
    g = []
    for i in range(depth):
        g += [inp["ffn1_norm"][i], inp["mix_norm"][i], inp["ffn2_norm"][i]]
    g.append(inp["final_norm"])
    g = np.stack(g, 0).astype(f32)
    gains = np.ascontiguousarray(g.reshape(-1, DC, 128).transpose(2, 0, 1))
    wi = np.empty((2 * depth, FC, 128, DC * 256), f32)
    wo = np.empty((2 * depth, 2, DC, 128, 11 * 128), f32)
    for i in range(depth):
        for which, (kin, kout) in enumerate((("ffn1_w_in", "ffn1_w_out"), ("ffn2_w_in", "ffn2_w_out"))):
            w_in = inp[kin][i]
            w_out = inp[kout][i]
            wi[2 * i + which] = w256_layout(w_in, FC, 0, FFN)
            b = w_out.reshape(2, 11, 128, DC, 128)
            b = b.transpose(0, 3, 2, 1, 4)
            wo[2 * i + which] = b.reshape(2, DC, 128, 11 * 128)
    m = {"gains": gains, "ident": np.eye(128, dtype=f32), "ffn_wi": wi, "ffn_wo": wo}
    if cfg.n_s5:
        n5 = cfg.n_s5
        par = np.empty((n5, 3, 128, 64), f32)
        sb_ = np.empty((n5, 2, 64, 1024), f32)
        sc_ = np.empty((n5, 2, 128, 1024), f32)
        sd_ = np.empty((n5, 128, 24), f32)
        sw_ = np.empty((n5, DC, 128, DC * 256), f32)
        for j in range(n5):
            lre = inp["s5_lam_re"][j].T
            lim = inp["s5_lam_im"][j].T
            lst = np.broadcast_to(inp["s5_log_step"][j][None, :], (64, 64))
            for a, t in enumerate((lre, lim, lst)):
                par[j, a, 0:64] = t
                par[j, a, 64:128] = t
            sb_[j, 0] = inp["s5_b_re"][j].transpose(1, 0, 2).reshape(64, 1024)
            sb_[j, 1] = inp["s5_b_im"][j].transpose(1, 0, 2).reshape(64, 1024)
            cre = inp["s5_c_re"][j].transpose(2, 0, 1).reshape(64, 1024)
            cim = inp["s5_c_im"][j].transpose(2, 0, 1).reshape(64, 1024)
            sc_[j, 0, 0:64] = cre
            sc_[j, 0, 64:128] = cim
            sc_[j, 1, 0:64] = cim
            sc_[j, 1, 64:128] = cre
            sd_[j, :, 0:8] = inp["s5_d"][j].reshape(DC, 128).T
            sd_[j, :, 8:24] = inp["s5_glu_b"][j].reshape(16, 128).T
            sw_[j] = w256_layout(inp["s5_glu_w"][j], DC, 0, D)
        kk = np.zeros((128, 16), f32)
        kk[0:64, 0] = 1.0
        kk[64:128, 0] = -1.0
        kk[:, 1] = -1.0
        for g_ in range(8):
            kk[16 * g_:16 * g_ + 16, 2 + g_] = 1.0
        m.update({"s5_par": par, "s5_b": sb_, "s5_c": sc_, "s5_k": kk, "s5_dv": sd_, "s5_glu": sw_,
                  "iota16": np.broadcast_to(np.arange(cfg.L, dtype=np.int16)[None, :], (128, cfg.L)).copy()})
    if cfg.n_ssd:
        prep_ssd(inp, cfg, m)
    if cfg.n_dsa:
        prep_dsa(inp, cfg, m)
    return m


def kernel(**inputs):
    cfg = Cfg()
    inp = {k: np.asarray(v) for k, v in inputs.items()}
    common = prep_common(inp, cfg)
    x = inp["x"].astype(np.float32)
    B = x.shape[0]
    per = B // N_CORES
    nc = build(cfg)
    in_maps = []
    for c in range(N_CORES):
        m = dict(common)
        m["x"] = np.ascontiguousarray(x[c * per:(c + 1) * per])
        in_maps.append(m)
    res = run_bass_kernel_spmd(nc, in_maps, core_ids=list(range(N_CORES)))
    out = np.concatenate([r["y"] for r in res.results], axis=0)
    return out.astype(np.float32)
```

```python
import math
import os
from contextlib import ExitStack

import numpy as np
import concourse.bass as bass
import concourse.mybir as mybir
from concourse.bass_utils import run_bass_kernel_spmd

F32 = mybir.dt.float32
BF16 = mybir.dt.bfloat16
AF = mybir.ActivationFunctionType
ALU = mybir.AluOpType
AX = mybir.AxisListType

D = 1024
DC = 8
FFN = 2816
FC = 22
EPS = 1e-6
N_CORES = 8

ENGS = ("pe", "act", "dve", "pool", "sp")
EPOCH = 12000


class Prog:
    def __init__(self, nc, es):
        self.nc = nc
        self.es = es
        self.eng = {"pe": nc.tensor, "act": nc.scalar, "dve": nc.vector,
                    "pool": nc.gpsimd, "sp": nc.sync}
        self.nsem = 0
        self.esem = {}
        self.ecnt = {}
        self.eepoch = {}
        for e in ENGS:
            self.eepoch[e] = 0
            self.ecnt[e] = 0
            self.esem[e] = self._newsem()
        self.seen = {e: {} for e in ENGS}
        self.res = {}
        self.dsem = {}
        self.free_ev = {}
        self.ninst = 0

    def _newsem(self):
        self.nsem += 1
        return self.es.enter_context(self.nc.semaphore("s%d" % self.nsem))

    def _need(self, eng, deps):
        out = {}
        for ev in deps:
            if ev is None:
                continue
            sem, val, kind = ev
            if kind == "pe" and eng == "pe":
                continue
            nm = sem.name
            if self.seen[eng].get(nm, 0) >= val:
                continue
            if nm not in out or out[nm][1] < val:
                out[nm] = ev
        return list(out.values())

    def I(self, eng, fn, R=(), W=(), dma=None):
        pr = [r for r in R if isinstance(r, str) and r.startswith("pb")]
        if pr:
            R = [r for r in R if r not in pr]
            W = list(W) + [r for r in pr if r not in W]
        deps = []
        for r in R:
            ent = self.res.get(r)
            if ent is not None:
                deps.append(ent[0])
        for w in W:
            ent = self.res.get(w)
            if ent is not None:
                deps.append(ent[0])
                deps.extend(ent[1].values())
            else:
                deps.extend(self.free_ev.values())
        need = self._need(eng, deps)
        e = self.eng[eng]
        for sem, val, kind in need:
            e.wait_ge(sem, val)
            self.seen[eng][sem.name] = val
        ins = fn(e)
        if dma is not None:
            ent = self.dsem.get(dma)
            if ent is None:
                ent = [self._newsem(), 0]
                self.dsem[dma] = ent
            ent[1] += 16
            ins.then_inc(ent[0], 16)
            ev = (ent[0], ent[1], "dma")
        else:
            if self.ecnt[eng] >= EPOCH:
                self.esem[eng] = self._newsem()
                self.ecnt[eng] = 0
                self.eepoch[eng] += 1
            self.ecnt[eng] += 1
            ins.then_inc(self.esem[eng], 1)
            ev = (self.esem[eng], self.ecnt[eng], eng)
        for r in R:
            ent = self.res.get(r)
            if ent is None:
                ent = [None, {}]
                self.res[r] = ent
            ent[1][ev[0].name] = ev
        for w in W:
            self.res[w] = [ev, {}]
        self.ninst += 1
        return ev

    def release(self, keys):
        for k in keys:
            ent = self.res.pop(k, None)
            if ent is None:
                continue
            for ev in [ent[0]] + list(ent[1].values()):
                if ev is None:
                    continue
                nm = ev[0].name
                if nm not in self.free_ev or self.free_ev[nm][1] < ev[1]:
                    self.free_ev[nm] = ev

    def release_prefix(self, prefixes):
        ks = [k for k in self.res if (k[0] if isinstance(k, tuple) else k) in prefixes]
        self.release(ks)

    def finish(self):
        allev = list(self.free_ev.values())
        for ent in self.res.values():
            allev.append(ent[0])
            allev.extend(ent[1].values())
        for eng in ENGS:
            for sem, val, kind in self._need(eng, allev):
                if kind == eng and eng != "sp":
                    pass
                self.eng[eng].wait_ge(sem, val)
                self.seen[eng][sem.name] = val


_UID = [0]


def sb(nc, es, name, shape, dt):
    _UID[0] += 1
    return es.enter_context(nc.sbuf_tensor("%s_u%d" % (name, _UID[0]), shape, dt))


def ps(nc, es, name, shape, dt):
    return es.enter_context(nc.psum_tensor(name, shape, dt))


class Cfg:
    def __init__(self, nseq=4, L=2048, layers=(0, 1, 2, 0), do_ffn=True):
        self.nseq = nseq
        self.L = L
        self.layers = tuple(layers)
        self.do_ffn = do_ffn
        self.depth = len(self.layers)
        self.n_ssd = sum(1 for k in self.layers if k == 0)
        self.n_s5 = sum(1 for k in self.layers if k == 1)
        self.n_dsa = sum(1 for k in self.layers if k == 2)


TWO_PI = 2.0 * math.pi
GELU_C0 = math.sqrt(2.0 / math.pi)
GELU_C1 = 0.044715
MAGIC = 12582912.0


def build(cfg):
    nc = bass.Bass("TRN2", target_bir_lowering=False)
    L = cfg.L
    NS = cfg.nseq
    NT = L // 512
    NB = L // 128
    depth = cfg.depth

    def din(name, shape, dt=F32):
        return nc.dram_tensor(name, list(shape), dt, kind="ExternalInput").ap()

    x_d = din("x", [NS, L, D])
    y_d = nc.dram_tensor("y", [NS, L, D], F32, kind="ExternalOutput").ap()
    gains_d = din("gains", [128, 3 * depth + 1, DC])
    ident_d = din("ident", [128, 128])
    wi_d = din("ffn_wi", [2 * depth, FC, 128, DC * 256])
    wo_d = din("ffn_wo", [2 * depth, 2, DC, 128, 11 * 128])
    if cfg.n_s5:
        n5 = cfg.n_s5
        s5p_d = din("s5_par", [n5, 3, 128, 64])
        s5b_d = din("s5_b", [n5, 2, 64, 1024])
        s5c_d = din("s5_c", [n5, 2, 128, 1024])
        s5k_d = din("s5_k", [128, 16])
        s5d_d = din("s5_dv", [n5, 128, 24])
        s5w_d = din("s5_glu", [n5, DC, 128, DC * 256])
        iota_d = din("iota16", [128, L], mybir.dt.int16)

    if cfg.n_ssd:
        n0 = cfg.n_ssd
        ssd_win_d = din("ssd_win", [n0, 24, 128, DC * 256])
        ssd_wdt_d = din("ssd_wdt", [n0, 128, DC * 32])
        ssd_wout_d = din("ssd_wout", [n0, DC, 128, 16 * 128])
        ssd_cw_d = din("ssd_cw", [n0, 128, 32 * 5])
        ssd_hp_d = din("ssd_hp", [n0, 128, 3 * 32 + 16])
        ssd_k_d = din("ssd_k", [128, 128 + 512 + 128])

    if cfg.n_dsa:
        n2 = cfg.n_dsa
        dsa_win_d = din("dsa_win", [n2, 14, 128, DC * 256])
        dsa_wvw_d = din("dsa_wvw", [n2, 128, DC * 72])
        dsa_wo_d = din("dsa_wo", [n2, 4, 128, DC * 256])
        dsa_rope_d = din("dsa_rope", [2, 128, L])
        dsa_k_d = din("dsa_k", [128, 640])

    with ExitStack() as es:
        P = Prog(nc, es)
        I = P.I
        xT = sb(nc, es, "xT", [128, DC, L], F32)
        hT = sb(nc, es, "hT", [128, DC, L], BF16)
        gains = sb(nc, es, "gains_sb", [128, 3 * depth + 1, DC], F32)
        ident = sb(nc, es, "ident_sb", [128, 128], F32)
        ones_bf = sb(nc, es, "ones_bf", [128, 128], BF16)
        epsb = sb(nc, es, "epsb", [128, 1], F32)
        negpi = sb(nc, es, "negpi", [128, 1], F32)
        sq = [sb(nc, es, "sq%d" % i, [128, 512], BF16) for i in range(2)]
        tA = [sb(nc, es, "tA%d" % i, [128, 512], F32) for i in range(2)]
        tB = [sb(nc, es, "tB%d" % i, [128, 512], F32) for i in range(2)]
        NWI = 2
        wi = [sb(nc, es, "wi%d" % i, [128, DC, 256], BF16) for i in range(NWI)]
        pb = [ps(nc, es, "pb%d" % i, [128, 512], F32) for i in range(7)]
        pbh = ps(nc, es, "pbh", [128, 1024], BF16)
        ident_bf = sb(nc, es, "ident_bf", [128, 128], BF16)

        cnt = {"wi": 0, "wo": 0, "sq": 0, "tA": 0, "tB": 0, "xin": 0, "pg": 0, "po": 0}

        I("sp", lambda e: e.dma_start(out=gains[:], in_=gains_d), W=["gains"], dma="gains")
        I("sp", lambda e: e.dma_start(out=ident[:], in_=ident_d), W=["ident"], dma="ident")
        I("dve", lambda e: e.memset(ones_bf[:], 1.0), W=["ones"])
        I("dve", lambda e: e.tensor_copy(out=ident_bf[:], in_=ident[:]), R=["ident"], W=["ident_bf"])
        oneb = sb(nc, es, "oneb", [128, 1], F32)
        I("dve", lambda e: e.memset(oneb[:], 1.0), W=["oneb"])
        I("dve", lambda e: e.memset(epsb[:], EPS), W=["epsb"])
        I("dve", lambda e: e.memset(negpi[:], -math.pi), W=["negpi"])
        halfpi = sb(nc, es, "halfpi", [128, 1], F32)
        I("dve", lambda e: e.memset(halfpi[:], math.pi / 2), W=["negpi"])

        def nxt(name, n=2):
            k = cnt[name] % n
            cnt[name] += 1
            return k

        def rmsnorm_to(dst_fn, gidx, dst_key_fn, tt):
            ts = slice(tt * 512, (tt + 1) * 512)
            pn = pb[6]
            for c in range(DC):
                k = nxt("sq")
                I("act", lambda e, c=c, k=k: e.activation(out=sq[k][:], in_=xT[:, c, ts], func=AF.Square),
                  R=[("xT", c, tt)], W=[("sq", k)])
                I("pe", lambda e, c=c, k=k: e.matmul(pn[:], lhsT=ones_bf[:], rhs=sq[k][:],
                                                      start=(c == 0), stop=(c == DC - 1)),
                  R=[("sq", k), "ones"], W=["pb6"])
            k = nxt("tA")
            I("act", lambda e, k=k: e.activation(out=tA[k][:], in_=pn[:], func=AF.Sqrt,
                                                 scale=1.0 / D, bias=epsb[:]),
              R=["pb6", "epsb"], W=[("tA", k)])
            I("dve", lambda e, k=k: e.reciprocal(out=tA[k][:], in_=tA[k][:]),
              R=[("tA", k)], W=[("tA", k)])
            for c in range(DC):
                I("dve", lambda e, c=c, k=k: e.scalar_tensor_tensor(
                    out=dst_fn(c), in0=xT[:, c, ts], scalar=gains[:, gidx, c:c + 1], in1=tA[k][:],
                    op0=ALU.mult, op1=ALU.mult),
                  R=[("xT", c, tt), ("tA", k), "gains"], W=[dst_key_fn(c)])

        def norm_to_hT(gidx):
            for tt in range(NT):
                ts = slice(tt * 512, (tt + 1) * 512)
                rmsnorm_to(lambda c, ts=ts: hT[:, c, ts], gidx, lambda c, tt=tt: ("hT", c, tt), tt)

        def load_w256(src_ap, key="wi"):
            s = nxt("wi", NWI)
            I("pool", lambda e, s=s: e.dma_start(out=wi[s][:], in_=src_ap.rearrange("p (k f) -> p k f", k=DC)),
              W=[("wi", s)], dma=("wi", s))
            return s

        def mm_pair(s, src, src_key_fn, tt):
            ts = slice(tt * 512, (tt + 1) * 512)
            b = nxt("pg")
            pg, pu = pb[2 * b], pb[2 * b + 1]
            for k in range(DC):
                I("pe", lambda e, k=k: e.matmul(pg[:], lhsT=wi[s][:, k, 0:128], rhs=src[:, k, ts],
                                                start=(k == 0), stop=(k == DC - 1)),
                  R=[("wi", s), src_key_fn(k, tt)], W=["pb%d" % (2 * b)])
            for k in range(DC):
                I("pe", lambda e, k=k: e.matmul(pu[:], lhsT=wi[s][:, k, 128:256], rhs=src[:, k, ts],
                                                start=(k == 0), stop=(k == DC - 1)),
                  R=[("wi", s), src_key_fn(k, tt)], W=["pb%d" % (2 * b + 1)])
            return pg, pu, "pb%d" % (2 * b), "pb%d" % (2 * b + 1)

        def ffn(fidx, gidx):
            def norm_tile(tt):
                ts = slice(tt * 512, (tt + 1) * 512)
                rmsnorm_to(lambda c, ts=ts: hT[:, c, ts], gidx, lambda c, tt=tt: ("hT", c, tt), tt)
            with ExitStack() as sc:
                aT = sb(nc, sc, "aT", [128, 11, L], BF16)
                NWO = 3
                wo = [sb(nc, sc, "wo%d" % i, [128, 11, 128], BF16) for i in range(NWO)]
                NWF = 4
                wf = [sb(nc, sc, "wf%d" % i, [128, DC, 256], BF16) for i in range(NWF)]
                fcn = [0]

                def load_wf(src_ap):
                    s = fcn[0] % NWF
                    fcn[0] += 1
                    I("pool", lambda e, s=s: e.dma_start(out=wf[s][:], in_=src_ap.rearrange("p (k f) -> p k f", k=DC)),
                      W=[("wf", s)], dma=("wf", s))
                    return s
                pre = [load_wf(wi_d[fidx, fc]) for fc in range(min(NWF - 1, FC))]
                NP = max(NT // 2, 1)
                TW = NT // NP
                for t0 in range(TW):
                    norm_tile(t0)
                bp = [0]

                def next_banks():
                    k0 = (bp[0] % 3) * 2
                    bp[0] += 1
                    return [k0 + i for i in range(TW)]
                for hf in range(2):
                    for j in range(11):
                        fc = hf * 11 + j
                        s = pre[fc] if fc < len(pre) else load_wf(wi_d[fidx, fc])
                        for tp in range(NP):
                            tts = [tp * TW + i for i in range(TW)]
                            gb = next_banks()
                            ub = next_banks()
                            for half, banks in ((0, gb), (1, ub)):
                                for k in range(DC):
                                    for i, tt in enumerate(tts):
                                        ts = slice(tt * 512, (tt + 1) * 512)
                                        I("pe", lambda e, k=k, bk=banks[i], s=s, ts=ts, half=half: e.matmul(
                                            pb[bk][:], lhsT=wf[s][:, k, half * 128:(half + 1) * 128], rhs=hT[:, k, ts],
                                            start=(k == 0), stop=(k == DC - 1)),
                                          R=[("wf", s), ("hT", k, tt)], W=["pb%d" % banks[i]])
                            if fc == 0 and tp + 1 < NP:
                                for i in range(TW):
                                    norm_tile((tp + 1) * TW + i)
                            for i, tt in enumerate(tts):
                                ts = slice(tt * 512, (tt + 1) * 512)
                                g = nxt("tB")
                                I("act", lambda e, g=g, bk=gb[i]: e.activation(out=tB[g][:], in_=pb[bk][:], func=AF.Silu),
                                  R=["pb%d" % gb[i]], W=[("tB", g)])
                                I("dve", lambda e, g=g, bk=ub[i], j=j, ts=ts: e.tensor_tensor(
                                    out=aT[:, j, ts], in0=tB[g][:], in1=pb[bk][:], op=ALU.mult),
                                  R=[("tB", g), "pb%d" % ub[i]], W=[("aT", j, tt)])
                    for c in range(DC):
                        s = nxt("wo", NWO)
                        I("pool", lambda e, s=s, c=c, hf=hf: e.dma_start(
                            out=wo[s][:], in_=wo_d[fidx, hf, c].rearrange("p (j d) -> p j d", j=11)),
                          W=[("wo", s)], dma=("wo", s))
                        for tp in range(NP):
                            tts = [tp * TW + i for i in range(TW)]
                            ob = next_banks()
                            for j in range(11):
                                for i, tt in enumerate(tts):
                                    ts = slice(tt * 512, (tt + 1) * 512)
                                    I("pe", lambda e, s=s, j=j, ts=ts, bk=ob[i]: e.matmul(
                                        pb[bk][:], lhsT=wo[s][:, j, :], rhs=aT[:, j, ts],
                                        start=(j == 0), stop=(j == 10)),
                                      R=[("wo", s), ("aT", j, tt)], W=["pb%d" % ob[i]])
                            for i, tt in enumerate(tts):
                                ts = slice(tt * 512, (tt + 1) * 512)
                                I("dve", lambda e, bk=ob[i], c=c, ts=ts: e.scalar_tensor_tensor(
                                    out=xT[:, c, ts], in0=pb[bk][:], scalar=0.5, in1=xT[:, c, ts],
                                    op0=ALU.mult, op1=ALU.add),
                                  R=["pb%d" % ob[i], ("xT", c, tt)], W=[("xT", c, tt)])
                P.release_prefix({"aT", "wo", "wf"})

        def ssd_mixer(j0, gidx):
            norm_to_hT(gidx)
            TT2 = 256
            with ExitStack() as sc:
                def T(name, shape, dt=F32):
                    return sb(nc, sc, "ssd_" + name, shape, dt)
                xc = T("xc", [128, 32, TT2], BF16)
                zT = T("zT", [128, 16, TT2], BF16)
                ynT = T("ynT", [128, 16, TT2], BF16)
                halo = T("halo", [128, 32, 3])
                xp = [T("xp%d" % i, [128, 3 + TT2]) for i in range(2)]
                acc = [T("acc%d" % i, [128, TT2]) for i in range(2)]
                wdt = T("wdt", [128, DC, 32], BF16)
                cw = T("cw", [128, 32, 5])
                hp = T("hp", [128, 3 * 32 + 16])
                Aneg = T("Aneg", [128, 32])
                dtr = T("dtr", [128, 32])
                dtv = T("dtv", [128, 32])
                av = T("av", [128, 32])
                ncs = T("ncs", [128, 32])
                ecs = T("ecs", [128, 32])
                cl = T("cl", [128, 32])
                wl = T("wl", [128, 32])
                dec = T("dec", [128, 32])
                xs_tok = T("xs_tok", [128, 32, 64], BF16)
                dtx = T("dtx", [128, 32, 64], BF16)
                Btok = T("Btok", [128, 8, 128], BF16)
                aU = [[T("aU%d_%d" % (i, j), [128, 4, 128], BF16) for j in range(3)] for i in range(2)]
                asp = [T("asp%d" % j, [128, 32], BF16) for j in range(3)]
                ares = [T("ares%d" % j, [128, 32]) for j in range(2)]
                kcb = T("kcb", [128, 640], BF16)
                Lt = [T("Lt%d" % i, [128, 4, 128], BF16) for i in range(1)]
                Mt = [T("Mt%d" % i, [128, 4, 128], BF16) for i in range(2)]
                CBs = [T("CBs%d" % i, [128, 128], BF16) for i in range(1)]
                tm1 = [tA[i][:, 0:256] for i in range(2)]
                tm2 = [tB[i][:, 0:256] for i in range(2)]
                Y = T("Y", [128, 2048])
                sz = T("sz", [128, 2048])
                Ynb = T("Ynb", [128, 2048], BF16)
                ss = T("ss", [128, 8])
                S = T("S", [128, 8, 256])
                Sbf = T("Sbf", [128, 8, 256], BF16)
                dtb, alog, Dbc, gn = hp[:, 0:32], hp[:, 32:64], hp[:, 64:96], hp[:, 96:112]

                def K_(n):
                    return "ssd_" + n

                def ld(dst, src, key, eng="sp"):
                    I(eng, lambda e: e.dma_start(out=dst, in_=src), W=[key], dma=key)
                ld(wdt[:], ssd_wdt_d[j0].rearrange("p (k f) -> p k f", k=DC), K_("wdt"), "pool")
                ld(cw[:], ssd_cw_d[j0].rearrange("p (c f) -> p c f", c=32), K_("cw"))
                ld(hp[:], ssd_hp_d[j0], K_("hp"))
                I("pool", lambda e: e.dma_start(out=kcb[:], in_=ssd_k_d[:, 0:640]), W=[K_("kcb")], dma=K_("kcb"))
                Ub = kcb[:, 0:128]
                c4b = kcb[:, 128:640]
                I("act", lambda e: e.activation(out=Aneg[:], in_=alog, func=AF.Exp), R=[K_("hp")], W=[K_("Aneg")])
                I("dve", lambda e: e.tensor_scalar(out=Aneg[:], in0=Aneg[:], scalar1=-1.0, scalar2=None, op0=ALU.mult),
                  R=[K_("Aneg")], W=[K_("Aneg")])
                I("dve", lambda e: e.memset(S[:], 0.0), W=[K_("S")])
                I("dve", lambda e: e.memset(Sbf[:], 0.0), W=[K_("Sbf")])
                I("dve", lambda e: e.memset(halo[:], 0.0), W=[K_("halo")])
                cn = {"xp": 0, "g": 0}
                pend = []
                print("ssd sbuf remaining", nc.sbuf_bytes_remaining)

                for t2 in range(L // TT2):
                    tsl = slice(t2 * TT2, (t2 + 1) * TT2)
                    tt = (t2 * TT2) // 512
                    for blk in range(24):
                        s = load_w256(ssd_win_d[j0, blk])
                        b = nxt("pg")
                        pg, pu = pb[2 * b], pb[2 * b + 1]
                        for hh, (pp, kp) in enumerate(((pg, "pb%d" % (2 * b)), (pu, "pb%d" % (2 * b + 1)))):
                            for k in range(DC):
                                I("pe", lambda e, k=k, pp=pp, hh=hh: e.matmul(pp[:, 0:TT2], lhsT=wi[s][:, k, hh * 128:(hh + 1) * 128],
                                                                             rhs=hT[:, k, tsl], start=(k == 0), stop=(k == DC - 1)),
                                  R=[("wi", s), ("hT", k, tt)], W=[kp])
                            ch = 2 * blk + hh
                            if ch < 16:
                                I("act", lambda e, pp=pp, ch=ch: e.activation(out=zT[:, ch, :], in_=pp[:, 0:TT2], func=AF.Copy),
                                  R=[kp], W=[K_("zT")])
                            else:
                                xch = ch - 16
                                r = cn["xp"] % 2
                                cn["xp"] += 1
                                I("act", lambda e, pp=pp, r=r: e.activation(out=xp[r][:, 3:3 + TT2], in_=pp[:, 0:TT2], func=AF.Copy),
                                  R=[kp], W=[(K_("xp"), r)])
                                I("act", lambda e, r=r, xch=xch: e.activation(out=xp[r][:, 0:3], in_=halo[:, xch, :], func=AF.Copy),
                                  R=[(K_("halo"), xch), K_("halo")], W=[(K_("xp"), r)])
                                I("act", lambda e, r=r, xch=xch: e.activation(out=halo[:, xch, :], in_=xp[r][:, TT2:TT2 + 3], func=AF.Copy),
                                  R=[(K_("xp"), r)], W=[(K_("halo"), xch)])
                                while len(pend) > 0:
                                    pend.pop(0)()
                                I("dve", lambda e, r=r, xch=xch: e.tensor_scalar(out=acc[r][:], in0=xp[r][:, 0:TT2], scalar1=cw[:, xch, 0:1],
                                                                                scalar2=None, op0=ALU.mult),
                                  R=[(K_("xp"), r), K_("cw")], W=[(K_("acc"), r)])
                                for tap in range(1, 4):
                                    I("dve", lambda e, r=r, xch=xch, tap=tap: e.scalar_tensor_tensor(
                                        out=acc[r][:], in0=xp[r][:, tap:tap + TT2], scalar=cw[:, xch, tap:tap + 1], in1=acc[r][:],
                                        op0=ALU.mult, op1=ALU.add),
                                      R=[(K_("xp"), r), K_("cw"), (K_("acc"), r)], W=[(K_("acc"), r)])
                                def silu_later(r=r, xch=xch):
                                    I("act", lambda e: e.activation(out=xc[:, xch, :], in_=acc[r][:], func=AF.Silu,
                                                                    bias=cw[:, xch, 4:5]),
                                      R=[(K_("acc"), r), K_("cw")], W=[(K_("xc"), xch)])
                                pend.append(silu_later)
                    while len(pend) > 0:
                        pend.pop(0)()
                    STG = int(os.environ.get('DBG_SSD', 99))
                    for ck in range(TT2 // 128 if STG >= 2 else 0):
                        lsl = slice(ck * 128, (ck + 1) * 128)
                        asl = slice(t2 * TT2 + ck * 128, t2 * TT2 + (ck + 1) * 128)
                        pdt = pb[3][:, 0:32]
                        pcs = pb[3][:, 64:96]
                        for k in range(DC):
                            I("pe", lambda e, k=k: e.matmul(pdt, lhsT=hT[:, k, asl], rhs=wdt[:, k, :], start=(k == 0), stop=(k == DC - 1)),
                              R=[("hT", k, tt), K_("wdt")], W=["pb3"])
                        SUB = int(os.environ.get('DBG_SUB', 99))
                        if SUB < 2:
                            continue
                        I("dve", lambda e: e.tensor_tensor(out=dtr[:], in0=pdt, in1=dtb, op=ALU.add), R=["pb3", K_("hp")], W=[K_("dtr")])
                        if SUB < 3:
                            continue
                        I("act", lambda e: e.activation(out=dtr[:], in_=dtr[:], func=AF.Exp), R=[K_("dtr")], W=[K_("dtr")])
                        I("act", lambda e: e.activation(out=dtv[:], in_=dtr[:], func=AF.Ln, bias=oneb[:]), R=[K_("dtr"), "oneb"], W=[K_("dtv")])
                        if SUB < 4:
                            continue
                        I("dve", lambda e: e.tensor_tensor(out=av[:], in0=dtv[:], in1=Aneg[:], op=ALU.mult), R=[K_("dtv"), K_("Aneg")], W=[K_("av")])
                        if SUB < 5:
                            continue
                        I("dve", lambda e: e.tensor_copy(out=asp[0][:], in_=av[:]), R=[K_("av")], W=[(K_("asp"), 0)])
                        I("dve", lambda e: e.tensor_tensor(out=ares[0][:], in0=av[:], in1=asp[0][:], op=ALU.subtract),
                          R=[K_("av"), (K_("asp"), 0)], W=[(K_("ares"), 0)])
                        I("dve", lambda e: e.tensor_copy(out=asp[1][:], in_=ares[0][:]), R=[(K_("ares"), 0)], W=[(K_("asp"), 1)])
                        I("dve", lambda e: e.tensor_tensor(out=ares[1][:], in0=ares[0][:], in1=asp[1][:], op=ALU.subtract),
                          R=[(K_("ares"), 0), (K_("asp"), 1)], W=[(K_("ares"), 1)])
                        I("dve", lambda e: e.tensor_copy(out=asp[2][:], in_=ares[1][:]), R=[(K_("ares"), 1)], W=[(K_("asp"), 2)])
                        for j3 in range(3):
                            I("pe", lambda e, j3=j3: e.matmul(pcs, lhsT=Ub, rhs=asp[j3][:], start=(j3 == 0), stop=(j3 == 2)),
                              R=[K_("kcb"), (K_("asp"), j3)], W=["pb3"])
                        I("dve", lambda e: e.tensor_scalar(out=ncs[:], in0=pcs, scalar1=-1.0, scalar2=None, op0=ALU.mult),
                          R=["pb3"], W=[K_("ncs")])
                        I("act", lambda e: e.activation(out=ecs[:], in_=pcs, func=AF.Exp), R=["pb3"], W=[K_("ecs")])
                        if STG < 3:
                            continue
                        for r in range(2):
                            for q in range(8):
                                I("pe", lambda e, r=r, q=q: e.transpose(out=pbh[:, q * 128:(q + 1) * 128], in_=xc[:, 8 * r + q, lsl],
                                                                       identity=ident_bf[:]),
                                  R=[(K_("xc"), 8 * r + q), "ident_bf"], W=["pbh"])
                            I("dve", lambda e, r=r: e.tensor_tensor(out=xs_tok[:, 16 * r:16 * r + 16, :],
                                                                    in0=pbh[:].rearrange("p (h d) -> p h d", h=16),
                                                                    in1=Dbc[:, 16 * r:16 * r + 16].unsqueeze(2).to_broadcast([128, 16, 64]),
                                                                    op=ALU.mult),
                              R=["pbh", K_("hp")], W=[K_("xs_tok")])
                            I("dve", lambda e, r=r: e.tensor_tensor(out=dtx[:, 16 * r:16 * r + 16, :],
                                                                    in0=pbh[:].rearrange("p (h d) -> p h d", h=16),
                                                                    in1=dtv[:, 16 * r:16 * r + 16].unsqueeze(2).to_broadcast([128, 16, 64]),
                                                                    op=ALU.mult),
                              R=["pbh", K_("dtv")], W=[K_("dtx")])
                        for q in range(8):
                            I("pe", lambda e, q=q: e.transpose(out=pbh[:, q * 128:(q + 1) * 128], in_=xc[:, 16 + q, lsl], identity=ident_bf[:]),
                              R=[(K_("xc"), 16 + q), "ident_bf"], W=["pbh"])
                        I("act", lambda e: e.activation(out=Btok[:], in_=pbh[:].rearrange("p (g n) -> p g n", g=8), func=AF.Copy),
                          R=["pbh"], W=[K_("Btok")])
                        if STG < 4:
                            continue
                        def GA1(g):
                            i2 = g % 2
                            X = pb[i2]
                            kX = "pb%d" % i2
                            for j3 in range(3):
                                I("dve", lambda e, g=g, i2=i2, j3=j3: e.tensor_tensor(
                                    out=aU[i2][j3][:], in0=Ub.unsqueeze(1).to_broadcast([128, 4, 128]),
                                    in1=asp[j3][:, 4 * g:4 * g + 4].unsqueeze(2).to_broadcast([128, 4, 128]), op=ALU.mult),
                                  R=[K_("kcb"), (K_("asp"), j3)], W=[(K_("aU"), i2, j3)])
                                I("pe", lambda e, i2=i2, X=X, j3=j3: e.matmul(X[:], lhsT=ones_bf[:], rhs=aU[i2][j3][:].rearrange("p h t -> p (h t)"),
                                                                          start=(j3 == 0), stop=False),
                                  R=["ones", (K_("aU"), i2, j3)], W=[kX])
                            I("pe", lambda e, X=X: e.matmul(X[:], lhsT=ident_bf[:], rhs=c4b, start=False, stop=True),
                              R=[K_("kcb"), "ident_bf"], W=[kX])
                        def GA2(g):
                            i2 = g % 2
                            X = pb[i2]
                            kX = "pb%d" % i2
                            for h in range(4):
                                I("act", lambda e, h=h, g=g, i2=i2, X=X: e.activation(
                                    out=Lt[0][:, h, :], in_=X[:, h * 128:(h + 1) * 128], func=AF.Exp,
                                    bias=ncs[:, 4 * g + h:4 * g + h + 1]),
                                  R=[kX, K_("ncs")], W=[(K_("Lt"), 0)])
                            I("dve", lambda e, g=g, X=X: e.tensor_copy(
                                out=cl[:, 4 * g:4 * g + 4].unsqueeze(2),
                                in_=X[:].rearrange("p (h t) -> p h t", h=4)[:, :, 127:128]),
                              R=[kX], W=[(K_("cl"), g)])
                            CBp = pb[2][:, 0:128]
                            I("pe", lambda e, g=g, CBp=CBp: e.matmul(CBp, lhsT=xc[:, 16 + g, lsl], rhs=xc[:, 24 + g, lsl], start=True, stop=True),
                              R=[(K_("xc"), 16 + g), (K_("xc"), 24 + g)], W=["pb2"])
                            I("act", lambda e, i2=i2, CBp=CBp: e.activation(out=CBs[0][:], in_=CBp, func=AF.Copy),
                              R=["pb2"], W=[(K_("CBs"), 0)])
                            I("dve", lambda e, i2=i2: e.tensor_tensor(out=Mt[i2][:], in0=Lt[0][:],
                                                                      in1=CBs[0][:].unsqueeze(1).to_broadcast([128, 4, 128]), op=ALU.mult),
                              R=[(K_("Lt"), 0), (K_("CBs"), 0)], W=[(K_("Mt"), i2)])
                        def GB(g):
                            i2 = g % 2
                            pY = pb[5 + i2]
                            kY = "pb%d" % (5 + i2)
                            for h in range(4):
                                I("pe", lambda e, h=h, g=g, i2=i2: e.matmul(pY[:, 256 + h * 64:256 + (h + 1) * 64], lhsT=Mt[i2][:, h, :],
                                                                           rhs=dtx[:, 4 * g + h, :], start=True, stop=True),
                                  R=[(K_("Mt"), i2), K_("dtx")], W=[kY])
                            I("pe", lambda e, g=g: e.matmul(pY[:, 0:256], lhsT=xc[:, 24 + g, lsl], rhs=Sbf[:, g, :], start=True, stop=True),
                              R=[(K_("xc"), 24 + g), (K_("Sbf"), g)], W=[kY])
                            I("dve", lambda e, g=g, i2=i2: e.tensor_tensor(
                                out=tm1[i2].rearrange("p (h d) -> p h d", h=4), in0=pY[:, 0:256].rearrange("p (h d) -> p h d", h=4),
                                in1=ecs[:, 4 * g:4 * g + 4].unsqueeze(2).to_broadcast([128, 4, 64]), op=ALU.mult),
                              R=[kY, K_("ecs")], W=[("tA", i2)])
                            I("dve", lambda e, i2=i2: e.tensor_tensor(out=tm1[i2], in0=tm1[i2], in1=pY[:, 256:512], op=ALU.add),
                              R=[kY, ("tA", i2)], W=[("tA", i2)])
                            I("dve", lambda e, g=g, i2=i2: e.tensor_tensor(out=Y[:, g * 256:(g + 1) * 256], in0=tm1[i2],
                                                                         in1=xs_tok[:, 4 * g:4 * g + 4, :].rearrange("p h d -> p (h d)"), op=ALU.add),
                              R=[("tA", i2), K_("xs_tok")], W=[(K_("Y"), g)])
                        GA1(0)
                        GA1(1)
                        GA2(0)
                        for g in range(8):
                            if g + 2 < 8:
                                GA1(g + 2)
                            if g + 1 < 8:
                                GA2(g + 1)
                            GB(g)
                        I("dve", lambda e: e.tensor_tensor(out=wl[:], in0=cl[:], in1=ncs[:], op=ALU.add),
                          R=[(K_("cl"), g) for g in range(8)] + [K_("ncs")], W=[K_("wl")])
                        I("act", lambda e: e.activation(out=wl[:], in_=wl[:], func=AF.Exp), R=[K_("wl")], W=[K_("wl")])
                        I("act", lambda e: e.activation(out=dec[:], in_=cl[:], func=AF.Exp), R=[(K_("cl"), g) for g in range(8)], W=[K_("dec")])
                        I("dve", lambda e: e.tensor_tensor(out=dtx[:], in0=dtx[:], in1=wl[:].unsqueeze(2).to_broadcast([128, 32, 64]), op=ALU.mult),
                          R=[K_("dtx"), K_("wl")], W=[K_("dtx")])
                        for g in range(8):
                            i2 = g % 2
                            SU = pb[4][:, 0:256]
                            I("pe", lambda e, g=g, SU=SU: e.matmul(SU, lhsT=Btok[:, g, :], rhs=dtx[:, 4 * g:4 * g + 4, :].rearrange("p h d -> p (h d)"),
                                                                  start=True, stop=True),
                              R=[K_("Btok"), K_("dtx")], W=["pb4"])
                            I("dve", lambda e, g=g: e.tensor_tensor(
                                out=S[:, g, :].rearrange("p (h d) -> p h d", h=4), in0=S[:, g, :].rearrange("p (h d) -> p h d", h=4),
                                in1=dec[:, 4 * g:4 * g + 4].unsqueeze(2).to_broadcast([128, 4, 64]), op=ALU.mult),
                              R=[(K_("S"), g), K_("S"), K_("dec")], W=[(K_("S"), g)])
                            I("dve", lambda e, g=g, SU=SU: e.tensor_tensor(out=S[:, g, :], in0=S[:, g, :], in1=SU, op=ALU.add),
                              R=[(K_("S"), g), "pb4"], W=[(K_("S"), g)])
                            I("act", lambda e, g=g: e.activation(out=Sbf[:, g, :], in_=S[:, g, :], func=AF.Copy),
                              R=[(K_("S"), g), K_("Sbf")], W=[(K_("Sbf"), g)])
                        if STG < 6:
                            continue
                        for r in range(2):
                            for q in range(8):
                                I("pe", lambda e, r=r, q=q: e.transpose(out=pbh[:, q * 128:(q + 1) * 128], in_=zT[:, 8 * r + q, lsl],
                                                                       identity=ident_bf[:]),
                                  R=[K_("zT"), "ident_bf"], W=["pbh"])
                            I("act", lambda e, r=r: e.activation(out=sz[:, r * 1024:(r + 1) * 1024], in_=pbh[:], func=AF.Silu),
                              R=["pbh"], W=[K_("sz")])
                        I("dve", lambda e: e.tensor_tensor(out=Y[:], in0=Y[:], in1=sz[:], op=ALU.mult),
                          R=[(K_("Y"), g) for g in range(8)] + [K_("sz")], W=[K_("Y2")])
                        I("act", lambda e: e.activation(out=sz[:], in_=Y[:], func=AF.Square), R=[K_("Y2")], W=[K_("sz")])
                        I("dve", lambda e: e.tensor_reduce(out=ss[:], in_=sz[:].rearrange("p (g c) -> p g c", g=8), axis=AX.X, op=ALU.add),
                          R=[K_("sz")], W=[K_("ss")])
                        I("act", lambda e: e.activation(out=ss[:], in_=ss[:], func=AF.Sqrt, scale=1.0 / 256.0, bias=epsb[:]),
                          R=[K_("ss"), "epsb"], W=[K_("ss")])
                        I("dve", lambda e: e.reciprocal(out=ss[:], in_=ss[:]), R=[K_("ss")], W=[K_("ss")])
                        I("dve", lambda e: e.tensor_tensor(out=Ynb[:].rearrange("p (g c) -> p g c", g=8), in0=Y[:].rearrange("p (g c) -> p g c", g=8),
                                                           in1=ss[:].unsqueeze(2).to_broadcast([128, 8, 256]), op=ALU.mult),
                          R=[K_("Y2"), K_("ss")], W=[K_("Ynb")] + [(K_("Y"), g) for g in range(8)])
                        for r in range(2):
                            for q in range(8):
                                I("pe", lambda e, r=r, q=q: e.transpose(out=pbh[:, q * 128:(q + 1) * 128],
                                                                       in_=Ynb[:, (8 * r + q) * 128:(8 * r + q + 1) * 128], identity=ident_bf[:]),
                                  R=[K_("Ynb"), "ident_bf"], W=["pbh"])
                            I("dve", lambda e, r=r: e.tensor_tensor(out=ynT[:, 8 * r:8 * r + 8, lsl], in0=pbh[:].rearrange("p (c t) -> p c t", c=8),
                                                                    in1=gn[:, 8 * r:8 * r + 8].unsqueeze(2).to_broadcast([128, 8, 128]), op=ALU.mult),
                              R=["pbh", K_("hp")], W=[K_("ynT")])
                    for dc in range(DC if STG >= 7 else 0):
                        s = load_w256(ssd_wout_d[j0, dc])
                        wv = wi[s][:].rearrange("p k (a f) -> p (k a) f", a=2)
                        b = nxt("po")
                        po = pb[4 + b] if False else pb[6]
                        for c in range(16):
                            I("pe", lambda e, c=c, wv=wv: e.matmul(pb[6][:, 0:TT2], lhsT=wv[:, c, :], rhs=ynT[:, c, :], start=(c == 0), stop=(c == 15)),
                              R=[("wi", s), K_("ynT")], W=["pb6"])
                        I("dve", lambda e, dc=dc: e.tensor_tensor(out=xT[:, dc, tsl], in0=xT[:, dc, tsl], in1=pb[6][:, 0:TT2], op=ALU.add),
                          R=["pb6", ("xT", dc, tt)], W=[("xT", dc, tt)])
                P.release([k for k in list(P.res) if (k[0] if isinstance(k, tuple) else k).startswith("ssd_")])


        def dsa_mixer(j2, gidx):
            norm_to_hT(gidx)
            NEG = -1.0e30
            TOPK = min(256, L // 4)
            NR = TOPK // 8
            with ExitStack() as sc:
                def T(name, shape, dt=F32):
                    return sb(nc, sc, "dsa_" + name, shape, dt)

                def K_(n):
                    return "dsa_" + n
                qT = T("qT", [128, 8, L], BF16)
                K2T = T("K2T", [128, L], BF16)
                qiT = T("qiT", [128, 4, L], BF16)
                ki2T = T("ki2T", [128, L], BF16)
                Vaug = T("Vaug", [128, NB, 65], BF16)
                witok = T("witok", [128, NB, 8])
                I("dve", lambda e: e.memset(Vaug[:], 1.0), W=[K_("Vaug")])
                with ExitStack() as sc2:
                    wvw = sb(nc, sc2, "dsa_wvw", [128, DC, 72], BF16)
                    I("pool", lambda e: e.dma_start(out=wvw[:], in_=dsa_wvw_d[j2].rearrange("p (k f) -> p k f", k=DC)),
                      W=[K_("wvw")], dma=K_("wvw"))
                    cosT = sb(nc, sc2, "dsa_cos", [128, L], F32)
                    sinS = sb(nc, sc2, "dsa_sin", [128, L], F32)
                    I("sp", lambda e: e.dma_start(out=cosT[:], in_=dsa_rope_d[0]), W=[K_("cos")], dma=K_("cos"))
                    I("sp", lambda e: e.dma_start(out=sinS[:], in_=dsa_rope_d[1]), W=[K_("sin")], dma=K_("sin"))
                    for blk in range(14):
                        s = load_w256(dsa_win_d[j2, blk])
                        if blk < 8:
                            dst, dkey, scl = (lambda ts, blk=blk: qT[:, blk, ts]), K_("qT"), 0.125
                        elif blk == 8:
                            dst, dkey, scl = (lambda ts: K2T[:, ts]), K_("K2T"), 1.0
                        elif blk < 13:
                            dst, dkey, scl = (lambda ts, blk=blk: qiT[:, blk - 9, ts]), K_("qiT"), 1.0
                        else:
                            dst, dkey, scl = (lambda ts: ki2T[:, ts]), K_("ki2T"), 1.0
                        for tt in range(NT):
                            ts = slice(tt * 512, (tt + 1) * 512)
                            pg, pu, kg, ku = mm_pair(s, hT, lambda k, tt: ("hT", k, tt), tt)
                            k1 = nxt("tA")
                            k2 = nxt("tB")
                            I("dve", lambda e, k1=k1, pg=pg, ts=ts, scl=scl: e.scalar_tensor_tensor(
                                out=tA[k1][:], in0=pg[:], scalar=scl, in1=cosT[:, ts], op0=ALU.mult, op1=ALU.mult),
                              R=[kg, K_("cos")], W=[("tA", k1)])
                            I("dve", lambda e, k2=k2, pu=pu, ts=ts, scl=scl: e.scalar_tensor_tensor(
                                out=tB[k2][:], in0=pu[:], scalar=scl, in1=sinS[:, ts], op0=ALU.mult, op1=ALU.mult),
                              R=[ku, K_("sin")], W=[("tB", k2)])
                            I("pool", lambda e, k1=k1, k2=k2, ts=ts, dst=dst: e.tensor_tensor(out=dst(ts), in0=tA[k1][:], in1=tB[k2][:], op=ALU.add),
                              R=[("tA", k1), ("tB", k2)], W=[(dkey, tt)])
                    for tb in range(NB):
                        bsl = slice(tb * 128, (tb + 1) * 128)
                        pv = pb[4][:, 0:72]
                        for k in range(DC):
                            I("pe", lambda e, k=k: e.matmul(pv, lhsT=hT[:, k, bsl], rhs=wvw[:, k, :], start=(k == 0), stop=(k == DC - 1)),
                              R=[("hT", k, tb // 4), K_("wvw")], W=["pb4"])
                        I("act", lambda e, tb=tb: e.activation(out=Vaug[:, tb, 0:64], in_=pb[4][:, 0:64], func=AF.Copy),
                          R=["pb4", K_("Vaug")], W=[(K_("Vaug"), tb)])
                        I("dve", lambda e, tb=tb: e.tensor_scalar(out=witok[:, tb, :], in0=pb[4][:, 64:72], scalar1=8.0 ** -0.5 * 64.0 ** -0.5,
                                                                  scalar2=None, op0=ALU.mult),
                          R=["pb4"], W=[(K_("witok"), tb)])
                    P.release([K_("cos"), K_("sin"), K_("wvw")])
                with ExitStack() as sc3:
                    def T3(name, shape, dt=F32):
                        return sb(nc, sc3, "dsa_" + name, shape, dt)
                    idx_ = [T3("idx%d" % i, [128, L]) for i in range(2)]
                    mb_ = [T3("mb%d" % i, [128, L], BF16) for i in range(2)]
                    Pt = [T3("Pt%d" % i, [128, 512], BF16) for i in range(2)]
                    Otok = T3("Otok", [128, 16, 64], BF16)
                    OT = T3("OT", [128, DC, 128], BF16)
                    kd = T3("kd", [128, 128])
                    irep = T3("irep", [128, 512], BF16)
                    m8 = T3("m8", [128, 8])
                    rcp = T3("rcp", [128, 16, 1])
                    zer = T3("zer", [128, 512], BF16)
                    I("dve", lambda e: e.memset(zer[:], 0.0), W=[K_("zer")])
                    I("sp", lambda e: e.dma_start(out=kd[:], in_=dsa_k_d[:, 0:128]), W=[K_("kd")], dma=K_("kd"))
                    I("pool", lambda e: e.dma_start(out=irep[:], in_=dsa_k_d[:, 128:640]), W=[K_("irep")], dma=K_("irep"))
                    negtri = kd[:, 0:128]
                    cn = {"pi": 0, "ps": 0}
                    def stageA(qb):
                        qsl = slice(qb * 128, (qb + 1) * 128)
                        S = 128 * (qb + 1)
                        bi = qb % 2
                        idx, mb = idx_[bi], mb_[bi]
                        npc = (S + 511) // 512
                        for hi in range(8):
                            c, half = hi // 2, hi % 2
                            hs = slice(half * 64, (half + 1) * 64)
                            for pc in range(npc):
                                cols = min(512, S - 512 * pc)
                                csl = slice(512 * pc, 512 * pc + cols)
                                ib = cn["pi"] % 2
                                cn["pi"] += 1
                                pI = pb[5 + ib]
                                I("pe", lambda e, c=c, hs=hs, csl=csl, cols=cols, pI=pI: e.matmul(
                                    pI[:, 0:cols], lhsT=qiT[hs, c, qsl], rhs=ki2T[hs, csl], start=True, stop=True),
                                  R=[(K_("qiT"), qb // 4), (K_("ki2T"), pc)], W=["pb%d" % (5 + ib)])
                                k1 = nxt("tA")
                                I("act", lambda e, k1=k1, cols=cols, pI=pI: e.activation(out=tA[k1][:, 0:cols], in_=pI[:, 0:cols], func=AF.Relu),
                                  R=["pb%d" % (5 + ib)], W=[("tA", k1)])
                                wcol = witok[:, qb, hi:hi + 1]
                                if hi == 0:
                                    I("dve", lambda e, k1=k1, cols=cols, csl=csl, wcol=wcol, idx=idx: e.tensor_scalar(
                                        out=idx[:, csl], in0=tA[k1][:, 0:cols], scalar1=wcol, scalar2=None, op0=ALU.mult),
                                      R=[("tA", k1), (K_("witok"), qb)], W=[(K_("idx"), bi, pc)])
                                else:
                                    I("dve", lambda e, k1=k1, cols=cols, csl=csl, wcol=wcol, idx=idx: e.scalar_tensor_tensor(
                                        out=idx[:, csl], in0=tA[k1][:, 0:cols], scalar=wcol, in1=idx[:, csl], op0=ALU.mult, op1=ALU.add),
                                      R=[("tA", k1), (K_("witok"), qb), (K_("idx"), bi, pc)], W=[(K_("idx"), bi, pc)])
                        allidx = [(K_("idx"), bi, pc) for pc in range(npc)]
                        I("dve", lambda e, S=S, idx=idx: e.tensor_tensor(out=idx[:, S - 128:S], in0=idx[:, S - 128:S], in1=negtri, op=ALU.add),
                          R=allidx + [K_("kd")], W=allidx)
                        if qb >= TOPK // 128:
                            for rd in range(NR):
                                I("dve", lambda e, S=S, idx=idx: e.max(out=m8[:], in_=idx[:, 0:S]), R=allidx, W=[K_("m8")])
                                I("dve", lambda e, S=S, idx=idx: e.match_replace(out=idx[:, 0:S], in_to_replace=m8[:], in_values=idx[:, 0:S],
                                                                                imm_value=NEG),
                                  R=[K_("m8")], W=allidx)
                            I("dve", lambda e, S=S, idx=idx, mb=mb: e.tensor_scalar(out=mb[:, 0:S], in0=idx[:, 0:S], scalar1=NEG, scalar2=-30000.0,
                                                                                  op0=ALU.not_equal, op1=ALU.mult),
                              R=allidx, W=[(K_("mb"), bi)])
                        else:
                            I("dve", lambda e, S=S, idx=idx, mb=mb: e.tensor_scalar(out=mb[:, 0:S], in0=idx[:, 0:S], scalar1=-1.0e29, scalar2=-30000.0,
                                                                                  op0=ALU.is_lt, op1=ALU.mult),
                              R=allidx, W=[(K_("mb"), bi)])
                    def stageB(qb):
                        qsl = slice(qb * 128, (qb + 1) * 128)
                        bi = qb % 2
                        mb = mb_[bi]
                        for bk, nh in ((2, 7), (3, 7), (4, 2)):
                            I("pe", lambda e, bk=bk, nh=nh: e.matmul(pb[bk][:, 0:nh * 65], lhsT=zer[:, 0:128], rhs=zer[:, 0:nh * 65],
                                                                    start=True, stop=False),
                              R=[K_("zer")], W=["pb%d" % bk])
                        for sbk in range(qb + 1):
                            ssl = slice(sbk * 128, (sbk + 1) * 128)
                            for half in range(2):
                                hs = slice(half * 64, (half + 1) * 64)
                                for cg in range(2):
                                    ib = cn["ps"] % 2
                                    cn["ps"] += 1
                                    pS = pb[ib]
                                    I("pe", lambda e, hs=hs, cg=cg, ssl=ssl, pS=pS: e.matmul(
                                        pS[:], lhsT=K2T[hs, ssl], rhs=qT[hs, 4 * cg:4 * cg + 4, qsl], start=True, stop=False),
                                      R=[(K_("K2T"), sbk // 4), (K_("qT"), qb // 4)], W=["pb%d" % ib])
                                    I("pe", lambda e, ssl=ssl, pS=pS, mb=mb: e.matmul(pS[:], lhsT=mb[:, ssl], rhs=irep[:], start=False, stop=True),
                                      R=[(K_("mb"), bi), K_("irep")], W=["pb%d" % ib])
                                    I("act", lambda e, ib=ib, pS=pS: e.activation(out=Pt[ib][:], in_=pS[:], func=AF.Exp),
                                      R=["pb%d" % ib], W=[(K_("Pt"), ib)])
                                    for j in range(4):
                                        head = 2 * (4 * cg + j) + half
                                        bk, off = 2 + head // 7, (head % 7) * 65
                                        I("pe", lambda e, ib=ib, j=j, bk=bk, off=off, sbk=sbk: e.matmul(
                                            pb[bk][:, off:off + 65], lhsT=Pt[ib][:, j * 128:(j + 1) * 128], rhs=Vaug[:, sbk, :],
                                            start=False, stop=False),
                                          R=[(K_("Pt"), ib), (K_("Vaug"), sbk), K_("Vaug")], W=["pb%d" % bk])
                        for bk, nh in ((2, 7), (3, 7), (4, 2)):
                            I("pe", lambda e, bk=bk, nh=nh: e.matmul(pb[bk][:, 0:nh * 65], lhsT=zer[:, 0:128], rhs=zer[:, 0:nh * 65],
                                                                    start=False, stop=True),
                              R=[K_("zer")], W=["pb%d" % bk])
                        for bk, h0, nh in ((2, 0, 7), (3, 7, 7), (4, 14, 2)):
                            pv3 = pb[bk][:, 0:nh * 65].rearrange("p (h e) -> p h e", e=65)
                            I("dve", lambda e, pv3=pv3, h0=h0, nh=nh: e.reciprocal(out=rcp[:, h0:h0 + nh, :], in_=pv3[:, :, 64:65]),
                              R=["pb%d" % bk], W=[(K_("rcp"), bk)])
                            I("dve", lambda e, pv3=pv3, h0=h0, nh=nh: e.tensor_tensor(
                                out=Otok[:, h0:h0 + nh, :], in0=pv3[:, :, 0:64], in1=rcp[:, h0:h0 + nh, :].to_broadcast([128, nh, 64]), op=ALU.mult),
                              R=["pb%d" % bk, (K_("rcp"), bk)], W=[(K_("Otok"), bk)])
                        lq = 0
                        for c in range(8):
                            I("pe", lambda e, c=c: e.transpose(out=pbh[:, c * 128:(c + 1) * 128],
                                                               in_=Otok[:, 2 * c:2 * c + 2, :].rearrange("p h d -> p (h d)"), identity=ident_bf[:]),
                              R=[(K_("Otok"), 2), (K_("Otok"), 3), (K_("Otok"), 4), "ident_bf"], W=["pbh"])
                        I("act", lambda e, lq=lq: e.activation(out=OT[:, :, lq * 128:(lq + 1) * 128],
                                                               in_=pbh[:].rearrange("p (c t) -> p c t", c=8), func=AF.Copy),
                          R=["pbh"], W=[K_("OT")])
                        if True:
                            osl = slice(qb * 128, (qb + 1) * 128)
                            tt = qb // 4
                            for b4 in range(4):
                                s = load_w256(dsa_wo_d[j2, b4])
                                for hh in range(2):
                                    dc = 2 * b4 + hh
                                    po = pb[5 + hh]
                                    for k in range(DC):
                                        I("pe", lambda e, k=k, hh=hh, po=po: e.matmul(po[:, 0:128], lhsT=wi[s][:, k, hh * 128:(hh + 1) * 128],
                                                                                     rhs=OT[:, k, :], start=(k == 0), stop=(k == DC - 1)),
                                          R=[("wi", s), K_("OT")], W=["pb%d" % (5 + hh)])
                                    I("dve", lambda e, dc=dc, po=po: e.tensor_tensor(out=xT[:, dc, osl], in0=xT[:, dc, osl], in1=po[:, 0:128], op=ALU.add),
                                      R=["pb%d" % (5 + hh), ("xT", dc, tt)], W=[("xT", dc, tt)])
                    stageA(0)
                    for qb in range(NB):
                        if qb + 1 < NB:
                            stageA(qb + 1)
                        stageB(qb)
                P.release([k for k in list(P.res) if (k[0] if isinstance(k, tuple) else k).startswith("dsa_")])

        def s5_mixer(j5, gidx):
            norm_to_hT(gidx)
            with ExitStack() as sc:
                def T(name, shape, dt=F32):
                    return sb(nc, sc, "s5_" + name, shape, dt)
                gT = T("gT", [128, DC, L], BF16)
                iota = T("iota", [128, L], mybir.dt.int16)
                par = T("par", [128, 3, 64])
                bre2 = [T("bre%d" % i, [64, 128]) for i in range(2)]
                bim2 = [T("bim%d" % i, [64, 128]) for i in range(2)]
                cst12 = [T("cst1%d" % i, [128, 128]) for i in range(2)]
                cst22 = [T("cst2%d" % i, [128, 128]) for i in range(2)]
                magic = T("magic", [128, 1])
                kk = T("kk", [128, 16])
                dv = T("dv", [128, 24])
                names = ["step", "lr", "th", "r", "f", "raw", "fs", "fc", "ms0", "mc0", "nr", "ni",
                         "den", "inv", "cr", "ci", "u1", "u2"]
                pt_ = {n: T(n, [128, 64]) for n in names}
                state = T("state", [128, 64])
                z = {n: T(n, [64, 128]) for n in ["t1", "t2", "zre", "zim", "nzre"]}
                wA = T("wA", [128, 8, 128], BF16)
                wA2 = T("wA2", [128, 8, 128], BF16)
                W1 = T("W1", [128, 8, 128], BF16)
                W2 = T("W2", [128, 8, 128], BF16)
                raw_ = [T("rawt%d" % i, [128, 512]) for i in range(2)]
                msin_ = [T("msin%d" % i, [128, 512]) for i in range(2)]
                mcos_ = [T("mcos%d" % i, [128, 512]) for i in range(2)]
                bt_ = [T("bt%d" % i, [128, 512]) for i in range(2)]
                st_ = [T("st%d" % i, [128, 512]) for i in range(2)]
                P1_ = [T("P1%d" % i, [128, 512], BF16) for i in range(2)]
                P2_ = [T("P2%d" % i, [128, 512], BF16) for i in range(2)]
                I("dve", lambda e: e.memset(magic[:], MAGIC), W=["s5_magic"])
                itc = [0]
                yv = T("yv", [128, 512])
                zz = T("zz", [128, 512])

                def ld(dst, src, key):
                    I("sp", lambda e: e.dma_start(out=dst, in_=src), W=[key], dma=key)
                ld(iota[:], iota_d, "s5_iota")
                ld(par[:], s5p_d[j5].rearrange("a p g -> p a g"), "s5_par")
                ld(kk[:], s5k_d, "s5_kk")
                ld(dv[:], s5d_d[j5], "s5_dv")

                def dv_(fn, R, W):
                    I("dve", fn, R=["s5_" + r for r in R], W=["s5_" + w for w in W])

                def po_(fn, R, W):
                    I("pool", fn, R=["s5_" + r for r in R], W=["s5_" + w for w in W])

                def ac_(fn, R, W):
                    I("act", fn, R=["s5_" + r for r in R] + ["negpi"], W=["s5_" + w for w in W])
                p = pt_
                lre, lim, lst = par[:, 0, :], par[:, 1, :], par[:, 2, :]
                ac_(lambda e: e.activation(out=p["step"][:], in_=lst, func=AF.Exp), ["par"], ["step"])
                dv_(lambda e: e.tensor_tensor(out=p["lr"][:], in0=lre, in1=p["step"][:], op=ALU.mult), ["par", "step"], ["lr"])
                dv_(lambda e: e.tensor_tensor(out=p["th"][:], in0=lim, in1=p["step"][:], op=ALU.mult), ["par", "step"], ["th"])
                ac_(lambda e: e.activation(out=p["r"][:], in_=p["lr"][:], func=AF.Exp), ["lr"], ["r"])
                dv_(lambda e: e.tensor_scalar(out=p["f"][:], in0=p["th"][:], scalar1=1.0 / TWO_PI, scalar2=None, op0=ALU.mult), ["th"], ["f"])
                dv_(lambda e: e.tensor_scalar(out=p["raw"][:], in0=p["f"][:], scalar1=MAGIC, scalar2=None, op0=ALU.add), ["f"], ["raw"])
                dv_(lambda e: e.scalar_tensor_tensor(out=p["fs"][:], in0=p["raw"][:], scalar=MAGIC, in1=p["f"][:], op0=ALU.subtract, op1=ALU.subtract), ["raw", "f"], ["fs"])
                dv_(lambda e: e.scalar_tensor_tensor(out=p["fc"][:], in0=p["fs"][:], scalar=-1.0, in1=p["fs"][:], op0=ALU.mult, op1=ALU.max), ["fs"], ["fc"])
                ac_(lambda e: e.activation(out=p["ms0"][:], in_=p["fs"][:], func=AF.Sin, scale=-TWO_PI), ["fs"], ["ms0"])
                ac_(lambda e: e.activation(out=p["mc0"][:], in_=p["fc"][:], func=AF.Sin, scale=-TWO_PI, bias=halfpi[:]), ["fc"], ["mc0"])
                dv_(lambda e: e.tensor_tensor(out=p["u1"][:], in0=p["mc0"][:], in1=p["r"][:], op=ALU.mult), ["mc0", "r"], ["u1"])
                dv_(lambda e: e.tensor_scalar(out=p["nr"][:], in0=p["u1"][:], scalar1=-1.0, scalar2=None, op0=ALU.add), ["u1"], ["nr"])
                dv_(lambda e: e.tensor_tensor(out=p["u2"][:], in0=p["ms0"][:], in1=p["r"][:], op=ALU.mult), ["ms0", "r"], ["u2"])
                dv_(lambda e: e.tensor_copy(out=p["ni"][:], in_=p["u2"][:]), ["u2"], ["ni"])
                dv_(lambda e: e.tensor_tensor(out=p["u1"][:], in0=lre, in1=lre, op=ALU.mult), ["par", "nr"], ["u1"])
                dv_(lambda e: e.tensor_tensor(out=p["u2"][:], in0=lim, in1=lim, op=ALU.mult), ["par", "ni"], ["u2"])
                dv_(lambda e: e.tensor_tensor(out=p["den"][:], in0=p["u1"][:], in1=p["u2"][:], op=ALU.add), ["u1", "u2"], ["den"])
                dv_(lambda e: e.reciprocal(out=p["inv"][:], in_=p["den"][:]), ["den"], ["inv"])
                dv_(lambda e: e.tensor_tensor(out=p["u1"][:], in0=p["nr"][:], in1=lre, op=ALU.mult), ["nr", "par", "den"], ["u1"])
                dv_(lambda e: e.tensor_tensor(out=p["u2"][:], in0=p["ni"][:], in1=lim, op=ALU.mult), ["ni", "par", "den"], ["u2"])
                dv_(lambda e: e.tensor_tensor(out=p["cr"][:], in0=p["u1"][:], in1=p["u2"][:], op=ALU.add), ["u1", "u2"], ["cr"])
                dv_(lambda e: e.tensor_tensor(out=p["cr"][:], in0=p["cr"][:], in1=p["inv"][:], op=ALU.mult), ["cr", "inv"], ["cr"])
                dv_(lambda e: e.tensor_tensor(out=p["u1"][:], in0=p["ni"][:], in1=lre, op=ALU.mult), ["ni", "par", "cr"], ["u1"])
                dv_(lambda e: e.tensor_tensor(out=p["u2"][:], in0=p["nr"][:], in1=lim, op=ALU.mult), ["nr", "par", "cr"], ["u2"])
                dv_(lambda e: e.tensor_tensor(out=p["ci"][:], in0=p["u1"][:], in1=p["u2"][:], op=ALU.subtract), ["u1", "u2"], ["ci"])
                dv_(lambda e: e.tensor_tensor(out=p["ci"][:], in0=p["ci"][:], in1=p["inv"][:], op=ALU.mult), ["ci", "inv"], ["ci"])

                def s5_tabA(q, g, si):
                    ts = slice(q * 512, (q + 1) * 512)
                    fcol = p["f"][:, g:g + 1]
                    raw, msin, mcos = raw_[si], msin_[si], mcos_[si]
                    kr, ks, kc_ = [("s5_" + n, si) for n in ("rawt", "msin", "mcos")]
                    I("act", lambda e, fcol=fcol, raw=raw: e.activation(out=raw[:], in_=iota[:, ts], func=AF.Identity, scale=fcol),
                      R=["s5_iota", "s5_f"], W=[kr])
                    I("act", lambda e, fcol=fcol, msin=msin: e.activation(out=msin[:], in_=iota[:, ts], func=AF.Identity, scale=fcol,
                                                                       bias=magic[:]),
                      R=["s5_iota", "s5_f", "s5_magic"], W=[ks])
                def s5_tabB(q, g, si):
                    ts = slice(q * 512, (q + 1) * 512)
                    fcol = p["f"][:, g:g + 1]
                    raw, msin, mcos = raw_[si], msin_[si], mcos_[si]
                    kr, ks, kc_ = [("s5_" + n, si) for n in ("rawt", "msin", "mcos")]
                    I("dve", lambda e, msin=msin, raw=raw: e.scalar_tensor_tensor(out=msin[:], in0=msin[:], scalar=MAGIC, in1=raw[:],
                                                                             op0=ALU.subtract, op1=ALU.subtract),
                      R=[kr, ks], W=[ks])
                    I("dve", lambda e, msin=msin, mcos=mcos: e.scalar_tensor_tensor(out=mcos[:], in0=msin[:], scalar=-1.0, in1=msin[:],
                                                                               op0=ALU.mult, op1=ALU.max),
                      R=[ks], W=[kc_])
                def s5_tabC(q, g, si):
                    ts = slice(q * 512, (q + 1) * 512)
                    fcol = p["f"][:, g:g + 1]
                    raw, msin, mcos = raw_[si], msin_[si], mcos_[si]
                    kr, ks, kc_ = [("s5_" + n, si) for n in ("rawt", "msin", "mcos")]
                    I("act", lambda e, msin=msin: e.activation(out=msin[:], in_=msin[:], func=AF.Sin, scale=-TWO_PI),
                      R=[ks], W=[ks])
                    I("act", lambda e, mcos=mcos: e.activation(out=mcos[:], in_=mcos[:], func=AF.Sin, scale=-TWO_PI, bias=halfpi[:]),
                      R=[kc_, "negpi"], W=[kc_])

                def s5_tab(q, g, si):
                    s5_tabA(q, g, si)
                    s5_tabB(q, g, si)
                    s5_tabC(q, g, si)
                order = [(q_, 8 * ct_ + g__) for ct_ in range(DC) for q_ in range(NT) for g__ in range(8)]
                s5_tab(order[0][0], order[0][1], 0)
                for ct in range(DC):
                    gs = slice(8 * ct, 8 * ct + 8)
                    cs_ = slice(128 * ct, 128 * ct + 128)
                    cb_ = ct % 2
                    bre, bim, cst1, cst2 = bre2[cb_], bim2[cb_], cst12[cb_], cst22[cb_]
                    I("sp", lambda e: e.dma_start(out=bre[:], in_=s5b_d[j5, 0][:, cs_]), W=["s5_bre"], dma=("s5_bre", cb_))
                    I("sp", lambda e: e.dma_start(out=bim[:], in_=s5b_d[j5, 1][:, cs_]), W=["s5_bim"], dma=("s5_bim", cb_))
                    I("sp", lambda e: e.dma_start(out=cst1[:], in_=s5c_d[j5, 0][:, cs_]), W=["s5_cst1"], dma=("s5_cst1", cb_))
                    I("sp", lambda e: e.dma_start(out=cst2[:], in_=s5c_d[j5, 1][:, cs_]), W=["s5_cst2"], dma=("s5_cst2", cb_))

                    def bc(t):
                        return t[0:64, gs].unsqueeze(2).to_broadcast([64, 8, 16])

                    def v3(t):
                        return t[:].rearrange("p (g c) -> p g c", g=8)
                    b_re = bre[:].rearrange("p (g c) -> p g c", g=8)
                    b_im = bim[:].rearrange("p (g c) -> p g c", g=8)
                    dv_(lambda e: e.tensor_tensor(out=v3(z["t1"]), in0=b_re, in1=bc(p["cr"]), op=ALU.mult), ["bre", "cr"], ["t1"])
                    dv_(lambda e: e.tensor_tensor(out=v3(z["t2"]), in0=b_im, in1=bc(p["ci"]), op=ALU.mult), ["bim", "ci"], ["t2"])
                    dv_(lambda e: e.tensor_tensor(out=z["zre"][:], in0=z["t1"][:], in1=z["t2"][:], op=ALU.subtract), ["t1", "t2", "wA", "wA2"], ["zre"])
                    dv_(lambda e: e.tensor_scalar(out=z["nzre"][:], in0=z["zre"][:], scalar1=-1.0, scalar2=None, op0=ALU.mult), ["zre", "wA2"], ["nzre"])
                    dv_(lambda e: e.tensor_tensor(out=v3(z["t1"]), in0=b_im, in1=bc(p["cr"]), op=ALU.mult), ["bim", "cr", "zre"], ["t1"])
                    dv_(lambda e: e.tensor_tensor(out=v3(z["t2"]), in0=b_re, in1=bc(p["ci"]), op=ALU.mult), ["bre", "ci", "zre"], ["t2"])
                    dv_(lambda e: e.tensor_tensor(out=z["zim"][:], in0=z["t1"][:], in1=z["t2"][:], op=ALU.add), ["t1", "t2", "wA", "wA2"], ["zim"])
                    pT = pb[6]
                    id64 = ident[0:64, 0:64]
                    for q, src in enumerate(["zre", "zim", "zim", "nzre"]):
                        I("pe", lambda e, q=q, src=src: e.transpose(out=pT[:, q * 64:(q + 1) * 64], in_=z[src][:], identity=id64),
                          R=["s5_" + src, "ident"], W=["pb6"])
                    for g_ in range(8):
                        I("dve", lambda e, g_=g_: e.tensor_scalar(out=wA[:, g_, :], in0=pT[:, 0:128], scalar1=kk[:, 2 + g_:3 + g_],
                                                                 scalar2=None, op0=ALU.mult),
                          R=["pb6", "s5_kk"], W=["s5_wA"])
                        I("dve", lambda e, g_=g_: e.tensor_scalar(out=wA2[:, g_, :], in0=pT[:, 128:256], scalar1=kk[:, 2 + g_:3 + g_],
                                                                 scalar2=None, op0=ALU.mult),
                          R=["pb6", "s5_kk"], W=["s5_wA2"])
                    I("dve", lambda e: e.memset(W1[:], 0.0), W=["s5_W1"])
                    I("dve", lambda e: e.memset(W2[:], 0.0), W=["s5_W2"])
                    for g_ in range(8):
                        gcs = slice(16 * g_, 16 * g_ + 16)
                        I("dve", lambda e, g_=g_, gcs=gcs: e.tensor_scalar(out=W1[:, g_, 16 * g_:16 * g_ + 16], in0=cst1[:, gcs],
                                                                          scalar1=kk[:, 0:1], scalar2=None, op0=ALU.mult),
                          R=["s5_cst1", "s5_kk"], W=["s5_W1"])
                        I("dve", lambda e, g_=g_, gcs=gcs: e.tensor_scalar(out=W2[:, g_, 16 * g_:16 * g_ + 16], in0=cst2[:, gcs],
                                                                          scalar1=kk[:, 1:2], scalar2=None, op0=ALU.mult),
                          R=["s5_cst2", "s5_kk"], W=["s5_W2"])
                    for q in range(NT):
                        ts = slice(q * 512, (q + 1) * 512)
                        py = pb[5]
                        for g_ in range(8):
                            g = 8 * ct + g_
                            n_it = itc[0]
                            si = n_it % 2
                            itc[0] += 1
                            if n_it + 1 < len(order):
                                s5_tabA(order[n_it + 1][0], order[n_it + 1][1], (n_it + 1) % 2)
                            raw, msin, mcos, bt, st, P1, P2 = raw_[si], msin_[si], mcos_[si], bt_[si], st_[si], P1_[si], P2_[si]
                            kr, ks, kc_, kb, kst, k1_, k2_ = [("s5_" + n, si) for n in ("rawt", "msin", "mcos", "bt", "st", "P1", "P2")]
                            pA, pA2 = pb[2 * si], pb[2 * si + 1]
                            kpA, kpA2 = "pb%d" % (2 * si), "pb%d" % (2 * si + 1)
                            I("pe", lambda e, g_=g_, pA=pA: e.matmul(pA[:], lhsT=wA[:, g_, :], rhs=hT[:, ct, ts], start=True, stop=True),
                              R=["s5_wA", ("hT", ct, q)], W=[kpA])
                            I("pe", lambda e, g_=g_, pA2=pA2: e.matmul(pA2[:], lhsT=wA2[:, g_, :], rhs=hT[:, ct, ts], start=True, stop=True),
                              R=["s5_wA2", ("hT", ct, q)], W=[kpA2])
                            k1 = nxt("tA")
                            k2 = nxt("tB")
                            I("dve", lambda e, k1=k1, mcos=mcos, pA=pA: e.tensor_tensor(out=tA[k1][:], in0=mcos[:], in1=pA[:], op=ALU.mult),
                              R=[kc_, kpA], W=[("tA", k1)])
                            I("dve", lambda e, k2=k2, msin=msin, pA2=pA2: e.tensor_tensor(out=tB[k2][:], in0=msin[:], in1=pA2[:], op=ALU.mult),
                              R=[ks, kpA2], W=[("tB", k2)])
                            I("dve", lambda e, k1=k1, k2=k2, bt=bt: e.tensor_tensor(out=bt[:], in0=tA[k1][:], in1=tB[k2][:], op=ALU.add),
                              R=[("tA", k1), ("tB", k2)], W=[kb])
                            if n_it + 1 < len(order):
                                s5_tabB(order[n_it + 1][0], order[n_it + 1][1], (n_it + 1) % 2)
                                s5_tabC(order[n_it + 1][0], order[n_it + 1][1], (n_it + 1) % 2)
                            if q == 0:
                                I("dve", lambda e, g=g, st=st, bt=bt: e.tensor_tensor_scan(
                                    out=st[:], data0=p["r"][:, g:g + 1].to_broadcast([128, 512]), data1=bt[:], initial=0.0,
                                    op0=ALU.mult, op1=ALU.add),
                                  R=["s5_r", kb], W=[kst])
                            else:
                                I("dve", lambda e, g=g, st=st, bt=bt: e.tensor_tensor_scan(
                                    out=st[:], data0=p["r"][:, g:g + 1].to_broadcast([128, 512]), data1=bt[:],
                                    initial=state[:, g:g + 1], op0=ALU.mult, op1=ALU.add),
                                  R=["s5_r", kb, ("s5_state", g)], W=[kst])
                            I("dve", lambda e, g=g, st=st: e.tensor_copy(out=state[:, g:g + 1], in_=st[:, 511:512]),
                              R=[kst], W=[("s5_state", g)])
                            I("dve", lambda e, mcos=mcos, st=st, P1=P1: e.tensor_tensor(out=P1[:], in0=mcos[:], in1=st[:], op=ALU.mult),
                              R=[kc_, kst], W=[k1_])
                            I("dve", lambda e, msin=msin, st=st, P2=P2: e.tensor_tensor(out=P2[:], in0=msin[:], in1=st[:], op=ALU.mult),
                              R=[ks, kst], W=[k2_])
                            I("pe", lambda e, g_=g_, P1=P1: e.matmul(py[:], lhsT=W1[:, g_, :], rhs=P1[:], start=(g_ == 0), stop=False),
                              R=["s5_W1", k1_], W=["pb5"])
                            I("pe", lambda e, g_=g_, P2=P2: e.matmul(py[:], lhsT=W2[:, g_, :], rhs=P2[:], start=False, stop=(g_ == 7)),
                              R=["s5_W2", k2_], W=["pb5"])
                        I("dve", lambda e: e.scalar_tensor_tensor(out=yv[:], in0=hT[:, ct, ts], scalar=dv[:, ct:ct + 1], in1=py[:],
                                                                 op0=ALU.mult, op1=ALU.add),
                          R=[("hT", ct, q), "s5_dv", "pb5"], W=["s5_yv"])
                        I("act", lambda e: e.activation(out=zz[:], in_=yv[:], func=AF.Square, scale=math.sqrt(GELU_C1)),
                          R=["s5_yv"], W=["s5_zz"])
                        I("dve", lambda e: e.scalar_tensor_tensor(out=zz[:], in0=zz[:], scalar=1.0, in1=yv[:], op0=ALU.add, op1=ALU.mult),
                          R=["s5_zz", "s5_yv"], W=["s5_zz"])
                        I("act", lambda e: e.activation(out=zz[:], in_=zz[:], func=AF.Sigmoid, scale=2.0 * GELU_C0),
                          R=["s5_zz"], W=["s5_zz"])
                        I("dve", lambda e: e.tensor_tensor(out=gT[:, ct, ts], in0=yv[:], in1=zz[:], op=ALU.mult),
                          R=["s5_zz", "s5_yv"], W=[("s5_gT", ct, q)])
                for oc in range(DC):
                    s = load_w256(s5w_d[j5, oc])
                    for tt in range(NT):
                        ts = slice(tt * 512, (tt + 1) * 512)
                        pv, pgt, kv, kg = mm_pair(s, gT, lambda k, tt: ("s5_gT", k, tt), tt)
                        k2 = nxt("tB")
                        I("act", lambda e, k2=k2: e.activation(out=tB[k2][:], in_=pgt[:], func=AF.Sigmoid, bias=dv[:, 16 + oc:17 + oc]),
                          R=[kg, "s5_dv"], W=[("tB", k2)])
                        k1 = nxt("tA")
                        I("dve", lambda e, k1=k1, k2=k2: e.scalar_tensor_tensor(out=tA[k1][:], in0=pv[:], scalar=dv[:, 8 + oc:9 + oc],
                                                                               in1=tB[k2][:], op0=ALU.add, op1=ALU.mult),
                          R=[kv, "s5_dv", ("tB", k2)], W=[("tA", k1)])
                        I("dve", lambda e, k1=k1: e.tensor_tensor(out=xT[:, oc, ts], in0=xT[:, oc, ts], in1=tA[k1][:], op=ALU.add),
                          R=[("tA", k1), ("xT", oc, tt)], W=[("xT", oc, tt)])
                P.release([k for k in list(P.res) if (k[0] if isinstance(k, tuple) else k).startswith("s5_")])

        for sq_i in range(NS):
            with ExitStack() as sc:
                xin = [sb(nc, sc, "xin%d" % i, [128, D], F32) for i in range(2)]
                for tb in range(NB):
                    s = nxt("xin")
                    I("sp", lambda e, s=s, tb=tb: e.dma_start(out=xin[s][:], in_=x_d[sq_i, tb * 128:(tb + 1) * 128, :]),
                      W=[("xin", s)], dma=("xin", s))
                    for half in range(2):
                        pt = pb[6]
                        for cc in range(4):
                            c = half * 4 + cc
                            I("pe", lambda e, s=s, c=c, cc=cc: e.transpose(
                                out=pt[:, cc * 128:(cc + 1) * 128], in_=xin[s][:, c * 128:(c + 1) * 128],
                                identity=ident[:]),
                              R=[("xin", s), "ident"], W=["pb6"])
                        I("act", lambda e, half=half, tb=tb: e.activation(
                            out=xT[:, half * 4:half * 4 + 4, tb * 128:(tb + 1) * 128],
                            in_=pt[:].rearrange("p (c t) -> p c t", c=4), func=AF.Copy),
                          R=["pb6"], W=[("xT", half * 4 + cc, tb // 4) for cc in range(4)])
                P.release_prefix({"xin"})
            i5 = 0
            i0 = 0
            i2 = 0
            for li, kind in enumerate(cfg.layers):
                if cfg.do_ffn:
                    ffn(2 * li, 3 * li)
                if kind == 1:
                    s5_mixer(i5, 3 * li + 1)
                    i5 += 1
                if kind == 0:
                    ssd_mixer(i0, 3 * li + 1)
                    i0 += 1
                if kind == 2:
                    dsa_mixer(i2, 3 * li + 1)
                    i2 += 1
                if cfg.do_ffn:
                    ffn(2 * li + 1, 3 * li + 2)
            with ExitStack() as sc:
                xin = [sb(nc, sc, "xin%d" % i, [128, D], F32) for i in range(2)]
                for tt in range(NT):
                    ts = slice(tt * 512, (tt + 1) * 512)
                    rmsnorm_to(lambda c, ts=ts: xT[:, c, ts], 3 * depth, lambda c, tt=tt: ("xT", c, tt), tt)
                    for q in range(4):
                        tb = tt * 4 + q
                        s = nxt("xin")
                        for half in range(2):
                            pt = pb[6]
                            for cc in range(4):
                                c = half * 4 + cc
                                I("pe", lambda e, c=c, cc=cc, tb=tb: e.transpose(
                                    out=pt[:, cc * 128:(cc + 1) * 128], in_=xT[:, c, tb * 128:(tb + 1) * 128],
                                    identity=ident[:]),
                                  R=[("xT", c, tt), "ident"], W=["pb6"])
                            I("act", lambda e, half=half, s=s: e.activation(
                                out=xin[s][:, half * 512:(half + 1) * 512], in_=pt[:], func=AF.Copy),
                              R=["pb6"], W=[("xin", s)])
                        I("sp", lambda e, s=s, tb=tb: e.dma_start(out=y_d[sq_i, tb * 128:(tb + 1) * 128, :], in_=xin[s][:]),
                          R=[("xin", s)], dma=("yout", s))
                P.release_prefix({"xin"})
        P.finish()
        print("instructions:", P.ninst, "sems:", P.nsem)
    return nc


def w256_layout(w, ncol_blocks, col_a, col_b):
    out = np.empty((ncol_blocks, 128, DC, 256), np.float32)
    wk = w.reshape(DC, 128, -1)
    for b in range(ncol_blocks):
        out[b, :, :, 0:128] = wk[:, :, col_a + b * 128: col_a + (b + 1) * 128].transpose(1, 0, 2)
        out[b, :, :, 128:256] = wk[:, :, col_b + b * 128: col_b + (b + 1) * 128].transpose(1, 0, 2)
    return out.reshape(ncol_blocks, 128, DC * 256)


def prep_ssd(inp, cfg, m):
    f32 = np.float32
    n0 = cfg.n_ssd
    win = np.empty((n0, 24, 128, DC, 256), f32)
    wdt = np.empty((n0, 128, DC * 32), f32)
    wout = np.empty((n0, DC, 128, 16 * 128), f32)
    cwb = np.empty((n0, 128, 32, 5), f32)
    hp = np.empty((n0, 128, 112), f32)
    for j in range(n0):
        W = inp["ssd_in_proj"][j]
        wk = W.reshape(DC, 128, -1)
        for b in range(24):
            win[j, b] = wk[:, :, 256 * b:256 * (b + 1)].transpose(1, 0, 2)
        wdt[j] = wk[:, :, 6144:6176].transpose(1, 0, 2).reshape(128, DC * 32)
        Wo = inp["ssd_out_proj"][j].reshape(16, 128, DC, 128)
        wout[j] = Wo.transpose(2, 1, 0, 3).reshape(DC, 128, 16 * 128)
        cw = inp["ssd_conv_w"][j].reshape(4, 32, 128)
        cwb[j, :, :, 0:4] = cw.transpose(2, 1, 0)
        cwb[j, :, :, 4] = inp["ssd_conv_b"][j].reshape(32, 128).T
        hp[j, :, 0:32] = np.broadcast_to(inp["ssd_dt_bias"][j][None, :], (128, 32))
        hp[j, :, 32:64] = np.broadcast_to(inp["ssd_a_log"][j][None, :], (128, 32))
        hp[j, :, 64:96] = np.broadcast_to(inp["ssd_d"][j][None, :], (128, 32))
        hp[j, :, 96:112] = inp["ssd_gate_norm"][j].reshape(16, 128).T
    kc = np.zeros((128, 768), f32)
    tri = (np.arange(128)[:, None] <= np.arange(128)[None, :])
    kc[:, 0:128] = tri.astype(f32)
    cb = np.where(np.arange(128)[:, None] > np.arange(128)[None, :], -30000.0, 0.0).astype(f32)
    kc[:, 128:640] = np.tile(cb, (1, 4))
    kc[:, 640:768] = 1.0
    m.update({"ssd_win": win.reshape(n0, 24, 128, DC * 256), "ssd_wdt": wdt, "ssd_wout": wout,
              "ssd_cw": cwb.reshape(n0, 128, 160), "ssd_hp": hp, "ssd_k": kc})


def w256_pairs(wa, wb):
    out = np.empty((128, DC, 256), np.float32)
    out[:, :, 0:128] = wa.reshape(DC, 128, 128).transpose(1, 0, 2)
    out[:, :, 128:256] = wb.reshape(DC, 128, 128).transpose(1, 0, 2)
    return out.reshape(128, DC * 256)


def prep_dsa(inp, cfg, m):
    f32 = np.float32
    n2 = cfg.n_dsa
    L = cfg.L
    win = np.empty((n2, 14, 128, DC * 256), f32)
    wvw = np.empty((n2, 128, DC * 72), f32)
    wo = np.empty((n2, 4, 128, DC * 256), f32)
    perm = np.concatenate([np.arange(32, 64), np.arange(0, 32)])
    perm128 = np.concatenate([perm, 64 + perm])
    for j in range(n2):
        W = inp["dsa_in_proj"][j]
        Wq, Wk, Wv = W[:, 0:1024], W[:, 1024:1088], W[:, 1088:1152]
        Wqi, Wki, Wwi = W[:, 1152:1664], W[:, 1664:1728], W[:, 1728:1736]
        blocks = [Wq[:, c * 128:(c + 1) * 128] for c in range(8)]
        blocks.append(np.concatenate([Wk, Wk], 1))
        blocks += [Wqi[:, c * 128:(c + 1) * 128] for c in range(4)]
        blocks.append(np.concatenate([Wki, Wki], 1))
        for b, A in enumerate(blocks):
            win[j, b] = w256_pairs(A, A[:, perm128])
        vw = np.concatenate([Wv, Wwi], 1)
        wvw[j] = vw.reshape(DC, 128, 72).transpose(1, 0, 2).reshape(128, DC * 72)
        Wo = inp["dsa_out_proj"][j]
        for b in range(4):
            wo[j, b] = w256_pairs(Wo[:, 256 * b:256 * b + 128], Wo[:, 256 * b + 128:256 * b + 256])
    inv = (10000.0 ** (-np.arange(32, dtype=np.float64) / 32.0))
    ang = np.arange(L, dtype=np.float64)[None, :] * inv[np.arange(128) % 32][:, None]
    sgn = np.where((np.arange(128) % 64) < 32, -1.0, 1.0)[:, None]
    rope = np.stack([np.cos(ang), np.sin(ang) * sgn], 0).astype(f32)
    kd = np.zeros((128, 640), f32)
    kd[:, 0:128] = np.where(np.arange(128)[None, :] > np.arange(128)[:, None], -3.0e30, 0.0)
    kd[:, 128:640] = np.tile(np.eye(128, dtype=f32), (1, 4))
    m.update({"dsa_win": win, "dsa_wvw": wvw, "dsa_wo": wo, "dsa_rope": rope, "dsa_k": kd})


def prep_common(inp, cfg):
    depth = cfg.depth
    f32 = np.float32
    g = []
    for i in range(depth):
        g += [inp["ffn1_norm"][i], inp["mix_norm"][i], inp["ffn2_norm"][i]]
    g.append(inp["final_norm"])
    g = np.stack(g, 0).astype(f32)
    gains = np.ascontiguousarray(g.reshape(-1, DC, 128).transpose(2, 0, 1))
    wi = np.empty((2 * depth, FC, 128, DC * 256), f32)
    wo = np.empty((2 * depth, 2, DC, 128, 11 * 128), f32)
    for i in range(depth):
        for which, (kin, kout) in enumerate((("ffn1_w_in", "ffn1_w_out"), ("ffn2_w_in", "ffn2_w_out"))):
            w_in = inp[kin][i]
            w_out = inp[kout][i]
            wi[2 * i + which] = w256_layout(w_in, FC, 0, FFN)
            b = w_out.reshape(2, 11, 128, DC, 128)
            b = b.transpose(0, 3, 2, 1, 4)
            wo[2 * i + which] = b.reshape(2, DC, 128, 11 * 128)
    m = {"gains": gains, "ident": np.eye(128, dtype=f32), "ffn_wi": wi, "ffn_wo": wo}
    if cfg.n_s5:
        n5 = cfg.n_s5
        par = np.empty((n5, 3, 128, 64), f32)
        sb_ = np.empty((n5, 2, 64, 1024), f32)
        sc_ = np.empty((n5, 2, 128, 1024), f32)
        sd_ = np.empty((n5, 128, 24), f32)
        sw_ = np.empty((n5, DC, 128, DC * 256), f32)
        for j in range(n5):
            lre = inp["s5_lam_re"][j].T
            lim = inp["s5_lam_im"][j].T
            lst = np.broadcast_to(inp["s5_log_step"][j][None, :], (64, 64))
            for a, t in enumerate((lre, lim, lst)):
                par[j, a, 0:64] = t
                par[j, a, 64:128] = t
            sb_[j, 0] = inp["s5_b_re"][j].transpose(1, 0, 2).reshape(64, 1024)
            sb_[j, 1] = inp["s5_b_im"][j].transpose(1, 0, 2).reshape(64, 1024)
            cre = inp["s5_c_re"][j].transpose(2, 0, 1).reshape(64, 1024)
            cim = inp["s5_c_im"][j].transpose(2, 0, 1).reshape(64, 1024)
            sc_[j, 0, 0:64] = cre
            sc_[j, 0, 64:128] = cim
            sc_[j, 1, 0:64] = cim
            sc_[j, 1, 64:128] = cre
            sd_[j, :, 0:8] = inp["s5_d"][j].reshape(DC, 128).T
            sd_[j, :, 8:24] = inp["s5_glu_b"][j].reshape(16, 128).T
            sw_[j] = w256_layout(inp["s5_glu_w"][j], DC, 0, D)
        kk = np.zeros((128, 16), f32)
        kk[0:64, 0] = 1.0
        kk[64:128, 0] = -1.0
        kk[:, 1] = -1.0
        for g_ in range(8):
            kk[16 * g_:16 * g_ + 16, 2 + g_] = 1.0
        m.update({"s5_par": par, "s5_b": sb_, "s5_c": sc_, "s5_k": kk, "s5_dv": sd_, "s5_glu": sw_,
                  "iota16": np.broadcast_to(np.arange(cfg.L, dtype=np.int16)[None, :], (128, cfg.L)).copy()})
    if cfg.n_ssd:
        prep_ssd(inp, cfg, m)
    if cfg.n_dsa:
        prep_dsa(inp, cfg, m)
    return m


def kernel(**inputs):
    cfg = Cfg()
    inp = {k: np.asarray(v) for k, v in inputs.items()}
    common = prep_common(inp, cfg)
    x = inp["x"].astype(np.float32)
    B = x.shape[0]
    per = B // N_CORES
    nc = build(cfg)
    in_maps = []
    for c in range(N_CORES):
        m = dict(common)
        m["x"] = np.ascontiguousarray(x[c * per:(c + 1) * per])
        in_maps.append(m)
    res = run_bass_kernel_spmd(nc, in_maps, core_ids=list(range(N_CORES)))
    out = np.concatenate([r["y"] for r in res.results], axis=0)
    return out.astype(np.float32)
```
